# Optimizing a Trainium2 kernel written in Bass

```python
import jax, jax.numpy as jnp
from jax import lax
import numpy as np

D_MODEL = 2048
BATCH = 16
SEQ = 256
DEPTH = 2
DEC_BATCH = 8
DEC_SEQ = 4096
PAST_LEN = 512

GRID_W = 64
N_EVEN = (DEPTH + 1) // 2
N_ODD = DEPTH // 2
EPS = 1e-6
ATTN_WIDTH = D_MODEL // 2
HEAD_DIM = 64
N_HEADS_A = ATTN_WIDTH // HEAD_DIM
N_KV_A = max(1, N_HEADS_A // 8)
GQA_GROUP = N_HEADS_A // N_KV_A
WINDOW = 128
BLOCK = 128
ROPE_BASE = 10000.0
HY_WIDTH = D_MODEL - ATTN_WIDTH
HY_ORDER = 2
HY_SHORT = 3
HY_BANDS = 16
HY_EMB = 2 * HY_BANDS + 1
HY_HIDDEN = 64
LRU_WIDTH = D_MODEL
N_HEADS_C = 8
LRU_BLOCK = LRU_WIDTH // N_HEADS_C
LRU_CONV = 4
LRU_C = 8.0

Q_COLS = N_HEADS_A * HEAD_DIM
KV_COLS = N_KV_A * HEAD_DIM
EVEN_SPLITS = (Q_COLS, KV_COLS, KV_COLS, ATTN_WIDTH, (HY_ORDER + 1) * HY_WIDTH, HY_WIDTH)
EVEN_IN_COLS = Q_COLS + 2 * KV_COLS + ATTN_WIDTH + (HY_ORDER + 2) * HY_WIDTH
EVEN_OUT_ROWS = ATTN_WIDTH + HY_WIDTH

kernel_name = 'hybrid_diffusion_attn_hyena_rglru_step'

F32 = jnp.float32


def _split_cols(x, sizes):
    idx = np.cumsum(sizes)[:-1].tolist()
    return jnp.split(x, idx, axis=-1)


def _rmsnorm(x, g):
    xf = x.astype(F32)
    y = xf * lax.rsqrt(jnp.mean(xf * xf, axis=-1, keepdims=True) + EPS)
    return (y * g.astype(F32)).astype(x.dtype)


def _modulation(cvec, w, b):
    m = jax.nn.silu(cvec) @ w + b
    return jnp.split(m[:, None, :], 3, axis=-1)


def _depthwise_conv(x, w, b, left):
    K = w.shape[0]
    L = x.shape[1]
    xp = jnp.pad(x, ((0, 0), (left, K - 1 - left), (0, 0)))
    out = xp[:, 0:L] * w[0]
    for j in range(1, K):
        out = out + xp[:, j:j + L] * w[j]
    return out + b


def _axial_rope(x):
    L = x.shape[1]
    rows = L // GRID_W
    row = jnp.repeat(jnp.arange(rows), GRID_W).astype(F32)
    col = jnp.tile(jnp.arange(GRID_W), rows).astype(F32)
    half = HEAD_DIM // 2
    nf = half // 2
    inv = ROPE_BASE ** (-jnp.arange(nf, dtype=F32) / nf)
    shape = (1, L) + (1,) * (x.ndim - 3) + (nf,)

    def rot(xa, pos):
        ang = pos[:, None] * inv[None, :]
        cos = jnp.cos(ang).reshape(shape).astype(x.dtype)
        sin = jnp.sin(ang).reshape(shape).astype(x.dtype)
        x1, x2 = xa[..., :nf], xa[..., nf:]
        return jnp.concatenate([x1 * cos - x2 * sin, x2 * cos + x1 * sin], axis=-1)

    return jnp.concatenate([rot(x[..., :half], row), rot(x[..., half:], col)], axis=-1)


def _sink_softmax(s, sink):
    s_sink = jnp.broadcast_to(sink.astype(F32)[None, :, :, None, None], s.shape[:-1] + (1,))
    return jax.nn.softmax(jnp.concatenate([s, s_sink], axis=-1), axis=-1)[..., :-1]


def _context_attention(q, k, v, sink):
    T = q.shape[1]
    scale = HEAD_DIM ** -0.5

    def block(i):
        qb = lax.dynamic_slice_in_dim(q, i * BLOCK, BLOCK, axis=1)
        s = jnp.einsum('bqkgd,bskd->bkgqs', qb, k).astype(F32) * scale
        p = _sink_softmax(s, sink).astype(v.dtype)
        return jnp.einsum('bkgqs,bskd->bqkgd', p, v)

    out = lax.map(block, jnp.arange(T // BLOCK))
    return jnp.moveaxis(out, 0, 1).reshape(q.shape)


def _latent_attention(q, k, v, k_ctx, v_ctx, sink):
    L = q.shape[1]
    Tc = k_ctx.shape[1]
    span = BLOCK + 2 * WINDOW
    scale = HEAD_DIM ** -0.5
    neg = jnp.finfo(F32).min
    kp = jnp.pad(k, ((0, 0), (WINDOW, WINDOW), (0, 0), (0, 0)))
    vp = jnp.pad(v, ((0, 0), (WINDOW, WINDOW), (0, 0), (0, 0)))

    def block(i):
        start = i * BLOCK
        qb = lax.dynamic_slice_in_dim(q, start, BLOCK, axis=1)
        kb = lax.dynamic_slice_in_dim(kp, start, span, axis=1)
        vb = lax.dynamic_slice_in_dim(vp, start, span, axis=1)
        qpos = start + jnp.arange(BLOCK)
        kpos = start - WINDOW + jnp.arange(span)
        valid = (jnp.abs(qpos[:, None] - kpos[None, :]) <= WINDOW) & (kpos >= 0)[None, :] & (kpos < L)[None, :]
        s_loc = jnp.einsum('bqkgd,bskd->bkgqs', qb, kb).astype(F32) * scale
        s_loc = jnp.where(valid, s_loc, neg)
        s_ctx = jnp.einsum('bqkgd,bskd->bkgqs', qb, k_ctx).astype(F32) * scale
        p = _sink_softmax(jnp.concatenate([s_ctx, s_loc], axis=-1), sink).astype(v.dtype)
        return (jnp.einsum('bkgqs,bskd->bqkgd', p[..., :Tc], v_ctx)
                + jnp.einsum('bkgqs,bskd->bqkgd', p[..., Tc:], vb))

    out = lax.map(block, jnp.arange(L // BLOCK))
    return jnp.moveaxis(out, 0, 1).reshape(q.shape)


def _hyena_filters(L, w1, b1, w2, b2, w3, decay):
    pos = jnp.arange(L, dtype=F32)
    t = pos / L
    freqs = jnp.linspace(1e-4, HY_BANDS - 1, HY_BANDS, dtype=F32)
    ang = 2.0 * jnp.pi * t[:, None] * freqs[None, :]
    z = jnp.concatenate([t[:, None], jnp.cos(ang), -jnp.sin(ang)], axis=-1)
    h = jnp.sin(z @ w1.astype(F32) + b1.astype(F32))
    h = jnp.sin(h @ w2.astype(F32) + b2.astype(F32))
    h = (h @ w3.astype(F32)).reshape(L, 2, HY_ORDER, HY_WIDTH)
    h = h * jnp.exp(-t[:, None, None, None] * jnp.abs(decay.astype(F32))[None])
    taps = jnp.concatenate([h[:, 0], jnp.zeros((1, HY_ORDER, HY_WIDTH), F32), h[:0:-1, 1]], axis=0)
    taps = taps * lax.rsqrt(jnp.sum(taps * taps, axis=0, keepdims=True) + EPS)
    return jnp.fft.rfft(taps, axis=0)


def _hyena(u, short_w, short_b, w1, b1, w2, b2, w3, decay, hy_bias):
    L = u.shape[1]
    uc = _depthwise_conv(u, short_w, short_b, left=(HY_SHORT - 1) // 2)
    x1, x2, z = jnp.split(uc, 3, axis=-1)
    hf = _hyena_filters(L, w1, b1, w2, b2, w3, decay)
    zf = z.astype(F32)
    for n, xn in enumerate((x1, x2)):
        conv = jnp.fft.irfft(jnp.fft.rfft(zf, n=2 * L, axis=1) * hf[None, :, n], n=2 * L, axis=1)[:, :L]
        zf = xn.astype(F32) * (conv + hy_bias[n].astype(F32) * zf)
    return zf.astype(u.dtype)


def _even_mixer(h, w_in, w_out, sink, short_w, short_b, w1, b1, w2, b2, w3, decay, hy_bias, ctx_k=None, ctx_v=None):
    B, L, _ = h.shape
    q, k, v, g_attn, hy_in, g_hy = _split_cols(h @ w_in, EVEN_SPLITS)
    q = q.reshape(B, L, N_KV_A, GQA_GROUP, HEAD_DIM)
    k = k.reshape(B, L, N_KV_A, HEAD_DIM)
    v = v.reshape(B, L, N_KV_A, HEAD_DIM)
    sink = sink.reshape(N_KV_A, GQA_GROUP)
    if ctx_k is None:
        att = _context_attention(q, k, v, sink)
        kv_out = (k, v)
    else:
        att = _latent_attention(_axial_rope(q), _axial_rope(k), v, ctx_k, ctx_v, sink)
        kv_out = None
    att = att.reshape(B, L, ATTN_WIDTH)
    hy = _hyena(hy_in, short_w, short_b, w1, b1, w2, b2, w3, decay, hy_bias)
    mixed = jnp.concatenate([att * jax.nn.silu(g_attn), hy * jax.nn.silu(g_hy)], axis=-1)
    return mixed @ w_out, kv_out


def _lru_combine(left, right):
    a1, b1 = left
    a2, b2 = right
    return a1 * a2, a2 * b1 + b2


def _odd_mixer(h, w_in, w_out, conv_w, conv_b, wa, ba, wx, bx, lam, h0=None):
    B, L, _ = h.shape
    xb, gate = jnp.split(h @ w_in, 2, axis=-1)
    xc = _depthwise_conv(xb, conv_w, conv_b, left=LRU_CONV // 2)
    xr = xc.reshape(B, L, N_HEADS_C, LRU_BLOCK)
    r = jax.nn.sigmoid((jnp.einsum('blhi,dhij->dblhj', xr, wa).reshape(2, B, L, LRU_WIDTH) + ba[:, None, None]).astype(F32))
    i_g = jax.nn.sigmoid((jnp.einsum('blhi,dhij->dblhj', xr, wx).reshape(2, B, L, LRU_WIDTH) + bx[:, None, None]).astype(F32))
    log_a = -LRU_C * r * jax.nn.softplus(-lam.astype(F32))[:, None, None]
    a = jnp.exp(log_a)
    b = jnp.sqrt(-jnp.expm1(2.0 * log_a)) * i_g * xc.astype(F32)[None]
    a = jnp.stack([a[0], a[1, :, ::-1]])
    b = jnp.stack([b[0], b[1, :, ::-1]])
    a_cum, hs = lax.associative_scan(_lru_combine, (a, b), axis=2)
    if h0 is not None:
        hs = hs + a_cum * h0.astype(F32)[:, :, None]
    final = hs[:, :, -1]
    y = (hs[0] + hs[1, :, ::-1]).astype(h.dtype)
    return (y * jax.nn.silu(gate)) @ w_out, final


def setup_inputs(seed: int = 0) -> dict:
    key = jax.random.key(seed)
    ks = jax.random.split(key, 32)

    def nrm(k, shape, scale):
        return jax.random.normal(k, shape, F32) * scale

    u = jax.random.uniform(ks[31], (N_ODD, 2, LRU_WIDTH), F32, 0.9, 0.999)
    s = u ** (1.0 / LRU_C)
    return {
        'x_prompt': nrm(ks[0], (BATCH, SEQ, D_MODEL), 1.0),
        'x_sample': nrm(ks[1], (DEC_BATCH, DEC_SEQ, D_MODEL), 1.0),
        'cache_k': nrm(ks[2], (DEC_BATCH, N_EVEN, PAST_LEN, N_KV_A, HEAD_DIM), 1.0),
        'cache_v': nrm(ks[3], (DEC_BATCH, N_EVEN, PAST_LEN, N_KV_A, HEAD_DIM), 1.0),
        'state_lru': nrm(ks[4], (DEC_BATCH, N_ODD, 2, LRU_WIDTH), 0.5),
        'c': nrm(ks[5], (DEC_BATCH, D_MODEL), 1.0),
        'c_ctx': nrm(ks[6], (D_MODEL,), 1.0),
        'mod_w': nrm(ks[7], (DEPTH, D_MODEL, 3 * D_MODEL), 0.5 * D_MODEL ** -0.5),
        'mod_b': nrm(ks[8], (DEPTH, 3 * D_MODEL), 0.02),
        'norm_g': 1.0 + nrm(ks[9], (DEPTH, D_MODEL), 0.05),
        'final_norm_g': 1.0 + nrm(ks[10], (D_MODEL,), 0.05),
        'a_w_in': nrm(ks[11], (N_EVEN, D_MODEL, EVEN_IN_COLS), D_MODEL ** -0.5),
        'a_w_out': nrm(ks[12], (N_EVEN, EVEN_OUT_ROWS, D_MODEL), EVEN_OUT_ROWS ** -0.5),
        'a_sink': nrm(ks[13], (N_EVEN, N_HEADS_A), 0.5),
        'hy_short_w': nrm(ks[14], (N_EVEN, HY_SHORT, 3 * HY_WIDTH), HY_SHORT ** -0.5),
        'hy_short_b': nrm(ks[15], (N_EVEN, 3 * HY_WIDTH), 0.02),
        'hy_w1': nrm(ks[16], (N_EVEN, HY_EMB, HY_HIDDEN), HY_EMB ** -0.5),
        'hy_b1': nrm(ks[17], (N_EVEN, HY_HIDDEN), 0.5),
        'hy_w2': nrm(ks[18], (N_EVEN, HY_HIDDEN, HY_HIDDEN), HY_HIDDEN ** -0.5),
        'hy_b2': nrm(ks[19], (N_EVEN, HY_HIDDEN), 0.5),
        'hy_w3': nrm(ks[20], (N_EVEN, HY_HIDDEN, 2 * HY_ORDER * HY_WIDTH), HY_HIDDEN ** -0.5),
        'hy_decay': jax.random.uniform(ks[21], (N_EVEN, 2, HY_ORDER, HY_WIDTH), F32, 3.07, 15.35),
        'hy_bias': nrm(ks[22], (N_EVEN, HY_ORDER, HY_WIDTH), 0.3),
        'c_w_in': nrm(ks[23], (N_ODD, D_MODEL, 2 * LRU_WIDTH), D_MODEL ** -0.5),
        'c_w_out': nrm(ks[24], (N_ODD, LRU_WIDTH, D_MODEL), LRU_WIDTH ** -0.5),
        'c_conv_w': nrm(ks[25], (N_ODD, LRU_CONV, LRU_WIDTH), LRU_CONV ** -0.5),
        'c_conv_b': nrm(ks[26], (N_ODD, LRU_WIDTH), 0.02),
        'c_wa': nrm(ks[27], (N_ODD, 2, N_HEADS_C, LRU_BLOCK, LRU_BLOCK), LRU_BLOCK ** -0.5),
        'c_ba': nrm(ks[28], (N_ODD, 2, LRU_WIDTH), 0.02),
        'c_wx': nrm(ks[29], (N_ODD, 2, N_HEADS_C, LRU_BLOCK, LRU_BLOCK), LRU_BLOCK ** -0.5),
        'c_bx': nrm(ks[30], (N_ODD, 2, LRU_WIDTH), 0.02),
        'c_lambda': jnp.log(s) - jnp.log1p(-s),
    }


def reference(x_prompt, x_sample, cache_k, cache_v, state_lru, c, c_ctx, mod_w, mod_b, norm_g, final_norm_g,
              a_w_in, a_w_out, a_sink, hy_short_w, hy_short_b, hy_w1, hy_b1, hy_w2, hy_b2, hy_w3, hy_decay, hy_bias,
              c_w_in, c_w_out, c_conv_w, c_conv_b, c_wa, c_ba, c_wx, c_bx, c_lambda):
    xp = x_prompt
    xs = x_sample
    new_k, new_v, new_s = [], [], []
    for l in range(DEPTH):
        j = l // 2
        sh_p, sc_p, g_p = _modulation(c_ctx[None, :], mod_w[l], mod_b[l])
        sh_s, sc_s, g_s = _modulation(c, mod_w[l], mod_b[l])
        hp = _rmsnorm(xp, norm_g[l]) * (1.0 + sc_p) + sh_p
        hs = _rmsnorm(xs, norm_g[l]) * (1.0 + sc_s) + sh_s
        if l % 2 == 0:
            ev = (a_w_in[j], a_w_out[j], a_sink[j], hy_short_w[j], hy_short_b[j], hy_w1[j], hy_b1[j],
                  hy_w2[j], hy_b2[j], hy_w3[j], hy_decay[j], hy_bias[j])
            out_p, (k_ctx, v_ctx) = _even_mixer(hp, *ev)
            out_s, _ = _even_mixer(hs, *ev, ctx_k=cache_k[:, j], ctx_v=cache_v[:, j])
            new_k.append(k_ctx)
            new_v.append(v_ctx)
        else:
            od = (c_w_in[j], c_w_out[j], c_conv_w[j], c_conv_b[j], c_wa[j], c_ba[j], c_wx[j], c_bx[j], c_lambda[j])
            out_p, st = _odd_mixer(hp, *od)
            out_s, _ = _odd_mixer(hs, *od, h0=jnp.moveaxis(state_lru[:, j], 1, 0))
            new_s.append(jnp.moveaxis(st, 0, 1).astype(xp.dtype))
        xp = xp + g_p * out_p
        xs = xs + g_s * out_s
    y_prompt = _rmsnorm(xp, final_norm_g)
    y_sample = _rmsnorm(xs, final_norm_g)
    new_cache_k = jnp.stack(new_k, axis=1)
    new_cache_v = jnp.stack(new_v, axis=1)
    new_state_lru = jnp.stack(new_s, axis=1)
    return (y_prompt, y_sample, new_cache_k, new_cache_v, new_state_lru)
```

```python
import numpy as np
import ml_dtypes
import concourse.bass as bass
import concourse.mybir as mybir
from concourse.bass_utils import run_bass_kernel_spmd

F32, BF16 = mybir.dt.float32, mybir.dt.bfloat16
AF = mybir.ActivationFunctionType
ALU = mybir.AluOpType
AX = mybir.AxisListType

D = 2048
LS = 4096
LP = 256
NPS = 2
NTOK = LS + NPS * LP
EPS = 1e-6
NCORES = 8
DBG = {}


class Sem:
    def __init__(self, h):
        self.h = h
        self.n = 0


class Buf:
    def __init__(self, name=""):
        self.w = []
        self.r = []
        self.gen_r = []
        self.name = name
        self.sem = None


class Prog:
    ENG = ("pe", "act", "dve", "pool", "sp")
    COMPUTE = ("pe", "act", "dve", "pool")

    def __init__(self, nc, handles):
        self.nc = nc
        self.q = {e: [] for e in self.ENG}
        hs = list(handles)
        self.csem = {e: Sem(hs.pop()) for e in self.COMPUTE}
        self.bar = Sem(hs.pop())
        self.dpool = [Sem(h) for h in hs]
        nsw = len(self.dpool) // 3
        self.dfree = {"pool": self.dpool[:nsw], "sp": self.dpool[nsw:]}
        self.waited = {e: {} for e in self.ENG}
        self.pending = []
        self.bufs = []
        self.nops = {e: 0 for e in self.COMPUTE}
        self.entries = {e: {} for e in self.COMPUTE}
        self.sig_idx = {e: [] for e in self.COMPUTE}
        self.sig_cnt = {e: [] for e in self.COMPUTE}

    def buf(self, name=""):
        b = Buf(name)
        self.bufs.append(b)
        return b

    def bufs_n(self, n, name=""):
        return [self.buf(name + str(i)) for i in range(n)]

    def _resolve(self, tok):
        import bisect
        _, eng, idx = tok
        si = self.sig_idx[eng]
        p = bisect.bisect_left(si, idx)
        if p < len(si):
            return self.csem[eng], self.sig_cnt[eng][p]
        ent = self.entries[eng][idx]
        sem = self.csem[eng]
        sem.n += 1
        ent[1] = sem.h
        si.append(idx)
        self.sig_cnt[eng].append(sem.n)
        return sem, sem.n

    def _wait(self, eng, tok):
        if tok[0] == "op":
            if tok[1] == eng and eng == "pe":
                return
            sem, tgt = self._resolve(tok)
        else:
            sem, tgt, _ = tok
        if self.waited[eng].get(id(sem), 0) >= tgt:
            return
        self.waited[eng][id(sem)] = tgt
        self.q[eng].append([lambda e, h=sem.h, t=tgt: e.wait_ge(h, t), None])

    def _wait_many(self, eng, toks):
        best = {}
        for t in toks:
            if t[0] == "op":
                k = ("op", t[1])
                if k not in best or best[k][2] < t[2]:
                    best[k] = t
            else:
                k = id(t[0])
                if k not in best or best[k][1] < t[1]:
                    best[k] = t
        for t in best.values():
            self._wait(eng, t)

    def _hazards(self, eng, reads, writes, partial):
        toks = []
        for b in reads:
            toks += b.w
        for b in writes:
            if partial == "nowaw":
                if b.r:
                    b.gen_r = b.r
                    b.r = []
                    b.w = []
                toks += b.gen_r
            else:
                toks += b.w
                toks += b.r
                toks += b.gen_r
        self._wait_many(eng, toks)

    def _commit(self, tok, reads, writes, partial):
        for b in reads:
            b.r.append(tok)
        for b in writes:
            if partial:
                b.w.append(tok)
                if partial != "nowaw":
                    b.r = []
            else:
                b.w = [tok]
                b.r = []
                b.gen_r = []

    def op(self, eng, fn, reads=(), writes=(), partial=False):
        self._hazards(eng, reads, writes, partial)
        idx = self.nops[eng]
        self.nops[eng] += 1
        ent = [fn, None]
        self.entries[eng][idx] = ent
        self.q[eng].append(ent)
        tok = ("op", eng, idx)
        self._commit(tok, reads, writes, partial)
        return tok

    def dma(self, eng, out, in_, sbuf_buf, reads=(), writes=(), partial=False):
        self._hazards(eng, reads, writes, partial)
        if sbuf_buf.sem is None:
            sbuf_buf.sem = {}
        if eng not in sbuf_buf.sem:
            sbuf_buf.sem[eng] = self.dfree[eng].pop()
        sem = sbuf_buf.sem[eng]
        sem.n += 16
        tok = (sem, sem.n, "dma")
        self.q[eng].append([lambda e, o=out, i=in_: e.dma_start(out=o, in_=i), sem.h, 16])
        self._commit(tok, reads, writes, partial)
        self.pending.append(tok)
        return tok

    def barrier(self):
        for e in self.COMPUTE:
            if self.nops[e] > 0:
                self._wait("sp", ("op", e, self.nops[e] - 1))
        self._wait_many("sp", self.pending)
        self.pending = []
        self.bar.n += 1
        k = self.bar.n
        self.q["sp"].append([lambda e, h=self.bar.h: e.sem_inc(h, 1), None])
        for e in self.COMPUTE:
            self._wait(e, (self.bar, k, "bar"))
        for b in self.bufs:
            if b.sem is not None:
                for en, sm in b.sem.items():
                    self.dfree[en].append(sm)
                b.sem = None
            b.w = []
            b.r = []
            b.gen_r = []
        self.bufs = []

    def flush(self):
        nc = self.nc
        q = self.q

        def run(e, lst):
            for ent in lst:
                ins = ent[0](e)
                if ent[1] is not None:
                    ins.then_inc(ent[1], ent[2] if len(ent) > 2 else 1)

        with nc.Block() as blk:
            @blk.tensor
            def _(e):
                run(e, q["pe"])

            @blk.scalar
            def _(e):
                run(e, q["act"])

            @blk.vector
            def _(e):
                run(e, q["dve"])

            @blk.gpsimd
            def _(e):
                run(e, q["pool"])

            @blk.sync
            def _(e):
                run(e, q["sp"])
        self.q = {e: [] for e in self.ENG}
        for e in self.COMPUTE:
            self.entries[e] = {}


_UID = [0]


def sb(es, nc, name, shape, dt):
    _UID[0] += 1
    return es.enter_context(nc.sbuf_tensor("%s_%d" % (name, _UID[0]), shape, dt))


def rev_ap(ap):
    a = [list(p) for p in ap.ap]
    step, cnt = a[-1]
    off = ap.offset + step * (cnt - 1)
    a[-1] = [-step, cnt]
    return bass.AP(ap.tensor, off, a)


def phase_M(P, nc, T):
    with (
        nc.sbuf_tensor("m_cT", [128, 32], F32) as cT,
        nc.sbuf_tensor("m_sg", [128, 32], F32) as sg,
        nc.sbuf_tensor("m_sT", [128, 32], BF16) as sT,
        nc.sbuf_tensor("m_w0", [128, 3072], BF16) as w0,
        nc.sbuf_tensor("m_w1", [128, 3072], BF16) as w1,
        nc.sbuf_tensor("m_mrow", [2, 6144], F32) as mrow,
        nc.sbuf_tensor("m_brow", [2, 6144], F32) as brow,
        nc.sbuf_tensor("m_ng", [2, 2048], F32) as ng,
        nc.sbuf_tensor("m_orow", [2, 6144], F32) as orow,
    ):
        ps = T["ps"]
        b_cT, b_sT = P.buf(), P.buf()
        b_w = P.bufs_n(2)
        wt = [w0, w1]
        b_ps = P.bufs_n(6)
        b_mrow, b_brow, b_ng, b_orow = P.buf(), P.buf(), P.buf(), P.buf()
        P.dma("sp", cT[:], T["cvecT"], b_cT, writes=[b_cT])
        P.op("act", lambda e: e.activation(sg[:], cT[:], AF.Sigmoid), reads=[b_cT], writes=[b_sT])
        P.op("dve", lambda e: e.tensor_tensor(sT[:], sg[:], cT[:], ALU.mult), reads=[b_cT, b_sT], writes=[b_sT])
        cnt = 0
        for l in range(2):
            P.dma("sp", brow[:], T["mod_b"][l:l + 1, :].partition_broadcast(2), b_brow, writes=[b_brow])
            P.dma("sp", ng[:], T["norm_g"][l:l + 1, :].partition_broadcast(2), b_ng, writes=[b_ng])
            for hf in range(2):
                for k in range(16):
                    s = cnt % 2
                    cnt += 1
                    P.dma("pool", wt[s][:], T["mod_w"][l, k * 128:(k + 1) * 128, hf * 3072:(hf + 1) * 3072],
                          b_w[s], writes=[b_w[s]])
                    for j in range(6):
                        P.op("pe", lambda e, s=s, j=j, k=k: e.matmul(
                            ps[j][0:2, :], sT[:, 2 * k:2 * k + 2], wt[s][:, j * 512:(j + 1) * 512],
                            start=(k == 0), stop=(k == 15)),
                            reads=[b_w[s], b_sT], writes=[b_ps[j]], partial=(k > 0))
                for j in range(6):
                    c0 = hf * 3072 + j * 512
                    P.op("dve", lambda e, j=j, c0=c0: e.tensor_tensor(
                        mrow[:, c0:c0 + 512], ps[j][0:2, :], brow[:, c0:c0 + 512], ALU.add),
                        reads=[b_ps[j], b_brow], writes=[b_mrow], partial=True)
            P.op("dve", lambda e: e.scalar_tensor_tensor(
                orow[:, 0:2048], mrow[:, 2048:4096], 1.0, ng[:], ALU.add, ALU.mult),
                reads=[b_mrow, b_ng], writes=[b_orow])
            P.op("dve", lambda e: e.tensor_copy(orow[:, 2048:4096], mrow[:, 0:2048]),
                 reads=[b_mrow], writes=[b_orow], partial=True)
            P.op("dve", lambda e: e.tensor_copy(orow[:, 4096:6144], mrow[:, 4096:6144]),
                 reads=[b_mrow], writes=[b_orow], partial=True)
            P.dma("sp", T["modv"][l], orow[:], b_orow, reads=[b_orow])
        P.barrier()
        P.flush()


def load_bcast_rows(P, nc, T, l, tiles, bufs):
    modv = T["modv"]
    for key, (j, r) in (("A_s", (0, 0)), ("sh_s", (1, 0)), ("A_p", (0, 1)), ("sh_p", (1, 1))):
        P.dma("sp", tiles[key][:], modv[l][r:r + 1, j * 2048:(j + 1) * 2048].partition_broadcast(128),
              bufs[key], writes=[bufs[key]])


def norm_transpose_half(P, nc, T, xsrc, tok_tiles, hT, b_hT, bc, b_bc, work):
    ps = T["ps"]
    ident = T["ident"]
    xt, b_xt = work["xt"], work["b_xt"]
    junk, b_junk = work["junk"], work["b_junk"]
    st, b_st = work["st"], work["b_st"]
    tmp, b_tmp = work["tmp"], work["b_tmp"]
    hb, b_hb = work["hb"], work["b_hb"]
    b_pst = work["b_pst"]
    n = len(tok_tiles)
    for i, (toff, isp) in enumerate(tok_tiles):
        s = i % 2
        x4 = i % len(xt)
        P.dma("sp", xt[x4][:], xsrc[toff:toff + 128, :], b_xt[x4], writes=[b_xt[x4]])
        jk, bjk = work["junk2"][s], work["b_junk2"][s]
        P.op("act", lambda e, s=s, jk=jk, x4=x4: e.activation(jk[:], xt[x4][:], AF.Square, accum_out=st[s][:, 0:1]),
             reads=[b_xt[x4]], writes=[bjk, b_st[s]])
        P.op("act", lambda e, s=s: e.activation(st[s][:, 1:2], st[s][:, 0:1], AF.Sqrt, scale=1.0 / D,
                                                 bias=T["epsb"][:, 0:1]),
             reads=[b_st[s]], writes=[b_st[s]], partial=True)
        P.op("dve", lambda e, s=s: e.reciprocal(st[s][:, 2:3], st[s][:, 1:2]), reads=[b_st[s]], writes=[b_st[s]],
             partial=True)
        A = bc["A_p" if isp else "A_s"]
        SH = bc["sh_p" if isp else "sh_s"]
        bA = b_bc["A_p" if isp else "A_s"]
        bS = b_bc["sh_p" if isp else "sh_s"]
        tm, btm = work["tmp2"][s], work["b_tmp2"][s]
        P.op("dve", lambda e, s=s, A=A, tm=tm, x4=x4: e.scalar_tensor_tensor(tm[:], xt[x4][:], st[s][:, 2:3], A[:], ALU.mult,
                                                                      ALU.mult),
             reads=[b_xt[x4], b_st[s], bA], writes=[btm])
        P.op("pool", lambda e, s=s, SH=SH, tm=tm: e.tensor_tensor(hb[s][:], tm[:], SH[:], ALU.add),
             reads=[btm, bS], writes=[b_hb[s]])
        for half in range(2):
            pb = ps[half * 1 + 2 * s] if False else ps[2 * s + half]
            bp = b_pst[2 * s + half]
            pbv = pb[:].bitcast(BF16)
            for kk in range(8):
                k = half * 8 + kk
                P.op("pe", lambda e, pbv=pbv, kk=kk, k=k, s=s: e.transpose(
                    pbv[:, kk * 128:(kk + 1) * 128], hb[s][:, k * 128:(k + 1) * 128], ident[:]),
                    reads=[b_hb[s]], writes=[bp], partial=(kk > 0))
            dst = hT[:, half * 8:(half + 1) * 8, i * 128:(i + 1) * 128]
            src = pbv.rearrange("p (k t) -> p k t", k=8)
            eng = "act" if half == 0 else "dve"
            if eng == "act":
                P.op("act", lambda e, dst=dst, src=src: e.copy(dst, src), reads=[bp], writes=[b_hT], partial="nowaw")
            else:
                P.op("dve", lambda e, dst=dst, src=src: e.tensor_copy(dst, src), reads=[bp], writes=[b_hT],
                     partial="nowaw")


def phase_A(P, nc, T, debug=False):
    ps = T["ps"]
    W = T["a_w_in"]
    groups = []
    for g in range(2):
        groups.append(("q", [("qT", (g * 4 + j) * 128, (g * 4 + j) * 128) for j in range(4)]))
    groups.append(("k", [("kT0", 0, 1024), ("kT1", 0, 1088)]))
    for g in range(2):
        groups.append(("ga", [("gaT", (g * 4 + j) * 128, 1280 + (g * 4 + j) * 128) for j in range(4)]))
    for g in range(6):
        groups.append(("hy", [("hyT", (g * 4 + j) * 128, 2304 + (g * 4 + j) * 128) for j in range(4)]))
    for g in range(2):
        groups.append(("gh", [("ghT", (g * 4 + j) * 128, 5376 + (g * 4 + j) * 128) for j in range(4)]))

    Wv = W.rearrange("(k p) c -> p k c", p=128)
    for half in range(DBG.get('halves', 2)):
        tok_tiles = [(half * 2048 + i * 128, False) for i in range(16)] + \
                    [(LS + half * 256 + i * 128, True) for i in range(2)]
        import contextlib
        with contextlib.ExitStack() as es0:
            hT = sb(es0, nc, "a_hT", [128, 16, 2304], BF16)
            b_hT = P.buf("hT")
            with contextlib.ExitStack() as es1:
                A_ = lambda n, shp, dt: sb(es1, nc, n, shp, dt)
                xt0 = A_("a_xt0", [128, 2048], F32); xt1 = A_("a_xt1", [128, 2048], F32)
                xt2 = A_("a_xt2", [128, 2048], F32); xt3 = A_("a_xt3", [128, 2048], F32)
                junk = A_("a_junk", [128, 2048], F32); tmp = A_("a_tmp", [128, 2048], F32)
                junkb = A_("a_junkb", [128, 2048], F32); tmpb = A_("a_tmpb", [128, 2048], F32)
                st0 = A_("a_st0", [128, 4], F32); st1 = A_("a_st1", [128, 4], F32)
                hb0 = A_("a_hb0", [128, 2048], BF16); hb1 = A_("a_hb1", [128, 2048], BF16)
                As = A_("a_As", [128, 2048], F32); shs = A_("a_shs", [128, 2048], F32)
                Ap = A_("a_Ap", [128, 2048], F32); shp = A_("a_shp", [128, 2048], F32)
                bc = {"A_s": As, "sh_s": shs, "A_p": Ap, "sh_p": shp}
                b_bc = {k: P.buf(k) for k in bc}
                load_bcast_rows(P, nc, T, 0, bc, b_bc)
                work = dict(xt=[xt0, xt1, xt2, xt3], b_xt=P.bufs_n(4), junk=junk, b_junk=P.buf(), st=[st0, st1],
                            b_st=P.bufs_n(2), tmp=tmp, b_tmp=P.buf(), hb=[hb0, hb1], b_hb=P.bufs_n(2),
                            b_pst=P.bufs_n(4), junk2=[junk, junkb], b_junk2=P.bufs_n(2), tmp2=[tmp, tmpb],
                            b_tmp2=P.bufs_n(2))
                norm_transpose_half(P, nc, T, T["x"], tok_tiles, hT, b_hT, bc, b_bc, work)
                P.barrier()
                P.flush()
            if DBG.get('norm_only'):
                continue
            with contextlib.ExitStack() as es2:
                A_ = lambda n, shp, dt: sb(es2, nc, n, shp, dt)
                wg0 = A_("a_w0", [128, 16, 512], BF16); wg1 = A_("a_w1", [128, 16, 512], BF16)
                wkv = A_("a_wkv", [128, 16, 256], BF16)
                cos_t = A_("a_cos", [128, 2048], F32); sin_t = A_("a_sin", [128, 2048], F32)
                stg0 = A_("a_stg0", [128, 512], BF16); stg1 = A_("a_stg1", [128, 512], BF16)
                stg2 = A_("a_stg2", [128, 512], BF16); stg3 = A_("a_stg3", [128, 512], BF16)
                qs0 = A_("a_qs0", [128, 512], BF16); qs1 = A_("a_qs1", [128, 512], BF16)
                t1 = A_("a_t1", [128, 512], F32); t2 = A_("a_t2", [128, 512], F32)
                kvf0 = A_("a_kvf0", [128, 256], F32); kvf1 = A_("a_kvf1", [128, 256], F32)
                vb0 = A_("a_vb0", [128, 128], BF16); vb1 = A_("a_vb1", [128, 128], BF16)
                b_hT = P.buf("hT2")
                wg = [wg0, wg1]
                b_wg = P.bufs_n(2)
                b_wkv = P.buf()
                b_cos, b_sin = P.buf(), P.buf()
                stg = [stg0, stg1, stg2, stg3]
                b_stg = P.bufs_n(4)
                qs = [qs0, qs1]
                b_qs = P.bufs_n(2)
                b_t1, b_t2 = P.buf(), P.buf()
                kvf = [kvf0, kvf1]
                b_kvf = P.bufs_n(2)
                vb = [vb0, vb1]
                b_vb = P.bufs_n(2)
                b_ps = P.bufs_n(8)
                P.dma("sp", cos_t[:], T["rope_cos"][:, half * 2048:(half + 1) * 2048], b_cos, writes=[b_cos])
                P.dma("sp", sin_t[:], T["rope_sin"][:, half * 2048:(half + 1) * 2048], b_sin, writes=[b_sin])
                for kq in range(16):
                    P.dma("pool", wkv[:, kq, :], Wv[:, kq, 1024:1280], b_wkv, writes=[b_wkv], partial=(kq > 0))
                for i, (toff, isp) in enumerate(tok_tiles[:DBG.get('nkv', 99)]):
                    pi = i % 2
                    pst = ps[6 + pi]
                    for k in range(16):
                        P.op("pe", lambda e, pst=pst, k=k, i=i: e.matmul(
                            pst[:, 0:256], hT[:, k, i * 128:(i + 1) * 128], wkv[:, k, :],
                            start=(k == 0), stop=(k == 15)),
                            reads=[b_hT, b_wkv], writes=[b_ps[6 + pi]], partial=(k > 0))
                    if isp and not DBG.get('no_isp'):
                        P.op("act", lambda e, pst=pst, pi=pi: e.copy(kvf[pi][:], pst[:, 0:256]),
                             reads=[b_ps[6 + pi]], writes=[b_kvf[pi]])
                        r0 = toff - LS
                        if not DBG.get('no_nk'):
                            P.dma("sp", T["nk"][r0:r0 + 128, :], kvf[pi][:, 0:128], b_kvf[pi], reads=[b_kvf[pi]])
                        if not DBG.get('no_nv'):
                            P.dma("sp", T["nv"][r0:r0 + 128, :], kvf[pi][:, 128:256], b_kvf[pi], reads=[b_kvf[pi]])
                    P.op("act", lambda e, pst=pst, pi=pi: e.copy(vb[pi][:], pst[:, 128:256]),
                         reads=[b_ps[6 + pi]], writes=[b_vb[pi]])
                    P.dma("sp", T["vtok"][toff:toff + 128, :], vb[pi][:], b_vb[pi], reads=[b_vb[pi]])
                chunks = [(c * 512, 512, half * 2048 + c * 512, False) for c in range(4)] + \
                         [(2048, 256, LS + half * 256, True)]
                evac_i = 0
                psi = 0
                for gi, (kind, blocks) in enumerate(groups[:DBG.get('ngroups', 99)] if not DBG.get('gsel') else [groups[i] for i in DBG['gsel']]):
                    s = gi % 2
                    if kind == "k":
                        for j, (name, r0, c0) in enumerate(blocks):
                            for dup in range(2):
                                for kq in range(16):
                                    P.dma("pool", wg[s][:, kq, j * 128 + dup * 64:j * 128 + dup * 64 + 64],
                                          Wv[:, kq, c0:c0 + 64], b_wg[s], writes=[b_wg[s]],
                                          partial=(j + dup + kq > 0))
                    else:
                        c0 = blocks[0][2]
                        for kq in range(16):
                            P.dma("pool", wg[s][:, kq, 0:512], Wv[:, kq, c0:c0 + 512],
                                  b_wg[s], writes=[b_wg[s]], partial=(kq > 0))
                    for (l0, n, g0, isp) in chunks:
                        for j, (name, r0, c0) in enumerate(blocks):
                            pj = psi % 4
                            psi += 1
                            pst = ps[pj]
                            for k in range(16):
                                P.op("pe", lambda e, pst=pst, k=k, s=s, j=j, l0=l0, n=n: e.matmul(
                                    pst[:, 0:n], wg[s][:, k, j * 128:(j + 1) * 128], hT[:, k, l0:l0 + n],
                                    start=(k == 0), stop=(k == 15)),
                                    reads=[b_hT, b_wg[s]], writes=[b_ps[pj]], partial=(k > 0))
                            if name.startswith("kT"):
                                dst = T["kT"][int(name[2]), :, g0:g0 + n]
                            else:
                                dst = T[name][r0:r0 + 128, g0:g0 + n]
                            si = evac_i % 4
                            evac_i += 1
                            if kind in ("q", "k") and not isp:
                                qi = evac_i % 2
                                P.op("act", lambda e, pst=pst, qi=qi, n=n: e.copy(qs[qi][:, 0:n], pst[:, 0:n]),
                                     reads=[b_ps[pj]], writes=[b_qs[qi]])
                                pr = ps[4 + qi]
                                P.op("pe", lambda e, pr=pr, qi=qi, n=n: e.matmul(
                                    pr[:, 0:n], T["ropeP"][:], qs[qi][:, 0:n], start=True, stop=True),
                                    reads=[b_qs[qi]], writes=[b_ps[4 + qi]])
                                P.op("dve", lambda e, qi=qi, l0=l0, n=n: e.tensor_tensor(
                                    t1[:, 0:n], qs[qi][:, 0:n], cos_t[:, l0:l0 + n], ALU.mult),
                                    reads=[b_qs[qi], b_cos], writes=[b_t1])
                                P.op("dve", lambda e, pr=pr, l0=l0, n=n: e.tensor_tensor(
                                    t2[:, 0:n], pr[:, 0:n], sin_t[:, l0:l0 + n], ALU.mult),
                                    reads=[b_ps[4 + qi], b_sin], writes=[b_t2])
                                P.op("dve", lambda e, si=si, n=n: e.tensor_tensor(
                                    stg[si][:, 0:n], t1[:, 0:n], t2[:, 0:n], ALU.add),
                                    reads=[b_t1, b_t2], writes=[b_stg[si]])
                            else:
                                if evac_i % 2 == 0:
                                    P.op("act", lambda e, pst=pst, si=si, n=n: e.copy(stg[si][:, 0:n], pst[:, 0:n]),
                                         reads=[b_ps[pj]], writes=[b_stg[si]])
                                else:
                                    P.op("dve", lambda e, pst=pst, si=si, n=n: e.tensor_copy(
                                        stg[si][:, 0:n], pst[:, 0:n]), reads=[b_ps[pj]], writes=[b_stg[si]])
                            P.dma("sp", dst, stg[si][:, 0:n], b_stg[si], reads=[b_stg[si]])
                P.barrier()
                P.flush()


def bcast_last(ap2d, n):
    a = [list(p) for p in ap2d.ap]
    return bass.AP(ap2d.tensor, ap2d.offset, a + [[0, n]])


def phase_B(P, nc, T):
    import contextlib
    ps = T["ps"]
    ident = T["ident"]
    seqs = [(0, LS, True), (LS, LP, False), (LS + LP, LP, False)]
    with contextlib.ExitStack() as es:
        A_ = lambda n, shp, dt: sb(es, nc, n, shp, dt)
        kT = [A_("b_kT0", [128, LS], BF16), A_("b_kT1", [128, LS], BF16)]
        kcT = [A_("b_kc0", [128, 512], BF16), A_("b_kc1", [128, 512], BF16)]
        ckd = A_("b_ckd", [128, 4, 2, 2, 64], BF16)
        vaug = A_("b_vaug", [128, LS // 128, 2, 65], BF16)
        cvaug = A_("b_cvaug", [128, 4, 2, 65], BF16)
        qT = [A_("b_q0", [128, LS], BF16), A_("b_q1", [128, LS], BF16)]
        ga = [A_("b_ga0", [128, LS], BF16), A_("b_ga1", [128, LS], BF16)]
        sga = A_("b_sga", [128, LS], BF16)
        ptA = [A_("b_ptA0", [128, 512], BF16), A_("b_ptA1", [128, 512], BF16)]
        ptB = [A_("b_ptB0", [128, 384], BF16), A_("b_ptB1", [128, 384], BF16)]
        att = [A_("b_att0", [128, 128], BF16), A_("b_att1", [128, 128], BF16)]
        stage = [A_("b_stg0", [128, 512], BF16), A_("b_stg1", [128, 512], BF16)]
        mask = A_("b_mask", [128, 384], BF16)
        sinkb = A_("b_sinkb", [128, 16], F32)
        esink = A_("b_esink", [128, 16], F32)
        den = [A_("b_den0", [128, 2], F32), A_("b_den1", [128, 2], F32)]
        rden = [A_("b_rden0", [128, 2], F32), A_("b_rden1", [128, 2], F32)]

        b_mask, b_es = P.buf(), P.buf()
        P.dma("sp", mask[:], T["d_amask"], b_mask, writes=[b_mask])
        P.dma("sp", sinkb[:], T["a_sink"][0:1, :].partition_broadcast(128), b_es, writes=[b_es])
        P.op("act", lambda e: e.activation(esink[:], sinkb[:], AF.Exp), reads=[b_es], writes=[b_es])
        P.op("dve", lambda e: e.memset(vaug[:], 1.0), writes=[b_mask], partial=True)
        P.op("dve", lambda e: e.memset(cvaug[:], 1.0), writes=[b_mask], partial=True)
        b_ckd, b_kc, b_cv = P.buf(), P.bufs_n(2), P.buf()
        ckv = T["ck"].rearrange("(t p) c -> p t c", p=128)
        cvv = T["cv"].rearrange("(t p) c -> p t c", p=128)
        for kv in range(2):
            for dup in range(2):
                P.dma("pool", ckd[:, :, kv, dup, :], ckv[:, :, kv * 64:(kv + 1) * 64], b_ckd, writes=[b_ckd],
                      partial=True)
            P.dma("pool", cvaug[:, :, kv, 0:64], cvv[:, :, kv * 64:(kv + 1) * 64], b_cv, reads=[b_mask],
                  writes=[b_cv], partial=True)
        b_pt = P.bufs_n(8)
        for kv in range(2):
            for st_ in range(4):
                pb = ps[st_ % 2][:].bitcast(BF16)
                P.op("pe", lambda e, pb=pb, st_=st_, kv=kv: e.transpose(
                    pb[:, 0:128], ckd[:, st_, kv].rearrange("p a b -> p (a b)"), ident[:]),
                    reads=[b_ckd], writes=[b_pt[st_ % 2]])
                P.op("dve", lambda e, pb=pb, st_=st_, kv=kv: e.tensor_copy(
                    kcT[kv][:, st_ * 128:(st_ + 1) * 128], pb[:, 0:128]),
                    reads=[b_pt[st_ % 2]], writes=[b_kc[kv]], partial=True)
        P.barrier()
        b_kT, b_v = P.bufs_n(2), P.buf()
        b_q, b_ga, b_sga = P.bufs_n(2), P.bufs_n(2), P.buf()
        b_ptA, b_ptB, b_att, b_stage = P.bufs_n(2), P.bufs_n(2), P.bufs_n(2), P.bufs_n(2)
        b_den = P.bufs_n(2)
        b_ps = P.bufs_n(8)
        cnt = 0
        for (tok0, L, has_ctx) in seqs:
            nqb = L // 128
            for kv in range(2):
                P.dma("sp", kT[kv][:, 0:L], T["kT"][kv, :, tok0:tok0 + L], b_kT[kv], writes=[b_kT[kv]])
                P.dma("sp", vaug[:, 0:nqb, kv, 0:64],
                      T["vtok"][tok0:tok0 + L, kv * 64:(kv + 1) * 64].rearrange("(t p) c -> p t c", p=128),
                      b_v, writes=[b_v], partial=(kv > 0))
            for hp in range(8):
                s = cnt % 2
                cnt += 1
                kv = hp // 4
                P.dma("sp", qT[s][:, 0:L], T["qT"][hp * 128:(hp + 1) * 128, tok0:tok0 + L], b_q[s], writes=[b_q[s]])
                P.dma("sp", ga[s][:, 0:L], T["gaT"][hp * 128:(hp + 1) * 128, tok0:tok0 + L], b_ga[s],
                      writes=[b_ga[s]])
                P.op("act", lambda e, s=s, L=L: e.activation(sga[:, 0:L], ga[s][:, 0:L], AF.Silu),
                     reads=[b_ga[s]], writes=[b_sga])
                for qb in range(nqb):
                    o = qb % 2
                    psO = ps[4 + o]
                    psOv = psO[:, 0:130].rearrange("p (h c) -> p h c", h=2)
                    if has_ctx:
                        loc = [(j, j - qb + 1) for j in (qb - 1, qb, qb + 1) if 0 <= j < nqb]
                    else:
                        loc = [(j, j) for j in range(nqb)]
                    lo, hi = loc[0][1], loc[-1][1] + 1
                    for hh in range(2):
                        pr = slice(hh * 64, (hh + 1) * 64)
                        pa, pbk = ps[hh * 2], ps[hh * 2 + 1]
                        qsl = qT[s][pr, qb * 128:(qb + 1) * 128]
                        if has_ctx:
                            for c in range(4):
                                P.op("pe", lambda e, pa=pa, c=c, pr=pr, qsl=qsl, kv=kv: e.matmul(
                                    pa[:, c * 128:(c + 1) * 128], kcT[kv][pr, c * 128:(c + 1) * 128], qsl,
                                    start=True, stop=True),
                                    reads=[b_kc[kv], b_q[s]], writes=[b_ps[hh * 2]], partial=(c > 0))
                            P.op("act", lambda e, pa=pa, hh=hh: e.activation(ptA[hh][:], pa[:], AF.Exp, scale=0.125),
                                 reads=[b_ps[hh * 2]], writes=[b_ptA[hh]])
                        for n_, (j, sl) in enumerate(loc):
                            P.op("pe", lambda e, pbk=pbk, sl=sl, j=j, pr=pr, qsl=qsl, kv=kv: e.matmul(
                                pbk[:, sl * 128:(sl + 1) * 128], kT[kv][pr, j * 128:(j + 1) * 128], qsl,
                                start=True, stop=True),
                                reads=[b_kT[kv], b_q[s]], writes=[b_ps[hh * 2 + 1]], partial=(n_ > 0))
                        P.op("act", lambda e, pbk=pbk, hh=hh, lo=lo, hi=hi: e.activation(
                            ptB[hh][:, lo * 128:hi * 128], pbk[:, lo * 128:hi * 128], AF.Exp, scale=0.125),
                            reads=[b_ps[hh * 2 + 1]], writes=[b_ptB[hh]])
                        if has_ctx:
                            P.op("dve", lambda e, hh=hh, lo=lo, hi=hi: e.tensor_tensor(
                                ptB[hh][:, lo * 128:hi * 128], ptB[hh][:, lo * 128:hi * 128],
                                mask[:, lo * 128:hi * 128], ALU.mult),
                                reads=[b_ptB[hh], b_mask], writes=[b_ptB[hh]])
                        mm = []
                        if has_ctx:
                            for c in range(4):
                                mm.append((ptA[hh][:, c * 128:(c + 1) * 128], cvaug[:, c, kv, :], b_ptA[hh], b_cv))
                        for (j, sl) in loc:
                            mm.append((ptB[hh][:, sl * 128:(sl + 1) * 128], vaug[:, j, kv, :], b_ptB[hh], b_v))
                        for n_, (lh, rh, bl, br) in enumerate(mm):
                            P.op("pe", lambda e, psOv=psOv, hh=hh, lh=lh, rh=rh, n_=n_, nm=len(mm): e.matmul(
                                psOv[:, hh, :], lh, rh, start=(n_ == 0), stop=(n_ == nm - 1)),
                                reads=[bl, br], writes=[b_ps[4 + o]], partial=(n_ > 0 or hh > 0))
                    P.op("dve", lambda e, psOv=psOv, o=o, hp=hp: e.tensor_tensor(
                        den[o][:], psOv[:, :, 64], esink[:, hp * 2:hp * 2 + 2], ALU.add),
                        reads=[b_ps[4 + o], b_es], writes=[b_den[o]])
                    P.op("dve", lambda e, o=o: e.reciprocal(rden[o][:], den[o][:]), reads=[b_den[o]],
                         writes=[b_den[o]], partial=True)
                    P.op("dve", lambda e, psOv=psOv, o=o: e.tensor_tensor(
                        att[o][:].rearrange("p (h c) -> p h c", h=2), psOv[:, :, 0:64], bcast_last(rden[o][:], 64),
                        ALU.mult),
                        reads=[b_ps[4 + o], b_den[o]], writes=[b_att[o]])
                    pT = ps[6 + o][:].bitcast(BF16)
                    P.op("pe", lambda e, pT=pT, o=o: e.transpose(pT[:, 0:128], att[o][:], ident[:]),
                         reads=[b_att[o]], writes=[b_ps[6 + o]])
                    g4 = qb // 4
                    sg_ = (cnt * 1024 + g4) % 2
                    P.op("dve", lambda e, pT=pT, sg_=sg_, qb=qb: e.tensor_tensor(
                        stage[sg_][:, (qb % 4) * 128:(qb % 4 + 1) * 128], pT[:, 0:128],
                        sga[:, qb * 128:(qb + 1) * 128], ALU.mult),
                        reads=[b_ps[6 + o], b_sga], writes=[b_stage[sg_]], partial=(qb % 4 > 0))
                    if qb % 4 == 3 or qb == nqb - 1:
                        n = (qb % 4 + 1) * 128
                        t0 = tok0 + g4 * 512
                        P.dma("sp", T["mixT"][hp * 128:(hp + 1) * 128, t0:t0 + n], stage[sg_][:, 0:n],
                              b_stage[sg_], reads=[b_stage[sg_]])
        P.barrier()
        P.flush()


def phase_C0(P, nc, T):
    import contextlib
    seqs = [(0, LS), (LS, LP), (LS + LP, LP)]
    with contextlib.ExitStack() as es:
        A_ = lambda n, shp, dt: sb(es, nc, n, shp, dt)
        hsw = A_("c0_hsw", [128, 4, 24], F32)
        ut = [A_("c0_ut0", [128, LS + 2], BF16), A_("c0_ut1", [128, LS + 2], BF16)]
        accs = [A_("c0_acc", [128, LS], F32), A_("c0_accb", [128, LS], F32)]
        acc2s = [A_("c0_acc2", [128, LS], F32), A_("c0_acc2b", [128, LS], F32)]
        ob = [A_("c0_ob0", [128, LS], BF16), A_("c0_ob1", [128, LS], BF16)]
        b_hsw, b_ut, b_accs, b_acc2s, b_ob = P.buf(), P.bufs_n(2), P.bufs_n(2), P.bufs_n(2), P.bufs_n(2)
        P.dma("sp", hsw[:], T["hsw"], b_hsw, writes=[b_hsw])
        it = 0
        for (tok0, L) in seqs:
            for ct in range(24):
                s = it % 2
                it += 1
                acc, acc2, b_acc, b_acc2 = accs[s], acc2s[s], b_accs[s], b_acc2s[s]
                P.op("pool", lambda e, s=s: e.memset(ut[s][:, 0:1], 0.0), writes=[b_ut[s]])
                P.op("pool", lambda e, s=s, L=L: e.memset(ut[s][:, L + 1:L + 2], 0.0), writes=[b_ut[s]], partial=True)
                P.dma("sp", ut[s][:, 1:L + 1], T["hyT"][ct * 128:(ct + 1) * 128, tok0:tok0 + L], b_ut[s],
                      writes=[b_ut[s]], partial=True)
                P.op("act", lambda e, s=s, L=L, ct=ct, acc=acc: e.activation(
                    acc[:, 0:L], ut[s][:, 0:L], AF.Identity, scale=hsw[:, 0, ct:ct + 1], bias=hsw[:, 3, ct:ct + 1]),
                    reads=[b_ut[s], b_hsw], writes=[b_acc])
                P.op("dve", lambda e, s=s, L=L, ct=ct, acc=acc, acc2=acc2: e.scalar_tensor_tensor(
                    acc2[:, 0:L], ut[s][:, 1:L + 1], hsw[:, 1, ct:ct + 1], acc[:, 0:L], ALU.mult, ALU.add),
                    reads=[b_ut[s], b_acc, b_hsw], writes=[b_acc2])
                P.op("dve", lambda e, s=s, L=L, ct=ct, acc2=acc2: e.scalar_tensor_tensor(
                    ob[s][:, 0:L], ut[s][:, 2:L + 2], hsw[:, 2, ct:ct + 1], acc2[:, 0:L], ALU.mult, ALU.add),
                    reads=[b_ut[s], b_acc2, b_hsw], writes=[b_ob[s]])
                P.dma("pool", T["ucT"][ct * 128:(ct + 1) * 128, tok0:tok0 + L], ob[s][:, 0:L], b_ob[s],
                      reads=[b_ob[s]])
        P.barrier()
        P.flush()


def phase_CF(P, nc, T, Lf, taps_dst, zemb, tposb):
    import contextlib
    import math
    ps = T["ps"]
    nch = max(1, Lf // 512)
    cw = min(512, Lf)
    with contextlib.ExitStack() as es:
        A_ = lambda n, shp, dt: sb(es, nc, n, shp, dt)
        ze = A_("cf_ze", [33, Lf], F32)
        tp_ = A_("cf_tpos", [128, Lf], F32)
        w1 = A_("cf_w1", [33, 64], F32)
        w2 = A_("cf_w2", [64, 64], F32)
        bb = A_("cf_bb", [64, 2], F32)
        w3 = A_("cf_w3", [64, 4096], BF16)
        dec = A_("cf_dec", [128, 32], F32)
        ndec = A_("cf_ndec", [128, 32], F32)
        h1 = A_("cf_h1", [64, Lf], F32)
        h2 = A_("cf_h2", [64, Lf], F32)
        h2b = A_("cf_h2b", [64, Lf], BF16)
        tmp = A_("cf_tmp", [64, 512], F32)
        tmq = A_("cf_tmq", [64, 512], F32)
        raw = [A_("cf_raw0", [128, Lf], F32), A_("cf_raw1", [128, Lf], F32)]
        win = [A_("cf_win0", [128, 512], F32), A_("cf_win1", [128, 512], F32)]
        junk = A_("cf_junk", [128, Lf], F32)
        ss = A_("cf_ss", [128, 8], F32)
        tpb = [A_("cf_tp0", [128, 2 * Lf], BF16), A_("cf_tp1", [128, 2 * Lf], BF16)]
        b_c = P.bufs_n(8)
        P.dma("sp", ze[:], zemb, b_c[0], writes=[b_c[0]])
        P.dma("sp", tp_[:], tposb, b_c[1], writes=[b_c[1]])
        P.dma("sp", w1[:], T["hy_w1"], b_c[2], writes=[b_c[2]])
        P.dma("sp", w2[:], T["hy_w2"], b_c[3], writes=[b_c[3]])
        P.dma("sp", bb[:], T["hy_bb"], b_c[4], writes=[b_c[4]])
        for q4 in range(4):
            P.dma("pool", w3[:, q4 * 1024:(q4 + 1) * 1024], T["hy_w3"][:, q4 * 1024:(q4 + 1) * 1024], b_c[5],
                  writes=[b_c[5]], partial=(q4 > 0))
        P.dma("sp", dec[:], T["hy_decT"], b_c[6], writes=[b_c[6]])
        P.op("act", lambda e: e.activation(dec[:], dec[:], AF.Abs), reads=[b_c[6]], writes=[b_c[6]])
        P.op("dve", lambda e: e.tensor_scalar(ndec[:], dec[:], -1.0, None, ALU.mult),
             reads=[b_c[6]], writes=[b_c[7]])
        b_h1, b_h2, b_h2b, b_tmp, b_tmq = P.buf(), P.buf(), P.buf(), P.buf(), P.buf()
        b_ps = P.bufs_n(8)
        TWO_PI = 2.0 * math.pi
        for layer in range(2):
            src = ze if layer == 0 else h1
            wm = w1 if layer == 0 else w2
            kk = 33 if layer == 0 else 64
            dst = h1 if layer == 0 else h2
            b_src = b_c[0] if layer == 0 else b_h1
            b_dst = b_h1 if layer == 0 else b_h2
            for c in range(nch):
                pj = c % 2
                P.op("pe", lambda e, pj=pj, c=c, src=src, wm=wm, kk=kk: e.matmul(
                    ps[pj][0:64, 0:cw], wm[0:kk, :], src[0:kk, c * cw:(c + 1) * cw], start=True, stop=True),
                    reads=[b_src, b_c[2], b_c[3]], writes=[b_ps[pj]])
                MAGIC = 12582912.0
                P.op("act", lambda e, pj=pj, layer=layer: e.activation(
                    tmp[:, 0:cw], ps[pj][0:64, 0:cw], AF.Identity, bias=bb[:, layer:layer + 1]),
                    reads=[b_ps[pj], b_c[4]], writes=[b_tmp])
                P.op("dve", lambda e: e.tensor_scalar(
                    tmq[:, 0:cw], tmp[:, 0:cw], 1.0 / TWO_PI, MAGIC, ALU.mult, ALU.add),
                    reads=[b_tmp], writes=[b_tmq])
                P.op("dve", lambda e: e.tensor_scalar(
                    tmq[:, 0:cw], tmq[:, 0:cw], -MAGIC, -TWO_PI, ALU.add, ALU.mult),
                    reads=[b_tmq], writes=[b_tmq])
                P.op("dve", lambda e: e.tensor_tensor(tmp[:, 0:cw], tmp[:, 0:cw], tmq[:, 0:cw], ALU.add),
                     reads=[b_tmp, b_tmq], writes=[b_tmp])
                P.op("act", lambda e, c=c, dst=dst: e.activation(dst[:, c * cw:(c + 1) * cw], tmp[:, 0:cw], AF.Sin),
                     reads=[b_tmp], writes=[b_dst], partial=(c > 0))
        P.op("act", lambda e: e.copy(h2b[:], h2[:]), reads=[b_h2], writes=[b_h2b])
        b_raw, b_win, b_junk, b_ss, b_tpb = P.bufs_n(2), P.bufs_n(2), P.buf(), P.buf(), P.bufs_n(2)
        it = 0
        for order in range(2):
            for blk in range(8):
                s = it % 2
                it += 1
                for d in range(2):
                    cbk = d * 16 + order * 8 + blk
                    for c in range(nch):
                        pj = 2 + (c % 2)
                        wj = c % 2
                        P.op("pe", lambda e, pj=pj, c=c, cbk=cbk: e.matmul(
                            ps[pj][:, 0:cw], w3[:, cbk * 128:(cbk + 1) * 128], h2b[:, c * cw:(c + 1) * cw],
                            start=True, stop=True),
                            reads=[b_h2b, b_c[5]], writes=[b_ps[pj]])
                        P.op("act", lambda e, wj=wj, c=c, cbk=cbk: e.activation(
                            win[wj][:, 0:cw], tp_[:, c * cw:(c + 1) * cw], AF.Exp, scale=ndec[:, cbk:cbk + 1]),
                            reads=[b_c[1], b_c[7]], writes=[b_win[wj]])
                        P.op("dve", lambda e, pj=pj, wj=wj, c=c, d=d: e.tensor_tensor(
                            raw[d][:, c * cw:(c + 1) * cw], ps[pj][:, 0:cw], win[wj][:, 0:cw], ALU.mult),
                            reads=[b_ps[pj], b_win[wj]], writes=[b_raw[d]], partial=(c > 0))
                P.op("act", lambda e: e.activation(junk[:], raw[0][:], AF.Square, accum_out=ss[:, 0:1]),
                     reads=[b_raw[0]], writes=[b_junk, b_ss])
                P.op("act", lambda e: e.activation(junk[:, 1:Lf], raw[1][:, 1:Lf], AF.Square, accum_out=ss[:, 1:2]),
                     reads=[b_raw[1]], writes=[b_junk, b_ss], partial=True)
                P.op("dve", lambda e: e.tensor_tensor(ss[:, 2:3], ss[:, 0:1], ss[:, 1:2], ALU.add),
                     reads=[b_ss], writes=[b_ss], partial=True)
                P.op("act", lambda e: e.activation(ss[:, 3:4], ss[:, 2:3], AF.Sqrt, bias=T["epsb"][:, 0:1]),
                     reads=[b_ss], writes=[b_ss], partial=True)
                P.op("dve", lambda e: e.reciprocal(ss[:, 4:5], ss[:, 3:4]), reads=[b_ss], writes=[b_ss], partial=True)
                P.op("dve", lambda e: e.tensor_scalar(ss[:, 5:6], ss[:, 4:5], -1.0, None, ALU.mult),
                     reads=[b_ss], writes=[b_ss], partial=True)
                P.op("act", lambda e, s=s: e.activation(tpb[s][:, 0:Lf], raw[0][:], AF.Copy, scale=ss[:, 4:5]),
                     reads=[b_raw[0], b_ss], writes=[b_tpb[s]])
                P.op("pool", lambda e, s=s: e.memset(tpb[s][:, Lf:Lf + 1], 0.0), writes=[b_tpb[s]], partial=True)
                P.op("dve", lambda e, s=s: e.tensor_scalar(
                    rev_ap(tpb[s][:, Lf + 1:2 * Lf]), raw[1][:, 1:Lf], ss[:, 5:6], None, ALU.mult),
                    reads=[b_raw[1], b_ss], writes=[b_tpb[s]], partial=True)
                P.dma("sp", taps_dst[order, blk * 128:(blk + 1) * 128, :], tpb[s][:], b_tpb[s], reads=[b_tpb[s]])
        P.barrier()
        P.flush()


def _split_dma(P, eng, dst, src, buf, nsplit, axis_len, mk_dst, mk_src, **kw):
    step = axis_len // nsplit
    for i in range(nsplit):
        P.dma(eng, mk_dst(i * step, (i + 1) * step), mk_src(i * step, (i + 1) * step), buf,
              partial=(i > 0 or kw.get("partial", False)), **{k: v for k, v in kw.items() if k != "partial"})


def _f1_stage(P, nc, T, xin, K, b_xin, F1, b_F1, Cb, b_Cb, b_ps, ev):
    ps = T["ps"]
    for g in range(16):
        pj = g % 2
        for cc in range(4):
            c = g * 4 + cc
            P.op("pe", lambda e, pj=pj, cc=cc, c=c: e.matmul(
                ps[pj][0:64, cc * 128:(cc + 1) * 128], xin[0:K, c, :], F1[0:K, 0:128], start=True, stop=True),
                reads=[b_xin, b_F1], writes=[b_ps[pj]], partial=(cc > 0))
            P.op("pe", lambda e, pj=pj, cc=cc, c=c: e.matmul(
                ps[pj][64:128, cc * 128:(cc + 1) * 128], xin[0:K, c, :], F1[0:K, 128:256], start=True, stop=True,
                tile_position=(0, 64)),
                reads=[b_xin, b_F1], writes=[b_ps[pj]], partial=True)
        src = ps[pj][:].rearrange("p (c k) -> p k c", c=4)
        dst = Cb[:, :, g * 4:(g + 1) * 4]
        ev(dst, src, [b_ps[pj]], [b_Cb], g)


def _evac_alt(P):
    def ev(dst, src, reads, writes, i):
        if i % 2 == 0:
            P.op("act", lambda e: e.copy(dst, src), reads=reads, writes=writes, partial="nowaw")
        else:
            P.op("dve", lambda e: e.tensor_copy(dst, src), reads=reads, writes=writes, partial="nowaw")
    return ev


def _evac_act(P):
    def ev(dst, src, reads, writes, i):
        P.op("act", lambda e: e.copy(dst, src), reads=reads, writes=writes, partial="nowaw")
    return ev


def phase_CT(P, nc, T):
    import contextlib
    ps = T["ps"]
    with contextlib.ExitStack() as es:
        A_ = lambda n, shp, dt: sb(es, nc, n, shp, dt)
        F1 = A_("ct_F1", [128, 256], BF16)
        Grr = A_("ct_Grr", [128, 128, 64], BF16)
        Gii = A_("ct_Gii", [128, 128, 64], BF16)
        xt = [A_("ct_xt0", [128, 64, 64], BF16), A_("ct_xt1", [128, 64, 64], BF16)]
        Cb = A_("ct_Cb", [128, 128, 64], BF16)
        Hr = [A_("ct_Hr0", [64, 64, 128], BF16), A_("ct_Hr1", [64, 64, 128], BF16)]
        Hi = [A_("ct_Hi0", [64, 64, 128], BF16), A_("ct_Hi1", [64, 64, 128], BF16)]
        b_F1, b_G = P.buf(), P.buf()
        P.dma("sp", F1[:], T["d_F1"], b_F1, writes=[b_F1])
        for q in range(4):
            P.dma("sp", Grr[:, q * 32:(q + 1) * 32, :], T["d_Grr"][:, q * 32:(q + 1) * 32, :], b_G, writes=[b_G],
                  partial=(q > 0))
            P.dma("sp", Gii[:, q * 32:(q + 1) * 32, :], T["d_Gii"][:, q * 32:(q + 1) * 32, :], b_G, writes=[b_G],
                  partial=True)
        b_xt, b_Cb, b_Hr, b_Hi = P.bufs_n(2), P.buf(), P.bufs_n(2), P.bufs_n(2)
        b_ps = P.bufs_n(8)
        ev = _evac_alt(P)
        it = 0
        for order in range(2):
            for cb in range(16):
                s = it % 2
                it += 1
                srcv = T["tapsS"][order, cb * 64:(cb + 1) * 64, :].rearrange("c (a b) -> a c b", b=64)
                for q in range(8):
                    P.dma("sp", xt[s][:, q * 8:(q + 1) * 8, :], srcv[:, q * 8:(q + 1) * 8, :], b_xt[s],
                          writes=[b_xt[s]], partial=(q > 0))
                _f1_stage(P, nc, T, xt[s], 128, b_xt[s], F1, b_F1, Cb, b_Cb, b_ps, ev)
                for g in range(16):
                    pj = 2 + (g % 2) * 2
                    for kk in range(8):
                        k1 = g * 8 + kk
                        P.op("pe", lambda e, pj=pj, kk=kk, k1=k1: e.matmul(
                            ps[pj][0:64, kk * 64:(kk + 1) * 64], Grr[:, k1, :], Cb[:, k1, :], start=True, stop=True),
                            reads=[b_G, b_Cb], writes=[b_ps[pj]], partial=(kk > 0))
                        P.op("pe", lambda e, pj=pj, kk=kk, k1=k1: e.matmul(
                            ps[pj + 1][0:64, kk * 64:(kk + 1) * 64], Gii[:, k1, :], Cb[:, k1, :], start=True,
                            stop=True),
                            reads=[b_G, b_Cb], writes=[b_ps[pj + 1]], partial=(kk > 0))
                    P.op("act", lambda e, pj=pj, g=g, s=s: e.copy(
                        Hr[s][:, :, g * 8:(g + 1) * 8], ps[pj][0:64, :].rearrange("p (k c) -> p c k", k=8)),
                        reads=[b_ps[pj]], writes=[b_Hr[s]], partial="nowaw")
                    P.op("dve", lambda e, pj=pj, g=g, s=s: e.tensor_copy(
                        Hi[s][:, :, g * 8:(g + 1) * 8], ps[pj + 1][0:64, :].rearrange("p (k c) -> p c k", k=8)),
                        reads=[b_ps[pj + 1]], writes=[b_Hi[s]], partial="nowaw")
                P.dma("sp", T["Hs"][order, cb, 0], Hr[s][:].rearrange("p c k -> p (c k)"), b_Hr[s], reads=[b_Hr[s]])
                P.dma("sp", T["Hs"][order, cb, 1], Hi[s][:].rearrange("p c k -> p (c k)"), b_Hi[s], reads=[b_Hi[s]])
        P.barrier()
        P.flush()


def phase_CS(P, nc, T):
    import contextlib
    ps = T["ps"]
    with contextlib.ExitStack() as es:
        A_ = lambda n, shp, dt: sb(es, nc, n, shp, dt)
        F1 = A_("cs_F1", [128, 256], BF16)
        G = A_("cs_G", [128, 128, 64], BF16)
        M1 = A_("cs_M1", [64, 128], BF16)
        M1p = A_("cs_M1p", [64, 128], BF16)
        T2r = A_("cs_T2r", [128, 64, 64], BF16)
        T2i = A_("cs_T2i", [128, 64, 64], BF16)
        zc = A_("cs_zc", [64, 64, 64], BF16)
        x1 = A_("cs_x1", [64, 64, 64], BF16)
        x2 = A_("cs_x2", [64, 64, 64], BF16)
        gh = A_("cs_gh", [64, 64, 64], BF16)
        z2 = A_("cs_z2", [64, 64, 64], BF16)
        bz = A_("cs_bz", [64, 64, 64], BF16)
        cv = A_("cs_cv", [64, 64, 64], F32)
        bias = A_("cs_bias", [64, 2, 64], F32)
        Cb = A_("cs_Cb", [128, 128, 64], BF16)
        Db = A_("cs_Db", [128, 128, 64], BF16)
        P1 = A_("cs_P1", [64, 64, 128], BF16)
        P2 = A_("cs_P2", [64, 64, 128], BF16)
        Hr = A_("cs_Hr", [64, 64, 128], BF16)
        Hi = A_("cs_Hi", [64, 64, 128], BF16)
        Xs = [A_("cs_Xs0", [64, 512], BF16), A_("cs_Xs1", [64, 512], BF16)]
        b_k = P.bufs_n(6)
        P.dma("sp", F1[:], T["d_F1"], b_k[0], writes=[b_k[0]])
        for q in range(4):
            P.dma("sp", G[:, q * 32:(q + 1) * 32, :], T["d_G"][:, q * 32:(q + 1) * 32, :], b_k[1], writes=[b_k[1]],
                  partial=(q > 0))
        P.dma("sp", M1[:], T["d_M1"], b_k[2], writes=[b_k[2]])
        P.dma("sp", M1p[:], T["d_M1p"], b_k[3], writes=[b_k[3]])
        P.dma("sp", T2r[:], T["d_T2r"], b_k[4], writes=[b_k[4]])
        P.dma("sp", T2i[:], T["d_T2in"], b_k[5], writes=[b_k[5]])
        b_F1, b_G, b_M1, b_M1p, b_T2r, b_T2i = b_k
        b_zc, b_x1, b_x2, b_gh, b_sgh, b_z2, b_bz, b_cv, b_mx, b_bias = [P.buf() for _ in range(10)]
        b_Cb, b_Db, b_P1, b_P2, b_Hr, b_Hi = [P.buf() for _ in range(6)]
        b_Xs = P.bufs_n(2)
        b_ps = P.bufs_n(8)
        ev = _evac_act(P)

        def t64(rows0):
            return lambda a, b: T["ucT"][rows0 + a:rows0 + b, 0:LS].rearrange("c (n1 n2) -> n1 c n2", n2=64)

        for cb in range(16):
            for (dst, bdst, r0, src_t) in ((zc, b_zc, 2048 + cb * 64, "ucT"), (x1, b_x1, cb * 64, "ucT"),
                                           (x2, b_x2, 1024 + cb * 64, "ucT"), (gh, b_gh, cb * 64, "ghT")):
                for q in range(4):
                    srcv = T[src_t][r0 + q * 16:r0 + (q + 1) * 16, 0:LS].rearrange("c (n1 n2) -> n1 c n2", n2=64)
                    P.dma("sp", dst[:, q * 16:(q + 1) * 16, :], srcv, bdst, writes=[bdst], partial=(q > 0))
            for o in range(2):
                P.dma("sp", bias[:, o, :], T["hy_bias"][o:o + 1, cb * 64:(cb + 1) * 64].partition_broadcast(64),
                      b_bias, writes=[b_bias], partial=(o > 0))
            P.op("act", lambda e: e.activation(gh[:], gh[:], AF.Silu), reads=[b_gh], writes=[b_gh])
            for o in range(2):
                zin, b_zin = (zc, b_zc) if o == 0 else (z2, b_z2)
                xo, b_xo = (x1, b_x1) if o == 0 else (x2, b_x2)
                P.dma("sp", Hr[:].rearrange("p c k -> p (c k)"), T["Hs"][o, cb, 0], b_Hr, writes=[b_Hr])
                P.dma("sp", Hi[:].rearrange("p c k -> p (c k)"), T["Hs"][o, cb, 1], b_Hi, writes=[b_Hi])
                P.op("pool", lambda e, zin=zin, o=o: e.tensor_tensor(
                    bz[:], zin[:], bcast_last(bias[:, o, :], 64), ALU.mult),
                    reads=[b_zin, b_bias], writes=[b_bz])
                _f1_stage(P, nc, T, zin, 64, b_zin, F1, b_F1, Cb, b_Cb, b_ps, ev)
                for g in range(16):
                    pj = 2 + (g % 2)
                    xs = g % 2
                    for kk in range(8):
                        k1 = g * 8 + kk
                        P.op("pe", lambda e, pj=pj, kk=kk, k1=k1: e.matmul(
                            ps[pj][0:64, kk * 64:(kk + 1) * 64], G[:, k1, :], Cb[:, k1, :], start=True, stop=True),
                            reads=[b_G, b_Cb], writes=[b_ps[pj]], partial=(kk > 0))
                    P.op("act", lambda e, pj=pj, xs=xs: e.copy(Xs[xs][:], ps[pj][0:64, :]),
                         reads=[b_ps[pj]], writes=[b_Xs[xs]])
                    xv = Xs[xs][:].rearrange("p (k c) -> p c k", k=8)
                    P.op("dve", lambda e, xv=xv, g=g: e.tensor_tensor(
                        P1[:, :, g * 8:(g + 1) * 8], xv, Hr[:, :, g * 8:(g + 1) * 8], ALU.mult),
                        reads=[b_Xs[xs], b_Hr], writes=[b_P1], partial="nowaw")
                    P.op("dve", lambda e, xv=xv, g=g: e.tensor_tensor(
                        P2[:, :, g * 8:(g + 1) * 8], xv, Hi[:, :, g * 8:(g + 1) * 8], ALU.mult),
                        reads=[b_Xs[xs], b_Hi], writes=[b_P2], partial="nowaw")
                for g in range(16):
                    pj = 4 + (g % 2)
                    for cc in range(4):
                        c = g * 4 + cc
                        P.op("pe", lambda e, pj=pj, cc=cc, c=c: e.matmul(
                            ps[pj][:, cc * 128:(cc + 1) * 128], P1[:, c, :], M1[:], start=True, stop=False),
                            reads=[b_P1, b_M1], writes=[b_ps[pj]], partial=(cc > 0))
                        P.op("pe", lambda e, pj=pj, cc=cc, c=c: e.matmul(
                            ps[pj][:, cc * 128:(cc + 1) * 128], P2[:, c, :], M1p[:], start=False, stop=True),
                            reads=[b_P2, b_M1p], writes=[b_ps[pj]], partial=True)
                    src = ps[pj][:].rearrange("p (c q) -> p q c", c=4)
                    ev(Db[:, :, g * 4:(g + 1) * 4], src, [b_ps[pj]], [b_Db], g)
                for g in range(8):
                    pj = 6 + (g % 2)
                    for nn in range(8):
                        n2 = g * 8 + nn
                        P.op("pe", lambda e, pj=pj, nn=nn, n2=n2: e.matmul(
                            ps[pj][0:64, nn * 64:(nn + 1) * 64], T2r[:, n2, :], Db[:, n2, :], start=True, stop=False),
                            reads=[b_T2r, b_Db], writes=[b_ps[pj]], partial=(nn > 0))
                        P.op("pe", lambda e, pj=pj, nn=nn, n2=n2: e.matmul(
                            ps[pj][0:64, nn * 64:(nn + 1) * 64], T2i[:, n2, :], Db[:, 64 + n2, :], start=False,
                            stop=True),
                            reads=[b_T2i, b_Db], writes=[b_ps[pj]], partial=True)
                    P.op("dve", lambda e, pj=pj, g=g: e.tensor_tensor(
                        cv[:, :, g * 8:(g + 1) * 8], ps[pj][0:64, :].rearrange("p (n c) -> p c n", n=8),
                        bz[:, :, g * 8:(g + 1) * 8], ALU.add),
                        reads=[b_ps[pj], b_bz], writes=[b_cv], partial=(g > 0))
                if o == 0:
                    P.op("dve", lambda e: e.tensor_tensor(z2[:], cv[:], x1[:], ALU.mult),
                         reads=[b_cv, b_x1], writes=[b_z2])
                else:
                    P.op("dve", lambda e: e.tensor_tensor(cv[:], cv[:], x2[:], ALU.mult),
                         reads=[b_cv, b_x2], writes=[b_cv])
                    P.op("pool", lambda e: e.tensor_tensor(z2[:], cv[:], gh[:], ALU.mult),
                         reads=[b_cv, b_gh], writes=[b_z2])
                    for q in range(4):
                        r0 = 1024 + cb * 64 + q * 16
                        dstv = T["mixT"][r0:r0 + 16, 0:LS].rearrange("c (n1 n2) -> n1 c n2", n2=64)
                        P.dma("sp", dstv, z2[:, q * 16:(q + 1) * 16, :], b_z2, reads=[b_z2])
        P.barrier()
        P.flush()


def phase_CP(P, nc, T):
    import contextlib
    ps = T["ps"]
    ident = T["ident"]
    with contextlib.ExitStack() as es:
        A_ = lambda n, shp, dt: sb(es, nc, n, shp, dt)
        FP = A_("cp_FP", [128, 4, 512], BF16)
        IP = A_("cp_IP", [128, 4, 256], BF16)
        HA = A_("cp_HA", [128, 16, 512], F32)
        HB = A_("cp_HB", [128, 16, 512], F32)
        biasT = A_("cp_biasT", [128, 16], F32)
        tp = [A_("cp_tp0", [128, 512], BF16), A_("cp_tp1", [128, 512], BF16)]
        tt = A_("cp_tt", [128, 4, 128], BF16)
        zc = [A_("cp_zc0", [128, 256], BF16), A_("cp_zc1", [128, 256], BF16)]
        x1 = [A_("cp_x10", [128, 256], BF16), A_("cp_x11", [128, 256], BF16)]
        x2 = [A_("cp_x20", [128, 256], BF16), A_("cp_x21", [128, 256], BF16)]
        gh = [A_("cp_gh0", [128, 256], BF16), A_("cp_gh1", [128, 256], BF16)]
        sgh = A_("cp_sgh", [128, 256], F32)
        z2 = A_("cp_z2", [128, 256], BF16)
        zt = A_("cp_zt", [128, 2, 128], BF16)
        Aa = A_("cp_A", [128, 512], F32)
        Bb = A_("cp_B", [128, 512], F32)
        Y = A_("cp_Y", [128, 512], BF16)
        Yt = A_("cp_Yt", [128, 4, 128], BF16)
        t1 = A_("cp_t1", [128, 256], F32)
        t2 = A_("cp_t2", [128, 256], F32)
        mx = [A_("cp_mx0", [128, 256], BF16), A_("cp_mx1", [128, 256], BF16)]
        b_FP, b_IP, b_H, b_bias = P.buf(), P.buf(), P.buf(), P.buf()
        P.dma("sp", FP[:], T["d_FP"], b_FP, writes=[b_FP])
        P.dma("sp", IP[:], T["d_IP"], b_IP, writes=[b_IP])
        P.dma("sp", biasT[:], T["hy_biasT"], b_bias, writes=[b_bias])
        b_tp, b_tt = P.bufs_n(2), P.buf()
        b_ps = P.bufs_n(8)
        it = 0
        for o in range(2):
            for t in range(8):
                s = it % 2
                it += 1
                P.dma("sp", tp[s][:], T["tapsP"][o, t * 128:(t + 1) * 128, :], b_tp[s], writes=[b_tp[s]])
                pb = ps[s][:].bitcast(BF16)
                for j in range(4):
                    P.op("pe", lambda e, pb=pb, j=j, s=s: e.transpose(
                        pb[:, j * 128:(j + 1) * 128], tp[s][:, j * 128:(j + 1) * 128], ident[:]),
                        reads=[b_tp[s]], writes=[b_ps[s]], partial=(j > 0))
                P.op("dve", lambda e, pb=pb: e.tensor_copy(tt[:].rearrange("p a b -> p (a b)"), pb[:, 0:512]),
                     reads=[b_ps[s]], writes=[b_tt])
                pj = 2 + s
                for j in range(4):
                    P.op("pe", lambda e, pj=pj, j=j: e.matmul(
                        ps[pj][:, :], tt[:, j, :], FP[:, j, :], start=(j == 0), stop=(j == 3)),
                        reads=[b_tt, b_FP], writes=[b_ps[pj]], partial=(j > 0))
                i = o * 8 + t
                for h in range(2):
                    P.op("act", lambda e, pj=pj, i=i, h=h: e.copy(HA[:, i, h * 256:(h + 1) * 256], ps[pj][:, 0:256]),
                         reads=[b_ps[pj]], writes=[b_H], partial=True)
                    P.op("act", lambda e, pj=pj, i=i, h=h: e.copy(HB[:, i, h * 256:(h + 1) * 256],
                                                                  ps[pj][:, 256:512]),
                         reads=[b_ps[pj]], writes=[b_H], partial=True)
        b_zc, b_x1, b_x2, b_gh = P.bufs_n(2), P.bufs_n(2), P.bufs_n(2), P.bufs_n(2)
        b_sgh, b_z2, b_zt, b_A, b_B, b_Y, b_Yt, b_t1, b_t2 = [P.buf() for _ in range(9)]
        b_mx = P.bufs_n(2)
        it = 0
        for sq in range(2):
            tok0 = LS + sq * LP
            for t in range(8):
                s = it % 2
                it += 1
                r = t * 128
                P.dma("sp", zc[s][:], T["ucT"][2048 + r:2048 + r + 128, tok0:tok0 + LP], b_zc[s], writes=[b_zc[s]])
                P.dma("sp", x1[s][:], T["ucT"][r:r + 128, tok0:tok0 + LP], b_x1[s], writes=[b_x1[s]])
                P.dma("sp", x2[s][:], T["ucT"][1024 + r:1024 + r + 128, tok0:tok0 + LP], b_x2[s], writes=[b_x2[s]])
                P.dma("sp", gh[s][:], T["ghT"][r:r + 128, tok0:tok0 + LP], b_gh[s], writes=[b_gh[s]])
                P.op("act", lambda e, s=s: e.activation(sgh[:], gh[s][:], AF.Silu), reads=[b_gh[s]], writes=[b_sgh])
                for o in range(2):
                    zin, b_zin = (zc[s], b_zc[s]) if o == 0 else (z2, b_z2)
                    xo, b_xo = (x1[s], b_x1[s]) if o == 0 else (x2[s], b_x2[s])
                    i = o * 8 + t
                    pb = ps[0][:].bitcast(BF16)
                    for j in range(2):
                        P.op("pe", lambda e, pb=pb, j=j, zin=zin: e.transpose(
                            pb[:, j * 128:(j + 1) * 128], zin[:, j * 128:(j + 1) * 128], ident[:]),
                            reads=[b_zin], writes=[b_ps[0]], partial=(j > 0))
                    P.op("dve", lambda e, pb=pb: e.tensor_copy(zt[:].rearrange("p a b -> p (a b)"), pb[:, 0:256]),
                         reads=[b_ps[0]], writes=[b_zt])
                    for j in range(2):
                        P.op("pe", lambda e, j=j: e.matmul(ps[1][:, :], zt[:, j, :], FP[:, j, :], start=(j == 0),
                                                           stop=(j == 1)),
                             reads=[b_zt, b_FP], writes=[b_ps[1]], partial=(j > 0))
                    P.op("dve", lambda e, i=i: e.tensor_tensor(Aa[:], ps[1][:, :], HA[:, i, :], ALU.mult),
                         reads=[b_ps[1], b_H], writes=[b_A])
                    P.op("dve", lambda e, i=i: e.tensor_tensor(Bb[:], ps[1][:, :], HB[:, i, :], ALU.mult),
                         reads=[b_ps[1], b_H], writes=[b_B])
                    P.op("pool", lambda e: e.tensor_tensor(Y[:, 0:256], Aa[:, 0:256], Bb[:, 256:512], ALU.subtract),
                         reads=[b_A, b_B], writes=[b_Y])
                    P.op("pool", lambda e: e.tensor_tensor(Y[:, 256:512], Bb[:, 0:256], Aa[:, 256:512], ALU.add),
                         reads=[b_A, b_B], writes=[b_Y], partial=True)
                    pb2 = ps[2][:].bitcast(BF16)
                    for j in range(4):
                        P.op("pe", lambda e, pb2=pb2, j=j: e.transpose(
                            pb2[:, j * 128:(j + 1) * 128], Y[:, j * 128:(j + 1) * 128], ident[:]),
                            reads=[b_Y], writes=[b_ps[2]], partial=(j > 0))
                    P.op("act", lambda e, pb2=pb2: e.copy(Yt[:].rearrange("p a b -> p (a b)"), pb2[:, 0:512]),
                         reads=[b_ps[2]], writes=[b_Yt])
                    for j in range(4):
                        P.op("pe", lambda e, j=j: e.matmul(ps[3][:, 0:256], Yt[:, j, :], IP[:, j, :], start=(j == 0),
                                                           stop=(j == 3)),
                             reads=[b_Yt, b_IP], writes=[b_ps[3]], partial=(j > 0))
                    P.op("dve", lambda e, zin=zin, i=i: e.scalar_tensor_tensor(
                        t1[:], zin[:], biasT[:, i:i + 1], ps[3][:, 0:256], ALU.mult, ALU.add),
                        reads=[b_zin, b_bias, b_ps[3]], writes=[b_t1])
                    if o == 0:
                        P.op("pool", lambda e, xo=xo: e.tensor_tensor(z2[:], t1[:], xo[:], ALU.mult),
                             reads=[b_t1, b_xo], writes=[b_z2])
                    else:
                        P.op("pool", lambda e, xo=xo: e.tensor_tensor(t2[:], t1[:], xo[:], ALU.mult),
                             reads=[b_t1, b_xo], writes=[b_t2])
                        P.op("pool", lambda e, s=s: e.tensor_tensor(mx[s][:], t2[:], sgh[:], ALU.mult),
                             reads=[b_t2, b_sgh], writes=[b_mx[s]])
                        P.dma("sp", T["mixT"][1024 + r:1024 + r + 128, tok0:tok0 + LP], mx[s][:], b_mx[s],
                              reads=[b_mx[s]])
        P.barrier()
        P.flush()


def phase_outproj(P, nc, T, l, Wdram, xsrc, mixname, dst, final):
    import contextlib
    ps = T["ps"]
    with contextlib.ExitStack() as es:
        A_ = lambda n, shp, dt: sb(es, nc, n, shp, dt)
        Wo = A_("o_W", [128, 16, 2048], BF16)
        mix = [A_("o_mix0", [128, 16, 512], BF16), A_("o_mix1", [128, 16, 512], BF16)]
        xt = [A_("o_x0", [128, 2048], F32), A_("o_x1", [128, 2048], F32)]
        xo = [A_("o_xo0", [128, 2048], F32), A_("o_xo1", [128, 2048], F32)]
        gs = A_("o_gs", [128, 2048], F32)
        gp = A_("o_gp", [128, 2048], F32)
        tmp = [A_("o_tmp0", [128, 512], F32), A_("o_tmp1", [128, 512], F32)]
        b_W, b_mix, b_xt, b_xo, b_g, b_tmp = P.buf(), P.bufs_n(2), P.bufs_n(2), P.bufs_n(2), P.buf(), P.bufs_n(2)
        b_ps = P.bufs_n(8)
        if final:
            fg = A_("o_fg", [128, 2048], F32)
            junk = A_("o_junk", [128, 2048], F32)
            st = [A_("o_st0", [128, 4], F32), A_("o_st1", [128, 4], F32)]
            b_fg, b_junk, b_st = P.buf(), P.buf(), P.bufs_n(2)
            P.dma("sp", fg[:], T["final_norm_g"][0:1, :].partition_broadcast(128), b_fg, writes=[b_fg])
        Wv = Wdram.rearrange("(k p) c -> p k c", p=128)
        for k in range(16):
            P.dma("pool", Wo[:, k, :], Wv[:, k, :], b_W, writes=[b_W], partial=(k > 0))
        P.dma("sp", gs[:], T["modv"][l][0:1, 4096:6144].partition_broadcast(128), b_g, writes=[b_g])
        P.dma("sp", gp[:], T["modv"][l][1:2, 4096:6144].partition_broadcast(128), b_g, writes=[b_g], partial=True)
        mv = T[mixname].rearrange("(k p) t -> p k t", p=128)
        ti = 0
        ei = 0
        for ch in range(NTOK // 512):
            s = ch % 2
            for q in range(4):
                P.dma("sp", mix[s][:, q * 4:(q + 1) * 4, :], mv[:, q * 4:(q + 1) * 4, ch * 512:(ch + 1) * 512],
                      b_mix[s], writes=[b_mix[s]], partial=(q > 0))
            for tt in range(4):
                tok = ch * 512 + tt * 128
                xs = ti % 2
                ti += 1
                gg = gs if tok < LS else gp
                P.dma("sp", xt[xs][:], xsrc[tok:tok + 128, :], b_xt[xs], writes=[b_xt[xs]])
                for cbk in range(4):
                    pj = ei % 4
                    tj = ei % 2
                    ei += 1
                    for k in range(16):
                        P.op("pe", lambda e, pj=pj, k=k, s=s, tt=tt, cbk=cbk: e.matmul(
                            ps[pj][:, :], mix[s][:, k, tt * 128:(tt + 1) * 128], Wo[:, k, cbk * 512:(cbk + 1) * 512],
                            start=(k == 0), stop=(k == 15)),
                            reads=[b_mix[s], b_W], writes=[b_ps[pj]], partial=(k > 0))
                    P.op("dve", lambda e, pj=pj, tj=tj, cbk=cbk, gg=gg: e.tensor_tensor(
                        tmp[tj][:], ps[pj][:, :], gg[:, cbk * 512:(cbk + 1) * 512], ALU.mult),
                        reads=[b_ps[pj], b_g], writes=[b_tmp[tj]])
                    P.op("pool", lambda e, tj=tj, xs=xs, cbk=cbk: e.tensor_tensor(
                        xo[xs][:, cbk * 512:(cbk + 1) * 512], tmp[tj][:], xt[xs][:, cbk * 512:(cbk + 1) * 512],
                        ALU.add),
                        reads=[b_tmp[tj], b_xt[xs]], writes=[b_xo[xs]], partial=(cbk > 0))
                if final:
                    P.op("act", lambda e, xs=xs: e.activation(junk[:], xo[xs][:], AF.Square,
                                                               accum_out=st[xs][:, 0:1]),
                         reads=[b_xo[xs]], writes=[b_junk, b_st[xs]])
                    P.op("act", lambda e, xs=xs: e.activation(st[xs][:, 1:2], st[xs][:, 0:1], AF.Sqrt, scale=1.0 / D,
                                                               bias=T["epsb"][:, 0:1]),
                         reads=[b_st[xs]], writes=[b_st[xs]], partial=True)
                    P.op("dve", lambda e, xs=xs: e.reciprocal(st[xs][:, 2:3], st[xs][:, 1:2]),
                         reads=[b_st[xs]], writes=[b_st[xs]], partial=True)
                    P.op("dve", lambda e, xs=xs: e.scalar_tensor_tensor(
                        xo[xs][:], xo[xs][:], st[xs][:, 2:3], fg[:], ALU.mult, ALU.mult),
                        reads=[b_xo[xs], b_st[xs], b_fg], writes=[b_xo[xs]])
                P.dma("sp", dst[tok:tok + 128, :], xo[xs][:], b_xo[xs], reads=[b_xo[xs]])
        P.barrier()
        P.flush()


def phase_E(P, nc, T):
    import contextlib
    ps = T["ps"]
    Wv = T["c_w_in"].rearrange("(k p) c -> p k c", p=128)
    for half in range(2):
        tok_tiles = [(half * 2048 + i * 128, False) for i in range(16)] + \
                    [(LS + half * 256 + i * 128, True) for i in range(2)]
        with contextlib.ExitStack() as es0:
            hT = sb(es0, nc, "e_hT", [128, 16, 2304], BF16)
            b_hT = P.buf("hT")
            with contextlib.ExitStack() as es1:
                A_ = lambda n, shp, dt: sb(es1, nc, n, shp, dt)
                xt0 = A_("e_xt0", [128, 2048], F32); xt1 = A_("e_xt1", [128, 2048], F32)
                xt2 = A_("e_xt2", [128, 2048], F32); xt3 = A_("e_xt3", [128, 2048], F32)
                junk = A_("e_junk", [128, 2048], F32); tmp = A_("e_tmp", [128, 2048], F32)
                junkb = A_("e_junkb", [128, 2048], F32); tmpb = A_("e_tmpb", [128, 2048], F32)
                st0 = A_("e_st0", [128, 4], F32); st1 = A_("e_st1", [128, 4], F32)
                hb0 = A_("e_hb0", [128, 2048], BF16); hb1 = A_("e_hb1", [128, 2048], BF16)
                As = A_("e_As", [128, 2048], F32); shs = A_("e_shs", [128, 2048], F32)
                Ap = A_("e_Ap", [128, 2048], F32); shp = A_("e_shp", [128, 2048], F32)
                bc = {"A_s": As, "sh_s": shs, "A_p": Ap, "sh_p": shp}
                b_bc = {k: P.buf(k) for k in bc}
                load_bcast_rows(P, nc, T, 1, bc, b_bc)
                work = dict(xt=[xt0, xt1, xt2, xt3], b_xt=P.bufs_n(4), junk=junk, b_junk=P.buf(), st=[st0, st1],
                            b_st=P.bufs_n(2), tmp=tmp, b_tmp=P.buf(), hb=[hb0, hb1], b_hb=P.bufs_n(2),
                            b_pst=P.bufs_n(4), junk2=[junk, junkb], b_junk2=P.bufs_n(2), tmp2=[tmp, tmpb],
                            b_tmp2=P.bufs_n(2))
                norm_transpose_half(P, nc, T, T["x1"], tok_tiles, hT, b_hT, bc, b_bc, work)
                P.barrier()
                P.flush()
            with contextlib.ExitStack() as es2:
                A_ = lambda n, shp, dt: sb(es2, nc, n, shp, dt)
                wg = [A_("e_w0", [128, 16, 512], BF16), A_("e_w1", [128, 16, 512], BF16)]
                sf = [A_("e_sf0", [128, 512], F32), A_("e_sf1", [128, 512], F32)]
                sg = [A_("e_sg0", [128, 512], BF16), A_("e_sg1", [128, 512], BF16)]
                b_hT = P.buf("hT2")
                b_wg, b_sf, b_sg = P.bufs_n(2), P.bufs_n(2), P.bufs_n(2)
                b_ps = P.bufs_n(8)
                chunks = [(c * 512, 512, half * 2048 + c * 512) for c in range(4)] + [(2048, 256, LS + half * 256)]
                psi = 0
                ei = 0
                for g in range(8):
                    s = g % 2
                    for kq in range(16):
                        P.dma("pool", wg[s][:, kq, :], Wv[:, kq, g * 512:(g + 1) * 512], b_wg[s], writes=[b_wg[s]],
                              partial=(kq > 0))
                    for (l0, n, g0) in chunks:
                        for j in range(4):
                            pj = psi % 4
                            psi += 1
                            for k in range(16):
                                P.op("pe", lambda e, pj=pj, k=k, s=s, j=j, l0=l0, n=n: e.matmul(
                                    ps[pj][:, 0:n], wg[s][:, k, j * 128:(j + 1) * 128], hT[:, k, l0:l0 + n],
                                    start=(k == 0), stop=(k == 15)),
                                    reads=[b_hT, b_wg[s]], writes=[b_ps[pj]], partial=(k > 0))
                            row = (g * 4 + j) * 128
                            si = ei % 2
                            ei += 1
                            if row < 2048:
                                P.op("act", lambda e, pj=pj, si=si, n=n: e.copy(sf[si][:, 0:n], ps[pj][:, 0:n]),
                                     reads=[b_ps[pj]], writes=[b_sf[si]])
                                P.dma("sp", T["xbT"][row:row + 128, g0:g0 + n], sf[si][:, 0:n], b_sf[si],
                                      reads=[b_sf[si]])
                            else:
                                P.op("dve", lambda e, pj=pj, si=si, n=n: e.tensor_copy(sg[si][:, 0:n], ps[pj][:, 0:n]),
                                     reads=[b_ps[pj]], writes=[b_sg[si]])
                                P.dma("sp", T["gateT"][row - 2048:row - 2048 + 128, g0:g0 + n], sg[si][:, 0:n],
                                      b_sg[si], reads=[b_sg[si]])
                P.barrier()
                P.flush()


def phase_F(P, nc, T):
    import contextlib
    ps = T["ps"]
    seqs = [(0, LS, 0), (LS, LP, 1), (LS + LP, LP, 2)]
    with contextlib.ExitStack() as es:
        A_ = lambda n, shp, dt: sb(es, nc, n, shp, dt)
        Rs = [A_("f_R0", [128, LS], F32), A_("f_R1", [128, LS], F32)]
        Is = [A_("f_I0", [128, LS], F32), A_("f_I1", [128, LS], F32)]
        Ss = [A_("f_S0", [128, LS], F32), A_("f_S1", [128, LS], F32)]
        H_ = [A_("f_H0", [128, LS + 3], F32), A_("f_H1", [128, LS + 3], F32)]
        xc = A_("f_xc", [128, 2, LS], F32)
        xp = H_[1]
        acc = H_[0]
        xcb = A_("f_xcb", [128, 2, LS], BF16)
        gate = A_("f_gate", [128, LS], BF16)
        stage = A_("f_stage", [128, LS], BF16)
        wq = A_("f_wq", [128, 2, 2, 2, 256], BF16)
        lcw = A_("f_lcw", [128, 5, 16], F32)
        lba = A_("f_lba", [128, 2, 16], F32)
        lbx = A_("f_lbx", [128, 2, 16], F32)
        llam = A_("f_llam", [128, 2, 16], F32)
        c8 = A_("f_c8", [128, 2, 16], F32)
        stT = A_("f_stT", [128, 2, 16], F32)
        nst = A_("f_nst", [128, 2, 2, 16], F32)
        b_k = P.bufs_n(6)
        P.dma("sp", lcw[:], T["lcw"], b_k[0], writes=[b_k[0]])
        P.dma("sp", lba[:], T["lba"], b_k[1], writes=[b_k[1]])
        P.dma("sp", lbx[:], T["lbx"], b_k[2], writes=[b_k[2]])
        P.dma("sp", llam[:], T["llam"], b_k[3], writes=[b_k[3]])
        P.dma("sp", stT[:], T["stT"], b_k[4], writes=[b_k[4]])
        P.op("act", lambda e: e.activation(c8[:], llam[:], AF.Exp, scale=-1.0), reads=[b_k[3]], writes=[b_k[5]])
        P.op("act", lambda e: e.activation(c8[:], c8[:], AF.Ln, bias=1.0), reads=[b_k[5]], writes=[b_k[5]])
        P.op("dve", lambda e: e.tensor_scalar(c8[:], c8[:], -8.0, None, ALU.mult), reads=[b_k[5]], writes=[b_k[5]])
        b_lcw, b_lba, b_lbx, _, b_stT, b_c8 = b_k
        b_xc, b_xcb, b_gate, b_stage, b_wq, b_nst = [P.buf() for _ in range(6)]
        b_H = P.bufs_n(2)
        b_Rs, b_Is, b_Ss = P.bufs_n(2), P.bufs_n(2), P.bufs_n(2)
        b_xp, b_acc = b_H[1], b_H[0]
        par = 0
        b_ps = P.bufs_n(8)
        psi = 0
        for h in range(8):
            for m in range(2):
                wsrc = T["c_wa"] if m == 0 else T["c_wx"]
                for d in range(2):
                    P.dma("pool", wq[:, m, d], wsrc[d, h].rearrange("(it p) j -> p it j", p=128), b_wq,
                          writes=[b_wq], partial=(m + d > 0))
            for (tok0, L, sidx) in seqs:
                nch = max(1, L // 512)
                cw = min(512, L)
                for ct in range(2):
                    cti = h * 2 + ct
                    row = cti * 128
                    P.op("pool", lambda e: e.memset(xp[:, 0:2], 0.0), writes=[b_xp])
                    P.op("pool", lambda e, L=L: e.memset(xp[:, L + 2:L + 3], 0.0), writes=[b_xp], partial=True)
                    P.dma("sp", xp[:, 2:L + 2], T["xbT"][row:row + 128, tok0:tok0 + L], b_xp, writes=[b_xp],
                          partial=True)
                    P.op("act", lambda e, L=L, cti=cti: e.activation(
                        acc[:, 0:L], xp[:, 0:L], AF.Identity, scale=lcw[:, 0, cti:cti + 1],
                        bias=lcw[:, 4, cti:cti + 1]), reads=[b_xp, b_lcw], writes=[b_acc])
                    for j in (1, 2):
                        P.op("dve", lambda e, L=L, cti=cti, j=j: e.scalar_tensor_tensor(
                            acc[:, 0:L], xp[:, j:j + L], lcw[:, j, cti:cti + 1], acc[:, 0:L], ALU.mult, ALU.add),
                            reads=[b_xp, b_acc, b_lcw], writes=[b_acc])
                    P.op("dve", lambda e, L=L, cti=cti, ct=ct: e.scalar_tensor_tensor(
                        xc[:, ct, 0:L], xp[:, 3:3 + L], lcw[:, 3, cti:cti + 1], acc[:, 0:L], ALU.mult, ALU.add),
                        reads=[b_xp, b_acc, b_lcw], writes=[b_xc], partial=(ct > 0))
                    P.op("act", lambda e, L=L, ct=ct: e.copy(xcb[:, ct, 0:L], xc[:, ct, 0:L]),
                         reads=[b_xc], writes=[b_xcb], partial=(ct > 0))
                for jt in range(2):
                    cti = h * 2 + jt
                    row = cti * 128
                    P.dma("sp", gate[:, 0:L], T["gateT"][row:row + 128, tok0:tok0 + L], b_gate, writes=[b_gate])
                    for d in range(2):
                        par ^= 1
                        R_, I_, S_ = Rs[par], Is[par], Ss[par]
                        b_R, b_I, b_S = b_Rs[par], b_Is[par], b_Ss[par]
                        for (m, dstT, b_dst, bias_t) in ((0, R_, b_R, lba), (1, I_, b_I, lbx)):
                            for c in range(nch):
                                pj = psi % 4
                                psi += 1
                                for it_ in range(2):
                                    P.op("pe", lambda e, pj=pj, m=m, d=d, it_=it_, jt=jt, c=c, cw=cw: e.matmul(
                                        ps[pj][:, 0:cw], wq[:, m, d, it_, jt * 128:(jt + 1) * 128],
                                        xcb[:, it_, c * cw:(c + 1) * cw], start=(it_ == 0), stop=(it_ == 1)),
                                        reads=[b_wq, b_xcb], writes=[b_ps[pj]], partial=(it_ > 0))
                                P.op("act", lambda e, pj=pj, dstT=dstT, c=c, bias_t=bias_t, d=d, cti=cti, cw=cw: e.activation(
                                    dstT[:, c * cw:(c + 1) * cw], ps[pj][:, 0:cw], AF.Sigmoid,
                                    bias=bias_t[:, d, cti:cti + 1]),
                                    reads=[b_ps[pj], b_lba, b_lbx], writes=[b_dst], partial=(c > 0))
                        P.op("act", lambda e, L=L, d=d, cti=cti, R_=R_: e.activation(
                            R_[:, 0:L], R_[:, 0:L], AF.Exp, scale=c8[:, d, cti:cti + 1]),
                            reads=[b_R, b_c8], writes=[b_R])
                        P.op("dve", lambda e, L=L, R_=R_, S_=S_: e.tensor_tensor(S_[:, 0:L], R_[:, 0:L], R_[:, 0:L], ALU.mult),
                             reads=[b_R], writes=[b_S])
                        P.op("act", lambda e, L=L, S_=S_: e.activation(S_[:, 0:L], S_[:, 0:L], AF.Sqrt, scale=-1.0, bias=1.0),
                             reads=[b_S], writes=[b_S])
                        P.op("dve", lambda e, L=L, I_=I_, S_=S_: e.tensor_tensor(I_[:, 0:L], I_[:, 0:L], S_[:, 0:L], ALU.mult),
                             reads=[b_I, b_S], writes=[b_I])
                        P.op("pool", lambda e, L=L, jt=jt, I_=I_: e.tensor_tensor(I_[:, 0:L], I_[:, 0:L], xc[:, jt, 0:L],
                                                                                 ALU.mult),
                             reads=[b_I, b_xc], writes=[b_I])
                        init = stT[:, d, cti:cti + 1] if sidx == 0 else 0.0
                        if d == 0:
                            P.op("dve", lambda e, L=L, init=init, R_=R_, I_=I_: e.tensor_tensor_scan(
                                H_[0][:, 0:L], R_[:, 0:L], I_[:, 0:L], init, ALU.mult, ALU.add),
                                reads=[b_R, b_I, b_stT], writes=[b_H[0]])
                        else:
                            P.op("dve", lambda e, L=L, init=init, R_=R_, I_=I_: e.tensor_tensor_scan(
                                rev_ap(H_[1][:, 0:L]), rev_ap(R_[:, 0:L]), rev_ap(I_[:, 0:L]), init, ALU.mult,
                                ALU.add),
                                reads=[b_R, b_I, b_stT], writes=[b_H[1]])
                        if sidx > 0:
                            col = L - 1 if d == 0 else 0
                            P.op("act", lambda e, d=d, col=col, sidx=sidx, cti=cti: e.copy(
                                nst[:, sidx - 1, d, cti:cti + 1], H_[d][:, col:col + 1]),
                                reads=[b_H[d]], writes=[b_nst], partial=True)
                    P.op("pool", lambda e, L=L: e.tensor_tensor(H_[0][:, 0:L], H_[0][:, 0:L], H_[1][:, 0:L], ALU.add),
                         reads=[b_H[0], b_H[1]], writes=[b_H[0]])
                    P.op("act", lambda e, L=L, S_=S_: e.activation(S_[:, 0:L], gate[:, 0:L], AF.Silu),
                         reads=[b_gate], writes=[b_S])
                    P.op("dve", lambda e, L=L, S_=S_: e.tensor_tensor(stage[:, 0:L], H_[0][:, 0:L], S_[:, 0:L], ALU.mult),
                         reads=[b_H[0], b_S], writes=[b_stage])
                    P.dma("sp", T["mix1T"][row:row + 128, tok0:tok0 + L], stage[:, 0:L], b_stage, reads=[b_stage])
        P.dma("sp", T["ns"], nst[:].rearrange("p a b c -> p (a b c)"), b_nst, reads=[b_nst])
        P.barrier()
        P.flush()


def _bf16(a):
    return np.asarray(a, dtype=np.float32).astype(ml_dtypes.bfloat16)


def _fft_consts():
    N = 2 * LS
    n1 = np.arange(128)[:, None]
    k1 = np.arange(128)[None, :]
    ang = 2 * np.pi * n1 * (k1 + 0.5) / 128
    F1 = np.concatenate([np.cos(ang), -np.sin(ang)], 1)
    n2 = np.arange(64)[:, None, None]
    k1_ = np.arange(128)[None, :, None]
    k2 = np.arange(32)[None, None, :]
    ang = 2 * np.pi * n2 * (k1_ + 128 * k2 + 0.5) / N
    gr, gi = np.cos(ang), -np.sin(ang)
    G = np.zeros((128, 128, 64))
    G[0:64, :, 0:32] = gr
    G[64:128, :, 0:32] = -gi
    G[0:64, :, 32:64] = gi
    G[64:128, :, 32:64] = gr
    Grr = np.concatenate([G[:, :, 0:32], G[:, :, 0:32]], 2)
    Gii = np.concatenate([G[:, :, 32:64], G[:, :, 32:64]], 2)
    k2 = np.arange(32)[:, None]
    n2 = np.arange(64)[None, :]
    ang = 2 * np.pi * n2 * k2 / 64
    mr, mi = np.cos(ang), np.sin(ang)
    M1 = np.zeros((64, 128))
    M1[0:32, 0:64] = mr
    M1[32:64, 0:64] = -mi
    M1[0:32, 64:128] = mi
    M1[32:64, 64:128] = mr
    M1p = np.concatenate([M1[32:64], -M1[0:32]], 0)
    k1 = np.arange(128)[:, None, None]
    n2 = np.arange(64)[None, :, None]
    n1 = np.arange(64)[None, None, :]
    ang = 2 * np.pi * (k1 + 0.5) * (n1 / 128 + n2 / N)
    T2r = (2.0 / N) * np.cos(ang)
    T2in = -(2.0 / N) * np.sin(ang)
    Np = 2 * LP
    n = np.arange(Np)[:, None]
    k = np.arange(LP)[None, :]
    ang = 2 * np.pi * n * (k + 0.5) / Np
    FPm = np.concatenate([np.cos(ang), -np.sin(ang)], 1)
    FP = FPm.reshape(4, 128, 512).transpose(1, 0, 2)
    k = np.arange(LP)[:, None]
    n = np.arange(LP)[None, :]
    ang = 2 * np.pi * n * (k + 0.5) / Np
    IPm = np.concatenate([(2.0 / Np) * np.cos(ang), -(2.0 / Np) * np.sin(ang)], 0)
    IP = IPm.reshape(4, 128, 256).transpose(1, 0, 2)
    return dict(F1=F1, G=G, Grr=Grr, Gii=Gii, M1=M1, M1p=M1p, T2r=T2r, T2in=T2in, FP=FP, IP=IP)


def make_consts():
    c = {}
    c["ident"] = _bf16(np.eye(128))
    pos = np.arange(LS)
    row = (pos // 64).astype(np.float64)
    col = (pos % 64).astype(np.float64)
    nf = 16
    inv = 10000.0 ** (-np.arange(nf, dtype=np.float64) / nf)
    cos = np.zeros((64, LS))
    sin = np.zeros((64, LS))
    for d in range(64):
        halfi = d // 32
        w = d % 32
        f = w % 16
        p = row if halfi == 0 else col
        ang = p * inv[f]
        cos[d] = np.cos(ang)
        sin[d] = -np.sin(ang) if w < 16 else np.sin(ang)
    c["rope_cos"] = np.tile(cos, (2, 1)).astype(np.float32)
    c["rope_sin"] = np.tile(sin, (2, 1)).astype(np.float32)
    Pm = np.zeros((128, 128))
    for m in range(128):
        hh = m // 64
        d = m % 64
        w = d % 32
        partner = d + 16 if w < 16 else d - 16
        Pm[hh * 64 + partner, m] = 1.0
    c["ropeP"] = _bf16(Pm)
    c["epsb"] = np.full((128, 1), EPS, np.float32)
    for nm, Lf in (("S", LS), ("P", LP)):
        t = (np.arange(Lf, dtype=np.float32) / np.float32(Lf)).astype(np.float64)
        freqs = np.linspace(1e-4, 15.0, 16).astype(np.float32).astype(np.float64)
        ang = 2.0 * np.pi * t[:, None] * freqs[None, :]
        z = np.concatenate([t[:, None], np.cos(ang), -np.sin(ang)], 1)
        c["zemb" + nm] = np.ascontiguousarray(z.T).astype(np.float32)
        c["tpos" + nm] = np.ascontiguousarray(np.tile(t[None, :], (128, 1))).astype(np.float32)
    c.update({k: _bf16(v) for k, v in _fft_consts().items()})
    si = np.arange(128)[:, None]
    qi = np.arange(128)[None, :]
    c["amask"] = _bf16(np.concatenate([(si >= qi), np.ones((128, 128)), (si <= qi)], 1).astype(np.float32))
    return c


CONST_SPECS = {
    "ident": ([128, 128], BF16), "rope_cos": ([128, LS], F32), "rope_sin": ([128, LS], F32),
    "ropeP": ([128, 128], BF16), "epsb": ([128, 1], F32), "amask": ([128, 384], BF16),
    "zembS": ([33, LS], F32), "zembP": ([33, LP], F32), "tposS": ([128, LS], F32), "tposP": ([128, LP], F32),
    "F1": ([128, 256], BF16), "G": ([128, 128, 64], BF16), "Grr": ([128, 128, 64], BF16),
    "Gii": ([128, 128, 64], BF16), "M1": ([64, 128], BF16), "M1p": ([64, 128], BF16),
    "T2r": ([128, 64, 64], BF16), "T2in": ([128, 64, 64], BF16),
    "FP": ([128, 4, 512], BF16), "IP": ([128, 4, 256], BF16),
}

IN_SPECS = {
    "x": [NTOK, D], "ck": [512, 128], "cv": [512, 128], "st": [2, D], "cvecT": [128, 32],
    "mod_w": [2, D, 3 * D], "mod_b": [2, 3 * D], "norm_g": [2, D], "final_norm_g": [1, D],
    "a_w_in": [D, 6400], "a_w_out": [D, D], "a_sink": [1, 16],
    "hsw": [128, 4, 24], "hy_w1": [33, 64], "hy_w2": [64, 64], "hy_bb": [64, 2], "hy_w3": [64, 4096],
    "hy_decT": [128, 32], "hy_biasT": [128, 16], "hy_bias": [2, 1024],
    "c_w_in": [D, 2 * D], "c_w_out": [D, D], "c_wa": [2, 8, 256, 256], "c_wx": [2, 8, 256, 256],
    "lcw": [128, 5, 16], "lba": [128, 2, 16], "lbx": [128, 2, 16], "llam": [128, 2, 16], "stT": [128, 2, 16],
}

SCRATCH = {
    "modv": ([2, 2, 3 * D], F32),
    "qT": ([1024, NTOK], BF16), "kT": ([2, 128, NTOK], BF16), "gaT": ([1024, NTOK], BF16),
    "hyT": ([3072, NTOK], BF16), "ghT": ([1024, NTOK], BF16), "vtok": ([NTOK, 128], BF16),
    "mixT": ([2048, NTOK], BF16),
    "ucT": ([3072, NTOK], BF16), "tapsS": ([2, 1024, 2 * LS], BF16), "tapsP": ([2, 1024, 2 * LP], BF16),
    "Hs": ([2, 16, 2, 64, 64 * 128], BF16),
    "x1": ([NTOK, D], F32), "xbT": ([D, NTOK], F32), "gateT": ([D, NTOK], BF16), "mix1T": ([D, NTOK], BF16),
}

OUT_SPECS = {"y": [NTOK, D], "nk": [512, 128], "nv": [512, 128], "ns": [128, 64]}


def build_program(debug_scratch=(), stop_after=None, skip=(), ext_in=()):
    nc = bass.Bass("TRN2", target_bir_lowering=False)
    T = {}
    for name, shp in IN_SPECS.items():
        T[name] = nc.dram_tensor(name, shp, F32, kind="ExternalInput").ap()
    for name, (shp, dt) in CONST_SPECS.items():
        T["d_" + name] = nc.dram_tensor("c_" + name, shp, dt, kind="ExternalInput").ap()
    for name, shp in OUT_SPECS.items():
        T[name] = nc.dram_tensor(name, shp, F32, kind="ExternalOutput").ap()
    for name, (shp, dt) in SCRATCH.items():
        kind = "ExternalOutput" if name in debug_scratch else ("ExternalInput" if name in ext_in else "Internal")
        T[name] = nc.dram_tensor("s_" + name, shp, dt, kind=kind).ap()
    T["rope_cos"] = T["d_rope_cos"]
    T["rope_sin"] = T["d_rope_sin"]
    import contextlib
    with contextlib.ExitStack() as es:
        sems = [es.enter_context(nc.semaphore("sem%d" % i)) for i in range(60)]
        T["ps"] = [es.enter_context(nc.psum_tensor("ps%d" % i, [128, 512], F32)) for i in range(8)]
        ident = es.enter_context(nc.sbuf_tensor("ident", [128, 128], BF16))
        ropeP = es.enter_context(nc.sbuf_tensor("ropeP", [128, 128], BF16))
        epsb = es.enter_context(nc.sbuf_tensor("epsb", [128, 1], F32))
        T["ident"], T["ropeP"], T["epsb"] = ident, ropeP, epsb
        P = Prog(nc, sems)
        b_c = P.bufs_n(3)
        P.dma("sp", ident[:], T["d_ident"], b_c[0], writes=[b_c[0]])
        P.dma("sp", ropeP[:], T["d_ropeP"], b_c[1], writes=[b_c[1]])
        P.dma("sp", epsb[:], T["d_epsb"], b_c[2], writes=[b_c[2]])
        P.barrier()
        if "M" not in skip:
            phase_M(P, nc, T)
        if stop_after != "M":
            if "A" not in skip:
                phase_A(P, nc, T)
        if stop_after not in ("M", "A") and "B" not in skip:
            phase_B(P, nc, T)
        if stop_after not in ("M", "A", "B"):
            if "C0" not in skip:
                phase_C0(P, nc, T)
            if "CF" not in skip:
                phase_CF(P, nc, T, LP, T["tapsP"], T["d_zembP"], T["d_tposP"])
                phase_CF(P, nc, T, LS, T["tapsS"], T["d_zembS"], T["d_tposS"])
        if stop_after not in ("M", "A", "B", "CF"):
            if "CT" not in skip:
                phase_CT(P, nc, T)
            if "CS" not in skip:
                phase_CS(P, nc, T)
            if "CP" not in skip:
                phase_CP(P, nc, T)
        if stop_after not in ("M", "A", "B", "CF", "C"):
            if "D" not in skip:
                phase_outproj(P, nc, T, 0, T["a_w_out"], T["x"], "mixT", T["x1"], False)
        if stop_after not in ("M", "A", "B", "CF", "C", "D"):
            if "E" not in skip:
                phase_E(P, nc, T)
        if stop_after not in ("M", "A", "B", "CF", "C", "D", "E"):
            if "F" not in skip:
                phase_F(P, nc, T)
        if stop_after not in ("M", "A", "B", "CF", "C", "D", "E", "F"):
            if "G" not in skip:
                phase_outproj(P, nc, T, 1, T["c_w_out"], T["x1"], "mix1T", T["y"], True)
        P.barrier()
        P.flush()
    return nc


def make_in_maps(inputs):
    consts = make_consts()
    f = lambda a: np.ascontiguousarray(np.asarray(a, dtype=np.float32))
    x_prompt, x_sample = f(inputs["x_prompt"]), f(inputs["x_sample"])
    ck, cv = f(inputs["cache_k"]), f(inputs["cache_v"])
    st = f(inputs["state_lru"])
    c, c_ctx = f(inputs["c"]), f(inputs["c_ctx"])
    shared = {
        "mod_w": f(inputs["mod_w"]), "mod_b": f(inputs["mod_b"]), "norm_g": f(inputs["norm_g"]),
        "final_norm_g": f(inputs["final_norm_g"]).reshape(1, D),
        "a_w_in": f(inputs["a_w_in"])[0], "a_w_out": f(inputs["a_w_out"])[0], "a_sink": f(inputs["a_sink"]),
        "hsw": np.ascontiguousarray(np.concatenate([f(inputs["hy_short_w"])[0], f(inputs["hy_short_b"])[0][None]], 0)
                                    .reshape(4, 24, 128).transpose(2, 0, 1)),
        "hy_w1": f(inputs["hy_w1"])[0], "hy_w2": f(inputs["hy_w2"])[0],
        "hy_bb": np.ascontiguousarray(np.stack([f(inputs["hy_b1"])[0], f(inputs["hy_b2"])[0]], 1)),
        "hy_w3": f(inputs["hy_w3"])[0],
        "hy_decT": np.ascontiguousarray(f(inputs["hy_decay"])[0].reshape(32, 128).T),
        "hy_biasT": np.ascontiguousarray(f(inputs["hy_bias"])[0].reshape(16, 128).T),
        "hy_bias": f(inputs["hy_bias"])[0],
        "c_w_in": f(inputs["c_w_in"])[0], "c_w_out": f(inputs["c_w_out"])[0],
        "c_wa": f(inputs["c_wa"])[0], "c_wx": f(inputs["c_wx"])[0],
        "lcw": np.ascontiguousarray(np.concatenate([f(inputs["c_conv_w"])[0], f(inputs["c_conv_b"])[0][None]], 0)
                                    .reshape(5, 16, 128).transpose(2, 0, 1)),
        "lba": np.ascontiguousarray(f(inputs["c_ba"])[0].reshape(2, 16, 128).transpose(2, 0, 1)),
        "lbx": np.ascontiguousarray(f(inputs["c_bx"])[0].reshape(2, 16, 128).transpose(2, 0, 1)),
        "llam": np.ascontiguousarray(f(inputs["c_lambda"])[0].reshape(2, 16, 128).transpose(2, 0, 1)),
    }
    for k, v in consts.items():
        shared["c_" + k] = v
    maps = []
    for i in range(NCORES):
        m = dict(shared)
        m["x"] = np.ascontiguousarray(np.concatenate([x_sample[i], x_prompt[2 * i], x_prompt[2 * i + 1]], 0))
        m["ck"] = np.ascontiguousarray(ck[i, 0].reshape(512, 128))
        m["cv"] = np.ascontiguousarray(cv[i, 0].reshape(512, 128))
        m["st"] = np.ascontiguousarray(st[i, 0])
        m["stT"] = np.ascontiguousarray(st[i, 0].reshape(2, 16, 128).transpose(2, 0, 1))
        cvec = np.stack([c[i], c_ctx], 0)
        m["cvecT"] = np.ascontiguousarray(cvec.reshape(2, 16, 128).transpose(2, 1, 0).reshape(128, 32))
        maps.append(m)
    return maps


def kernel(**inputs):
    nc = build_program()
    maps = make_in_maps(inputs)
    res = run_bass_kernel_spmd(nc, maps, core_ids=list(range(NCORES)))
    R = res.results
    y_s = np.stack([R[i]["y"][:LS] for i in range(NCORES)], 0)
    y_p = np.concatenate([R[i]["y"][LS:].reshape(2, LP, D) for i in range(NCORES)], 0)
    nk = np.concatenate([R[i]["nk"].reshape(2, 1, LP, 2, 64) for i in range(NCORES)], 0)
    nv = np.concatenate([R[i]["nv"].reshape(2, 1, LP, 2, 64) for i in range(NCORES)], 0)
    ns = np.concatenate([R[i]["ns"].reshape(128, 2, 2, 16).transpose(1, 2, 3, 0).reshape(2, 1, 2, D)
                         for i in range(NCORES)], 0)
    return (y_p.astype(np.float32), y_s.astype(np.float32), nk.astype(np.float32), nv.astype(np.float32),
            ns.astype(np.float32))
```

```python
import numpy as np
import ml_dtypes
import concourse.bass as bass
import concourse.mybir as mybir
from concourse.bass_utils import run_bass_kernel_spmd

F32, BF16 = mybir.dt.float32, mybir.dt.bfloat16
AF = mybir.ActivationFunctionType
ALU = mybir.AluOpType
AX = mybir.AxisListType

D = 2048
LS = 4096
LP = 256
NPS = 2
NTOK = LS + NPS * LP
EPS = 1e-6
NCORES = 8
DBG = {}


class Sem:
    def __init__(self, h):
        self.h = h
        self.n = 0


class Buf:
    def __init__(self, name=""):
        self.w = []
        self.r = []
        self.gen_r = []
        self.name = name
        self.sem = None


class Prog:
    ENG = ("pe", "act", "dve", "pool", "sp")
    COMPUTE = ("pe", "act", "dve", "pool")

    def __init__(self, nc, handles):
        self.nc = nc
        self.q = {e: [] for e in self.ENG}
        hs = list(handles)
        self.csem = {e: Sem(hs.pop()) for e in self.COMPUTE}
        self.bar = Sem(hs.pop())
        self.dpool = [Sem(h) for h in hs]
        nsw = len(self.dpool) // 3
        self.dfree = {"pool": self.dpool[:nsw], "sp": self.dpool[nsw:]}
        self.waited = {e: {} for e in self.ENG}
        self.pending = []
        self.bufs = []
        self.nops = {e: 0 for e in self.COMPUTE}
        self.entries = {e: {} for e in self.COMPUTE}
        self.sig_idx = {e: [] for e in self.COMPUTE}
        self.sig_cnt = {e: [] for e in self.COMPUTE}

    def buf(self, name=""):
        b = Buf(name)
        self.bufs.append(b)
        return b

    def bufs_n(self, n, name=""):
        return [self.buf(name + str(i)) for i in range(n)]

    def _resolve(self, tok):
        import bisect
        _, eng, idx = tok
        si = self.sig_idx[eng]
        p = bisect.bisect_left(si, idx)
        if p < len(si):
            return self.csem[eng], self.sig_cnt[eng][p]
        ent = self.entries[eng][idx]
        sem = self.csem[eng]
        sem.n += 1
        ent[1] = sem.h
        si.append(idx)
        self.sig_cnt[eng].append(sem.n)
        return sem, sem.n

    def _wait(self, eng, tok):
        if tok[0] == "op":
            if tok[1] == eng and eng == "pe":
                return
            sem, tgt = self._resolve(tok)
        else:
            sem, tgt, _ = tok
        if self.waited[eng].get(id(sem), 0) >= tgt:
            return
        self.waited[eng][id(sem)] = tgt
        self.q[eng].append([lambda e, h=sem.h, t=tgt: e.wait_ge(h, t), None])

    def _wait_many(self, eng, toks):
        best = {}
        for t in toks:
            if t[0] == "op":
                k = ("op", t[1])
                if k not in best or best[k][2] < t[2]:
                    best[k] = t
            else:
                k = id(t[0])
                if k not in best or best[k][1] < t[1]:
                    best[k] = t
        for t in best.values():
            self._wait(eng, t)

    def _hazards(self, eng, reads, writes, partial):
        toks = []
        for b in reads:
            toks += b.w
        for b in writes:
            if partial == "nowaw":
                if b.r:
                    b.gen_r = b.r
                    b.r = []
                    b.w = []
                toks += b.gen_r
            else:
                toks += b.w
                toks += b.r
                toks += b.gen_r
        self._wait_many(eng, toks)

    def _commit(self, tok, reads, writes, partial):
        for b in reads:
            b.r.append(tok)
        for b in writes:
            if partial:
                b.w.append(tok)
                if partial != "nowaw":
                    b.r = []
            else:
                b.w = [tok]
                b.r = []
                b.gen_r = []

    def op(self, eng, fn, reads=(), writes=(), partial=False):
        self._hazards(eng, reads, writes, partial)
        idx = self.nops[eng]
        self.nops[eng] += 1
        ent = [fn, None]
        self.entries[eng][idx] = ent
        self.q[eng].append(ent)
        tok = ("op", eng, idx)
        self._commit(tok, reads, writes, partial)
        return tok

    def dma(self, eng, out, in_, sbuf_buf, reads=(), writes=(), partial=False):
        self._hazards(eng, reads, writes, partial)
        if sbuf_buf.sem is None:
            sbuf_buf.sem = {}
        if eng not in sbuf_buf.sem:
            sbuf_buf.sem[eng] = self.dfree[eng].pop()
        sem = sbuf_buf.sem[eng]
        sem.n += 16
        tok = (sem, sem.n, "dma")
        self.q[eng].append([lambda e, o=out, i=in_: e.dma_start(out=o, in_=i), sem.h, 16])
        self._commit(tok, reads, writes, partial)
        self.pending.append(tok)
        return tok

    def barrier(self):
        for e in self.COMPUTE:
            if self.nops[e] > 0:
                self._wait("sp", ("op", e, self.nops[e] - 1))
        self._wait_many("sp", self.pending)
        self.pending = []
        self.bar.n += 1
        k = self.bar.n
        self.q["sp"].append([lambda e, h=self.bar.h: e.sem_inc(h, 1), None])
        for e in self.COMPUTE:
            self._wait(e, (self.bar, k, "bar"))
        for b in self.bufs:
            if b.sem is not None:
                for en, sm in b.sem.items():
                    self.dfree[en].append(sm)
                b.sem = None
            b.w = []
            b.r = []
            b.gen_r = []
        self.bufs = []

    def flush(self):
        nc = self.nc
        q = self.q

        def run(e, lst):
            for ent in lst:
                ins = ent[0](e)
                if ent[1] is not None:
                    ins.then_inc(ent[1], ent[2] if len(ent) > 2 else 1)

        with nc.Block() as blk:
            @blk.tensor
            def _(e):
                run(e, q["pe"])

            @blk.scalar
            def _(e):
                run(e, q["act"])

            @blk.vector
            def _(e):
                run(e, q["dve"])

            @blk.gpsimd
            def _(e):
                run(e, q["pool"])

            @blk.sync
            def _(e):
                run(e, q["sp"])
        self.q = {e: [] for e in self.ENG}
        for e in self.COMPUTE:
            self.entries[e] = {}


_UID = [0]


def sb(es, nc, name, shape, dt):
    _UID[0] += 1
    return es.enter_context(nc.sbuf_tensor("%s_%d" % (name, _UID[0]), shape, dt))


def rev_ap(ap):
    a = [list(p) for p in ap.ap]
    step, cnt = a[-1]
    off = ap.offset + step * (cnt - 1)
    a[-1] = [-step, cnt]
    return bass.AP(ap.tensor, off, a)


def phase_M(P, nc, T):
    with (
        nc.sbuf_tensor("m_cT", [128, 32], F32) as cT,
        nc.sbuf_tensor("m_sg", [128, 32], F32) as sg,
        nc.sbuf_tensor("m_sT", [128, 32], BF16) as sT,
        nc.sbuf_tensor("m_w0", [128, 3072], BF16) as w0,
        nc.sbuf_tensor("m_w1", [128, 3072], BF16) as w1,
        nc.sbuf_tensor("m_mrow", [2, 6144], F32) as mrow,
        nc.sbuf_tensor("m_brow", [2, 6144], F32) as brow,
        nc.sbuf_tensor("m_ng", [2, 2048], F32) as ng,
        nc.sbuf_tensor("m_orow", [2, 6144], F32) as orow,
    ):
        ps = T["ps"]
        b_cT, b_sT = P.buf(), P.buf()
        b_w = P.bufs_n(2)
        wt = [w0, w1]
        b_ps = P.bufs_n(6)
        b_mrow, b_brow, b_ng, b_orow = P.buf(), P.buf(), P.buf(), P.buf()
        P.dma("sp", cT[:], T["cvecT"], b_cT, writes=[b_cT])
        P.op("act", lambda e: e.activation(sg[:], cT[:], AF.Sigmoid), reads=[b_cT], writes=[b_sT])
        P.op("dve", lambda e: e.tensor_tensor(sT[:], sg[:], cT[:], ALU.mult), reads=[b_cT, b_sT], writes=[b_sT])
        cnt = 0
        for l in range(2):
            P.dma("sp", brow[:], T["mod_b"][l:l + 1, :].partition_broadcast(2), b_brow, writes=[b_brow])
            P.dma("sp", ng[:], T["norm_g"][l:l + 1, :].partition_broadcast(2), b_ng, writes=[b_ng])
            for hf in range(2):
                for k in range(16):
                    s = cnt % 2
                    cnt += 1
                    P.dma("pool", wt[s][:], T["mod_w"][l, k * 128:(k + 1) * 128, hf * 3072:(hf + 1) * 3072],
                          b_w[s], writes=[b_w[s]])
                    for j in range(6):
                        P.op("pe", lambda e, s=s, j=j, k=k: e.matmul(
                            ps[j][0:2, :], sT[:, 2 * k:2 * k + 2], wt[s][:, j * 512:(j + 1) * 512],
                            start=(k == 0), stop=(k == 15)),
                            reads=[b_w[s], b_sT], writes=[b_ps[j]], partial=(k > 0))
                for j in range(6):
                    c0 = hf * 3072 + j * 512
                    P.op("dve", lambda e, j=j, c0=c0: e.tensor_tensor(
                        mrow[:, c0:c0 + 512], ps[j][0:2, :], brow[:, c0:c0 + 512], ALU.add),
                        reads=[b_ps[j], b_brow], writes=[b_mrow], partial=True)
            P.op("dve", lambda e: e.scalar_tensor_tensor(
                orow[:, 0:2048], mrow[:, 2048:4096], 1.0, ng[:], ALU.add, ALU.mult),
                reads=[b_mrow, b_ng], writes=[b_orow])
            P.op("dve", lambda e: e.tensor_copy(orow[:, 2048:4096], mrow[:, 0:2048]),
                 reads=[b_mrow], writes=[b_orow], partial=True)
            P.op("dve", lambda e: e.tensor_copy(orow[:, 4096:6144], mrow[:, 4096:6144]),
                 reads=[b_mrow], writes=[b_orow], partial=True)
            P.dma("sp", T["modv"][l], orow[:], b_orow, reads=[b_orow])
        P.barrier()
        P.flush()


def load_bcast_rows(P, nc, T, l, tiles, bufs):
    modv = T["modv"]
    for key, (j, r) in (("A_s", (0, 0)), ("sh_s", (1, 0)), ("A_p", (0, 1)), ("sh_p", (1, 1))):
        P.dma("sp", tiles[key][:], modv[l][r:r + 1, j * 2048:(j + 1) * 2048].partition_broadcast(128),
              bufs[key], writes=[bufs[key]])


def norm_transpose_half(P, nc, T, xsrc, tok_tiles, hT, b_hT, bc, b_bc, work):
    ps = T["ps"]
    ident = T["ident"]
    xt, b_xt = work["xt"], work["b_xt"]
    st, b_st = work["st"], work["b_st"]
    hb, b_hb = work["hb"], work["b_hb"]
    b_pst = work["b_pst"]

    def front(i):
        toff, isp = tok_tiles[i]
        s = i % 2
        x4 = i % len(xt)
        P.dma("sp", xt[x4][:], xsrc[toff:toff + 128, :], b_xt[x4], writes=[b_xt[x4]])
        jk, bjk = work["junk2"][s], work["b_junk2"][s]
        P.op("act", lambda e, s=s, jk=jk, x4=x4: e.activation(jk[:], xt[x4][:], AF.Square, accum_out=st[s][:, 0:1]),
             reads=[b_xt[x4]], writes=[bjk, b_st[s]])
        P.op("act", lambda e, s=s: e.activation(st[s][:, 1:2], st[s][:, 0:1], AF.Sqrt, scale=1.0 / D,
                                                 bias=T["epsb"][:, 0:1]),
             reads=[b_st[s]], writes=[b_st[s]], partial=True)
        P.op("dve", lambda e, s=s: e.reciprocal(st[s][:, 2:3], st[s][:, 1:2]), reads=[b_st[s]], writes=[b_st[s]],
             partial=True)
        A = bc["A_p" if isp else "A_s"]
        SH = bc["sh_p" if isp else "sh_s"]
        bA = b_bc["A_p" if isp else "A_s"]
        bS = b_bc["sh_p" if isp else "sh_s"]
        tm, btm = work["tmp2"][s], work["b_tmp2"][s]
        P.op("dve", lambda e, s=s, A=A, tm=tm, x4=x4: e.scalar_tensor_tensor(tm[:], xt[x4][:], st[s][:, 2:3], A[:],
                                                                             ALU.mult, ALU.mult),
             reads=[b_xt[x4], b_st[s], bA], writes=[btm])
        P.op("pool", lambda e, s=s, SH=SH, tm=tm: e.tensor_tensor(hb[s][:], tm[:], SH[:], ALU.add),
             reads=[btm, bS], writes=[b_hb[s]])

    def back(i):
        s = i % 2
        for half in range(2):
            pb = ps[2 * s + half]
            bp = b_pst[2 * s + half]
            pbv = pb[:].bitcast(BF16)
            for kk in range(8):
                k = half * 8 + kk
                P.op("pe", lambda e, pbv=pbv, kk=kk, k=k, s=s: e.transpose(
                    pbv[:, kk * 128:(kk + 1) * 128], hb[s][:, k * 128:(k + 1) * 128], ident[:]),
                    reads=[b_hb[s]], writes=[bp], partial=(kk > 0))
            dst = hT[:, half * 8:(half + 1) * 8, i * 128:(i + 1) * 128]
            src = pbv.rearrange("p (k t) -> p k t", k=8)
            if half == 0:
                P.op("act", lambda e, dst=dst, src=src: e.copy(dst, src), reads=[bp], writes=[b_hT], partial="nowaw")
            else:
                P.op("dve", lambda e, dst=dst, src=src: e.tensor_copy(dst, src), reads=[bp], writes=[b_hT],
                     partial="nowaw")

    n = len(tok_tiles)
    front(0)
    for i in range(n):
        if i + 1 < n:
            front(i + 1)
        back(i)


def phase_A(P, nc, T, debug=False):
    ps = T["ps"]
    W = T["a_w_in"]
    groups = []
    for g in range(2):
        groups.append(("q", [("qT", (g * 4 + j) * 128, (g * 4 + j) * 128) for j in range(4)]))
    groups.append(("k", [("kT0", 0, 1024), ("kT1", 0, 1088)]))
    for g in range(2):
        groups.append(("ga", [("gaT", (g * 4 + j) * 128, 1280 + (g * 4 + j) * 128) for j in range(4)]))
    for g in range(6):
        groups.append(("hy", [("hyT", (g * 4 + j) * 128, 2304 + (g * 4 + j) * 128) for j in range(4)]))
    for g in range(2):
        groups.append(("gh", [("ghT", (g * 4 + j) * 128, 5376 + (g * 4 + j) * 128) for j in range(4)]))

    Wv = W.rearrange("(k p) c -> p k c", p=128)
    for half in range(DBG.get('halves', 2)):
        tok_tiles = [(half * 2048 + i * 128, False) for i in range(16)] + \
                    [(LS + half * 256 + i * 128, True) for i in range(2)]
        import contextlib
        with contextlib.ExitStack() as es0:
            hT = sb(es0, nc, "a_hT", [128, 16, 2304], BF16)
            b_hT = P.buf("hT")
            with contextlib.ExitStack() as es1:
                A_ = lambda n, shp, dt: sb(es1, nc, n, shp, dt)
                xt0 = A_("a_xt0", [128, 2048], F32); xt1 = A_("a_xt1", [128, 2048], F32)
                xt2 = A_("a_xt2", [128, 2048], F32); xt3 = A_("a_xt3", [128, 2048], F32)
                junk = A_("a_junk", [128, 2048], F32); tmp = A_("a_tmp", [128, 2048], F32)
                junkb = A_("a_junkb", [128, 2048], F32); tmpb = A_("a_tmpb", [128, 2048], F32)
                st0 = A_("a_st0", [128, 4], F32); st1 = A_("a_st1", [128, 4], F32)
                hb0 = A_("a_hb0", [128, 2048], BF16); hb1 = A_("a_hb1", [128, 2048], BF16)
                As = A_("a_As", [128, 2048], F32); shs = A_("a_shs", [128, 2048], F32)
                Ap = A_("a_Ap", [128, 2048], F32); shp = A_("a_shp", [128, 2048], F32)
                bc = {"A_s": As, "sh_s": shs, "A_p": Ap, "sh_p": shp}
                b_bc = {k: P.buf(k) for k in bc}
                load_bcast_rows(P, nc, T, 0, bc, b_bc)
                work = dict(xt=[xt0, xt1, xt2, xt3], b_xt=P.bufs_n(4), junk=junk, b_junk=P.buf(), st=[st0, st1],
                            b_st=P.bufs_n(2), tmp=tmp, b_tmp=P.buf(), hb=[hb0, hb1], b_hb=P.bufs_n(2),
                            b_pst=P.bufs_n(4), junk2=[junk, junkb], b_junk2=P.bufs_n(2), tmp2=[tmp, tmpb],
                            b_tmp2=P.bufs_n(2))
                norm_transpose_half(P, nc, T, T["x"], tok_tiles, hT, b_hT, bc, b_bc, work)
                P.barrier()
                P.flush()
            if DBG.get('norm_only'):
                continue
            with contextlib.ExitStack() as es2:
                A_ = lambda n, shp, dt: sb(es2, nc, n, shp, dt)
                wg0 = A_("a_w0", [128, 16, 512], BF16); wg1 = A_("a_w1", [128, 16, 512], BF16)
                wkv = A_("a_wkv", [128, 16, 256], BF16)
                cos_t = A_("a_cos", [128, 2048], F32); sin_t = A_("a_sin", [128, 2048], F32)
                stg0 = A_("a_stg0", [128, 512], BF16); stg1 = A_("a_stg1", [128, 512], BF16)
                stg2 = A_("a_stg2", [128, 512], BF16); stg3 = A_("a_stg3", [128, 512], BF16)
                qs0 = A_("a_qs0", [128, 512], BF16); qs1 = A_("a_qs1", [128, 512], BF16)
                t1 = A_("a_t1", [128, 512], F32); t2 = A_("a_t2", [128, 512], F32)
                kvf0 = A_("a_kvf0", [128, 256], F32); kvf1 = A_("a_kvf1", [128, 256], F32)
                vb0 = A_("a_vb0", [128, 128], BF16); vb1 = A_("a_vb1", [128, 128], BF16)
                b_hT = P.buf("hT2")
                wg = [wg0, wg1]
                b_wg = P.bufs_n(2)
                b_wkv = P.buf()
                b_cos, b_sin = P.buf(), P.buf()
                stg = [stg0, stg1, stg2, stg3]
                b_stg = P.bufs_n(4)
                qs = [qs0, qs1]
                b_qs = P.bufs_n(2)
                b_t1, b_t2 = P.buf(), P.buf()
                kvf = [kvf0, kvf1]
                b_kvf = P.bufs_n(2)
                vb = [vb0, vb1]
                b_vb = P.bufs_n(2)
                b_ps = P.bufs_n(8)
                P.dma("sp", cos_t[:], T["rope_cos"][:, half * 2048:(half + 1) * 2048], b_cos, writes=[b_cos])
                P.dma("sp", sin_t[:], T["rope_sin"][:, half * 2048:(half + 1) * 2048], b_sin, writes=[b_sin])
                for kq in range(16):
                    P.dma("pool", wkv[:, kq, :], Wv[:, kq, 1024:1280], b_wkv, writes=[b_wkv], partial=(kq > 0))
                for i, (toff, isp) in enumerate(tok_tiles[:DBG.get('nkv', 99)]):
                    pi = i % 2
                    pst = ps[6 + pi]
                    for k in range(16):
                        P.op("pe", lambda e, pst=pst, k=k, i=i: e.matmul(
                            pst[:, 0:256], hT[:, k, i * 128:(i + 1) * 128], wkv[:, k, :],
                            start=(k == 0), stop=(k == 15)),
                            reads=[b_hT, b_wkv], writes=[b_ps[6 + pi]], partial=(k > 0))
                    if isp and not DBG.get('no_isp'):
                        P.op("act", lambda e, pst=pst, pi=pi: e.copy(kvf[pi][:], pst[:, 0:256]),
                             reads=[b_ps[6 + pi]], writes=[b_kvf[pi]])
                        r0 = toff - LS
                        if not DBG.get('no_nk'):
                            P.dma("sp", T["nk"][r0:r0 + 128, :], kvf[pi][:, 0:128], b_kvf[pi], reads=[b_kvf[pi]])
                        if not DBG.get('no_nv'):
                            P.dma("sp", T["nv"][r0:r0 + 128, :], kvf[pi][:, 128:256], b_kvf[pi], reads=[b_kvf[pi]])
                    P.op("act", lambda e, pst=pst, pi=pi: e.copy(vb[pi][:], pst[:, 128:256]),
                         reads=[b_ps[6 + pi]], writes=[b_vb[pi]])
                    P.dma("sp", T["vtok"][toff:toff + 128, :], vb[pi][:], b_vb[pi], reads=[b_vb[pi]])
                chunks = [(c * 512, 512, half * 2048 + c * 512, False) for c in range(4)] + \
                         [(2048, 256, LS + half * 256, True)]
                evac_i = 0
                psi = 0
                for gi, (kind, blocks) in enumerate(groups[:DBG.get('ngroups', 99)] if not DBG.get('gsel') else [groups[i] for i in DBG['gsel']]):
                    s = gi % 2
                    if kind == "k":
                        for j, (name, r0, c0) in enumerate(blocks):
                            for dup in range(2):
                                for kq in range(16):
                                    P.dma("pool", wg[s][:, kq, j * 128 + dup * 64:j * 128 + dup * 64 + 64],
                                          Wv[:, kq, c0:c0 + 64], b_wg[s], writes=[b_wg[s]],
                                          partial=(j + dup + kq > 0))
                    else:
                        c0 = blocks[0][2]
                        for kq in range(16):
                            P.dma("pool", wg[s][:, kq, 0:512], Wv[:, kq, c0:c0 + 512],
                                  b_wg[s], writes=[b_wg[s]], partial=(kq > 0))
                    for (l0, n, g0, isp) in chunks:
                        for j, (name, r0, c0) in enumerate(blocks):
                            pj = psi % 4
                            psi += 1
                            pst = ps[pj]
                            for k in range(16):
                                P.op("pe", lambda e, pst=pst, k=k, s=s, j=j, l0=l0, n=n: e.matmul(
                                    pst[:, 0:n], wg[s][:, k, j * 128:(j + 1) * 128], hT[:, k, l0:l0 + n],
                                    start=(k == 0), stop=(k == 15)),
                                    reads=[b_hT, b_wg[s]], writes=[b_ps[pj]], partial=(k > 0))
                            if name.startswith("kT"):
                                dst = T["kT"][int(name[2]), :, g0:g0 + n]
                            else:
                                dst = T[name][r0:r0 + 128, g0:g0 + n]
                            si = evac_i % 4
                            evac_i += 1
                            if kind in ("q", "k") and not isp:
                                qi = evac_i % 2
                                P.op("act", lambda e, pst=pst, qi=qi, n=n: e.copy(qs[qi][:, 0:n], pst[:, 0:n]),
                                     reads=[b_ps[pj]], writes=[b_qs[qi]])
                                pr = ps[4 + qi]
                                P.op("pe", lambda e, pr=pr, qi=qi, n=n: e.matmul(
                                    pr[:, 0:n], T["ropeP"][:], qs[qi][:, 0:n], start=True, stop=True),
                                    reads=[b_qs[qi]], writes=[b_ps[4 + qi]])
                                P.op("dve", lambda e, qi=qi, l0=l0, n=n: e.tensor_tensor(
                                    t1[:, 0:n], qs[qi][:, 0:n], cos_t[:, l0:l0 + n], ALU.mult),
                                    reads=[b_qs[qi], b_cos], writes=[b_t1])
                                P.op("dve", lambda e, pr=pr, l0=l0, n=n: e.tensor_tensor(
                                    t2[:, 0:n], pr[:, 0:n], sin_t[:, l0:l0 + n], ALU.mult),
                                    reads=[b_ps[4 + qi], b_sin], writes=[b_t2])
                                P.op("dve", lambda e, si=si, n=n: e.tensor_tensor(
                                    stg[si][:, 0:n], t1[:, 0:n], t2[:, 0:n], ALU.add),
                                    reads=[b_t1, b_t2], writes=[b_stg[si]])
                            else:
                                if evac_i % 2 == 0:
                                    P.op("act", lambda e, pst=pst, si=si, n=n: e.copy(stg[si][:, 0:n], pst[:, 0:n]),
                                         reads=[b_ps[pj]], writes=[b_stg[si]])
                                else:
                                    P.op("dve", lambda e, pst=pst, si=si, n=n: e.tensor_copy(
                                        stg[si][:, 0:n], pst[:, 0:n]), reads=[b_ps[pj]], writes=[b_stg[si]])
                            P.dma("sp", dst, stg[si][:, 0:n], b_stg[si], reads=[b_stg[si]])
                P.barrier()
                P.flush()


def bcast_last(ap2d, n):
    a = [list(p) for p in ap2d.ap]
    return bass.AP(ap2d.tensor, ap2d.offset, a + [[0, n]])


def phase_B(P, nc, T):
    import contextlib
    ps = T["ps"]
    ident = T["ident"]
    seqs = [(0, LS, True), (LS, LP, False), (LS + LP, LP, False)]
    with contextlib.ExitStack() as es:
        A_ = lambda n, shp, dt: sb(es, nc, n, shp, dt)
        kT = [A_("b_kT0", [128, LS], BF16), A_("b_kT1", [128, LS], BF16)]
        kcT = [A_("b_kc0", [128, 512], BF16), A_("b_kc1", [128, 512], BF16)]
        ckd = A_("b_ckd", [128, 4, 2, 2, 64], BF16)
        vaug = A_("b_vaug", [128, LS // 128, 2, 65], BF16)
        cvaug = A_("b_cvaug", [128, 4, 2, 65], BF16)
        qT = [A_("b_q0", [128, LS], BF16), A_("b_q1", [128, LS], BF16)]
        ga = [A_("b_ga0", [128, LS], BF16), A_("b_ga1", [128, LS], BF16)]
        sga = A_("b_sga", [128, LS], BF16)
        ptA = [A_("b_ptA0", [128, 512], BF16), A_("b_ptA1", [128, 512], BF16)]
        ptB = [A_("b_ptB0", [128, 384], BF16), A_("b_ptB1", [128, 384], BF16)]
        att = [A_("b_att0", [128, 128], BF16), A_("b_att1", [128, 128], BF16)]
        stage = [A_("b_stg0", [128, 512], BF16), A_("b_stg1", [128, 512], BF16)]
        mask = A_("b_mask", [128, 384], BF16)
        sinkb = A_("b_sinkb", [128, 16], F32)
        esink = A_("b_esink", [128, 16], F32)
        den = [A_("b_den0", [128, 2], F32), A_("b_den1", [128, 2], F32)]
        rden = [A_("b_rden0", [128, 2], F32), A_("b_rden1", [128, 2], F32)]

        b_mask, b_es = P.buf(), P.buf()
        P.dma("sp", mask[:], T["d_amask"], b_mask, writes=[b_mask])
        P.dma("sp", sinkb[:], T["a_sink"][0:1, :].partition_broadcast(128), b_es, writes=[b_es])
        P.op("act", lambda e: e.activation(esink[:], sinkb[:], AF.Exp), reads=[b_es], writes=[b_es])
        P.op("dve", lambda e: e.memset(vaug[:], 1.0), writes=[b_mask], partial=True)
        P.op("dve", lambda e: e.memset(cvaug[:], 1.0), writes=[b_mask], partial=True)
        b_ckd, b_kc, b_cv = P.buf(), P.bufs_n(2), P.buf()
        ckv = T["ck"].rearrange("(t p) c -> p t c", p=128)
        cvv = T["cv"].rearrange("(t p) c -> p t c", p=128)
        for kv in range(2):
            for dup in range(2):
                P.dma("pool", ckd[:, :, kv, dup, :], ckv[:, :, kv * 64:(kv + 1) * 64], b_ckd, writes=[b_ckd],
                      partial=True)
            P.dma("pool", cvaug[:, :, kv, 0:64], cvv[:, :, kv * 64:(kv + 1) * 64], b_cv, reads=[b_mask],
                  writes=[b_cv], partial=True)
        b_pt = P.bufs_n(8)
        for kv in range(2):
            for st_ in range(4):
                pb = ps[st_ % 2][:].bitcast(BF16)
                P.op("pe", lambda e, pb=pb, st_=st_, kv=kv: e.transpose(
                    pb[:, 0:128], ckd[:, st_, kv].rearrange("p a b -> p (a b)"), ident[:]),
                    reads=[b_ckd], writes=[b_pt[st_ % 2]])
                P.op("dve", lambda e, pb=pb, st_=st_, kv=kv: e.tensor_copy(
                    kcT[kv][:, st_ * 128:(st_ + 1) * 128], pb[:, 0:128]),
                    reads=[b_pt[st_ % 2]], writes=[b_kc[kv]], partial=True)
        P.barrier()
        b_kT, b_v = P.bufs_n(2), P.buf()
        b_q, b_ga, b_sga = P.bufs_n(2), P.bufs_n(2), P.buf()
        b_ptA, b_ptB, b_att, b_stage = P.bufs_n(2), P.bufs_n(2), P.bufs_n(2), P.bufs_n(2)
        b_den = P.bufs_n(2)
        b_ps = P.bufs_n(8)
        cnt = 0
        for (tok0, L, has_ctx) in seqs:
            nqb = L // 128
            for kv in range(2):
                P.dma("sp", kT[kv][:, 0:L], T["kT"][kv, :, tok0:tok0 + L], b_kT[kv], writes=[b_kT[kv]])
                P.dma("sp", vaug[:, 0:nqb, kv, 0:64],
                      T["vtok"][tok0:tok0 + L, kv * 64:(kv + 1) * 64].rearrange("(t p) c -> p t c", p=128),
                      b_v, writes=[b_v], partial=(kv > 0))
            for hp in range(8):
                s = cnt % 2
                cnt += 1
                kv = hp // 4
                P.dma("sp", qT[s][:, 0:L], T["qT"][hp * 128:(hp + 1) * 128, tok0:tok0 + L], b_q[s], writes=[b_q[s]])
                P.dma("sp", ga[s][:, 0:L], T["gaT"][hp * 128:(hp + 1) * 128, tok0:tok0 + L], b_ga[s],
                      writes=[b_ga[s]])
                P.op("act", lambda e, s=s, L=L: e.activation(sga[:, 0:L], ga[s][:, 0:L], AF.Silu),
                     reads=[b_ga[s]], writes=[b_sga])
                def geom(qb, nqb=nqb, has_ctx=has_ctx):
                    if has_ctx:
                        loc = [(j, j - qb + 1) for j in (qb - 1, qb, qb + 1) if 0 <= j < nqb]
                    else:
                        loc = [(j, j) for j in range(nqb)]
                    return loc, loc[0][1], loc[-1][1] + 1

                def S_unit(qb, hh, s=s, kv=kv, has_ctx=has_ctx):
                    loc, lo, hi = geom(qb)
                    pr = slice(hh * 64, (hh + 1) * 64)
                    pa, pbk = ps[hh * 2], ps[hh * 2 + 1]
                    qsl = qT[s][pr, qb * 128:(qb + 1) * 128]
                    if has_ctx:
                        for c in range(4):
                            P.op("pe", lambda e, pa=pa, c=c, pr=pr, qsl=qsl, kv=kv: e.matmul(
                                pa[:, c * 128:(c + 1) * 128], kcT[kv][pr, c * 128:(c + 1) * 128], qsl,
                                start=True, stop=True),
                                reads=[b_kc[kv], b_q[s]], writes=[b_ps[hh * 2]], partial=(c > 0))
                        P.op("act", lambda e, pa=pa, hh=hh: e.activation(ptA[hh][:], pa[:], AF.Exp, scale=0.125),
                             reads=[b_ps[hh * 2]], writes=[b_ptA[hh]])
                    for n_, (j, sl) in enumerate(loc):
                        P.op("pe", lambda e, pbk=pbk, sl=sl, j=j, pr=pr, qsl=qsl, kv=kv: e.matmul(
                            pbk[:, sl * 128:(sl + 1) * 128], kT[kv][pr, j * 128:(j + 1) * 128], qsl,
                            start=True, stop=True),
                            reads=[b_kT[kv], b_q[s]], writes=[b_ps[hh * 2 + 1]], partial=(n_ > 0))
                    P.op("act", lambda e, pbk=pbk, hh=hh, lo=lo, hi=hi: e.activation(
                        ptB[hh][:, lo * 128:hi * 128], pbk[:, lo * 128:hi * 128], AF.Exp, scale=0.125),
                        reads=[b_ps[hh * 2 + 1]], writes=[b_ptB[hh]])
                    if has_ctx:
                        P.op("dve", lambda e, hh=hh, lo=lo, hi=hi: e.tensor_tensor(
                            ptB[hh][:, lo * 128:hi * 128], ptB[hh][:, lo * 128:hi * 128],
                            mask[:, lo * 128:hi * 128], ALU.mult),
                            reads=[b_ptB[hh], b_mask], writes=[b_ptB[hh]])

                def PV_unit(qb, hh, kv=kv, has_ctx=has_ctx):
                    loc, lo, hi = geom(qb)
                    o = qb % 2
                    psOv = ps[4 + o][:, 0:130].rearrange("p (h c) -> p h c", h=2)
                    mm = []
                    if has_ctx:
                        for c in range(4):
                            mm.append((ptA[hh][:, c * 128:(c + 1) * 128], cvaug[:, c, kv, :], b_ptA[hh], b_cv))
                    for (j, sl) in loc:
                        mm.append((ptB[hh][:, sl * 128:(sl + 1) * 128], vaug[:, j, kv, :], b_ptB[hh], b_v))
                    for n_, (lh, rh, bl, br) in enumerate(mm):
                        P.op("pe", lambda e, psOv=psOv, hh=hh, lh=lh, rh=rh, n_=n_, nm=len(mm): e.matmul(
                            psOv[:, hh, :], lh, rh, start=(n_ == 0), stop=(n_ == nm - 1)),
                            reads=[bl, br], writes=[b_ps[4 + o]], partial=(n_ > 0 or hh > 0))

                def FIN_unit(qb, s=s, hp=hp, nqb=nqb, tok0=tok0, cnt=cnt):
                    o = qb % 2
                    psOv = ps[4 + o][:, 0:130].rearrange("p (h c) -> p h c", h=2)
                    P.op("dve", lambda e, psOv=psOv, o=o, hp=hp: e.tensor_tensor(
                        den[o][:], psOv[:, :, 64], esink[:, hp * 2:hp * 2 + 2], ALU.add),
                        reads=[b_ps[4 + o], b_es], writes=[b_den[o]])
                    P.op("dve", lambda e, o=o: e.reciprocal(rden[o][:], den[o][:]), reads=[b_den[o]],
                         writes=[b_den[o]], partial=True)
                    P.op("dve", lambda e, psOv=psOv, o=o: e.tensor_tensor(
                        att[o][:].rearrange("p (h c) -> p h c", h=2), psOv[:, :, 0:64], bcast_last(rden[o][:], 64),
                        ALU.mult),
                        reads=[b_ps[4 + o], b_den[o]], writes=[b_att[o]])
                    pT = ps[6 + o][:].bitcast(BF16)
                    P.op("pe", lambda e, pT=pT, o=o: e.transpose(pT[:, 0:128], att[o][:], ident[:]),
                         reads=[b_att[o]], writes=[b_ps[6 + o]])
                    g4 = qb // 4
                    sg_ = (cnt * 1024 + g4) % 2
                    P.op("dve", lambda e, pT=pT, sg_=sg_, qb=qb: e.tensor_tensor(
                        stage[sg_][:, (qb % 4) * 128:(qb % 4 + 1) * 128], pT[:, 0:128],
                        sga[:, qb * 128:(qb + 1) * 128], ALU.mult),
                        reads=[b_ps[6 + o], b_sga], writes=[b_stage[sg_]], partial=(qb % 4 > 0))
                    if qb % 4 == 3 or qb == nqb - 1:
                        n = (qb % 4 + 1) * 128
                        t0 = tok0 + g4 * 512
                        P.dma("sp", T["mixT"][hp * 128:(hp + 1) * 128, t0:t0 + n], stage[sg_][:, 0:n],
                              b_stage[sg_], reads=[b_stage[sg_]])

                units = [(qb, hh) for qb in range(nqb) for hh in range(2)]
                S_unit(*units[0])
                pend_fin = None
                for ui, (qb, hh) in enumerate(units):
                    if ui + 1 < len(units):
                        S_unit(*units[ui + 1])
                    PV_unit(qb, hh)
                    if pend_fin is not None:
                        FIN_unit(pend_fin)
                        pend_fin = None
                    if hh == 1:
                        pend_fin = qb
                if pend_fin is not None:
                    FIN_unit(pend_fin)
        P.barrier()
        P.flush()


def phase_C0(P, nc, T):
    import contextlib
    seqs = [(0, LS), (LS, LP), (LS + LP, LP)]
    with contextlib.ExitStack() as es:
        A_ = lambda n, shp, dt: sb(es, nc, n, shp, dt)
        hsw = A_("c0_hsw", [128, 4, 24], F32)
        ut = [A_("c0_ut0", [128, LS + 2], BF16), A_("c0_ut1", [128, LS + 2], BF16)]
        accs = [A_("c0_acc", [128, LS], F32), A_("c0_accb", [128, LS], F32)]
        acc2s = [A_("c0_acc2", [128, LS], F32), A_("c0_acc2b", [128, LS], F32)]
        ob = [A_("c0_ob0", [128, LS], BF16), A_("c0_ob1", [128, LS], BF16)]
        b_hsw, b_ut, b_accs, b_acc2s, b_ob = P.buf(), P.bufs_n(2), P.bufs_n(2), P.bufs_n(2), P.bufs_n(2)
        P.dma("sp", hsw[:], T["hsw"], b_hsw, writes=[b_hsw])
        it = 0
        for (tok0, L) in seqs:
            for ct in range(24):
                s = it % 2
                it += 1
                acc, acc2, b_acc, b_acc2 = accs[s], acc2s[s], b_accs[s], b_acc2s[s]
                P.op("pool", lambda e, s=s: e.memset(ut[s][:, 0:1], 0.0), writes=[b_ut[s]])
                P.op("pool", lambda e, s=s, L=L: e.memset(ut[s][:, L + 1:L + 2], 0.0), writes=[b_ut[s]], partial=True)
                P.dma("sp", ut[s][:, 1:L + 1], T["hyT"][ct * 128:(ct + 1) * 128, tok0:tok0 + L], b_ut[s],
                      writes=[b_ut[s]], partial=True)
                P.op("act", lambda e, s=s, L=L, ct=ct, acc=acc: e.activation(
                    acc[:, 0:L], ut[s][:, 0:L], AF.Identity, scale=hsw[:, 0, ct:ct + 1], bias=hsw[:, 3, ct:ct + 1]),
                    reads=[b_ut[s], b_hsw], writes=[b_acc])
                P.op("dve", lambda e, s=s, L=L, ct=ct, acc=acc, acc2=acc2: e.scalar_tensor_tensor(
                    acc2[:, 0:L], ut[s][:, 1:L + 1], hsw[:, 1, ct:ct + 1], acc[:, 0:L], ALU.mult, ALU.add),
                    reads=[b_ut[s], b_acc, b_hsw], writes=[b_acc2])
                P.op("dve", lambda e, s=s, L=L, ct=ct, acc2=acc2: e.scalar_tensor_tensor(
                    ob[s][:, 0:L], ut[s][:, 2:L + 2], hsw[:, 2, ct:ct + 1], acc2[:, 0:L], ALU.mult, ALU.add),
                    reads=[b_ut[s], b_acc2, b_hsw], writes=[b_ob[s]])
                P.dma("sp", T["ucT"][ct * 128:(ct + 1) * 128, tok0:tok0 + L], ob[s][:, 0:L], b_ob[s],
                      reads=[b_ob[s]])
        P.barrier()
        P.flush()


def phase_CF(P, nc, T, Lf, taps_dst, zemb, tposb):
    import contextlib
    import math
    ps = T["ps"]
    nch = max(1, Lf // 512)
    cw = min(512, Lf)
    with contextlib.ExitStack() as es:
        A_ = lambda n, shp, dt: sb(es, nc, n, shp, dt)
        ze = A_("cf_ze", [33, Lf], F32)
        tp_ = A_("cf_tpos", [128, Lf], F32)
        w1 = A_("cf_w1", [33, 64], F32)
        w2 = A_("cf_w2", [64, 64], F32)
        bb = A_("cf_bb", [64, 2], F32)
        w3 = A_("cf_w3", [64, 4096], BF16)
        dec = A_("cf_dec", [128, 32], F32)
        ndec = A_("cf_ndec", [128, 32], F32)
        h1 = A_("cf_h1", [64, Lf], F32)
        h2 = A_("cf_h2", [64, Lf], F32)
        h2b = A_("cf_h2b", [64, Lf], BF16)
        tmp = A_("cf_tmp", [64, 512], F32)
        tmq = A_("cf_tmq", [64, 512], F32)
        raw = [A_("cf_raw0", [128, Lf], F32), A_("cf_raw1", [128, Lf], F32)]
        win = [A_("cf_win0", [128, 512], F32), A_("cf_win1", [128, 512], F32)]
        junk = A_("cf_junk", [128, Lf], F32)
        ss = A_("cf_ss", [128, 8], F32)
        tpb = [A_("cf_tp0", [128, 2 * Lf], BF16), A_("cf_tp1", [128, 2 * Lf], BF16)]
        b_c = P.bufs_n(8)
        P.dma("sp", ze[:], zemb, b_c[0], writes=[b_c[0]])
        P.dma("sp", tp_[:], tposb, b_c[1], writes=[b_c[1]])
        P.dma("sp", w1[:], T["hy_w1"], b_c[2], writes=[b_c[2]])
        P.dma("sp", w2[:], T["hy_w2"], b_c[3], writes=[b_c[3]])
        P.dma("sp", bb[:], T["hy_bb"], b_c[4], writes=[b_c[4]])
        for q4 in range(4):
            P.dma("pool", w3[:, q4 * 1024:(q4 + 1) * 1024], T["hy_w3"][:, q4 * 1024:(q4 + 1) * 1024], b_c[5],
                  writes=[b_c[5]], partial=(q4 > 0))
        P.dma("sp", dec[:], T["hy_decT"], b_c[6], writes=[b_c[6]])
        P.op("act", lambda e: e.activation(dec[:], dec[:], AF.Abs), reads=[b_c[6]], writes=[b_c[6]])
        P.op("dve", lambda e: e.tensor_scalar(ndec[:], dec[:], -1.0, None, ALU.mult),
             reads=[b_c[6]], writes=[b_c[7]])
        b_h1, b_h2, b_h2b, b_tmp, b_tmq = P.buf(), P.buf(), P.buf(), P.buf(), P.buf()
        b_ps = P.bufs_n(8)
        TWO_PI = 2.0 * math.pi
        for layer in range(2):
            src = ze if layer == 0 else h1
            wm = w1 if layer == 0 else w2
            kk = 33 if layer == 0 else 64
            dst = h1 if layer == 0 else h2
            b_src = b_c[0] if layer == 0 else b_h1
            b_dst = b_h1 if layer == 0 else b_h2
            for c in range(nch):
                pj = c % 2
                P.op("pe", lambda e, pj=pj, c=c, src=src, wm=wm, kk=kk: e.matmul(
                    ps[pj][0:64, 0:cw], wm[0:kk, :], src[0:kk, c * cw:(c + 1) * cw], start=True, stop=True),
                    reads=[b_src, b_c[2], b_c[3]], writes=[b_ps[pj]])
                MAGIC = 12582912.0
                P.op("act", lambda e, pj=pj, layer=layer: e.activation(
                    tmp[:, 0:cw], ps[pj][0:64, 0:cw], AF.Identity, bias=bb[:, layer:layer + 1]),
                    reads=[b_ps[pj], b_c[4]], writes=[b_tmp])
                P.op("dve", lambda e: e.tensor_scalar(
                    tmq[:, 0:cw], tmp[:, 0:cw], 1.0 / TWO_PI, MAGIC, ALU.mult, ALU.add),
                    reads=[b_tmp], writes=[b_tmq])
                P.op("dve", lambda e: e.tensor_scalar(
                    tmq[:, 0:cw], tmq[:, 0:cw], -MAGIC, -TWO_PI, ALU.add, ALU.mult),
                    reads=[b_tmq], writes=[b_tmq])
                P.op("dve", lambda e: e.tensor_tensor(tmp[:, 0:cw], tmp[:, 0:cw], tmq[:, 0:cw], ALU.add),
                     reads=[b_tmp, b_tmq], writes=[b_tmp])
                P.op("act", lambda e, c=c, dst=dst: e.activation(dst[:, c * cw:(c + 1) * cw], tmp[:, 0:cw], AF.Sin),
                     reads=[b_tmp], writes=[b_dst], partial=(c > 0))
        P.op("act", lambda e: e.copy(h2b[:], h2[:]), reads=[b_h2], writes=[b_h2b])
        b_raw, b_win, b_junk, b_ss, b_tpb = P.bufs_n(2), P.bufs_n(2), P.buf(), P.buf(), P.bufs_n(2)
        it = 0
        for order in range(2):
            for blk in range(8):
                s = it % 2
                it += 1
                for d in range(2):
                    cbk = d * 16 + order * 8 + blk
                    for c in range(nch):
                        pj = 2 + (c % 2)
                        wj = c % 2
                        P.op("pe", lambda e, pj=pj, c=c, cbk=cbk: e.matmul(
                            ps[pj][:, 0:cw], w3[:, cbk * 128:(cbk + 1) * 128], h2b[:, c * cw:(c + 1) * cw],
                            start=True, stop=True),
                            reads=[b_h2b, b_c[5]], writes=[b_ps[pj]])
                        P.op("act", lambda e, wj=wj, c=c, cbk=cbk: e.activation(
                            win[wj][:, 0:cw], tp_[:, c * cw:(c + 1) * cw], AF.Exp, scale=ndec[:, cbk:cbk + 1]),
                            reads=[b_c[1], b_c[7]], writes=[b_win[wj]])
                        P.op("dve", lambda e, pj=pj, wj=wj, c=c, d=d: e.tensor_tensor(
                            raw[d][:, c * cw:(c + 1) * cw], ps[pj][:, 0:cw], win[wj][:, 0:cw], ALU.mult),
                            reads=[b_ps[pj], b_win[wj]], writes=[b_raw[d]], partial=(c > 0))
                P.op("act", lambda e: e.activation(junk[:], raw[0][:], AF.Square, accum_out=ss[:, 0:1]),
                     reads=[b_raw[0]], writes=[b_junk, b_ss])
                P.op("act", lambda e: e.activation(junk[:, 1:Lf], raw[1][:, 1:Lf], AF.Square, accum_out=ss[:, 1:2]),
                     reads=[b_raw[1]], writes=[b_junk, b_ss], partial=True)
                P.op("dve", lambda e: e.tensor_tensor(ss[:, 2:3], ss[:, 0:1], ss[:, 1:2], ALU.add),
                     reads=[b_ss], writes=[b_ss], partial=True)
                P.op("act", lambda e: e.activation(ss[:, 3:4], ss[:, 2:3], AF.Sqrt, bias=T["epsb"][:, 0:1]),
                     reads=[b_ss], writes=[b_ss], partial=True)
                P.op("dve", lambda e: e.reciprocal(ss[:, 4:5], ss[:, 3:4]), reads=[b_ss], writes=[b_ss], partial=True)
                P.op("dve", lambda e: e.tensor_scalar(ss[:, 5:6], ss[:, 4:5], -1.0, None, ALU.mult),
                     reads=[b_ss], writes=[b_ss], partial=True)
                P.op("act", lambda e, s=s: e.activation(tpb[s][:, 0:Lf], raw[0][:], AF.Copy, scale=ss[:, 4:5]),
                     reads=[b_raw[0], b_ss], writes=[b_tpb[s]])
                P.op("pool", lambda e, s=s: e.memset(tpb[s][:, Lf:Lf + 1], 0.0), writes=[b_tpb[s]], partial=True)
                P.op("dve", lambda e, s=s: e.tensor_scalar(
                    rev_ap(tpb[s][:, Lf + 1:2 * Lf]), raw[1][:, 1:Lf], ss[:, 5:6], None, ALU.mult),
                    reads=[b_raw[1], b_ss], writes=[b_tpb[s]], partial=True)
                P.dma("sp", taps_dst[order, blk * 128:(blk + 1) * 128, :], tpb[s][:], b_tpb[s], reads=[b_tpb[s]])
        P.barrier()
        P.flush()


def _split_dma(P, eng, dst, src, buf, nsplit, axis_len, mk_dst, mk_src, **kw):
    step = axis_len // nsplit
    for i in range(nsplit):
        P.dma(eng, mk_dst(i * step, (i + 1) * step), mk_src(i * step, (i + 1) * step), buf,
              partial=(i > 0 or kw.get("partial", False)), **{k: v for k, v in kw.items() if k != "partial"})


def _f1_stage(P, nc, T, xin, K, b_xin, F1, b_F1, Cb, b_Cb, b_ps, ev):
    ps = T["ps"]
    for g in range(16):
        pj = g % 2
        for cc in range(4):
            c = g * 4 + cc
            P.op("pe", lambda e, pj=pj, cc=cc, c=c: e.matmul(
                ps[pj][0:64, cc * 128:(cc + 1) * 128], xin[0:K, c, :], F1[0:K, 0:128], start=True, stop=True),
                reads=[b_xin, b_F1], writes=[b_ps[pj]], partial=(cc > 0))
            P.op("pe", lambda e, pj=pj, cc=cc, c=c: e.matmul(
                ps[pj][64:128, cc * 128:(cc + 1) * 128], xin[0:K, c, :], F1[0:K, 128:256], start=True, stop=True,
                tile_position=(0, 64)),
                reads=[b_xin, b_F1], writes=[b_ps[pj]], partial=True)
        src = ps[pj][:].rearrange("p (c k) -> p k c", c=4)
        dst = Cb[:, :, g * 4:(g + 1) * 4]
        ev(dst, src, [b_ps[pj]], [b_Cb], g)


def _evac_alt(P):
    def ev(dst, src, reads, writes, i):
        if i % 2 == 0:
            P.op("act", lambda e: e.copy(dst, src), reads=reads, writes=writes, partial="nowaw")
        else:
            P.op("dve", lambda e: e.tensor_copy(dst, src), reads=reads, writes=writes, partial="nowaw")
    return ev


def _evac_act(P):
    def ev(dst, src, reads, writes, i):
        P.op("act", lambda e: e.copy(dst, src), reads=reads, writes=writes, partial="nowaw")
    return ev


def phase_CT(P, nc, T):
    import contextlib
    ps = T["ps"]
    with contextlib.ExitStack() as es:
        A_ = lambda n, shp, dt: sb(es, nc, n, shp, dt)
        F1 = A_("ct_F1", [128, 256], BF16)
        Grr = A_("ct_Grr", [128, 128, 64], BF16)
        Gii = A_("ct_Gii", [128, 128, 64], BF16)
        xt = [A_("ct_xt0", [128, 64, 64], BF16), A_("ct_xt1", [128, 64, 64], BF16)]
        Cb = A_("ct_Cb", [128, 128, 64], BF16)
        Hr = [A_("ct_Hr0", [64, 64, 128], BF16), A_("ct_Hr1", [64, 64, 128], BF16)]
        Hi = [A_("ct_Hi0", [64, 64, 128], BF16), A_("ct_Hi1", [64, 64, 128], BF16)]
        b_F1, b_G = P.buf(), P.buf()
        P.dma("sp", F1[:], T["d_F1"], b_F1, writes=[b_F1])
        for q in range(4):
            P.dma("sp", Grr[:, q * 32:(q + 1) * 32, :], T["d_Grr"][:, q * 32:(q + 1) * 32, :], b_G, writes=[b_G],
                  partial=(q > 0))
            P.dma("sp", Gii[:, q * 32:(q + 1) * 32, :], T["d_Gii"][:, q * 32:(q + 1) * 32, :], b_G, writes=[b_G],
                  partial=True)
        b_xt, b_Cb, b_Hr, b_Hi = P.bufs_n(2), P.buf(), P.bufs_n(2), P.bufs_n(2)
        b_ps = P.bufs_n(8)
        ev = _evac_alt(P)
        it = 0
        for order in range(2):
            for cb in range(16):
                s = it % 2
                it += 1
                srcv = T["tapsS"][order, cb * 64:(cb + 1) * 64, :].rearrange("c (a b) -> a c b", b=64)
                for q in range(8):
                    P.dma("sp", xt[s][:, q * 8:(q + 1) * 8, :], srcv[:, q * 8:(q + 1) * 8, :], b_xt[s],
                          writes=[b_xt[s]], partial=(q > 0))
                _f1_stage(P, nc, T, xt[s], 128, b_xt[s], F1, b_F1, Cb, b_Cb, b_ps, ev)
                for g in range(16):
                    pj = 2 + (g % 2) * 2
                    for kk in range(8):
                        k1 = g * 8 + kk
                        P.op("pe", lambda e, pj=pj, kk=kk, k1=k1: e.matmul(
                            ps[pj][0:64, kk * 64:(kk + 1) * 64], Grr[:, k1, :], Cb[:, k1, :], start=True, stop=True),
                            reads=[b_G, b_Cb], writes=[b_ps[pj]], partial=(kk > 0))
                        P.op("pe", lambda e, pj=pj, kk=kk, k1=k1: e.matmul(
                            ps[pj + 1][0:64, kk * 64:(kk + 1) * 64], Gii[:, k1, :], Cb[:, k1, :], start=True,
                            stop=True),
                            reads=[b_G, b_Cb], writes=[b_ps[pj + 1]], partial=(kk > 0))
                    P.op("act", lambda e, pj=pj, g=g, s=s: e.copy(
                        Hr[s][:, :, g * 8:(g + 1) * 8], ps[pj][0:64, :].rearrange("p (k c) -> p c k", k=8)),
                        reads=[b_ps[pj]], writes=[b_Hr[s]], partial="nowaw")
                    P.op("dve", lambda e, pj=pj, g=g, s=s: e.tensor_copy(
                        Hi[s][:, :, g * 8:(g + 1) * 8], ps[pj + 1][0:64, :].rearrange("p (k c) -> p c k", k=8)),
                        reads=[b_ps[pj + 1]], writes=[b_Hi[s]], partial="nowaw")
                P.dma("sp", T["Hs"][order, cb, 0], Hr[s][:].rearrange("p c k -> p (c k)"), b_Hr[s], reads=[b_Hr[s]])
                P.dma("sp", T["Hs"][order, cb, 1], Hi[s][:].rearrange("p c k -> p (c k)"), b_Hi[s], reads=[b_Hi[s]])
        P.barrier()
        P.flush()


def phase_CS(P, nc, T):
    import contextlib
    ps = T["ps"]
    with contextlib.ExitStack() as es:
        A_ = lambda n, shp, dt: sb(es, nc, n, shp, dt)
        F1 = A_("cs_F1", [128, 256], BF16)
        G = A_("cs_G", [128, 128, 64], BF16)
        M1 = A_("cs_M1", [64, 128], BF16)
        M1p = A_("cs_M1p", [64, 128], BF16)
        T2r = A_("cs_T2r", [128, 64, 64], BF16)
        T2i = A_("cs_T2i", [128, 64, 64], BF16)
        zc = A_("cs_zc", [64, 64, 64], BF16)
        x1 = A_("cs_x1", [64, 64, 64], BF16)
        x2 = A_("cs_x2", [64, 64, 64], BF16)
        gh = A_("cs_gh", [64, 64, 64], BF16)
        z2 = A_("cs_z2", [64, 64, 64], BF16)
        bz = A_("cs_bz", [64, 64, 64], BF16)
        cv = A_("cs_cv", [64, 64, 64], F32)
        bias = A_("cs_bias", [64, 2, 64], F32)
        Cb = A_("cs_Cb", [128, 128, 64], BF16)
        Db = A_("cs_Db", [128, 128, 64], BF16)
        P1 = A_("cs_P1", [64, 64, 128], BF16)
        P2 = A_("cs_P2", [64, 64, 128], BF16)
        Hr = A_("cs_Hr", [64, 64, 128], BF16)
        Hi = A_("cs_Hi", [64, 64, 128], BF16)
        Xs = [A_("cs_Xs0", [64, 512], BF16), A_("cs_Xs1", [64, 512], BF16)]
        b_k = P.bufs_n(6)
        P.dma("sp", F1[:], T["d_F1"], b_k[0], writes=[b_k[0]])
        for q in range(4):
            P.dma("sp", G[:, q * 32:(q + 1) * 32, :], T["d_G"][:, q * 32:(q + 1) * 32, :], b_k[1], writes=[b_k[1]],
                  partial=(q > 0))
        P.dma("sp", M1[:], T["d_M1"], b_k[2], writes=[b_k[2]])
        P.dma("sp", M1p[:], T["d_M1p"], b_k[3], writes=[b_k[3]])
        P.dma("sp", T2r[:], T["d_T2r"], b_k[4], writes=[b_k[4]])
        P.dma("sp", T2i[:], T["d_T2in"], b_k[5], writes=[b_k[5]])
        b_F1, b_G, b_M1, b_M1p, b_T2r, b_T2i = b_k
        b_zc, b_x1, b_x2, b_gh, b_sgh, b_z2, b_bz, b_cv, b_mx, b_bias = [P.buf() for _ in range(10)]
        b_Cb, b_Db, b_P1, b_P2, b_Hr, b_Hi = [P.buf() for _ in range(6)]
        b_Xs = P.bufs_n(2)
        b_ps = P.bufs_n(8)
        ev = _evac_act(P)

        def t64(rows0):
            return lambda a, b: T["ucT"][rows0 + a:rows0 + b, 0:LS].rearrange("c (n1 n2) -> n1 c n2", n2=64)

        for cb in range(16):
            for (dst, bdst, r0, src_t) in ((zc, b_zc, 2048 + cb * 64, "ucT"), (x1, b_x1, cb * 64, "ucT"),
                                           (x2, b_x2, 1024 + cb * 64, "ucT"), (gh, b_gh, cb * 64, "ghT")):
                for q in range(4):
                    srcv = T[src_t][r0 + q * 16:r0 + (q + 1) * 16, 0:LS].rearrange("c (n1 n2) -> n1 c n2", n2=64)
                    P.dma("sp", dst[:, q * 16:(q + 1) * 16, :], srcv, bdst, writes=[bdst], partial=(q > 0))
            for o in range(2):
                P.dma("sp", bias[:, o, :], T["hy_bias"][o:o + 1, cb * 64:(cb + 1) * 64].partition_broadcast(64),
                      b_bias, writes=[b_bias], partial=(o > 0))
            P.op("act", lambda e: e.activation(gh[:], gh[:], AF.Silu), reads=[b_gh], writes=[b_gh])
            for o in range(2):
                zin, b_zin = (zc, b_zc) if o == 0 else (z2, b_z2)
                xo, b_xo = (x1, b_x1) if o == 0 else (x2, b_x2)
                P.dma("sp", Hr[:].rearrange("p c k -> p (c k)"), T["Hs"][o, cb, 0], b_Hr, writes=[b_Hr])
                P.dma("sp", Hi[:].rearrange("p c k -> p (c k)"), T["Hs"][o, cb, 1], b_Hi, writes=[b_Hi])
                P.op("pool", lambda e, zin=zin, o=o: e.tensor_tensor(
                    bz[:], zin[:], bcast_last(bias[:, o, :], 64), ALU.mult),
                    reads=[b_zin, b_bias], writes=[b_bz])
                _f1_stage(P, nc, T, zin, 64, b_zin, F1, b_F1, Cb, b_Cb, b_ps, ev)
                for g in range(16):
                    pj = 2 + (g % 2)
                    xs = g % 2
                    for kk in range(8):
                        k1 = g * 8 + kk
                        P.op("pe", lambda e, pj=pj, kk=kk, k1=k1: e.matmul(
                            ps[pj][0:64, kk * 64:(kk + 1) * 64], G[:, k1, :], Cb[:, k1, :], start=True, stop=True),
                            reads=[b_G, b_Cb], writes=[b_ps[pj]], partial=(kk > 0))
                    P.op("act", lambda e, pj=pj, xs=xs: e.copy(Xs[xs][:], ps[pj][0:64, :]),
                         reads=[b_ps[pj]], writes=[b_Xs[xs]])
                    xv = Xs[xs][:].rearrange("p (k c) -> p c k", k=8)
                    P.op("dve", lambda e, xv=xv, g=g: e.tensor_tensor(
                        P1[:, :, g * 8:(g + 1) * 8], xv, Hr[:, :, g * 8:(g + 1) * 8], ALU.mult),
                        reads=[b_Xs[xs], b_Hr], writes=[b_P1], partial="nowaw")
                    P.op("dve", lambda e, xv=xv, g=g: e.tensor_tensor(
                        P2[:, :, g * 8:(g + 1) * 8], xv, Hi[:, :, g * 8:(g + 1) * 8], ALU.mult),
                        reads=[b_Xs[xs], b_Hi], writes=[b_P2], partial="nowaw")
                for g in range(16):
                    pj = 4 + (g % 2)
                    for cc in range(4):
                        c = g * 4 + cc
                        P.op("pe", lambda e, pj=pj, cc=cc, c=c: e.matmul(
                            ps[pj][:, cc * 128:(cc + 1) * 128], P1[:, c, :], M1[:], start=True, stop=False),
                            reads=[b_P1, b_M1], writes=[b_ps[pj]], partial=(cc > 0))
                        P.op("pe", lambda e, pj=pj, cc=cc, c=c: e.matmul(
                            ps[pj][:, cc * 128:(cc + 1) * 128], P2[:, c, :], M1p[:], start=False, stop=True),
                            reads=[b_P2, b_M1p], writes=[b_ps[pj]], partial=True)
                    src = ps[pj][:].rearrange("p (c q) -> p q c", c=4)
                    ev(Db[:, :, g * 4:(g + 1) * 4], src, [b_ps[pj]], [b_Db], g)
                for g in range(8):
                    pj = 6 + (g % 2)
                    for nn in range(8):
                        n2 = g * 8 + nn
                        P.op("pe", lambda e, pj=pj, nn=nn, n2=n2: e.matmul(
                            ps[pj][0:64, nn * 64:(nn + 1) * 64], T2r[:, n2, :], Db[:, n2, :], start=True, stop=False),
                            reads=[b_T2r, b_Db], writes=[b_ps[pj]], partial=(nn > 0))
                        P.op("pe", lambda e, pj=pj, nn=nn, n2=n2: e.matmul(
                            ps[pj][0:64, nn * 64:(nn + 1) * 64], T2i[:, n2, :], Db[:, 64 + n2, :], start=False,
                            stop=True),
                            reads=[b_T2i, b_Db], writes=[b_ps[pj]], partial=True)
                    P.op("dve", lambda e, pj=pj, g=g: e.tensor_tensor(
                        cv[:, :, g * 8:(g + 1) * 8], ps[pj][0:64, :].rearrange("p (n c) -> p c n", n=8),
                        bz[:, :, g * 8:(g + 1) * 8], ALU.add),
                        reads=[b_ps[pj], b_bz], writes=[b_cv], partial=(g > 0))
                if o == 0:
                    P.op("dve", lambda e: e.tensor_tensor(z2[:], cv[:], x1[:], ALU.mult),
                         reads=[b_cv, b_x1], writes=[b_z2])
                else:
                    P.op("dve", lambda e: e.tensor_tensor(cv[:], cv[:], x2[:], ALU.mult),
                         reads=[b_cv, b_x2], writes=[b_cv])
                    P.op("pool", lambda e: e.tensor_tensor(z2[:], cv[:], gh[:], ALU.mult),
                         reads=[b_cv, b_gh], writes=[b_z2])
                    for q in range(4):
                        r0 = 1024 + cb * 64 + q * 16
                        dstv = T["mixT"][r0:r0 + 16, 0:LS].rearrange("c (n1 n2) -> n1 c n2", n2=64)
                        P.dma("sp", dstv, z2[:, q * 16:(q + 1) * 16, :], b_z2, reads=[b_z2])
        P.barrier()
        P.flush()


def phase_CP(P, nc, T):
    import contextlib
    ps = T["ps"]
    ident = T["ident"]
    with contextlib.ExitStack() as es:
        A_ = lambda n, shp, dt: sb(es, nc, n, shp, dt)
        FP = A_("cp_FP", [128, 4, 512], BF16)
        IP = A_("cp_IP", [128, 4, 256], BF16)
        HA = A_("cp_HA", [128, 16, 512], F32)
        HB = A_("cp_HB", [128, 16, 512], F32)
        biasT = A_("cp_biasT", [128, 16], F32)
        tp = [A_("cp_tp0", [128, 512], BF16), A_("cp_tp1", [128, 512], BF16)]
        tt = A_("cp_tt", [128, 4, 128], BF16)
        zc = [A_("cp_zc0", [128, 256], BF16), A_("cp_zc1", [128, 256], BF16)]
        x1 = [A_("cp_x10", [128, 256], BF16), A_("cp_x11", [128, 256], BF16)]
        x2 = [A_("cp_x20", [128, 256], BF16), A_("cp_x21", [128, 256], BF16)]
        gh = [A_("cp_gh0", [128, 256], BF16), A_("cp_gh1", [128, 256], BF16)]
        sgh = A_("cp_sgh", [128, 256], F32)
        z2 = A_("cp_z2", [128, 256], BF16)
        zt = A_("cp_zt", [128, 2, 128], BF16)
        Aa = A_("cp_A", [128, 512], F32)
        Bb = A_("cp_B", [128, 512], F32)
        Y = A_("cp_Y", [128, 512], BF16)
        Yt = A_("cp_Yt", [128, 4, 128], BF16)
        t1 = A_("cp_t1", [128, 256], F32)
        t2 = A_("cp_t2", [128, 256], F32)
        mx = [A_("cp_mx0", [128, 256], BF16), A_("cp_mx1", [128, 256], BF16)]
        b_FP, b_IP, b_H, b_bias = P.buf(), P.buf(), P.buf(), P.buf()
        P.dma("sp", FP[:], T["d_FP"], b_FP, writes=[b_FP])
        P.dma("sp", IP[:], T["d_IP"], b_IP, writes=[b_IP])
        P.dma("sp", biasT[:], T["hy_biasT"], b_bias, writes=[b_bias])
        b_tp, b_tt = P.bufs_n(2), P.buf()
        b_ps = P.bufs_n(8)
        it = 0
        for o in range(2):
            for t in range(8):
                s = it % 2
                it += 1
                P.dma("sp", tp[s][:], T["tapsP"][o, t * 128:(t + 1) * 128, :], b_tp[s], writes=[b_tp[s]])
                pb = ps[s][:].bitcast(BF16)
                for j in range(4):
                    P.op("pe", lambda e, pb=pb, j=j, s=s: e.transpose(
                        pb[:, j * 128:(j + 1) * 128], tp[s][:, j * 128:(j + 1) * 128], ident[:]),
                        reads=[b_tp[s]], writes=[b_ps[s]], partial=(j > 0))
                P.op("dve", lambda e, pb=pb: e.tensor_copy(tt[:].rearrange("p a b -> p (a b)"), pb[:, 0:512]),
                     reads=[b_ps[s]], writes=[b_tt])
                pj = 2 + s
                for j in range(4):
                    P.op("pe", lambda e, pj=pj, j=j: e.matmul(
                        ps[pj][:, :], tt[:, j, :], FP[:, j, :], start=(j == 0), stop=(j == 3)),
                        reads=[b_tt, b_FP], writes=[b_ps[pj]], partial=(j > 0))
                i = o * 8 + t
                for h in range(2):
                    P.op("act", lambda e, pj=pj, i=i, h=h: e.copy(HA[:, i, h * 256:(h + 1) * 256], ps[pj][:, 0:256]),
                         reads=[b_ps[pj]], writes=[b_H], partial=True)
                    P.op("act", lambda e, pj=pj, i=i, h=h: e.copy(HB[:, i, h * 256:(h + 1) * 256],
                                                                  ps[pj][:, 256:512]),
                         reads=[b_ps[pj]], writes=[b_H], partial=True)
        b_zc, b_x1, b_x2, b_gh = P.bufs_n(2), P.bufs_n(2), P.bufs_n(2), P.bufs_n(2)
        b_sgh, b_z2, b_zt, b_A, b_B, b_Y, b_Yt, b_t1, b_t2 = [P.buf() for _ in range(9)]
        b_mx = P.bufs_n(2)
        it = 0
        for sq in range(2):
            tok0 = LS + sq * LP
            for t in range(8):
                s = it % 2
                it += 1
                r = t * 128
                P.dma("sp", zc[s][:], T["ucT"][2048 + r:2048 + r + 128, tok0:tok0 + LP], b_zc[s], writes=[b_zc[s]])
                P.dma("sp", x1[s][:], T["ucT"][r:r + 128, tok0:tok0 + LP], b_x1[s], writes=[b_x1[s]])
                P.dma("sp", x2[s][:], T["ucT"][1024 + r:1024 + r + 128, tok0:tok0 + LP], b_x2[s], writes=[b_x2[s]])
                P.dma("sp", gh[s][:], T["ghT"][r:r + 128, tok0:tok0 + LP], b_gh[s], writes=[b_gh[s]])
                P.op("act", lambda e, s=s: e.activation(sgh[:], gh[s][:], AF.Silu), reads=[b_gh[s]], writes=[b_sgh])
                for o in range(2):
                    zin, b_zin = (zc[s], b_zc[s]) if o == 0 else (z2, b_z2)
                    xo, b_xo = (x1[s], b_x1[s]) if o == 0 else (x2[s], b_x2[s])
                    i = o * 8 + t
                    pb = ps[0][:].bitcast(BF16)
                    for j in range(2):
                        P.op("pe", lambda e, pb=pb, j=j, zin=zin: e.transpose(
                            pb[:, j * 128:(j + 1) * 128], zin[:, j * 128:(j + 1) * 128], ident[:]),
                            reads=[b_zin], writes=[b_ps[0]], partial=(j > 0))
                    P.op("dve", lambda e, pb=pb: e.tensor_copy(zt[:].rearrange("p a b -> p (a b)"), pb[:, 0:256]),
                         reads=[b_ps[0]], writes=[b_zt])
                    for j in range(2):
                        P.op("pe", lambda e, j=j: e.matmul(ps[1][:, :], zt[:, j, :], FP[:, j, :], start=(j == 0),
                                                           stop=(j == 1)),
                             reads=[b_zt, b_FP], writes=[b_ps[1]], partial=(j > 0))
                    P.op("dve", lambda e, i=i: e.tensor_tensor(Aa[:], ps[1][:, :], HA[:, i, :], ALU.mult),
                         reads=[b_ps[1], b_H], writes=[b_A])
                    P.op("dve", lambda e, i=i: e.tensor_tensor(Bb[:], ps[1][:, :], HB[:, i, :], ALU.mult),
                         reads=[b_ps[1], b_H], writes=[b_B])
                    P.op("pool", lambda e: e.tensor_tensor(Y[:, 0:256], Aa[:, 0:256], Bb[:, 256:512], ALU.subtract),
                         reads=[b_A, b_B], writes=[b_Y])
                    P.op("pool", lambda e: e.tensor_tensor(Y[:, 256:512], Bb[:, 0:256], Aa[:, 256:512], ALU.add),
                         reads=[b_A, b_B], writes=[b_Y], partial=True)
                    pb2 = ps[2][:].bitcast(BF16)
                    for j in range(4):
                        P.op("pe", lambda e, pb2=pb2, j=j: e.transpose(
                            pb2[:, j * 128:(j + 1) * 128], Y[:, j * 128:(j + 1) * 128], ident[:]),
                            reads=[b_Y], writes=[b_ps[2]], partial=(j > 0))
                    P.op("act", lambda e, pb2=pb2: e.copy(Yt[:].rearrange("p a b -> p (a b)"), pb2[:, 0:512]),
                         reads=[b_ps[2]], writes=[b_Yt])
                    for j in range(4):
                        P.op("pe", lambda e, j=j: e.matmul(ps[3][:, 0:256], Yt[:, j, :], IP[:, j, :], start=(j == 0),
                                                           stop=(j == 3)),
                             reads=[b_Yt, b_IP], writes=[b_ps[3]], partial=(j > 0))
                    P.op("dve", lambda e, zin=zin, i=i: e.scalar_tensor_tensor(
                        t1[:], zin[:], biasT[:, i:i + 1], ps[3][:, 0:256], ALU.mult, ALU.add),
                        reads=[b_zin, b_bias, b_ps[3]], writes=[b_t1])
                    if o == 0:
                        P.op("pool", lambda e, xo=xo: e.tensor_tensor(z2[:], t1[:], xo[:], ALU.mult),
                             reads=[b_t1, b_xo], writes=[b_z2])
                    else:
                        P.op("pool", lambda e, xo=xo: e.tensor_tensor(t2[:], t1[:], xo[:], ALU.mult),
                             reads=[b_t1, b_xo], writes=[b_t2])
                        P.op("pool", lambda e, s=s: e.tensor_tensor(mx[s][:], t2[:], sgh[:], ALU.mult),
                             reads=[b_t2, b_sgh], writes=[b_mx[s]])
                        P.dma("sp", T["mixT"][1024 + r:1024 + r + 128, tok0:tok0 + LP], mx[s][:], b_mx[s],
                              reads=[b_mx[s]])
        P.barrier()
        P.flush()


def phase_outproj(P, nc, T, l, Wdram, xsrc, mixname, dst, final):
    import contextlib
    ps = T["ps"]
    with contextlib.ExitStack() as es:
        A_ = lambda n, shp, dt: sb(es, nc, n, shp, dt)
        Wo = A_("o_W", [128, 16, 2048], BF16)
        mix = [A_("o_mix0", [128, 16, 512], BF16), A_("o_mix1", [128, 16, 512], BF16)]
        xt = [A_("o_x0", [128, 2048], F32), A_("o_x1", [128, 2048], F32)]
        xo = [A_("o_xo0", [128, 2048], F32), A_("o_xo1", [128, 2048], F32)]
        gs = A_("o_gs", [128, 2048], F32)
        gp = A_("o_gp", [128, 2048], F32)
        tmp = [A_("o_tmp0", [128, 512], F32), A_("o_tmp1", [128, 512], F32)]
        b_W, b_mix, b_xt, b_xo, b_g, b_tmp = P.buf(), P.bufs_n(2), P.bufs_n(2), P.bufs_n(2), P.buf(), P.bufs_n(2)
        b_ps = P.bufs_n(8)
        if final:
            fg = A_("o_fg", [128, 2048], F32)
            junk = A_("o_junk", [128, 2048], F32)
            st = [A_("o_st0", [128, 4], F32), A_("o_st1", [128, 4], F32)]
            b_fg, b_junk, b_st = P.buf(), P.buf(), P.bufs_n(2)
            P.dma("sp", fg[:], T["final_norm_g"][0:1, :].partition_broadcast(128), b_fg, writes=[b_fg])
        Wv = Wdram.rearrange("(k p) c -> p k c", p=128)
        for k in range(16):
            P.dma("pool", Wo[:, k, :], Wv[:, k, :], b_W, writes=[b_W], partial=(k > 0))
        P.dma("sp", gs[:], T["modv"][l][0:1, 4096:6144].partition_broadcast(128), b_g, writes=[b_g])
        P.dma("sp", gp[:], T["modv"][l][1:2, 4096:6144].partition_broadcast(128), b_g, writes=[b_g], partial=True)
        mv = T[mixname].rearrange("(k p) t -> p k t", p=128)
        ti = 0
        ei = 0
        for ch in range(NTOK // 512):
            s = ch % 2
            for q in range(4):
                P.dma("sp", mix[s][:, q * 4:(q + 1) * 4, :], mv[:, q * 4:(q + 1) * 4, ch * 512:(ch + 1) * 512],
                      b_mix[s], writes=[b_mix[s]], partial=(q > 0))
            for tt in range(4):
                tok = ch * 512 + tt * 128
                xs = ti % 2
                ti += 1
                gg = gs if tok < LS else gp
                P.dma("sp", xt[xs][:], xsrc[tok:tok + 128, :], b_xt[xs], writes=[b_xt[xs]])
                for cbk in range(4):
                    pj = ei % 4
                    tj = ei % 2
                    ei += 1
                    for k in range(16):
                        P.op("pe", lambda e, pj=pj, k=k, s=s, tt=tt, cbk=cbk: e.matmul(
                            ps[pj][:, :], mix[s][:, k, tt * 128:(tt + 1) * 128], Wo[:, k, cbk * 512:(cbk + 1) * 512],
                            start=(k == 0), stop=(k == 15)),
                            reads=[b_mix[s], b_W], writes=[b_ps[pj]], partial=(k > 0))
                    P.op("dve", lambda e, pj=pj, tj=tj, cbk=cbk, gg=gg: e.tensor_tensor(
                        tmp[tj][:], ps[pj][:, :], gg[:, cbk * 512:(cbk + 1) * 512], ALU.mult),
                        reads=[b_ps[pj], b_g], writes=[b_tmp[tj]])
                    P.op("pool", lambda e, tj=tj, xs=xs, cbk=cbk: e.tensor_tensor(
                        xo[xs][:, cbk * 512:(cbk + 1) * 512], tmp[tj][:], xt[xs][:, cbk * 512:(cbk + 1) * 512],
                        ALU.add),
                        reads=[b_tmp[tj], b_xt[xs]], writes=[b_xo[xs]], partial=(cbk > 0))
                if final:
                    P.op("act", lambda e, xs=xs: e.activation(junk[:], xo[xs][:], AF.Square,
                                                               accum_out=st[xs][:, 0:1]),
                         reads=[b_xo[xs]], writes=[b_junk, b_st[xs]])
                    P.op("act", lambda e, xs=xs: e.activation(st[xs][:, 1:2], st[xs][:, 0:1], AF.Sqrt, scale=1.0 / D,
                                                               bias=T["epsb"][:, 0:1]),
                         reads=[b_st[xs]], writes=[b_st[xs]], partial=True)
                    P.op("dve", lambda e, xs=xs: e.reciprocal(st[xs][:, 2:3], st[xs][:, 1:2]),
                         reads=[b_st[xs]], writes=[b_st[xs]], partial=True)
                    P.op("dve", lambda e, xs=xs: e.scalar_tensor_tensor(
                        xo[xs][:], xo[xs][:], st[xs][:, 2:3], fg[:], ALU.mult, ALU.mult),
                        reads=[b_xo[xs], b_st[xs], b_fg], writes=[b_xo[xs]])
                P.dma("sp", dst[tok:tok + 128, :], xo[xs][:], b_xo[xs], reads=[b_xo[xs]])
        P.barrier()
        P.flush()


def phase_E(P, nc, T):
    import contextlib
    ps = T["ps"]
    Wv = T["c_w_in"].rearrange("(k p) c -> p k c", p=128)
    for half in range(2):
        tok_tiles = [(half * 2048 + i * 128, False) for i in range(16)] + \
                    [(LS + half * 256 + i * 128, True) for i in range(2)]
        with contextlib.ExitStack() as es0:
            hT = sb(es0, nc, "e_hT", [128, 16, 2304], BF16)
            b_hT = P.buf("hT")
            with contextlib.ExitStack() as es1:
                A_ = lambda n, shp, dt: sb(es1, nc, n, shp, dt)
                xt0 = A_("e_xt0", [128, 2048], F32); xt1 = A_("e_xt1", [128, 2048], F32)
                xt2 = A_("e_xt2", [128, 2048], F32); xt3 = A_("e_xt3", [128, 2048], F32)
                junk = A_("e_junk", [128, 2048], F32); tmp = A_("e_tmp", [128, 2048], F32)
                junkb = A_("e_junkb", [128, 2048], F32); tmpb = A_("e_tmpb", [128, 2048], F32)
                st0 = A_("e_st0", [128, 4], F32); st1 = A_("e_st1", [128, 4], F32)
                hb0 = A_("e_hb0", [128, 2048], BF16); hb1 = A_("e_hb1", [128, 2048], BF16)
                As = A_("e_As", [128, 2048], F32); shs = A_("e_shs", [128, 2048], F32)
                Ap = A_("e_Ap", [128, 2048], F32); shp = A_("e_shp", [128, 2048], F32)
                bc = {"A_s": As, "sh_s": shs, "A_p": Ap, "sh_p": shp}
                b_bc = {k: P.buf(k) for k in bc}
                load_bcast_rows(P, nc, T, 1, bc, b_bc)
                work = dict(xt=[xt0, xt1, xt2, xt3], b_xt=P.bufs_n(4), junk=junk, b_junk=P.buf(), st=[st0, st1],
                            b_st=P.bufs_n(2), tmp=tmp, b_tmp=P.buf(), hb=[hb0, hb1], b_hb=P.bufs_n(2),
                            b_pst=P.bufs_n(4), junk2=[junk, junkb], b_junk2=P.bufs_n(2), tmp2=[tmp, tmpb],
                            b_tmp2=P.bufs_n(2))
                norm_transpose_half(P, nc, T, T["x1"], tok_tiles, hT, b_hT, bc, b_bc, work)
                P.barrier()
                P.flush()
            with contextlib.ExitStack() as es2:
                A_ = lambda n, shp, dt: sb(es2, nc, n, shp, dt)
                wg = [A_("e_w0", [128, 16, 512], BF16), A_("e_w1", [128, 16, 512], BF16)]
                sf = [A_("e_sf0", [128, 512], F32), A_("e_sf1", [128, 512], F32)]
                sg = [A_("e_sg0", [128, 512], BF16), A_("e_sg1", [128, 512], BF16)]
                b_hT = P.buf("hT2")
                b_wg, b_sf, b_sg = P.bufs_n(2), P.bufs_n(2), P.bufs_n(2)
                b_ps = P.bufs_n(8)
                chunks = [(c * 512, 512, half * 2048 + c * 512) for c in range(4)] + [(2048, 256, LS + half * 256)]
                psi = 0
                ei = 0
                for g in range(8):
                    s = g % 2
                    for kq in range(16):
                        P.dma("pool", wg[s][:, kq, :], Wv[:, kq, g * 512:(g + 1) * 512], b_wg[s], writes=[b_wg[s]],
                              partial=(kq > 0))
                    for (l0, n, g0) in chunks:
                        for j in range(4):
                            pj = psi % 4
                            psi += 1
                            for k in range(16):
                                P.op("pe", lambda e, pj=pj, k=k, s=s, j=j, l0=l0, n=n: e.matmul(
                                    ps[pj][:, 0:n], wg[s][:, k, j * 128:(j + 1) * 128], hT[:, k, l0:l0 + n],
                                    start=(k == 0), stop=(k == 15)),
                                    reads=[b_hT, b_wg[s]], writes=[b_ps[pj]], partial=(k > 0))
                            row = (g * 4 + j) * 128
                            si = ei % 2
                            ei += 1
                            if row < 2048:
                                P.op("act", lambda e, pj=pj, si=si, n=n: e.copy(sf[si][:, 0:n], ps[pj][:, 0:n]),
                                     reads=[b_ps[pj]], writes=[b_sf[si]])
                                P.dma("sp", T["xbT"][row:row + 128, g0:g0 + n], sf[si][:, 0:n], b_sf[si],
                                      reads=[b_sf[si]])
                            else:
                                P.op("dve", lambda e, pj=pj, si=si, n=n: e.tensor_copy(sg[si][:, 0:n], ps[pj][:, 0:n]),
                                     reads=[b_ps[pj]], writes=[b_sg[si]])
                                P.dma("sp", T["gateT"][row - 2048:row - 2048 + 128, g0:g0 + n], sg[si][:, 0:n],
                                      b_sg[si], reads=[b_sg[si]])
                P.barrier()
                P.flush()


def phase_F(P, nc, T):
    import contextlib
    ps = T["ps"]
    seqs = [(0, LS, 0), (LS, LP, 1), (LS + LP, LP, 2)]
    with contextlib.ExitStack() as es:
        A_ = lambda n, shp, dt: sb(es, nc, n, shp, dt)
        Rs = [A_("f_R0", [128, LS], F32), A_("f_R1", [128, LS], F32)]
        Is = [A_("f_I0", [128, LS], F32), A_("f_I1", [128, LS], F32)]
        Ss = [A_("f_S0", [128, LS], F32), A_("f_S1", [128, LS], F32)]
        H_ = [A_("f_H0", [128, LS + 3], F32), A_("f_H1", [128, LS + 3], F32)]
        xc = A_("f_xc", [128, 2, LS], F32)
        xp = H_[1]
        acc = H_[0]
        xcb = A_("f_xcb", [128, 2, LS], BF16)
        gate = A_("f_gate", [128, LS], BF16)
        stage = A_("f_stage", [128, LS], BF16)
        wq = A_("f_wq", [128, 2, 2, 2, 256], BF16)
        lcw = A_("f_lcw", [128, 5, 16], F32)
        lba = A_("f_lba", [128, 2, 16], F32)
        lbx = A_("f_lbx", [128, 2, 16], F32)
        llam = A_("f_llam", [128, 2, 16], F32)
        c8 = A_("f_c8", [128, 2, 16], F32)
        stT = A_("f_stT", [128, 2, 16], F32)
        nst = A_("f_nst", [128, 2, 2, 16], F32)
        b_k = P.bufs_n(6)
        P.dma("sp", lcw[:], T["lcw"], b_k[0], writes=[b_k[0]])
        P.dma("sp", lba[:], T["lba"], b_k[1], writes=[b_k[1]])
        P.dma("sp", lbx[:], T["lbx"], b_k[2], writes=[b_k[2]])
        P.dma("sp", llam[:], T["llam"], b_k[3], writes=[b_k[3]])
        P.dma("sp", stT[:], T["stT"], b_k[4], writes=[b_k[4]])
        P.op("act", lambda e: e.activation(c8[:], llam[:], AF.Exp, scale=-1.0), reads=[b_k[3]], writes=[b_k[5]])
        P.op("act", lambda e: e.activation(c8[:], c8[:], AF.Ln, bias=1.0), reads=[b_k[5]], writes=[b_k[5]])
        P.op("dve", lambda e: e.tensor_scalar(c8[:], c8[:], -8.0, None, ALU.mult), reads=[b_k[5]], writes=[b_k[5]])
        b_lcw, b_lba, b_lbx, _, b_stT, b_c8 = b_k
        b_xc, b_xcb, b_gate, b_stage, b_wq, b_nst = [P.buf() for _ in range(6)]
        b_H = P.bufs_n(2)
        b_Rs, b_Is, b_Ss = P.bufs_n(2), P.bufs_n(2), P.bufs_n(2)
        b_xp, b_acc = b_H[1], b_H[0]
        par = 0
        b_ps = P.bufs_n(8)
        psi = 0
        for h in range(8):
            for m in range(2):
                wsrc = T["c_wa"] if m == 0 else T["c_wx"]
                for d in range(2):
                    P.dma("pool", wq[:, m, d], wsrc[d, h].rearrange("(it p) j -> p it j", p=128), b_wq,
                          writes=[b_wq], partial=(m + d > 0))
            for (tok0, L, sidx) in seqs:
                nch = max(1, L // 512)
                cw = min(512, L)
                for ct in range(2):
                    cti = h * 2 + ct
                    row = cti * 128
                    P.op("pool", lambda e: e.memset(xp[:, 0:2], 0.0), writes=[b_xp])
                    P.op("pool", lambda e, L=L: e.memset(xp[:, L + 2:L + 3], 0.0), writes=[b_xp], partial=True)
                    P.dma("sp", xp[:, 2:L + 2], T["xbT"][row:row + 128, tok0:tok0 + L], b_xp, writes=[b_xp],
                          partial=True)
                    P.op("act", lambda e, L=L, cti=cti: e.activation(
                        acc[:, 0:L], xp[:, 0:L], AF.Identity, scale=lcw[:, 0, cti:cti + 1],
                        bias=lcw[:, 4, cti:cti + 1]), reads=[b_xp, b_lcw], writes=[b_acc])
                    for j in (1, 2):
                        P.op("dve", lambda e, L=L, cti=cti, j=j: e.scalar_tensor_tensor(
                            acc[:, 0:L], xp[:, j:j + L], lcw[:, j, cti:cti + 1], acc[:, 0:L], ALU.mult, ALU.add),
                            reads=[b_xp, b_acc, b_lcw], writes=[b_acc])
                    P.op("dve", lambda e, L=L, cti=cti, ct=ct: e.scalar_tensor_tensor(
                        xc[:, ct, 0:L], xp[:, 3:3 + L], lcw[:, 3, cti:cti + 1], acc[:, 0:L], ALU.mult, ALU.add),
                        reads=[b_xp, b_acc, b_lcw], writes=[b_xc], partial=(ct > 0))
                    P.op("act", lambda e, L=L, ct=ct: e.copy(xcb[:, ct, 0:L], xc[:, ct, 0:L]),
                         reads=[b_xc], writes=[b_xcb], partial=(ct > 0))
                for jt in range(2):
                    cti = h * 2 + jt
                    row = cti * 128
                    P.dma("sp", gate[:, 0:L], T["gateT"][row:row + 128, tok0:tok0 + L], b_gate, writes=[b_gate])
                    for d in range(2):
                        par ^= 1
                        R_, I_, S_ = Rs[par], Is[par], Ss[par]
                        b_R, b_I, b_S = b_Rs[par], b_Is[par], b_Ss[par]
                        for (m, dstT, b_dst, bias_t) in ((0, R_, b_R, lba), (1, I_, b_I, lbx)):
                            for c in range(nch):
                                pj = psi % 4
                                psi += 1
                                for it_ in range(2):
                                    P.op("pe", lambda e, pj=pj, m=m, d=d, it_=it_, jt=jt, c=c, cw=cw: e.matmul(
                                        ps[pj][:, 0:cw], wq[:, m, d, it_, jt * 128:(jt + 1) * 128],
                                        xcb[:, it_, c * cw:(c + 1) * cw], start=(it_ == 0), stop=(it_ == 1)),
                                        reads=[b_wq, b_xcb], writes=[b_ps[pj]], partial=(it_ > 0))
                                P.op("act", lambda e, pj=pj, dstT=dstT, c=c, bias_t=bias_t, d=d, cti=cti, cw=cw: e.activation(
                                    dstT[:, c * cw:(c + 1) * cw], ps[pj][:, 0:cw], AF.Sigmoid,
                                    bias=bias_t[:, d, cti:cti + 1]),
                                    reads=[b_ps[pj], b_lba, b_lbx], writes=[b_dst], partial=(c > 0))
                        P.op("act", lambda e, L=L, d=d, cti=cti, R_=R_: e.activation(
                            R_[:, 0:L], R_[:, 0:L], AF.Exp, scale=c8[:, d, cti:cti + 1]),
                            reads=[b_R, b_c8], writes=[b_R])
                        P.op("dve", lambda e, L=L, R_=R_, S_=S_: e.tensor_tensor(S_[:, 0:L], R_[:, 0:L], R_[:, 0:L], ALU.mult),
                             reads=[b_R], writes=[b_S])
                        P.op("act", lambda e, L=L, S_=S_: e.activation(S_[:, 0:L], S_[:, 0:L], AF.Sqrt, scale=-1.0, bias=1.0),
                             reads=[b_S], writes=[b_S])
                        P.op("dve", lambda e, L=L, I_=I_, S_=S_: e.tensor_tensor(I_[:, 0:L], I_[:, 0:L], S_[:, 0:L], ALU.mult),
                             reads=[b_I, b_S], writes=[b_I])
                        P.op("pool", lambda e, L=L, jt=jt, I_=I_: e.tensor_tensor(I_[:, 0:L], I_[:, 0:L], xc[:, jt, 0:L],
                                                                                 ALU.mult),
                             reads=[b_I, b_xc], writes=[b_I])
                        init = stT[:, d, cti:cti + 1] if sidx == 0 else 0.0
                        if d == 0:
                            P.op("dve", lambda e, L=L, init=init, R_=R_, I_=I_: e.tensor_tensor_scan(
                                H_[0][:, 0:L], R_[:, 0:L], I_[:, 0:L], init, ALU.mult, ALU.add),
                                reads=[b_R, b_I, b_stT], writes=[b_H[0]])
                        else:
                            P.op("dve", lambda e, L=L, init=init, R_=R_, I_=I_: e.tensor_tensor_scan(
                                rev_ap(H_[1][:, 0:L]), rev_ap(R_[:, 0:L]), rev_ap(I_[:, 0:L]), init, ALU.mult,
                                ALU.add),
                                reads=[b_R, b_I, b_stT], writes=[b_H[1]])
                        if sidx > 0:
                            col = L - 1 if d == 0 else 0
                            P.op("act", lambda e, d=d, col=col, sidx=sidx, cti=cti: e.copy(
                                nst[:, sidx - 1, d, cti:cti + 1], H_[d][:, col:col + 1]),
                                reads=[b_H[d]], writes=[b_nst], partial=True)
                    P.op("pool", lambda e, L=L: e.tensor_tensor(H_[0][:, 0:L], H_[0][:, 0:L], H_[1][:, 0:L], ALU.add),
                         reads=[b_H[0], b_H[1]], writes=[b_H[0]])
                    P.op("act", lambda e, L=L, S_=S_: e.activation(S_[:, 0:L], gate[:, 0:L], AF.Silu),
                         reads=[b_gate], writes=[b_S])
                    P.op("dve", lambda e, L=L, S_=S_: e.tensor_tensor(stage[:, 0:L], H_[0][:, 0:L], S_[:, 0:L], ALU.mult),
                         reads=[b_H[0], b_S], writes=[b_stage])
                    P.dma("sp", T["mix1T"][row:row + 128, tok0:tok0 + L], stage[:, 0:L], b_stage, reads=[b_stage])
        P.dma("sp", T["ns"], nst[:].rearrange("p a b c -> p (a b c)"), b_nst, reads=[b_nst])
        P.barrier()
        P.flush()


def _bf16(a):
    return np.asarray(a, dtype=np.float32).astype(ml_dtypes.bfloat16)


def _fft_consts():
    N = 2 * LS
    n1 = np.arange(128)[:, None]
    k1 = np.arange(128)[None, :]
    ang = 2 * np.pi * n1 * (k1 + 0.5) / 128
    F1 = np.concatenate([np.cos(ang), -np.sin(ang)], 1)
    n2 = np.arange(64)[:, None, None]
    k1_ = np.arange(128)[None, :, None]
    k2 = np.arange(32)[None, None, :]
    ang = 2 * np.pi * n2 * (k1_ + 128 * k2 + 0.5) / N
    gr, gi = np.cos(ang), -np.sin(ang)
    G = np.zeros((128, 128, 64))
    G[0:64, :, 0:32] = gr
    G[64:128, :, 0:32] = -gi
    G[0:64, :, 32:64] = gi
    G[64:128, :, 32:64] = gr
    Grr = np.concatenate([G[:, :, 0:32], G[:, :, 0:32]], 2)
    Gii = np.concatenate([G[:, :, 32:64], G[:, :, 32:64]], 2)
    k2 = np.arange(32)[:, None]
    n2 = np.arange(64)[None, :]
    ang = 2 * np.pi * n2 * k2 / 64
    mr, mi = np.cos(ang), np.sin(ang)
    M1 = np.zeros((64, 128))
    M1[0:32, 0:64] = mr
    M1[32:64, 0:64] = -mi
    M1[0:32, 64:128] = mi
    M1[32:64, 64:128] = mr
    M1p = np.concatenate([M1[32:64], -M1[0:32]], 0)
    k1 = np.arange(128)[:, None, None]
    n2 = np.arange(64)[None, :, None]
    n1 = np.arange(64)[None, None, :]
    ang = 2 * np.pi * (k1 + 0.5) * (n1 / 128 + n2 / N)
    T2r = (2.0 / N) * np.cos(ang)
    T2in = -(2.0 / N) * np.sin(ang)
    Np = 2 * LP
    n = np.arange(Np)[:, None]
    k = np.arange(LP)[None, :]
    ang = 2 * np.pi * n * (k + 0.5) / Np
    FPm = np.concatenate([np.cos(ang), -np.sin(ang)], 1)
    FP = FPm.reshape(4, 128, 512).transpose(1, 0, 2)
    k = np.arange(LP)[:, None]
    n = np.arange(LP)[None, :]
    ang = 2 * np.pi * n * (k + 0.5) / Np
    IPm = np.concatenate([(2.0 / Np) * np.cos(ang), -(2.0 / Np) * np.sin(ang)], 0)
    IP = IPm.reshape(4, 128, 256).transpose(1, 0, 2)
    return dict(F1=F1, G=G, Grr=Grr, Gii=Gii, M1=M1, M1p=M1p, T2r=T2r, T2in=T2in, FP=FP, IP=IP)


def make_consts():
    c = {}
    c["ident"] = _bf16(np.eye(128))
    pos = np.arange(LS)
    row = (pos // 64).astype(np.float64)
    col = (pos % 64).astype(np.float64)
    nf = 16
    inv = 10000.0 ** (-np.arange(nf, dtype=np.float64) / nf)
    cos = np.zeros((64, LS))
    sin = np.zeros((64, LS))
    for d in range(64):
        halfi = d // 32
        w = d % 32
        f = w % 16
        p = row if halfi == 0 else col
        ang = p * inv[f]
        cos[d] = np.cos(ang)
        sin[d] = -np.sin(ang) if w < 16 else np.sin(ang)
    c["rope_cos"] = np.tile(cos, (2, 1)).astype(np.float32)
    c["rope_sin"] = np.tile(sin, (2, 1)).astype(np.float32)
    Pm = np.zeros((128, 128))
    for m in range(128):
        hh = m // 64
        d = m % 64
        w = d % 32
        partner = d + 16 if w < 16 else d - 16
        Pm[hh * 64 + partner, m] = 1.0
    c["ropeP"] = _bf16(Pm)
    c["epsb"] = np.full((128, 1), EPS, np.float32)
    for nm, Lf in (("S", LS), ("P", LP)):
        t = (np.arange(Lf, dtype=np.float32) / np.float32(Lf)).astype(np.float64)
        freqs = np.linspace(1e-4, 15.0, 16).astype(np.float32).astype(np.float64)
        ang = 2.0 * np.pi * t[:, None] * freqs[None, :]
        z = np.concatenate([t[:, None], np.cos(ang), -np.sin(ang)], 1)
        c["zemb" + nm] = np.ascontiguousarray(z.T).astype(np.float32)
        c["tpos" + nm] = np.ascontiguousarray(np.tile(t[None, :], (128, 1))).astype(np.float32)
    c.update({k: _bf16(v) for k, v in _fft_consts().items()})
    si = np.arange(128)[:, None]
    qi = np.arange(128)[None, :]
    c["amask"] = _bf16(np.concatenate([(si >= qi), np.ones((128, 128)), (si <= qi)], 1).astype(np.float32))
    return c


CONST_SPECS = {
    "ident": ([128, 128], BF16), "rope_cos": ([128, LS], F32), "rope_sin": ([128, LS], F32),
    "ropeP": ([128, 128], BF16), "epsb": ([128, 1], F32), "amask": ([128, 384], BF16),
    "zembS": ([33, LS], F32), "zembP": ([33, LP], F32), "tposS": ([128, LS], F32), "tposP": ([128, LP], F32),
    "F1": ([128, 256], BF16), "G": ([128, 128, 64], BF16), "Grr": ([128, 128, 64], BF16),
    "Gii": ([128, 128, 64], BF16), "M1": ([64, 128], BF16), "M1p": ([64, 128], BF16),
    "T2r": ([128, 64, 64], BF16), "T2in": ([128, 64, 64], BF16),
    "FP": ([128, 4, 512], BF16), "IP": ([128, 4, 256], BF16),
}

IN_SPECS = {
    "x": [NTOK, D], "ck": [512, 128], "cv": [512, 128], "st": [2, D], "cvecT": [128, 32],
    "mod_w": [2, D, 3 * D], "mod_b": [2, 3 * D], "norm_g": [2, D], "final_norm_g": [1, D],
    "a_w_in": [D, 6400], "a_w_out": [D, D], "a_sink": [1, 16],
    "hsw": [128, 4, 24], "hy_w1": [33, 64], "hy_w2": [64, 64], "hy_bb": [64, 2], "hy_w3": [64, 4096],
    "hy_decT": [128, 32], "hy_biasT": [128, 16], "hy_bias": [2, 1024],
    "c_w_in": [D, 2 * D], "c_w_out": [D, D], "c_wa": [2, 8, 256, 256], "c_wx": [2, 8, 256, 256],
    "lcw": [128, 5, 16], "lba": [128, 2, 16], "lbx": [128, 2, 16], "llam": [128, 2, 16], "stT": [128, 2, 16],
}

SCRATCH = {
    "modv": ([2, 2, 3 * D], F32),
    "qT": ([1024, NTOK], BF16), "kT": ([2, 128, NTOK], BF16), "gaT": ([1024, NTOK], BF16),
    "hyT": ([3072, NTOK], BF16), "ghT": ([1024, NTOK], BF16), "vtok": ([NTOK, 128], BF16),
    "mixT": ([2048, NTOK], BF16),
    "ucT": ([3072, NTOK], BF16), "tapsS": ([2, 1024, 2 * LS], BF16), "tapsP": ([2, 1024, 2 * LP], BF16),
    "Hs": ([2, 16, 2, 64, 64 * 128], BF16),
    "x1": ([NTOK, D], F32), "xbT": ([D, NTOK], F32), "gateT": ([D, NTOK], BF16), "mix1T": ([D, NTOK], BF16),
}

OUT_SPECS = {"y": [NTOK, D], "nk": [512, 128], "nv": [512, 128], "ns": [128, 64]}


def build_program(debug_scratch=(), stop_after=None, skip=(), ext_in=()):
    nc = bass.Bass("TRN2", target_bir_lowering=False)
    T = {}
    for name, shp in IN_SPECS.items():
        T[name] = nc.dram_tensor(name, shp, F32, kind="ExternalInput").ap()
    for name, (shp, dt) in CONST_SPECS.items():
        T["d_" + name] = nc.dram_tensor("c_" + name, shp, dt, kind="ExternalInput").ap()
    for name, shp in OUT_SPECS.items():
        T[name] = nc.dram_tensor(name, shp, F32, kind="ExternalOutput").ap()
    for name, (shp, dt) in SCRATCH.items():
        kind = "ExternalOutput" if name in debug_scratch else ("ExternalInput" if name in ext_in else "Internal")
        T[name] = nc.dram_tensor("s_" + name, shp, dt, kind=kind).ap()
    T["rope_cos"] = T["d_rope_cos"]
    T["rope_sin"] = T["d_rope_sin"]
    import contextlib
    with contextlib.ExitStack() as es:
        sems = [es.enter_context(nc.semaphore("sem%d" % i)) for i in range(60)]
        T["ps"] = [es.enter_context(nc.psum_tensor("ps%d" % i, [128, 512], F32)) for i in range(8)]
        ident = es.enter_context(nc.sbuf_tensor("ident", [128, 128], BF16))
        ropeP = es.enter_context(nc.sbuf_tensor("ropeP", [128, 128], BF16))
        epsb = es.enter_context(nc.sbuf_tensor("epsb", [128, 1], F32))
        T["ident"], T["ropeP"], T["epsb"] = ident, ropeP, epsb
        P = Prog(nc, sems)
        b_c = P.bufs_n(3)
        P.dma("sp", ident[:], T["d_ident"], b_c[0], writes=[b_c[0]])
        P.dma("sp", ropeP[:], T["d_ropeP"], b_c[1], writes=[b_c[1]])
        P.dma("sp", epsb[:], T["d_epsb"], b_c[2], writes=[b_c[2]])
        P.barrier()
        if "M" not in skip:
            phase_M(P, nc, T)
        if stop_after != "M":
            if "A" not in skip:
                phase_A(P, nc, T)
        if stop_after not in ("M", "A") and "B" not in skip:
            phase_B(P, nc, T)
        if stop_after not in ("M", "A", "B"):
            if "C0" not in skip:
                phase_C0(P, nc, T)
            if "CF" not in skip:
                phase_CF(P, nc, T, LP, T["tapsP"], T["d_zembP"], T["d_tposP"])
                phase_CF(P, nc, T, LS, T["tapsS"], T["d_zembS"], T["d_tposS"])
        if stop_after not in ("M", "A", "B", "CF"):
            if "CT" not in skip:
                phase_CT(P, nc, T)
            if "CS" not in skip:
                phase_CS(P, nc, T)
            if "CP" not in skip:
                phase_CP(P, nc, T)
        if stop_after not in ("M", "A", "B", "CF", "C"):
            if "D" not in skip:
                phase_outproj(P, nc, T, 0, T["a_w_out"], T["x"], "mixT", T["x1"], False)
        if stop_after not in ("M", "A", "B", "CF", "C", "D"):
            if "E" not in skip:
                phase_E(P, nc, T)
        if stop_after not in ("M", "A", "B", "CF", "C", "D", "E"):
            if "F" not in skip:
                phase_F(P, nc, T)
        if stop_after not in ("M", "A", "B", "CF", "C", "D", "E", "F"):
            if "G" not in skip:
                phase_outproj(P, nc, T, 1, T["c_w_out"], T["x1"], "mix1T", T["y"], True)
        P.barrier()
        P.flush()
    return nc


def make_in_maps(inputs):
    consts = make_consts()
    f = lambda a: np.ascontiguousarray(np.asarray(a, dtype=np.float32))
    x_prompt, x_sample = f(inputs["x_prompt"]), f(inputs["x_sample"])
    ck, cv = f(inputs["cache_k"]), f(inputs["cache_v"])
    st = f(inputs["state_lru"])
    c, c_ctx = f(inputs["c"]), f(inputs["c_ctx"])
    shared = {
        "mod_w": f(inputs["mod_w"]), "mod_b": f(inputs["mod_b"]), "norm_g": f(inputs["norm_g"]),
        "final_norm_g": f(inputs["final_norm_g"]).reshape(1, D),
        "a_w_in": f(inputs["a_w_in"])[0], "a_w_out": f(inputs["a_w_out"])[0], "a_sink": f(inputs["a_sink"]),
        "hsw": np.ascontiguousarray(np.concatenate([f(inputs["hy_short_w"])[0], f(inputs["hy_short_b"])[0][None]], 0)
                                    .reshape(4, 24, 128).transpose(2, 0, 1)),
        "hy_w1": f(inputs["hy_w1"])[0], "hy_w2": f(inputs["hy_w2"])[0],
        "hy_bb": np.ascontiguousarray(np.stack([f(inputs["hy_b1"])[0], f(inputs["hy_b2"])[0]], 1)),
        "hy_w3": f(inputs["hy_w3"])[0],
        "hy_decT": np.ascontiguousarray(f(inputs["hy_decay"])[0].reshape(32, 128).T),
        "hy_biasT": np.ascontiguousarray(f(inputs["hy_bias"])[0].reshape(16, 128).T),
        "hy_bias": f(inputs["hy_bias"])[0],
        "c_w_in": f(inputs["c_w_in"])[0], "c_w_out": f(inputs["c_w_out"])[0],
        "c_wa": f(inputs["c_wa"])[0], "c_wx": f(inputs["c_wx"])[0],
        "lcw": np.ascontiguousarray(np.concatenate([f(inputs["c_conv_w"])[0], f(inputs["c_conv_b"])[0][None]], 0)
                                    .reshape(5, 16, 128).transpose(2, 0, 1)),
        "lba": np.ascontiguousarray(f(inputs["c_ba"])[0].reshape(2, 16, 128).transpose(2, 0, 1)),
        "lbx": np.ascontiguousarray(f(inputs["c_bx"])[0].reshape(2, 16, 128).transpose(2, 0, 1)),
        "llam": np.ascontiguousarray(f(inputs["c_lambda"])[0].reshape(2, 16, 128).transpose(2, 0, 1)),
    }
    for k, v in consts.items():
        shared["c_" + k] = v
    maps = []
    for i in range(NCORES):
        m = dict(shared)
        m["x"] = np.ascontiguousarray(np.concatenate([x_sample[i], x_prompt[2 * i], x_prompt[2 * i + 1]], 0))
        m["ck"] = np.ascontiguousarray(ck[i, 0].reshape(512, 128))
        m["cv"] = np.ascontiguousarray(cv[i, 0].reshape(512, 128))
        m["st"] = np.ascontiguousarray(st[i, 0])
        m["stT"] = np.ascontiguousarray(st[i, 0].reshape(2, 16, 128).transpose(2, 0, 1))
        cvec = np.stack([c[i], c_ctx], 0)
        m["cvecT"] = np.ascontiguousarray(cvec.reshape(2, 16, 128).transpose(2, 1, 0).reshape(128, 32))
        maps.append(m)
    return maps


def kernel(**inputs):
    nc = build_program()
    maps = make_in_maps(inputs)
    res = run_bass_kernel_spmd(nc, maps, core_ids=list(range(NCORES)))
    R = res.results
    y_s = np.stack([R[i]["y"][:LS] for i in range(NCORES)], 0)
    y_p = np.concatenate([R[i]["y"][LS:].reshape(2, LP, D) for i in range(NCORES)], 0)
    nk = np.concatenate([R[i]["nk"].reshape(2, 1, LP, 2, 64) for i in range(NCORES)], 0)
    nv = np.concatenate([R[i]["nv"].reshape(2, 1, LP, 2, 64) for i in range(NCORES)], 0)
    ns = np.concatenate([R[i]["ns"].reshape(128, 2, 2, 16).transpose(1, 2, 3, 0).reshape(2, 1, 2, D)
                         for i in range(NCORES)], 0)
    return (y_p.astype(np.float32), y_s.astype(np.float32), nk.astype(np.float32), nv.astype(np.float32),
            ns.astype(np.float32))
```

```python
import numpy as np
import ml_dtypes
import concourse.bass as bass
import concourse.mybir as mybir
from concourse.bass_utils import run_bass_kernel_spmd

F32, BF16 = mybir.dt.float32, mybir.dt.bfloat16
AF = mybir.ActivationFunctionType
ALU = mybir.AluOpType
AX = mybir.AxisListType

D = 2048
LS = 4096
LP = 256
NPS = 2
NTOK = LS + NPS * LP
EPS = 1e-6
NCORES = 8
DBG = {}


class Sem:
    def __init__(self, h):
        self.h = h
        self.n = 0


class Buf:
    def __init__(self, name=""):
        self.w = []
        self.r = []
        self.gen_r = []
        self.name = name
        self.sem = None


class Prog:
    ENG = ("pe", "act", "dve", "pool", "sp")
    COMPUTE = ("pe", "act", "dve", "pool")

    def __init__(self, nc, handles):
        self.nc = nc
        self.q = {e: [] for e in self.ENG}
        hs = list(handles)
        self.csem = {e: Sem(hs.pop()) for e in self.COMPUTE}
        self.bar = Sem(hs.pop())
        self.dpool = [Sem(h) for h in hs]
        nsw = len(self.dpool) // 3
        self.dfree = {"pool": self.dpool[:nsw], "sp": self.dpool[nsw:]}
        self.waited = {e: {} for e in self.ENG}
        self.pending = []
        self.bufs = []
        self.nops = {e: 0 for e in self.COMPUTE}
        self.entries = {e: {} for e in self.COMPUTE}
        self.sig_idx = {e: [] for e in self.COMPUTE}
        self.sig_cnt = {e: [] for e in self.COMPUTE}

    def buf(self, name=""):
        b = Buf(name)
        self.bufs.append(b)
        return b

    def bufs_n(self, n, name=""):
        return [self.buf(name + str(i)) for i in range(n)]

    def _resolve(self, tok):
        import bisect
        _, eng, idx = tok
        si = self.sig_idx[eng]
        p = bisect.bisect_left(si, idx)
        if p < len(si):
            return self.csem[eng], self.sig_cnt[eng][p]
        ent = self.entries[eng][idx]
        sem = self.csem[eng]
        sem.n += 1
        ent[1] = sem.h
        si.append(idx)
        self.sig_cnt[eng].append(sem.n)
        return sem, sem.n

    def _wait(self, eng, tok):
        if tok[0] == "op":
            if tok[1] == eng and eng == "pe":
                return
            sem, tgt = self._resolve(tok)
        else:
            sem, tgt, _ = tok
        if self.waited[eng].get(id(sem), 0) >= tgt:
            return
        self.waited[eng][id(sem)] = tgt
        self.q[eng].append([lambda e, h=sem.h, t=tgt: e.wait_ge(h, t), None])

    def _wait_many(self, eng, toks):
        best = {}
        for t in toks:
            if t[0] == "op":
                k = ("op", t[1])
                if k not in best or best[k][2] < t[2]:
                    best[k] = t
            else:
                k = id(t[0])
                if k not in best or best[k][1] < t[1]:
                    best[k] = t
        for t in best.values():
            self._wait(eng, t)

    def _hazards(self, eng, reads, writes, partial):
        toks = []
        for b in reads:
            toks += b.w
        for b in writes:
            if partial == "nowaw":
                if b.r:
                    b.gen_r = b.r
                    b.r = []
                    b.w = []
                toks += b.gen_r
            else:
                toks += b.w
                toks += b.r
                toks += b.gen_r
        self._wait_many(eng, toks)

    def _commit(self, tok, reads, writes, partial):
        for b in reads:
            b.r.append(tok)
        for b in writes:
            if partial:
                b.w.append(tok)
                if partial != "nowaw":
                    b.r = []
            else:
                b.w = [tok]
                b.r = []
                b.gen_r = []

    def op(self, eng, fn, reads=(), writes=(), partial=False):
        self._hazards(eng, reads, writes, partial)
        idx = self.nops[eng]
        self.nops[eng] += 1
        ent = [fn, None]
        self.entries[eng][idx] = ent
        self.q[eng].append(ent)
        tok = ("op", eng, idx)
        self._commit(tok, reads, writes, partial)
        return tok

    def dma(self, eng, out, in_, sbuf_buf, reads=(), writes=(), partial=False):
        self._hazards(eng, reads, writes, partial)
        if sbuf_buf.sem is None:
            sbuf_buf.sem = {}
        if eng not in sbuf_buf.sem:
            sbuf_buf.sem[eng] = self.dfree[eng].pop()
        sem = sbuf_buf.sem[eng]
        sem.n += 16
        tok = (sem, sem.n, "dma")
        self.q[eng].append([lambda e, o=out, i=in_: e.dma_start(out=o, in_=i), sem.h, 16])
        self._commit(tok, reads, writes, partial)
        self.pending.append(tok)
        return tok

    def barrier(self):
        for e in self.COMPUTE:
            if self.nops[e] > 0:
                self._wait("sp", ("op", e, self.nops[e] - 1))
        self._wait_many("sp", self.pending)
        self.pending = []
        self.bar.n += 1
        k = self.bar.n
        self.q["sp"].append([lambda e, h=self.bar.h: e.sem_inc(h, 1), None])
        for e in self.COMPUTE:
            self._wait(e, (self.bar, k, "bar"))
        for b in self.bufs:
            if b.sem is not None:
                for en, sm in b.sem.items():
                    self.dfree[en].append(sm)
                b.sem = None
            b.w = []
            b.r = []
            b.gen_r = []
        self.bufs = []

    def flush(self):
        nc = self.nc
        q = self.q

        def run(e, lst):
            for ent in lst:
                ins = ent[0](e)
                if ent[1] is not None:
                    ins.then_inc(ent[1], ent[2] if len(ent) > 2 else 1)

        with nc.Block() as blk:
            @blk.tensor
            def _(e):
                run(e, q["pe"])

            @blk.scalar
            def _(e):
                run(e, q["act"])

            @blk.vector
            def _(e):
                run(e, q["dve"])

            @blk.gpsimd
            def _(e):
                run(e, q["pool"])

            @blk.sync
            def _(e):
                run(e, q["sp"])
        self.q = {e: [] for e in self.ENG}
        for e in self.COMPUTE:
            self.entries[e] = {}


_UID = [0]


def sb(es, nc, name, shape, dt):
    _UID[0] += 1
    return es.enter_context(nc.sbuf_tensor("%s_%d" % (name, _UID[0]), shape, dt))


def rev_ap(ap):
    a = [list(p) for p in ap.ap]
    step, cnt = a[-1]
    off = ap.offset + step * (cnt - 1)
    a[-1] = [-step, cnt]
    return bass.AP(ap.tensor, off, a)


def phase_M(P, nc, T):
    with (
        nc.sbuf_tensor("m_cT", [128, 32], F32) as cT,
        nc.sbuf_tensor("m_sg", [128, 32], F32) as sg,
        nc.sbuf_tensor("m_sT", [128, 32], BF16) as sT,
        nc.sbuf_tensor("m_w0", [128, 3072], BF16) as w0,
        nc.sbuf_tensor("m_w1", [128, 3072], BF16) as w1,
        nc.sbuf_tensor("m_mrow", [2, 6144], F32) as mrow,
        nc.sbuf_tensor("m_brow", [2, 6144], F32) as brow,
        nc.sbuf_tensor("m_ng", [2, 2048], F32) as ng,
        nc.sbuf_tensor("m_orow", [2, 6144], F32) as orow,
    ):
        ps = T["ps"]
        b_cT, b_sT = P.buf(), P.buf()
        b_w = P.bufs_n(2)
        wt = [w0, w1]
        b_ps = P.bufs_n(6)
        b_mrow, b_brow, b_ng, b_orow = P.buf(), P.buf(), P.buf(), P.buf()
        P.dma("sp", cT[:], T["cvecT"], b_cT, writes=[b_cT])
        P.op("act", lambda e: e.activation(sg[:], cT[:], AF.Sigmoid), reads=[b_cT], writes=[b_sT])
        P.op("dve", lambda e: e.tensor_tensor(sT[:], sg[:], cT[:], ALU.mult), reads=[b_cT, b_sT], writes=[b_sT])
        cnt = 0
        for l in range(2):
            P.dma("sp", brow[:], T["mod_b"][l:l + 1, :].partition_broadcast(2), b_brow, writes=[b_brow])
            P.dma("sp", ng[:], T["norm_g"][l:l + 1, :].partition_broadcast(2), b_ng, writes=[b_ng])
            for hf in range(2):
                for k in range(16):
                    s = cnt % 2
                    cnt += 1
                    P.dma("pool", wt[s][:], T["mod_w"][l, k * 128:(k + 1) * 128, hf * 3072:(hf + 1) * 3072],
                          b_w[s], writes=[b_w[s]])
                    for j in range(6):
                        P.op("pe", lambda e, s=s, j=j, k=k: e.matmul(
                            ps[j][0:2, :], sT[:, 2 * k:2 * k + 2], wt[s][:, j * 512:(j + 1) * 512],
                            start=(k == 0), stop=(k == 15)),
                            reads=[b_w[s], b_sT], writes=[b_ps[j]], partial=(k > 0))
                for j in range(6):
                    c0 = hf * 3072 + j * 512
                    P.op("dve", lambda e, j=j, c0=c0: e.tensor_tensor(
                        mrow[:, c0:c0 + 512], ps[j][0:2, :], brow[:, c0:c0 + 512], ALU.add),
                        reads=[b_ps[j], b_brow], writes=[b_mrow], partial=True)
            P.op("dve", lambda e: e.scalar_tensor_tensor(
                orow[:, 0:2048], mrow[:, 2048:4096], 1.0, ng[:], ALU.add, ALU.mult),
                reads=[b_mrow, b_ng], writes=[b_orow])
            P.op("dve", lambda e: e.tensor_copy(orow[:, 2048:4096], mrow[:, 0:2048]),
                 reads=[b_mrow], writes=[b_orow], partial=True)
            P.op("dve", lambda e: e.tensor_copy(orow[:, 4096:6144], mrow[:, 4096:6144]),
                 reads=[b_mrow], writes=[b_orow], partial=True)
            P.dma("sp", T["modv"][l], orow[:], b_orow, reads=[b_orow])
        P.barrier()
        P.flush()


def load_bcast_rows(P, nc, T, l, tiles, bufs):
    modv = T["modv"]
    for key, (j, r) in (("A_s", (0, 0)), ("sh_s", (1, 0)), ("A_p", (0, 1)), ("sh_p", (1, 1))):
        P.dma("sp", tiles[key][:], modv[l][r:r + 1, j * 2048:(j + 1) * 2048].partition_broadcast(128),
              bufs[key], writes=[bufs[key]])


def norm_transpose_half(P, nc, T, xsrc, tok_tiles, hT, b_hT, bc, b_bc, work):
    ps = T["ps"]
    ident = T["ident"]
    xt, b_xt = work["xt"], work["b_xt"]
    st, b_st = work["st"], work["b_st"]
    hb, b_hb = work["hb"], work["b_hb"]
    b_pst = work["b_pst"]

    def front(i):
        toff, isp = tok_tiles[i]
        s = i % 2
        x4 = i % len(xt)
        P.dma("sp", xt[x4][:], xsrc[toff:toff + 128, :], b_xt[x4], writes=[b_xt[x4]])
        jk, bjk = work["junk2"][s], work["b_junk2"][s]
        P.op("act", lambda e, s=s, jk=jk, x4=x4: e.activation(jk[:], xt[x4][:], AF.Square, accum_out=st[s][:, 0:1]),
             reads=[b_xt[x4]], writes=[bjk, b_st[s]])
        P.op("act", lambda e, s=s: e.activation(st[s][:, 1:2], st[s][:, 0:1], AF.Sqrt, scale=1.0 / D,
                                                 bias=T["epsb"][:, 0:1]),
             reads=[b_st[s]], writes=[b_st[s]], partial=True)
        P.op("dve", lambda e, s=s: e.reciprocal(st[s][:, 2:3], st[s][:, 1:2]), reads=[b_st[s]], writes=[b_st[s]],
             partial=True)
        A = bc["A_p" if isp else "A_s"]
        SH = bc["sh_p" if isp else "sh_s"]
        bA = b_bc["A_p" if isp else "A_s"]
        bS = b_bc["sh_p" if isp else "sh_s"]
        tm, btm = work["tmp2"][s], work["b_tmp2"][s]
        P.op("dve", lambda e, s=s, A=A, tm=tm, x4=x4: e.scalar_tensor_tensor(tm[:], xt[x4][:], st[s][:, 2:3], A[:],
                                                                             ALU.mult, ALU.mult),
             reads=[b_xt[x4], b_st[s], bA], writes=[btm])
        P.op("pool", lambda e, s=s, SH=SH, tm=tm: e.tensor_tensor(hb[s][:], tm[:], SH[:], ALU.add),
             reads=[btm, bS], writes=[b_hb[s]])

    def back(i):
        s = i % 2
        for half in range(2):
            pb = ps[2 * s + half]
            bp = b_pst[2 * s + half]
            pbv = pb[:].bitcast(BF16)
            for kk in range(8):
                k = half * 8 + kk
                P.op("pe", lambda e, pbv=pbv, kk=kk, k=k, s=s: e.transpose(
                    pbv[:, kk * 128:(kk + 1) * 128], hb[s][:, k * 128:(k + 1) * 128], ident[:]),
                    reads=[b_hb[s]], writes=[bp], partial=(kk > 0))
            dst = hT[:, half * 8:(half + 1) * 8, i * 128:(i + 1) * 128]
            src = pbv.rearrange("p (k t) -> p k t", k=8)
            if half == 0:
                P.op("act", lambda e, dst=dst, src=src: e.copy(dst, src), reads=[bp], writes=[b_hT], partial="nowaw")
            else:
                P.op("dve", lambda e, dst=dst, src=src: e.tensor_copy(dst, src), reads=[bp], writes=[b_hT],
                     partial="nowaw")

    n = len(tok_tiles)
    front(0)
    for i in range(n):
        if i + 1 < n:
            front(i + 1)
        back(i)


def phase_A(P, nc, T, debug=False):
    ps = T["ps"]
    W = T["a_w_in"]
    groups = []
    for g in range(2):
        groups.append(("q", [("qT", (g * 4 + j) * 128, (g * 4 + j) * 128) for j in range(4)]))
    groups.append(("k", [("kT0", 0, 1024), ("kT1", 0, 1088)]))
    for g in range(2):
        groups.append(("ga", [("gaT", (g * 4 + j) * 128, 1280 + (g * 4 + j) * 128) for j in range(4)]))
    for g in range(6):
        groups.append(("hy", [("hyT", (g * 4 + j) * 128, 2304 + (g * 4 + j) * 128) for j in range(4)]))
    for g in range(2):
        groups.append(("gh", [("ghT", (g * 4 + j) * 128, 5376 + (g * 4 + j) * 128) for j in range(4)]))

    Wv = W.rearrange("(k p) c -> p k c", p=128)
    for half in range(DBG.get('halves', 2)):
        tok_tiles = [(half * 2048 + i * 128, False) for i in range(16)] + \
                    [(LS + half * 256 + i * 128, True) for i in range(2)]
        import contextlib
        with contextlib.ExitStack() as es0:
            hT = sb(es0, nc, "a_hT", [128, 16, 2304], BF16)
            b_hT = P.buf("hT")
            with contextlib.ExitStack() as es1:
                A_ = lambda n, shp, dt: sb(es1, nc, n, shp, dt)
                xt0 = A_("a_xt0", [128, 2048], F32); xt1 = A_("a_xt1", [128, 2048], F32)
                xt2 = A_("a_xt2", [128, 2048], F32); xt3 = A_("a_xt3", [128, 2048], F32)
                junk = A_("a_junk", [128, 2048], F32); tmp = A_("a_tmp", [128, 2048], F32)
                junkb = A_("a_junkb", [128, 2048], F32); tmpb = A_("a_tmpb", [128, 2048], F32)
                st0 = A_("a_st0", [128, 4], F32); st1 = A_("a_st1", [128, 4], F32)
                hb0 = A_("a_hb0", [128, 2048], BF16); hb1 = A_("a_hb1", [128, 2048], BF16)
                As = A_("a_As", [128, 2048], F32); shs = A_("a_shs", [128, 2048], F32)
                Ap = A_("a_Ap", [128, 2048], F32); shp = A_("a_shp", [128, 2048], F32)
                bc = {"A_s": As, "sh_s": shs, "A_p": Ap, "sh_p": shp}
                b_bc = {k: P.buf(k) for k in bc}
                load_bcast_rows(P, nc, T, 0, bc, b_bc)
                work = dict(xt=[xt0, xt1, xt2, xt3], b_xt=P.bufs_n(4), junk=junk, b_junk=P.buf(), st=[st0, st1],
                            b_st=P.bufs_n(2), tmp=tmp, b_tmp=P.buf(), hb=[hb0, hb1], b_hb=P.bufs_n(2),
                            b_pst=P.bufs_n(4), junk2=[junk, junkb], b_junk2=P.bufs_n(2), tmp2=[tmp, tmpb],
                            b_tmp2=P.bufs_n(2))
                norm_transpose_half(P, nc, T, T["x"], tok_tiles, hT, b_hT, bc, b_bc, work)
                P.barrier()
                P.flush()
            if DBG.get('norm_only'):
                continue
            with contextlib.ExitStack() as es2:
                A_ = lambda n, shp, dt: sb(es2, nc, n, shp, dt)
                wg0 = A_("a_w0", [128, 16, 512], BF16); wg1 = A_("a_w1", [128, 16, 512], BF16)
                wkv = A_("a_wkv", [128, 16, 256], BF16)
                cos_t = A_("a_cos", [128, 2048], F32); sin_t = A_("a_sin", [128, 2048], F32)
                stg0 = A_("a_stg0", [128, 512], BF16); stg1 = A_("a_stg1", [128, 512], BF16)
                stg2 = A_("a_stg2", [128, 512], BF16); stg3 = A_("a_stg3", [128, 512], BF16)
                qs0 = A_("a_qs0", [128, 512], BF16); qs1 = A_("a_qs1", [128, 512], BF16)
                t1 = A_("a_t1", [128, 512], F32); t2 = A_("a_t2", [128, 512], F32)
                kvf0 = A_("a_kvf0", [128, 256], F32); kvf1 = A_("a_kvf1", [128, 256], F32)
                vb0 = A_("a_vb0", [128, 128], BF16); vb1 = A_("a_vb1", [128, 128], BF16)
                b_hT = P.buf("hT2")
                wg = [wg0, wg1]
                b_wg = P.bufs_n(2)
                b_wkv = P.buf()
                b_cos, b_sin = P.buf(), P.buf()
                stg = [stg0, stg1, stg2, stg3]
                b_stg = P.bufs_n(4)
                qs = [qs0, qs1]
                b_qs = P.bufs_n(2)
                b_t1, b_t2 = P.buf(), P.buf()
                kvf = [kvf0, kvf1]
                b_kvf = P.bufs_n(2)
                vb = [vb0, vb1]
                b_vb = P.bufs_n(2)
                b_ps = P.bufs_n(8)
                P.dma("sp", cos_t[:], T["rope_cos"][:, half * 2048:(half + 1) * 2048], b_cos, writes=[b_cos])
                P.dma("sp", sin_t[:], T["rope_sin"][:, half * 2048:(half + 1) * 2048], b_sin, writes=[b_sin])
                for kq in range(16):
                    P.dma("pool", wkv[:, kq, :], Wv[:, kq, 1024:1280], b_wkv, writes=[b_wkv], partial=(kq > 0))
                for i, (toff, isp) in enumerate(tok_tiles[:DBG.get('nkv', 99)]):
                    pi = i % 2
                    pst = ps[6 + pi]
                    for k in range(16):
                        P.op("pe", lambda e, pst=pst, k=k, i=i: e.matmul(
                            pst[:, 0:256], hT[:, k, i * 128:(i + 1) * 128], wkv[:, k, :],
                            start=(k == 0), stop=(k == 15)),
                            reads=[b_hT, b_wkv], writes=[b_ps[6 + pi]], partial=(k > 0))
                    if isp and not DBG.get('no_isp'):
                        P.op("act", lambda e, pst=pst, pi=pi: e.copy(kvf[pi][:], pst[:, 0:256]),
                             reads=[b_ps[6 + pi]], writes=[b_kvf[pi]])
                        r0 = toff - LS
                        if not DBG.get('no_nk'):
                            P.dma("sp", T["nk"][r0:r0 + 128, :], kvf[pi][:, 0:128], b_kvf[pi], reads=[b_kvf[pi]])
                        if not DBG.get('no_nv'):
                            P.dma("sp", T["nv"][r0:r0 + 128, :], kvf[pi][:, 128:256], b_kvf[pi], reads=[b_kvf[pi]])
                    P.op("act", lambda e, pst=pst, pi=pi: e.copy(vb[pi][:], pst[:, 128:256]),
                         reads=[b_ps[6 + pi]], writes=[b_vb[pi]])
                    P.dma("sp", T["vtok"][toff:toff + 128, :], vb[pi][:], b_vb[pi], reads=[b_vb[pi]])
                chunks = [(c * 512, 512, half * 2048 + c * 512, False) for c in range(4)] + \
                         [(2048, 256, LS + half * 256, True)]
                evac_i = 0
                psi = 0
                for gi, (kind, blocks) in enumerate(groups[:DBG.get('ngroups', 99)] if not DBG.get('gsel') else [groups[i] for i in DBG['gsel']]):
                    s = gi % 2
                    if kind == "k":
                        for j, (name, r0, c0) in enumerate(blocks):
                            for dup in range(2):
                                for kq in range(16):
                                    P.dma("pool", wg[s][:, kq, j * 128 + dup * 64:j * 128 + dup * 64 + 64],
                                          Wv[:, kq, c0:c0 + 64], b_wg[s], writes=[b_wg[s]],
                                          partial=(j + dup + kq > 0))
                    else:
                        c0 = blocks[0][2]
                        for kq in range(16):
                            P.dma("pool", wg[s][:, kq, 0:512], Wv[:, kq, c0:c0 + 512],
                                  b_wg[s], writes=[b_wg[s]], partial=(kq > 0))
                    for (l0, n, g0, isp) in chunks:
                        for j, (name, r0, c0) in enumerate(blocks):
                            pj = psi % 4
                            psi += 1
                            pst = ps[pj]
                            for k in range(16):
                                P.op("pe", lambda e, pst=pst, k=k, s=s, j=j, l0=l0, n=n: e.matmul(
                                    pst[:, 0:n], wg[s][:, k, j * 128:(j + 1) * 128], hT[:, k, l0:l0 + n],
                                    start=(k == 0), stop=(k == 15)),
                                    reads=[b_hT, b_wg[s]], writes=[b_ps[pj]], partial=(k > 0))
                            if name.startswith("kT"):
                                dst = T["kT"][int(name[2]), :, g0:g0 + n]
                            else:
                                dst = T[name][r0:r0 + 128, g0:g0 + n]
                            si = evac_i % 4
                            evac_i += 1
                            if kind in ("q", "k") and not isp:
                                qi = evac_i % 2
                                P.op("act", lambda e, pst=pst, qi=qi, n=n: e.copy(qs[qi][:, 0:n], pst[:, 0:n]),
                                     reads=[b_ps[pj]], writes=[b_qs[qi]])
                                pr = ps[4 + qi]
                                P.op("pe", lambda e, pr=pr, qi=qi, n=n: e.matmul(
                                    pr[:, 0:n], T["ropeP"][:], qs[qi][:, 0:n], start=True, stop=True),
                                    reads=[b_qs[qi]], writes=[b_ps[4 + qi]])
                                P.op("dve", lambda e, qi=qi, l0=l0, n=n: e.tensor_tensor(
                                    t1[:, 0:n], qs[qi][:, 0:n], cos_t[:, l0:l0 + n], ALU.mult),
                                    reads=[b_qs[qi], b_cos], writes=[b_t1])
                                P.op("dve", lambda e, pr=pr, l0=l0, n=n: e.tensor_tensor(
                                    t2[:, 0:n], pr[:, 0:n], sin_t[:, l0:l0 + n], ALU.mult),
                                    reads=[b_ps[4 + qi], b_sin], writes=[b_t2])
                                P.op("dve", lambda e, si=si, n=n: e.tensor_tensor(
                                    stg[si][:, 0:n], t1[:, 0:n], t2[:, 0:n], ALU.add),
                                    reads=[b_t1, b_t2], writes=[b_stg[si]])
                            else:
                                if evac_i % 2 == 0:
                                    P.op("act", lambda e, pst=pst, si=si, n=n: e.copy(stg[si][:, 0:n], pst[:, 0:n]),
                                         reads=[b_ps[pj]], writes=[b_stg[si]])
                                else:
                                    P.op("dve", lambda e, pst=pst, si=si, n=n: e.tensor_copy(
                                        stg[si][:, 0:n], pst[:, 0:n]), reads=[b_ps[pj]], writes=[b_stg[si]])
                            P.dma("sp", dst, stg[si][:, 0:n], b_stg[si], reads=[b_stg[si]])
                P.barrier()
                P.flush()


def bcast_last(ap2d, n):
    a = [list(p) for p in ap2d.ap]
    return bass.AP(ap2d.tensor, ap2d.offset, a + [[0, n]])


def phase_B(P, nc, T):
    import contextlib
    ps = T["ps"]
    ident = T["ident"]
    seqs = [(0, LS, True), (LS, LP, False), (LS + LP, LP, False)]
    with contextlib.ExitStack() as es:
        A_ = lambda n, shp, dt: sb(es, nc, n, shp, dt)
        kT = [A_("b_kT0", [128, LS], BF16), A_("b_kT1", [128, LS], BF16)]
        kcT = [A_("b_kc0", [128, 512], BF16), A_("b_kc1", [128, 512], BF16)]
        ckd = A_("b_ckd", [128, 4, 2, 2, 64], BF16)
        vaug = A_("b_vaug", [128, LS // 128, 2, 65], BF16)
        cvaug = A_("b_cvaug", [128, 4, 2, 65], BF16)
        qT = [A_("b_q0", [128, LS], BF16), A_("b_q1", [128, LS], BF16)]
        ga = [A_("b_ga0", [128, LS], BF16), A_("b_ga1", [128, LS], BF16)]
        sga = A_("b_sga", [128, LS], BF16)
        ptA = [A_("b_ptA0", [128, 512], BF16), A_("b_ptA1", [128, 512], BF16)]
        ptB = [A_("b_ptB0", [128, 384], BF16), A_("b_ptB1", [128, 384], BF16)]
        att = [A_("b_att0", [128, 128], BF16), A_("b_att1", [128, 128], BF16)]
        stage = [A_("b_stg0", [128, 512], BF16), A_("b_stg1", [128, 512], BF16)]
        mask = A_("b_mask", [128, 384], BF16)
        sinkb = A_("b_sinkb", [128, 16], F32)
        esink = A_("b_esink", [128, 16], F32)
        den = [A_("b_den0", [128, 2], F32), A_("b_den1", [128, 2], F32)]
        rden = [A_("b_rden0", [128, 2], F32), A_("b_rden1", [128, 2], F32)]

        b_mask, b_es = P.buf(), P.buf()
        P.dma("sp", mask[:], T["d_amask"], b_mask, writes=[b_mask])
        P.dma("sp", sinkb[:], T["a_sink"][0:1, :].partition_broadcast(128), b_es, writes=[b_es])
        P.op("act", lambda e: e.activation(esink[:], sinkb[:], AF.Exp), reads=[b_es], writes=[b_es])
        P.op("dve", lambda e: e.memset(vaug[:], 1.0), writes=[b_mask], partial=True)
        P.op("dve", lambda e: e.memset(cvaug[:], 1.0), writes=[b_mask], partial=True)
        b_ckd, b_kc, b_cv = P.buf(), P.bufs_n(2), P.buf()
        ckv = T["ck"].rearrange("(t p) c -> p t c", p=128)
        cvv = T["cv"].rearrange("(t p) c -> p t c", p=128)
        for kv in range(2):
            for dup in range(2):
                P.dma("pool", ckd[:, :, kv, dup, :], ckv[:, :, kv * 64:(kv + 1) * 64], b_ckd, writes=[b_ckd],
                      partial=True)
            P.dma("pool", cvaug[:, :, kv, 0:64], cvv[:, :, kv * 64:(kv + 1) * 64], b_cv, reads=[b_mask],
                  writes=[b_cv], partial=True)
        b_pt = P.bufs_n(8)
        for kv in range(2):
            for st_ in range(4):
                pb = ps[st_ % 2][:].bitcast(BF16)
                P.op("pe", lambda e, pb=pb, st_=st_, kv=kv: e.transpose(
                    pb[:, 0:128], ckd[:, st_, kv].rearrange("p a b -> p (a b)"), ident[:]),
                    reads=[b_ckd], writes=[b_pt[st_ % 2]])
                P.op("dve", lambda e, pb=pb, st_=st_, kv=kv: e.tensor_copy(
                    kcT[kv][:, st_ * 128:(st_ + 1) * 128], pb[:, 0:128]),
                    reads=[b_pt[st_ % 2]], writes=[b_kc[kv]], partial=True)
        P.barrier()
        b_kT, b_v = P.bufs_n(2), P.buf()
        b_q, b_ga, b_sga = P.bufs_n(2), P.bufs_n(2), P.buf()
        b_ptA, b_ptB, b_att, b_stage = P.bufs_n(2), P.bufs_n(2), P.bufs_n(2), P.bufs_n(2)
        b_den = P.bufs_n(2)
        b_ps = P.bufs_n(8)
        cnt = 0
        for (tok0, L, has_ctx) in seqs:
            nqb = L // 128
            for kv in range(2):
                P.dma("sp", kT[kv][:, 0:L], T["kT"][kv, :, tok0:tok0 + L], b_kT[kv], writes=[b_kT[kv]])
                P.dma("sp", vaug[:, 0:nqb, kv, 0:64],
                      T["vtok"][tok0:tok0 + L, kv * 64:(kv + 1) * 64].rearrange("(t p) c -> p t c", p=128),
                      b_v, writes=[b_v], partial=(kv > 0))
            for hp in range(8):
                s = cnt % 2
                cnt += 1
                kv = hp // 4
                P.dma("sp", qT[s][:, 0:L], T["qT"][hp * 128:(hp + 1) * 128, tok0:tok0 + L], b_q[s], writes=[b_q[s]])
                P.dma("sp", ga[s][:, 0:L], T["gaT"][hp * 128:(hp + 1) * 128, tok0:tok0 + L], b_ga[s],
                      writes=[b_ga[s]])
                P.op("act", lambda e, s=s, L=L: e.activation(sga[:, 0:L], ga[s][:, 0:L], AF.Silu),
                     reads=[b_ga[s]], writes=[b_sga])
                def geom(qb, nqb=nqb, has_ctx=has_ctx):
                    if has_ctx:
                        loc = [(j, j - qb + 1) for j in (qb - 1, qb, qb + 1) if 0 <= j < nqb]
                    else:
                        loc = [(j, j) for j in range(nqb)]
                    return loc, loc[0][1], loc[-1][1] + 1

                def S_unit(qb, hh, s=s, kv=kv, has_ctx=has_ctx):
                    loc, lo, hi = geom(qb)
                    pr = slice(hh * 64, (hh + 1) * 64)
                    pa, pbk = ps[hh * 2], ps[hh * 2 + 1]
                    qsl = qT[s][pr, qb * 128:(qb + 1) * 128]
                    if has_ctx:
                        for c in range(4):
                            P.op("pe", lambda e, pa=pa, c=c, pr=pr, qsl=qsl, kv=kv: e.matmul(
                                pa[:, c * 128:(c + 1) * 128], kcT[kv][pr, c * 128:(c + 1) * 128], qsl,
                                start=True, stop=True),
                                reads=[b_kc[kv], b_q[s]], writes=[b_ps[hh * 2]], partial=(c > 0))
                        P.op("act", lambda e, pa=pa, hh=hh: e.activation(ptA[hh][:], pa[:], AF.Exp, scale=0.125),
                             reads=[b_ps[hh * 2]], writes=[b_ptA[hh]])
                    for n_, (j, sl) in enumerate(loc):
                        P.op("pe", lambda e, pbk=pbk, sl=sl, j=j, pr=pr, qsl=qsl, kv=kv: e.matmul(
                            pbk[:, sl * 128:(sl + 1) * 128], kT[kv][pr, j * 128:(j + 1) * 128], qsl,
                            start=True, stop=True),
                            reads=[b_kT[kv], b_q[s]], writes=[b_ps[hh * 2 + 1]], partial=(n_ > 0))
                    P.op("act", lambda e, pbk=pbk, hh=hh, lo=lo, hi=hi: e.activation(
                        ptB[hh][:, lo * 128:hi * 128], pbk[:, lo * 128:hi * 128], AF.Exp, scale=0.125),
                        reads=[b_ps[hh * 2 + 1]], writes=[b_ptB[hh]])
                    if has_ctx:
                        P.op("dve", lambda e, hh=hh, lo=lo, hi=hi: e.tensor_tensor(
                            ptB[hh][:, lo * 128:hi * 128], ptB[hh][:, lo * 128:hi * 128],
                            mask[:, lo * 128:hi * 128], ALU.mult),
                            reads=[b_ptB[hh], b_mask], writes=[b_ptB[hh]])

                def PV_unit(qb, hh, kv=kv, has_ctx=has_ctx):
                    loc, lo, hi = geom(qb)
                    o = qb % 2
                    psOv = ps[4 + o][:, 0:130].rearrange("p (h c) -> p h c", h=2)
                    mm = []
                    if has_ctx:
                        for c in range(4):
                            mm.append((ptA[hh][:, c * 128:(c + 1) * 128], cvaug[:, c, kv, :], b_ptA[hh], b_cv))
                    for (j, sl) in loc:
                        mm.append((ptB[hh][:, sl * 128:(sl + 1) * 128], vaug[:, j, kv, :], b_ptB[hh], b_v))
                    for n_, (lh, rh, bl, br) in enumerate(mm):
                        P.op("pe", lambda e, psOv=psOv, hh=hh, lh=lh, rh=rh, n_=n_, nm=len(mm): e.matmul(
                            psOv[:, hh, :], lh, rh, start=(n_ == 0), stop=(n_ == nm - 1)),
                            reads=[bl, br], writes=[b_ps[4 + o]], partial=(n_ > 0 or hh > 0))

                def FIN_unit(qb, s=s, hp=hp, nqb=nqb, tok0=tok0, cnt=cnt):
                    o = qb % 2
                    psOv = ps[4 + o][:, 0:130].rearrange("p (h c) -> p h c", h=2)
                    P.op("dve", lambda e, psOv=psOv, o=o, hp=hp: e.tensor_tensor(
                        den[o][:], psOv[:, :, 64], esink[:, hp * 2:hp * 2 + 2], ALU.add),
                        reads=[b_ps[4 + o], b_es], writes=[b_den[o]])
                    P.op("dve", lambda e, o=o: e.reciprocal(rden[o][:], den[o][:]), reads=[b_den[o]],
                         writes=[b_den[o]], partial=True)
                    P.op("dve", lambda e, psOv=psOv, o=o: e.tensor_tensor(
                        att[o][:].rearrange("p (h c) -> p h c", h=2), psOv[:, :, 0:64], bcast_last(rden[o][:], 64),
                        ALU.mult),
                        reads=[b_ps[4 + o], b_den[o]], writes=[b_att[o]])
                    pT = ps[6 + o][:].bitcast(BF16)
                    P.op("pe", lambda e, pT=pT, o=o: e.transpose(pT[:, 0:128], att[o][:], ident[:]),
                         reads=[b_att[o]], writes=[b_ps[6 + o]])
                    g4 = qb // 4
                    sg_ = (cnt * 1024 + g4) % 2
                    P.op("dve", lambda e, pT=pT, sg_=sg_, qb=qb: e.tensor_tensor(
                        stage[sg_][:, (qb % 4) * 128:(qb % 4 + 1) * 128], pT[:, 0:128],
                        sga[:, qb * 128:(qb + 1) * 128], ALU.mult),
                        reads=[b_ps[6 + o], b_sga], writes=[b_stage[sg_]], partial=(qb % 4 > 0))
                    if qb % 4 == 3 or qb == nqb - 1:
                        n = (qb % 4 + 1) * 128
                        t0 = tok0 + g4 * 512
                        P.dma("sp", T["mixT"][hp * 128:(hp + 1) * 128, t0:t0 + n], stage[sg_][:, 0:n],
                              b_stage[sg_], reads=[b_stage[sg_]])

                units = [(qb, hh) for qb in range(nqb) for hh in range(2)]
                S_unit(*units[0])
                pend_fin = None
                for ui, (qb, hh) in enumerate(units):
                    if ui + 1 < len(units):
                        S_unit(*units[ui + 1])
                    PV_unit(qb, hh)
                    if pend_fin is not None:
                        FIN_unit(pend_fin)
                        pend_fin = None
                    if hh == 1:
                        pend_fin = qb
                if pend_fin is not None:
                    FIN_unit(pend_fin)
        P.barrier()
        P.flush()


def phase_C0(P, nc, T):
    import contextlib
    seqs = [(0, LS), (LS, LP), (LS + LP, LP)]
    with contextlib.ExitStack() as es:
        A_ = lambda n, shp, dt: sb(es, nc, n, shp, dt)
        hsw = A_("c0_hsw", [128, 4, 24], F32)
        ut = [A_("c0_ut0", [128, LS + 2], BF16), A_("c0_ut1", [128, LS + 2], BF16), A_("c0_ut2", [128, LS + 2], BF16)]
        accs = [A_("c0_acc", [128, LS], F32), A_("c0_accb", [128, LS], F32)]
        acc2s = [A_("c0_acc2", [128, LS], F32), A_("c0_acc2b", [128, LS], F32)]
        ob = [A_("c0_ob0", [128, LS], BF16), A_("c0_ob1", [128, LS], BF16)]
        b_hsw, b_ut, b_accs, b_acc2s, b_ob = P.buf(), P.bufs_n(3), P.bufs_n(2), P.bufs_n(2), P.bufs_n(2)
        P.dma("sp", hsw[:], T["hsw"], b_hsw, writes=[b_hsw])
        items = [(tok0, L, ct) for (tok0, L) in seqs for ct in range(24)]

        def front(it):
            tok0, L, ct = items[it]
            u = it % 3
            P.op("pool", lambda e, u=u: e.memset(ut[u][:, 0:1], 0.0), writes=[b_ut[u]])
            P.op("pool", lambda e, u=u, L=L: e.memset(ut[u][:, L + 1:L + 2], 0.0), writes=[b_ut[u]], partial=True)
            P.dma("sp", ut[u][:, 1:L + 1], T["hyT"][ct * 128:(ct + 1) * 128, tok0:tok0 + L], b_ut[u],
                  writes=[b_ut[u]], partial=True)

        def back(it):
            tok0, L, ct = items[it]
            u = it % 3
            s = it % 2
            acc, acc2, b_acc, b_acc2 = accs[s], acc2s[s], b_accs[s], b_acc2s[s]
            P.op("act", lambda e, u=u, L=L, ct=ct, acc=acc: e.activation(
                acc[:, 0:L], ut[u][:, 0:L], AF.Identity, scale=hsw[:, 0, ct:ct + 1], bias=hsw[:, 3, ct:ct + 1]),
                reads=[b_ut[u], b_hsw], writes=[b_acc])
            P.op("dve", lambda e, u=u, L=L, ct=ct, acc=acc, acc2=acc2: e.scalar_tensor_tensor(
                acc2[:, 0:L], ut[u][:, 1:L + 1], hsw[:, 1, ct:ct + 1], acc[:, 0:L], ALU.mult, ALU.add),
                reads=[b_ut[u], b_acc, b_hsw], writes=[b_acc2])
            P.op("dve", lambda e, u=u, s=s, L=L, ct=ct, acc2=acc2: e.scalar_tensor_tensor(
                ob[s][:, 0:L], ut[u][:, 2:L + 2], hsw[:, 2, ct:ct + 1], acc2[:, 0:L], ALU.mult, ALU.add),
                reads=[b_ut[u], b_acc2, b_hsw], writes=[b_ob[s]])
            P.dma("sp", T["ucT"][ct * 128:(ct + 1) * 128, tok0:tok0 + L], ob[s][:, 0:L], b_ob[s],
                  reads=[b_ob[s]])

        n = len(items)
        front(0)
        if n > 1:
            front(1)
        for it in range(n):
            if it + 2 < n:
                front(it + 2)
            back(it)
        P.barrier()
        P.flush()


def phase_CF(P, nc, T, Lf, taps_dst, zemb, tposb):
    import contextlib
    import math
    ps = T["ps"]
    nch = max(1, Lf // 512)
    cw = min(512, Lf)
    with contextlib.ExitStack() as es:
        A_ = lambda n, shp, dt: sb(es, nc, n, shp, dt)
        ze = A_("cf_ze", [33, Lf], F32)
        tp_ = A_("cf_tpos", [128, Lf], F32)
        w1 = A_("cf_w1", [33, 64], F32)
        w2 = A_("cf_w2", [64, 64], F32)
        bb = A_("cf_bb", [64, 2], F32)
        w3 = A_("cf_w3", [64, 4096], BF16)
        dec = A_("cf_dec", [128, 32], F32)
        ndec = A_("cf_ndec", [128, 32], F32)
        h1 = A_("cf_h1", [64, Lf], F32)
        h2 = A_("cf_h2", [64, Lf], F32)
        h2b = A_("cf_h2b", [64, Lf], BF16)
        tmp = A_("cf_tmp", [64, 512], F32)
        tmq = A_("cf_tmq", [64, 512], F32)
        raw = [A_("cf_raw0", [128, Lf], F32), A_("cf_raw1", [128, Lf], F32)]
        win = [A_("cf_win0", [128, 512], F32), A_("cf_win1", [128, 512], F32)]
        junk = A_("cf_junk", [128, Lf], F32)
        ss = A_("cf_ss", [128, 8], F32)
        tpb = [A_("cf_tp0", [128, 2 * Lf], BF16), A_("cf_tp1", [128, 2 * Lf], BF16)]
        b_c = P.bufs_n(8)
        P.dma("sp", ze[:], zemb, b_c[0], writes=[b_c[0]])
        P.dma("sp", tp_[:], tposb, b_c[1], writes=[b_c[1]])
        P.dma("sp", w1[:], T["hy_w1"], b_c[2], writes=[b_c[2]])
        P.dma("sp", w2[:], T["hy_w2"], b_c[3], writes=[b_c[3]])
        P.dma("sp", bb[:], T["hy_bb"], b_c[4], writes=[b_c[4]])
        for q4 in range(4):
            P.dma("pool", w3[:, q4 * 1024:(q4 + 1) * 1024], T["hy_w3"][:, q4 * 1024:(q4 + 1) * 1024], b_c[5],
                  writes=[b_c[5]], partial=(q4 > 0))
        P.dma("sp", dec[:], T["hy_decT"], b_c[6], writes=[b_c[6]])
        P.op("act", lambda e: e.activation(dec[:], dec[:], AF.Abs), reads=[b_c[6]], writes=[b_c[6]])
        P.op("dve", lambda e: e.tensor_scalar(ndec[:], dec[:], -1.0, None, ALU.mult),
             reads=[b_c[6]], writes=[b_c[7]])
        b_h1, b_h2, b_h2b, b_tmp, b_tmq = P.buf(), P.buf(), P.buf(), P.buf(), P.buf()
        b_ps = P.bufs_n(8)
        TWO_PI = 2.0 * math.pi
        for layer in range(2):
            src = ze if layer == 0 else h1
            wm = w1 if layer == 0 else w2
            kk = 33 if layer == 0 else 64
            dst = h1 if layer == 0 else h2
            b_src = b_c[0] if layer == 0 else b_h1
            b_dst = b_h1 if layer == 0 else b_h2
            for c in range(nch):
                pj = c % 2
                P.op("pe", lambda e, pj=pj, c=c, src=src, wm=wm, kk=kk: e.matmul(
                    ps[pj][0:64, 0:cw], wm[0:kk, :], src[0:kk, c * cw:(c + 1) * cw], start=True, stop=True),
                    reads=[b_src, b_c[2], b_c[3]], writes=[b_ps[pj]])
                MAGIC = 12582912.0
                P.op("act", lambda e, pj=pj, layer=layer: e.activation(
                    tmp[:, 0:cw], ps[pj][0:64, 0:cw], AF.Identity, bias=bb[:, layer:layer + 1]),
                    reads=[b_ps[pj], b_c[4]], writes=[b_tmp])
                P.op("dve", lambda e: e.tensor_scalar(
                    tmq[:, 0:cw], tmp[:, 0:cw], 1.0 / TWO_PI, MAGIC, ALU.mult, ALU.add),
                    reads=[b_tmp], writes=[b_tmq])
                P.op("dve", lambda e: e.tensor_scalar(
                    tmq[:, 0:cw], tmq[:, 0:cw], -MAGIC, -TWO_PI, ALU.add, ALU.mult),
                    reads=[b_tmq], writes=[b_tmq])
                P.op("dve", lambda e: e.tensor_tensor(tmp[:, 0:cw], tmp[:, 0:cw], tmq[:, 0:cw], ALU.add),
                     reads=[b_tmp, b_tmq], writes=[b_tmp])
                P.op("act", lambda e, c=c, dst=dst: e.activation(dst[:, c * cw:(c + 1) * cw], tmp[:, 0:cw], AF.Sin),
                     reads=[b_tmp], writes=[b_dst], partial=(c > 0))
        P.op("act", lambda e: e.copy(h2b[:], h2[:]), reads=[b_h2], writes=[b_h2b])
        b_raw, b_win, b_junk, b_ss, b_tpb = P.bufs_n(2), P.bufs_n(2), P.buf(), P.buf(), P.bufs_n(2)
        it = 0
        for order in range(2):
            for blk in range(8):
                s = it % 2
                it += 1
                for d in range(2):
                    cbk = d * 16 + order * 8 + blk
                    for c in range(nch):
                        pj = 2 + (c % 2)
                        wj = c % 2
                        P.op("pe", lambda e, pj=pj, c=c, cbk=cbk: e.matmul(
                            ps[pj][:, 0:cw], w3[:, cbk * 128:(cbk + 1) * 128], h2b[:, c * cw:(c + 1) * cw],
                            start=True, stop=True),
                            reads=[b_h2b, b_c[5]], writes=[b_ps[pj]])
                        P.op("act", lambda e, wj=wj, c=c, cbk=cbk: e.activation(
                            win[wj][:, 0:cw], tp_[:, c * cw:(c + 1) * cw], AF.Exp, scale=ndec[:, cbk:cbk + 1]),
                            reads=[b_c[1], b_c[7]], writes=[b_win[wj]])
                        P.op("dve", lambda e, pj=pj, wj=wj, c=c, d=d: e.tensor_tensor(
                            raw[d][:, c * cw:(c + 1) * cw], ps[pj][:, 0:cw], win[wj][:, 0:cw], ALU.mult),
                            reads=[b_ps[pj], b_win[wj]], writes=[b_raw[d]], partial=(c > 0))
                P.op("act", lambda e: e.activation(junk[:], raw[0][:], AF.Square, accum_out=ss[:, 0:1]),
                     reads=[b_raw[0]], writes=[b_junk, b_ss])
                P.op("act", lambda e: e.activation(junk[:, 1:Lf], raw[1][:, 1:Lf], AF.Square, accum_out=ss[:, 1:2]),
                     reads=[b_raw[1]], writes=[b_junk, b_ss], partial=True)
                P.op("dve", lambda e: e.tensor_tensor(ss[:, 2:3], ss[:, 0:1], ss[:, 1:2], ALU.add),
                     reads=[b_ss], writes=[b_ss], partial=True)
                P.op("act", lambda e: e.activation(ss[:, 3:4], ss[:, 2:3], AF.Sqrt, bias=T["epsb"][:, 0:1]),
                     reads=[b_ss], writes=[b_ss], partial=True)
                P.op("dve", lambda e: e.reciprocal(ss[:, 4:5], ss[:, 3:4]), reads=[b_ss], writes=[b_ss], partial=True)
                P.op("dve", lambda e: e.tensor_scalar(ss[:, 5:6], ss[:, 4:5], -1.0, None, ALU.mult),
                     reads=[b_ss], writes=[b_ss], partial=True)
                P.op("act", lambda e, s=s: e.activation(tpb[s][:, 0:Lf], raw[0][:], AF.Copy, scale=ss[:, 4:5]),
                     reads=[b_raw[0], b_ss], writes=[b_tpb[s]])
                P.op("pool", lambda e, s=s: e.memset(tpb[s][:, Lf:Lf + 1], 0.0), writes=[b_tpb[s]], partial=True)
                P.op("dve", lambda e, s=s: e.tensor_scalar(
                    rev_ap(tpb[s][:, Lf + 1:2 * Lf]), raw[1][:, 1:Lf], ss[:, 5:6], None, ALU.mult),
                    reads=[b_raw[1], b_ss], writes=[b_tpb[s]], partial=True)
                P.dma("sp", taps_dst[order, blk * 128:(blk + 1) * 128, :], tpb[s][:], b_tpb[s], reads=[b_tpb[s]])
        P.barrier()
        P.flush()


def _split_dma(P, eng, dst, src, buf, nsplit, axis_len, mk_dst, mk_src, **kw):
    step = axis_len // nsplit
    for i in range(nsplit):
        P.dma(eng, mk_dst(i * step, (i + 1) * step), mk_src(i * step, (i + 1) * step), buf,
              partial=(i > 0 or kw.get("partial", False)), **{k: v for k, v in kw.items() if k != "partial"})


def _f1_stage(P, nc, T, xin, K, b_xin, F1, b_F1, Cb, b_Cb, b_ps, ev):
    ps = T["ps"]
    for g in range(16):
        pj = g % 2
        for cc in range(4):
            c = g * 4 + cc
            P.op("pe", lambda e, pj=pj, cc=cc, c=c: e.matmul(
                ps[pj][0:64, cc * 128:(cc + 1) * 128], xin[0:K, c, :], F1[0:K, 0:128], start=True, stop=True),
                reads=[b_xin, b_F1], writes=[b_ps[pj]], partial=(cc > 0))
            P.op("pe", lambda e, pj=pj, cc=cc, c=c: e.matmul(
                ps[pj][64:128, cc * 128:(cc + 1) * 128], xin[0:K, c, :], F1[0:K, 128:256], start=True, stop=True,
                tile_position=(0, 64)),
                reads=[b_xin, b_F1], writes=[b_ps[pj]], partial=True)
        src = ps[pj][:].rearrange("p (c k) -> p k c", c=4)
        dst = Cb[:, :, g * 4:(g + 1) * 4]
        ev(dst, src, [b_ps[pj]], [b_Cb], g)


def _evac_alt(P):
    def ev(dst, src, reads, writes, i):
        if i % 2 == 0:
            P.op("act", lambda e: e.copy(dst, src), reads=reads, writes=writes, partial="nowaw")
        else:
            P.op("dve", lambda e: e.tensor_copy(dst, src), reads=reads, writes=writes, partial="nowaw")
    return ev


def _evac_act(P):
    def ev(dst, src, reads, writes, i):
        P.op("act", lambda e: e.copy(dst, src), reads=reads, writes=writes, partial="nowaw")
    return ev


def phase_CT(P, nc, T):
    import contextlib
    ps = T["ps"]
    with contextlib.ExitStack() as es:
        A_ = lambda n, shp, dt: sb(es, nc, n, shp, dt)
        F1 = A_("ct_F1", [128, 256], BF16)
        Grr = A_("ct_Grr", [128, 128, 64], BF16)
        Gii = A_("ct_Gii", [128, 128, 64], BF16)
        xt = [A_("ct_xt0", [128, 64, 64], BF16), A_("ct_xt1", [128, 64, 64], BF16)]
        Cb = A_("ct_Cb", [128, 128, 64], BF16)
        Hr = [A_("ct_Hr0", [64, 64, 128], BF16), A_("ct_Hr1", [64, 64, 128], BF16)]
        Hi = [A_("ct_Hi0", [64, 64, 128], BF16), A_("ct_Hi1", [64, 64, 128], BF16)]
        b_F1, b_G = P.buf(), P.buf()
        P.dma("sp", F1[:], T["d_F1"], b_F1, writes=[b_F1])
        for q in range(4):
            P.dma("sp", Grr[:, q * 32:(q + 1) * 32, :], T["d_Grr"][:, q * 32:(q + 1) * 32, :], b_G, writes=[b_G],
                  partial=(q > 0))
            P.dma("sp", Gii[:, q * 32:(q + 1) * 32, :], T["d_Gii"][:, q * 32:(q + 1) * 32, :], b_G, writes=[b_G],
                  partial=True)
        b_xt, b_Cb, b_Hr, b_Hi = P.bufs_n(2), P.buf(), P.bufs_n(2), P.bufs_n(2)
        b_ps = P.bufs_n(8)
        ev = _evac_alt(P)
        it = 0
        def ct_load(order, cb, s):
            srcv = T["tapsS"][order, cb * 64:(cb + 1) * 64, :].rearrange("c (a b) -> a c b", b=64)
            for q in range(8):
                P.dma("sp", xt[s][:, q * 8:(q + 1) * 8, :], srcv[:, q * 8:(q + 1) * 8, :], b_xt[s],
                      writes=[b_xt[s]], partial=(q > 0))

        blocks = [(order, cb) for order in range(2) for cb in range(16)]
        ct_load(0, 0, 0)
        for (order, cb) in blocks:
                s = it % 2
                it += 1
                _f1_stage(P, nc, T, xt[s], 128, b_xt[s], F1, b_F1, Cb, b_Cb, b_ps, ev)
                if it < len(blocks):
                    ct_load(blocks[it][0], blocks[it][1], it % 2)
                for g in range(16):
                    pj = 2 + (g % 2) * 2
                    for kk in range(8):
                        k1 = g * 8 + kk
                        P.op("pe", lambda e, pj=pj, kk=kk, k1=k1: e.matmul(
                            ps[pj][0:64, kk * 64:(kk + 1) * 64], Grr[:, k1, :], Cb[:, k1, :], start=True, stop=True),
                            reads=[b_G, b_Cb], writes=[b_ps[pj]], partial=(kk > 0))
                        P.op("pe", lambda e, pj=pj, kk=kk, k1=k1: e.matmul(
                            ps[pj + 1][0:64, kk * 64:(kk + 1) * 64], Gii[:, k1, :], Cb[:, k1, :], start=True,
                            stop=True),
                            reads=[b_G, b_Cb], writes=[b_ps[pj + 1]], partial=(kk > 0))
                    P.op("act", lambda e, pj=pj, g=g, s=s: e.copy(
                        Hr[s][:, :, g * 8:(g + 1) * 8], ps[pj][0:64, :].rearrange("p (k c) -> p c k", k=8)),
                        reads=[b_ps[pj]], writes=[b_Hr[s]], partial="nowaw")
                    P.op("dve", lambda e, pj=pj, g=g, s=s: e.tensor_copy(
                        Hi[s][:, :, g * 8:(g + 1) * 8], ps[pj + 1][0:64, :].rearrange("p (k c) -> p c k", k=8)),
                        reads=[b_ps[pj + 1]], writes=[b_Hi[s]], partial="nowaw")
                P.dma("sp", T["Hs"][order, cb, 0], Hr[s][:].rearrange("p c k -> p (c k)"), b_Hr[s], reads=[b_Hr[s]])
                P.dma("sp", T["Hs"][order, cb, 1], Hi[s][:].rearrange("p c k -> p (c k)"), b_Hi[s], reads=[b_Hi[s]])
        P.barrier()
        P.flush()


def phase_CS(P, nc, T):
    import contextlib
    ps = T["ps"]
    with contextlib.ExitStack() as es:
        A_ = lambda n, shp, dt: sb(es, nc, n, shp, dt)
        F1 = A_("cs_F1", [128, 256], BF16)
        G = A_("cs_G", [128, 128, 64], BF16)
        M1 = A_("cs_M1", [64, 128], BF16)
        M1p = A_("cs_M1p", [64, 128], BF16)
        T2r = A_("cs_T2r", [128, 64, 64], BF16)
        T2i = A_("cs_T2i", [128, 64, 64], BF16)
        zc = A_("cs_zc", [64, 64, 64], BF16)
        x1 = A_("cs_x1", [64, 64, 64], BF16)
        x2 = A_("cs_x2", [64, 64, 64], BF16)
        gh = A_("cs_gh", [64, 64, 64], BF16)
        z2 = A_("cs_z2", [64, 64, 64], BF16)
        bz = A_("cs_bz", [64, 64, 64], BF16)
        cv = A_("cs_cv", [64, 64, 64], F32)
        bias = A_("cs_bias", [64, 2, 64], F32)
        Cb = A_("cs_Cb", [128, 128, 64], BF16)
        Db = A_("cs_Db", [128, 128, 64], BF16)
        P1 = A_("cs_P1", [64, 64, 128], BF16)
        P2 = A_("cs_P2", [64, 64, 128], BF16)
        Hr = A_("cs_Hr", [64, 64, 128], BF16)
        Hi = A_("cs_Hi", [64, 64, 128], BF16)
        Xs = [A_("cs_Xs0", [64, 512], BF16), A_("cs_Xs1", [64, 512], BF16)]
        b_k = P.bufs_n(6)
        P.dma("sp", F1[:], T["d_F1"], b_k[0], writes=[b_k[0]])
        for q in range(4):
            P.dma("sp", G[:, q * 32:(q + 1) * 32, :], T["d_G"][:, q * 32:(q + 1) * 32, :], b_k[1], writes=[b_k[1]],
                  partial=(q > 0))
        P.dma("sp", M1[:], T["d_M1"], b_k[2], writes=[b_k[2]])
        P.dma("sp", M1p[:], T["d_M1p"], b_k[3], writes=[b_k[3]])
        P.dma("sp", T2r[:], T["d_T2r"], b_k[4], writes=[b_k[4]])
        P.dma("sp", T2i[:], T["d_T2in"], b_k[5], writes=[b_k[5]])
        b_F1, b_G, b_M1, b_M1p, b_T2r, b_T2i = b_k
        b_zc, b_x1, b_x2, b_gh, b_sgh, b_z2, b_bz, b_cv, b_mx, b_bias = [P.buf() for _ in range(10)]
        b_Cb, b_Db, b_P1, b_P2, b_Hr, b_Hi = [P.buf() for _ in range(6)]
        b_Xs = P.bufs_n(2)
        b_ps = P.bufs_n(8)
        ev = _evac_act(P)

        def t64(rows0):
            return lambda a, b: T["ucT"][rows0 + a:rows0 + b, 0:LS].rearrange("c (n1 n2) -> n1 c n2", n2=64)

        for cb in range(16):
            for (dst, bdst, r0, src_t) in ((zc, b_zc, 2048 + cb * 64, "ucT"), (x1, b_x1, cb * 64, "ucT"),
                                           (x2, b_x2, 1024 + cb * 64, "ucT"), (gh, b_gh, cb * 64, "ghT")):
                for q in range(4):
                    srcv = T[src_t][r0 + q * 16:r0 + (q + 1) * 16, 0:LS].rearrange("c (n1 n2) -> n1 c n2", n2=64)
                    P.dma("sp", dst[:, q * 16:(q + 1) * 16, :], srcv, bdst, writes=[bdst], partial=(q > 0))
            for o in range(2):
                P.dma("sp", bias[:, o, :], T["hy_bias"][o:o + 1, cb * 64:(cb + 1) * 64].partition_broadcast(64),
                      b_bias, writes=[b_bias], partial=(o > 0))
            P.op("act", lambda e: e.activation(gh[:], gh[:], AF.Silu), reads=[b_gh], writes=[b_gh])
            for o in range(2):
                zin, b_zin = (zc, b_zc) if o == 0 else (z2, b_z2)
                xo, b_xo = (x1, b_x1) if o == 0 else (x2, b_x2)
                P.dma("sp", Hr[:].rearrange("p c k -> p (c k)"), T["Hs"][o, cb, 0], b_Hr, writes=[b_Hr])
                P.dma("sp", Hi[:].rearrange("p c k -> p (c k)"), T["Hs"][o, cb, 1], b_Hi, writes=[b_Hi])
                P.op("pool", lambda e, zin=zin, o=o: e.tensor_tensor(
                    bz[:], zin[:], bcast_last(bias[:, o, :], 64), ALU.mult),
                    reads=[b_zin, b_bias], writes=[b_bz])
                _f1_stage(P, nc, T, zin, 64, b_zin, F1, b_F1, Cb, b_Cb, b_ps, ev)
                for g in range(16):
                    pj = 2 + (g % 2)
                    xs = g % 2
                    for kk in range(8):
                        k1 = g * 8 + kk
                        P.op("pe", lambda e, pj=pj, kk=kk, k1=k1: e.matmul(
                            ps[pj][0:64, kk * 64:(kk + 1) * 64], G[:, k1, :], Cb[:, k1, :], start=True, stop=True),
                            reads=[b_G, b_Cb], writes=[b_ps[pj]], partial=(kk > 0))
                    P.op("act", lambda e, pj=pj, xs=xs: e.copy(Xs[xs][:], ps[pj][0:64, :]),
                         reads=[b_ps[pj]], writes=[b_Xs[xs]])
                    xv = Xs[xs][:].rearrange("p (k c) -> p c k", k=8)
                    P.op("dve", lambda e, xv=xv, g=g: e.tensor_tensor(
                        P1[:, :, g * 8:(g + 1) * 8], xv, Hr[:, :, g * 8:(g + 1) * 8], ALU.mult),
                        reads=[b_Xs[xs], b_Hr], writes=[b_P1], partial="nowaw")
                    P.op("dve", lambda e, xv=xv, g=g: e.tensor_tensor(
                        P2[:, :, g * 8:(g + 1) * 8], xv, Hi[:, :, g * 8:(g + 1) * 8], ALU.mult),
                        reads=[b_Xs[xs], b_Hi], writes=[b_P2], partial="nowaw")
                for g in range(16):
                    pj = 4 + (g % 2)
                    for cc in range(4):
                        c = g * 4 + cc
                        P.op("pe", lambda e, pj=pj, cc=cc, c=c: e.matmul(
                            ps[pj][:, cc * 128:(cc + 1) * 128], P1[:, c, :], M1[:], start=True, stop=False),
                            reads=[b_P1, b_M1], writes=[b_ps[pj]], partial=(cc > 0))
                        P.op("pe", lambda e, pj=pj, cc=cc, c=c: e.matmul(
                            ps[pj][:, cc * 128:(cc + 1) * 128], P2[:, c, :], M1p[:], start=False, stop=True),
                            reads=[b_P2, b_M1p], writes=[b_ps[pj]], partial=True)
                    src = ps[pj][:].rearrange("p (c q) -> p q c", c=4)
                    ev(Db[:, :, g * 4:(g + 1) * 4], src, [b_ps[pj]], [b_Db], g)
                for g in range(8):
                    pj = 6 + (g % 2)
                    for nn in range(8):
                        n2 = g * 8 + nn
                        P.op("pe", lambda e, pj=pj, nn=nn, n2=n2: e.matmul(
                            ps[pj][0:64, nn * 64:(nn + 1) * 64], T2r[:, n2, :], Db[:, n2, :], start=True, stop=False),
                            reads=[b_T2r, b_Db], writes=[b_ps[pj]], partial=(nn > 0))
                        P.op("pe", lambda e, pj=pj, nn=nn, n2=n2: e.matmul(
                            ps[pj][0:64, nn * 64:(nn + 1) * 64], T2i[:, n2, :], Db[:, 64 + n2, :], start=False,
                            stop=True),
                            reads=[b_T2i, b_Db], writes=[b_ps[pj]], partial=True)
                    P.op("dve", lambda e, pj=pj, g=g: e.tensor_tensor(
                        cv[:, :, g * 8:(g + 1) * 8], ps[pj][0:64, :].rearrange("p (n c) -> p c n", n=8),
                        bz[:, :, g * 8:(g + 1) * 8], ALU.add),
                        reads=[b_ps[pj], b_bz], writes=[b_cv], partial=(g > 0))
                if o == 0:
                    P.op("dve", lambda e: e.tensor_tensor(z2[:], cv[:], x1[:], ALU.mult),
                         reads=[b_cv, b_x1], writes=[b_z2])
                else:
                    P.op("dve", lambda e: e.tensor_tensor(cv[:], cv[:], x2[:], ALU.mult),
                         reads=[b_cv, b_x2], writes=[b_cv])
                    P.op("pool", lambda e: e.tensor_tensor(z2[:], cv[:], gh[:], ALU.mult),
                         reads=[b_cv, b_gh], writes=[b_z2])
                    for q in range(4):
                        r0 = 1024 + cb * 64 + q * 16
                        dstv = T["mixT"][r0:r0 + 16, 0:LS].rearrange("c (n1 n2) -> n1 c n2", n2=64)
                        P.dma("sp", dstv, z2[:, q * 16:(q + 1) * 16, :], b_z2, reads=[b_z2])
        P.barrier()
        P.flush()


def phase_CP(P, nc, T):
    import contextlib
    ps = T["ps"]
    ident = T["ident"]
    with contextlib.ExitStack() as es:
        A_ = lambda n, shp, dt: sb(es, nc, n, shp, dt)
        FP = A_("cp_FP", [128, 4, 512], BF16)
        IP = A_("cp_IP", [128, 4, 256], BF16)
        HA = A_("cp_HA", [128, 16, 512], F32)
        HB = A_("cp_HB", [128, 16, 512], F32)
        biasT = A_("cp_biasT", [128, 16], F32)
        tp = [A_("cp_tp0", [128, 512], BF16), A_("cp_tp1", [128, 512], BF16)]
        tt = A_("cp_tt", [128, 4, 128], BF16)
        zc = [A_("cp_zc0", [128, 256], BF16), A_("cp_zc1", [128, 256], BF16)]
        x1 = [A_("cp_x10", [128, 256], BF16), A_("cp_x11", [128, 256], BF16)]
        x2 = [A_("cp_x20", [128, 256], BF16), A_("cp_x21", [128, 256], BF16)]
        gh = [A_("cp_gh0", [128, 256], BF16), A_("cp_gh1", [128, 256], BF16)]
        sgh = A_("cp_sgh", [128, 256], F32)
        z2 = A_("cp_z2", [128, 256], BF16)
        zt = A_("cp_zt", [128, 2, 128], BF16)
        Aa = A_("cp_A", [128, 512], F32)
        Bb = A_("cp_B", [128, 512], F32)
        Y = A_("cp_Y", [128, 512], BF16)
        Yt = A_("cp_Yt", [128, 4, 128], BF16)
        t1 = A_("cp_t1", [128, 256], F32)
        t2 = A_("cp_t2", [128, 256], F32)
        mx = [A_("cp_mx0", [128, 256], BF16), A_("cp_mx1", [128, 256], BF16)]
        b_FP, b_IP, b_H, b_bias = P.buf(), P.buf(), P.buf(), P.buf()
        P.dma("sp", FP[:], T["d_FP"], b_FP, writes=[b_FP])
        P.dma("sp", IP[:], T["d_IP"], b_IP, writes=[b_IP])
        P.dma("sp", biasT[:], T["hy_biasT"], b_bias, writes=[b_bias])
        b_tp, b_tt = P.bufs_n(2), P.buf()
        b_ps = P.bufs_n(8)
        it = 0
        for o in range(2):
            for t in range(8):
                s = it % 2
                it += 1
                P.dma("sp", tp[s][:], T["tapsP"][o, t * 128:(t + 1) * 128, :], b_tp[s], writes=[b_tp[s]])
                pb = ps[s][:].bitcast(BF16)
                for j in range(4):
                    P.op("pe", lambda e, pb=pb, j=j, s=s: e.transpose(
                        pb[:, j * 128:(j + 1) * 128], tp[s][:, j * 128:(j + 1) * 128], ident[:]),
                        reads=[b_tp[s]], writes=[b_ps[s]], partial=(j > 0))
                P.op("dve", lambda e, pb=pb: e.tensor_copy(tt[:].rearrange("p a b -> p (a b)"), pb[:, 0:512]),
                     reads=[b_ps[s]], writes=[b_tt])
                pj = 2 + s
                for j in range(4):
                    P.op("pe", lambda e, pj=pj, j=j: e.matmul(
                        ps[pj][:, :], tt[:, j, :], FP[:, j, :], start=(j == 0), stop=(j == 3)),
                        reads=[b_tt, b_FP], writes=[b_ps[pj]], partial=(j > 0))
                i = o * 8 + t
                for h in range(2):
                    P.op("act", lambda e, pj=pj, i=i, h=h: e.copy(HA[:, i, h * 256:(h + 1) * 256], ps[pj][:, 0:256]),
                         reads=[b_ps[pj]], writes=[b_H], partial=True)
                    P.op("act", lambda e, pj=pj, i=i, h=h: e.copy(HB[:, i, h * 256:(h + 1) * 256],
                                                                  ps[pj][:, 256:512]),
                         reads=[b_ps[pj]], writes=[b_H], partial=True)
        b_zc, b_x1, b_x2, b_gh = P.bufs_n(2), P.bufs_n(2), P.bufs_n(2), P.bufs_n(2)
        b_sgh, b_z2, b_zt, b_A, b_B, b_Y, b_Yt, b_t1, b_t2 = [P.buf() for _ in range(9)]
        b_mx = P.bufs_n(2)
        it = 0
        for sq in range(2):
            tok0 = LS + sq * LP
            for t in range(8):
                s = it % 2
                it += 1
                r = t * 128
                P.dma("sp", zc[s][:], T["ucT"][2048 + r:2048 + r + 128, tok0:tok0 + LP], b_zc[s], writes=[b_zc[s]])
                P.dma("sp", x1[s][:], T["ucT"][r:r + 128, tok0:tok0 + LP], b_x1[s], writes=[b_x1[s]])
                P.dma("sp", x2[s][:], T["ucT"][1024 + r:1024 + r + 128, tok0:tok0 + LP], b_x2[s], writes=[b_x2[s]])
                P.dma("sp", gh[s][:], T["ghT"][r:r + 128, tok0:tok0 + LP], b_gh[s], writes=[b_gh[s]])
                P.op("act", lambda e, s=s: e.activation(sgh[:], gh[s][:], AF.Silu), reads=[b_gh[s]], writes=[b_sgh])
                for o in range(2):
                    zin, b_zin = (zc[s], b_zc[s]) if o == 0 else (z2, b_z2)
                    xo, b_xo = (x1[s], b_x1[s]) if o == 0 else (x2[s], b_x2[s])
                    i = o * 8 + t
                    pb = ps[0][:].bitcast(BF16)
                    for j in range(2):
                        P.op("pe", lambda e, pb=pb, j=j, zin=zin: e.transpose(
                            pb[:, j * 128:(j + 1) * 128], zin[:, j * 128:(j + 1) * 128], ident[:]),
                            reads=[b_zin], writes=[b_ps[0]], partial=(j > 0))
                    P.op("dve", lambda e, pb=pb: e.tensor_copy(zt[:].rearrange("p a b -> p (a b)"), pb[:, 0:256]),
                         reads=[b_ps[0]], writes=[b_zt])
                    for j in range(2):
                        P.op("pe", lambda e, j=j: e.matmul(ps[1][:, :], zt[:, j, :], FP[:, j, :], start=(j == 0),
                                                           stop=(j == 1)),
                             reads=[b_zt, b_FP], writes=[b_ps[1]], partial=(j > 0))
                    P.op("dve", lambda e, i=i: e.tensor_tensor(Aa[:], ps[1][:, :], HA[:, i, :], ALU.mult),
                         reads=[b_ps[1], b_H], writes=[b_A])
                    P.op("dve", lambda e, i=i: e.tensor_tensor(Bb[:], ps[1][:, :], HB[:, i, :], ALU.mult),
                         reads=[b_ps[1], b_H], writes=[b_B])
                    P.op("pool", lambda e: e.tensor_tensor(Y[:, 0:256], Aa[:, 0:256], Bb[:, 256:512], ALU.subtract),
                         reads=[b_A, b_B], writes=[b_Y])
                    P.op("pool", lambda e: e.tensor_tensor(Y[:, 256:512], Bb[:, 0:256], Aa[:, 256:512], ALU.add),
                         reads=[b_A, b_B], writes=[b_Y], partial=True)
                    pb2 = ps[2][:].bitcast(BF16)
                    for j in range(4):
                        P.op("pe", lambda e, pb2=pb2, j=j: e.transpose(
                            pb2[:, j * 128:(j + 1) * 128], Y[:, j * 128:(j + 1) * 128], ident[:]),
                            reads=[b_Y], writes=[b_ps[2]], partial=(j > 0))
                    P.op("act", lambda e, pb2=pb2: e.copy(Yt[:].rearrange("p a b -> p (a b)"), pb2[:, 0:512]),
                         reads=[b_ps[2]], writes=[b_Yt])
                    for j in range(4):
                        P.op("pe", lambda e, j=j: e.matmul(ps[3][:, 0:256], Yt[:, j, :], IP[:, j, :], start=(j == 0),
                                                           stop=(j == 3)),
                             reads=[b_Yt, b_IP], writes=[b_ps[3]], partial=(j > 0))
                    P.op("dve", lambda e, zin=zin, i=i: e.scalar_tensor_tensor(
                        t1[:], zin[:], biasT[:, i:i + 1], ps[3][:, 0:256], ALU.mult, ALU.add),
                        reads=[b_zin, b_bias, b_ps[3]], writes=[b_t1])
                    if o == 0:
                        P.op("pool", lambda e, xo=xo: e.tensor_tensor(z2[:], t1[:], xo[:], ALU.mult),
                             reads=[b_t1, b_xo], writes=[b_z2])
                    else:
                        P.op("pool", lambda e, xo=xo: e.tensor_tensor(t2[:], t1[:], xo[:], ALU.mult),
                             reads=[b_t1, b_xo], writes=[b_t2])
                        P.op("pool", lambda e, s=s: e.tensor_tensor(mx[s][:], t2[:], sgh[:], ALU.mult),
                             reads=[b_t2, b_sgh], writes=[b_mx[s]])
                        P.dma("sp", T["mixT"][1024 + r:1024 + r + 128, tok0:tok0 + LP], mx[s][:], b_mx[s],
                              reads=[b_mx[s]])
        P.barrier()
        P.flush()


def phase_outproj(P, nc, T, l, Wdram, xsrc, mixname, dst, final):
    import contextlib
    ps = T["ps"]
    with contextlib.ExitStack() as es:
        A_ = lambda n, shp, dt: sb(es, nc, n, shp, dt)
        Wo = A_("o_W", [128, 16, 2048], BF16)
        mix = [A_("o_mix0", [128, 16, 512], BF16), A_("o_mix1", [128, 16, 512], BF16)]
        xt = [A_("o_x0", [128, 2048], F32), A_("o_x1", [128, 2048], F32)]
        xo = [A_("o_xo0", [128, 2048], F32), A_("o_xo1", [128, 2048], F32)]
        gs = A_("o_gs", [128, 2048], F32)
        gp = A_("o_gp", [128, 2048], F32)
        tmp = [A_("o_tmp0", [128, 512], F32), A_("o_tmp1", [128, 512], F32)]
        b_W, b_mix, b_xt, b_xo, b_g, b_tmp = P.buf(), P.bufs_n(2), P.bufs_n(2), P.bufs_n(2), P.buf(), P.bufs_n(2)
        b_ps = P.bufs_n(8)
        if final:
            fg = A_("o_fg", [128, 2048], F32)
            junk = A_("o_junk", [128, 2048], F32)
            st = [A_("o_st0", [128, 4], F32), A_("o_st1", [128, 4], F32)]
            b_fg, b_junk, b_st = P.buf(), P.buf(), P.bufs_n(2)
            P.dma("sp", fg[:], T["final_norm_g"][0:1, :].partition_broadcast(128), b_fg, writes=[b_fg])
        Wv = Wdram.rearrange("(k p) c -> p k c", p=128)
        for k in range(16):
            P.dma("pool", Wo[:, k, :], Wv[:, k, :], b_W, writes=[b_W], partial=(k > 0))
        P.dma("sp", gs[:], T["modv"][l][0:1, 4096:6144].partition_broadcast(128), b_g, writes=[b_g])
        P.dma("sp", gp[:], T["modv"][l][1:2, 4096:6144].partition_broadcast(128), b_g, writes=[b_g], partial=True)
        mv = T[mixname].rearrange("(k p) t -> p k t", p=128)
        ti = 0
        ei = 0
        def mix_load(ch):
            s = ch % 2
            for q in range(4):
                P.dma("sp", mix[s][:, q * 4:(q + 1) * 4, :], mv[:, q * 4:(q + 1) * 4, ch * 512:(ch + 1) * 512],
                      b_mix[s], writes=[b_mix[s]], partial=(q > 0))

        mix_load(0)
        for ch in range(NTOK // 512):
            s = ch % 2
            for tt in range(4):
                if tt == 1 and ch + 1 < NTOK // 512:
                    mix_load(ch + 1)
                tok = ch * 512 + tt * 128
                xs = ti % 2
                ti += 1
                gg = gs if tok < LS else gp
                P.dma("sp", xt[xs][:], xsrc[tok:tok + 128, :], b_xt[xs], writes=[b_xt[xs]])
                for cbk in range(4):
                    pj = ei % 8
                    tj = ei % 2
                    ei += 1
                    for k in range(16):
                        P.op("pe", lambda e, pj=pj, k=k, s=s, tt=tt, cbk=cbk: e.matmul(
                            ps[pj][:, :], mix[s][:, k, tt * 128:(tt + 1) * 128], Wo[:, k, cbk * 512:(cbk + 1) * 512],
                            start=(k == 0), stop=(k == 15)),
                            reads=[b_mix[s], b_W], writes=[b_ps[pj]], partial=(k > 0))
                    P.op("dve", lambda e, pj=pj, tj=tj, cbk=cbk, gg=gg: e.tensor_tensor(
                        tmp[tj][:], ps[pj][:, :], gg[:, cbk * 512:(cbk + 1) * 512], ALU.mult),
                        reads=[b_ps[pj], b_g], writes=[b_tmp[tj]])
                    P.op("pool", lambda e, tj=tj, xs=xs, cbk=cbk: e.tensor_tensor(
                        xo[xs][:, cbk * 512:(cbk + 1) * 512], tmp[tj][:], xt[xs][:, cbk * 512:(cbk + 1) * 512],
                        ALU.add),
                        reads=[b_tmp[tj], b_xt[xs]], writes=[b_xo[xs]], partial=(cbk > 0))
                if final:
                    P.op("act", lambda e, xs=xs: e.activation(junk[:], xo[xs][:], AF.Square,
                                                               accum_out=st[xs][:, 0:1]),
                         reads=[b_xo[xs]], writes=[b_junk, b_st[xs]])
                    P.op("act", lambda e, xs=xs: e.activation(st[xs][:, 1:2], st[xs][:, 0:1], AF.Sqrt, scale=1.0 / D,
                                                               bias=T["epsb"][:, 0:1]),
                         reads=[b_st[xs]], writes=[b_st[xs]], partial=True)
                    P.op("dve", lambda e, xs=xs: e.reciprocal(st[xs][:, 2:3], st[xs][:, 1:2]),
                         reads=[b_st[xs]], writes=[b_st[xs]], partial=True)
                    P.op("dve", lambda e, xs=xs: e.scalar_tensor_tensor(
                        xo[xs][:], xo[xs][:], st[xs][:, 2:3], fg[:], ALU.mult, ALU.mult),
                        reads=[b_xo[xs], b_st[xs], b_fg], writes=[b_xo[xs]])
                P.dma("sp", dst[tok:tok + 128, :], xo[xs][:], b_xo[xs], reads=[b_xo[xs]])
        P.barrier()
        P.flush()


def phase_E(P, nc, T):
    import contextlib
    ps = T["ps"]
    Wv = T["c_w_in"].rearrange("(k p) c -> p k c", p=128)
    for half in range(2):
        tok_tiles = [(half * 2048 + i * 128, False) for i in range(16)] + \
                    [(LS + half * 256 + i * 128, True) for i in range(2)]
        with contextlib.ExitStack() as es0:
            hT = sb(es0, nc, "e_hT", [128, 16, 2304], BF16)
            b_hT = P.buf("hT")
            with contextlib.ExitStack() as es1:
                A_ = lambda n, shp, dt: sb(es1, nc, n, shp, dt)
                xt0 = A_("e_xt0", [128, 2048], F32); xt1 = A_("e_xt1", [128, 2048], F32)
                xt2 = A_("e_xt2", [128, 2048], F32); xt3 = A_("e_xt3", [128, 2048], F32)
                junk = A_("e_junk", [128, 2048], F32); tmp = A_("e_tmp", [128, 2048], F32)
                junkb = A_("e_junkb", [128, 2048], F32); tmpb = A_("e_tmpb", [128, 2048], F32)
                st0 = A_("e_st0", [128, 4], F32); st1 = A_("e_st1", [128, 4], F32)
                hb0 = A_("e_hb0", [128, 2048], BF16); hb1 = A_("e_hb1", [128, 2048], BF16)
                As = A_("e_As", [128, 2048], F32); shs = A_("e_shs", [128, 2048], F32)
                Ap = A_("e_Ap", [128, 2048], F32); shp = A_("e_shp", [128, 2048], F32)
                bc = {"A_s": As, "sh_s": shs, "A_p": Ap, "sh_p": shp}
                b_bc = {k: P.buf(k) for k in bc}
                load_bcast_rows(P, nc, T, 1, bc, b_bc)
                work = dict(xt=[xt0, xt1, xt2, xt3], b_xt=P.bufs_n(4), junk=junk, b_junk=P.buf(), st=[st0, st1],
                            b_st=P.bufs_n(2), tmp=tmp, b_tmp=P.buf(), hb=[hb0, hb1], b_hb=P.bufs_n(2),
                            b_pst=P.bufs_n(4), junk2=[junk, junkb], b_junk2=P.bufs_n(2), tmp2=[tmp, tmpb],
                            b_tmp2=P.bufs_n(2))
                norm_transpose_half(P, nc, T, T["x1"], tok_tiles, hT, b_hT, bc, b_bc, work)
                P.barrier()
                P.flush()
            with contextlib.ExitStack() as es2:
                A_ = lambda n, shp, dt: sb(es2, nc, n, shp, dt)
                wg = [A_("e_w0", [128, 16, 512], BF16), A_("e_w1", [128, 16, 512], BF16)]
                sf = [A_("e_sf0", [128, 512], F32), A_("e_sf1", [128, 512], F32)]
                sg = [A_("e_sg0", [128, 512], BF16), A_("e_sg1", [128, 512], BF16)]
                b_hT = P.buf("hT2")
                b_wg, b_sf, b_sg = P.bufs_n(2), P.bufs_n(2), P.bufs_n(2)
                b_ps = P.bufs_n(8)
                chunks = [(c * 512, 512, half * 2048 + c * 512) for c in range(4)] + [(2048, 256, LS + half * 256)]
                psi = 0
                ei = 0
                for g in range(8):
                    s = g % 2
                    for kq in range(16):
                        P.dma("pool", wg[s][:, kq, :], Wv[:, kq, g * 512:(g + 1) * 512], b_wg[s], writes=[b_wg[s]],
                              partial=(kq > 0))
                    for (l0, n, g0) in chunks:
                        for j in range(4):
                            pj = psi % 4
                            psi += 1
                            for k in range(16):
                                P.op("pe", lambda e, pj=pj, k=k, s=s, j=j, l0=l0, n=n: e.matmul(
                                    ps[pj][:, 0:n], wg[s][:, k, j * 128:(j + 1) * 128], hT[:, k, l0:l0 + n],
                                    start=(k == 0), stop=(k == 15)),
                                    reads=[b_hT, b_wg[s]], writes=[b_ps[pj]], partial=(k > 0))
                            row = (g * 4 + j) * 128
                            si = ei % 2
                            ei += 1
                            if row < 2048:
                                P.op("act", lambda e, pj=pj, si=si, n=n: e.copy(sf[si][:, 0:n], ps[pj][:, 0:n]),
                                     reads=[b_ps[pj]], writes=[b_sf[si]])
                                P.dma("sp", T["xbT"][row:row + 128, g0:g0 + n], sf[si][:, 0:n], b_sf[si],
                                      reads=[b_sf[si]])
                            else:
                                P.op("dve", lambda e, pj=pj, si=si, n=n: e.tensor_copy(sg[si][:, 0:n], ps[pj][:, 0:n]),
                                     reads=[b_ps[pj]], writes=[b_sg[si]])
                                P.dma("sp", T["gateT"][row - 2048:row - 2048 + 128, g0:g0 + n], sg[si][:, 0:n],
                                      b_sg[si], reads=[b_sg[si]])
                P.barrier()
                P.flush()


def phase_F(P, nc, T):
    import contextlib
    ps = T["ps"]
    seqs = [(0, LS, 0), (LS, LP, 1), (LS + LP, LP, 2)]
    with contextlib.ExitStack() as es:
        A_ = lambda n, shp, dt: sb(es, nc, n, shp, dt)
        Rs = [A_("f_R0", [128, LS], F32), A_("f_R1", [128, LS], F32)]
        Is = [A_("f_I0", [128, LS], F32), A_("f_I1", [128, LS], F32)]
        Ss = [A_("f_S0", [128, LS], F32), A_("f_S1", [128, LS], F32)]
        H_ = [A_("f_H0", [128, LS + 3], F32), A_("f_H1", [128, LS + 3], F32)]
        xc = A_("f_xc", [128, 2, LS], F32)
        xp = H_[1]
        acc = H_[0]
        xcb = A_("f_xcb", [128, 2, LS], BF16)
        gate = A_("f_gate", [128, LS], BF16)
        stage = A_("f_stage", [128, LS], BF16)
        wq = A_("f_wq", [128, 2, 2, 2, 256], BF16)
        lcw = A_("f_lcw", [128, 5, 16], F32)
        lba = A_("f_lba", [128, 2, 16], F32)
        lbx = A_("f_lbx", [128, 2, 16], F32)
        llam = A_("f_llam", [128, 2, 16], F32)
        c8 = A_("f_c8", [128, 2, 16], F32)
        stT = A_("f_stT", [128, 2, 16], F32)
        nst = A_("f_nst", [128, 2, 2, 16], F32)
        b_k = P.bufs_n(6)
        P.dma("sp", lcw[:], T["lcw"], b_k[0], writes=[b_k[0]])
        P.dma("sp", lba[:], T["lba"], b_k[1], writes=[b_k[1]])
        P.dma("sp", lbx[:], T["lbx"], b_k[2], writes=[b_k[2]])
        P.dma("sp", llam[:], T["llam"], b_k[3], writes=[b_k[3]])
        P.dma("sp", stT[:], T["stT"], b_k[4], writes=[b_k[4]])
        P.op("act", lambda e: e.activation(c8[:], llam[:], AF.Exp, scale=-1.0), reads=[b_k[3]], writes=[b_k[5]])
        P.op("act", lambda e: e.activation(c8[:], c8[:], AF.Ln, bias=1.0), reads=[b_k[5]], writes=[b_k[5]])
        P.op("dve", lambda e: e.tensor_scalar(c8[:], c8[:], -8.0, None, ALU.mult), reads=[b_k[5]], writes=[b_k[5]])
        b_lcw, b_lba, b_lbx, _, b_stT, b_c8 = b_k
        b_xc, b_xcb, b_gate, b_stage, b_wq, b_nst = [P.buf() for _ in range(6)]
        b_H = P.bufs_n(2)
        b_Rs, b_Is, b_Ss = P.bufs_n(2), P.bufs_n(2), P.bufs_n(2)
        b_xp, b_acc = b_H[1], b_H[0]
        par = 0
        b_ps = P.bufs_n(8)
        psi = 0
        for h in range(8):
            for m in range(2):
                wsrc = T["c_wa"] if m == 0 else T["c_wx"]
                for d in range(2):
                    P.dma("pool", wq[:, m, d], wsrc[d, h].rearrange("(it p) j -> p it j", p=128), b_wq,
                          writes=[b_wq], partial=(m + d > 0))
            for (tok0, L, sidx) in seqs:
                nch = max(1, L // 512)
                cw = min(512, L)
                for ct in range(2):
                    cti = h * 2 + ct
                    row = cti * 128
                    P.op("pool", lambda e: e.memset(xp[:, 0:2], 0.0), writes=[b_xp])
                    P.op("pool", lambda e, L=L: e.memset(xp[:, L + 2:L + 3], 0.0), writes=[b_xp], partial=True)
                    P.dma("sp", xp[:, 2:L + 2], T["xbT"][row:row + 128, tok0:tok0 + L], b_xp, writes=[b_xp],
                          partial=True)
                    P.op("act", lambda e, L=L, cti=cti: e.activation(
                        acc[:, 0:L], xp[:, 0:L], AF.Identity, scale=lcw[:, 0, cti:cti + 1],
                        bias=lcw[:, 4, cti:cti + 1]), reads=[b_xp, b_lcw], writes=[b_acc])
                    for j in (1, 2):
                        P.op("dve", lambda e, L=L, cti=cti, j=j: e.scalar_tensor_tensor(
                            acc[:, 0:L], xp[:, j:j + L], lcw[:, j, cti:cti + 1], acc[:, 0:L], ALU.mult, ALU.add),
                            reads=[b_xp, b_acc, b_lcw], writes=[b_acc])
                    P.op("dve", lambda e, L=L, cti=cti, ct=ct: e.scalar_tensor_tensor(
                        xc[:, ct, 0:L], xp[:, 3:3 + L], lcw[:, 3, cti:cti + 1], acc[:, 0:L], ALU.mult, ALU.add),
                        reads=[b_xp, b_acc, b_lcw], writes=[b_xc], partial=(ct > 0))
                    P.op("act", lambda e, L=L, ct=ct: e.copy(xcb[:, ct, 0:L], xc[:, ct, 0:L]),
                         reads=[b_xc], writes=[b_xcb], partial=(ct > 0))
                for jt in range(2):
                    cti = h * 2 + jt
                    row = cti * 128
                    P.dma("sp", gate[:, 0:L], T["gateT"][row:row + 128, tok0:tok0 + L], b_gate, writes=[b_gate])
                    for d in range(2):
                        par ^= 1
                        R_, I_, S_ = Rs[par], Is[par], Ss[par]
                        b_R, b_I, b_S = b_Rs[par], b_Is[par], b_Ss[par]
                        for (m, dstT, b_dst, bias_t) in ((0, R_, b_R, lba), (1, I_, b_I, lbx)):
                            for c in range(nch):
                                pj = psi % 4
                                psi += 1
                                for it_ in range(2):
                                    P.op("pe", lambda e, pj=pj, m=m, d=d, it_=it_, jt=jt, c=c, cw=cw: e.matmul(
                                        ps[pj][:, 0:cw], wq[:, m, d, it_, jt * 128:(jt + 1) * 128],
                                        xcb[:, it_, c * cw:(c + 1) * cw], start=(it_ == 0), stop=(it_ == 1)),
                                        reads=[b_wq, b_xcb], writes=[b_ps[pj]], partial=(it_ > 0))
                                P.op("act", lambda e, pj=pj, dstT=dstT, c=c, bias_t=bias_t, d=d, cti=cti, cw=cw: e.activation(
                                    dstT[:, c * cw:(c + 1) * cw], ps[pj][:, 0:cw], AF.Sigmoid,
                                    bias=bias_t[:, d, cti:cti + 1]),
                                    reads=[b_ps[pj], b_lba, b_lbx], writes=[b_dst], partial=(c > 0))
                        P.op("pool", lambda e, L=L, jt=jt, I_=I_: e.tensor_tensor(I_[:, 0:L], I_[:, 0:L], xc[:, jt, 0:L],
                                                                                 ALU.mult),
                             reads=[b_I, b_xc], writes=[b_I])
                        P.op("act", lambda e, L=L, d=d, cti=cti, R_=R_: e.activation(
                            R_[:, 0:L], R_[:, 0:L], AF.Exp, scale=c8[:, d, cti:cti + 1]),
                            reads=[b_R, b_c8], writes=[b_R])
                        P.op("dve", lambda e, L=L, R_=R_, S_=S_: e.tensor_tensor(S_[:, 0:L], R_[:, 0:L], R_[:, 0:L], ALU.mult),
                             reads=[b_R], writes=[b_S])
                        P.op("act", lambda e, L=L, S_=S_: e.activation(S_[:, 0:L], S_[:, 0:L], AF.Sqrt, scale=-1.0, bias=1.0),
                             reads=[b_S], writes=[b_S])
                        P.op("dve", lambda e, L=L, I_=I_, S_=S_: e.tensor_tensor(I_[:, 0:L], I_[:, 0:L], S_[:, 0:L], ALU.mult),
                             reads=[b_I, b_S], writes=[b_I])
                        init = stT[:, d, cti:cti + 1] if sidx == 0 else 0.0
                        if d == 0:
                            P.op("dve", lambda e, L=L, init=init, R_=R_, I_=I_: e.tensor_tensor_scan(
                                H_[0][:, 0:L], R_[:, 0:L], I_[:, 0:L], init, ALU.mult, ALU.add),
                                reads=[b_R, b_I, b_stT], writes=[b_H[0]])
                        else:
                            P.op("dve", lambda e, L=L, init=init, R_=R_, I_=I_: e.tensor_tensor_scan(
                                rev_ap(H_[1][:, 0:L]), rev_ap(R_[:, 0:L]), rev_ap(I_[:, 0:L]), init, ALU.mult,
                                ALU.add),
                                reads=[b_R, b_I, b_stT], writes=[b_H[1]])
                        if sidx > 0:
                            col = L - 1 if d == 0 else 0
                            P.op("act", lambda e, d=d, col=col, sidx=sidx, cti=cti: e.copy(
                                nst[:, sidx - 1, d, cti:cti + 1], H_[d][:, col:col + 1]),
                                reads=[b_H[d]], writes=[b_nst], partial=True)
                    P.op("pool", lambda e, L=L: e.tensor_tensor(H_[0][:, 0:L], H_[0][:, 0:L], H_[1][:, 0:L], ALU.add),
                         reads=[b_H[0], b_H[1]], writes=[b_H[0]])
                    P.op("act", lambda e, L=L, S_=S_: e.activation(S_[:, 0:L], gate[:, 0:L], AF.Silu),
                         reads=[b_gate], writes=[b_S])
                    P.op("dve", lambda e, L=L, S_=S_: e.tensor_tensor(stage[:, 0:L], H_[0][:, 0:L], S_[:, 0:L], ALU.mult),
                         reads=[b_H[0], b_S], writes=[b_stage])
                    P.dma("sp", T["mix1T"][row:row + 128, tok0:tok0 + L], stage[:, 0:L], b_stage, reads=[b_stage])
        P.dma("sp", T["ns"], nst[:].rearrange("p a b c -> p (a b c)"), b_nst, reads=[b_nst])
        P.barrier()
        P.flush()


def _bf16(a):
    return np.asarray(a, dtype=np.float32).astype(ml_dtypes.bfloat16)


def _fft_consts():
    N = 2 * LS
    n1 = np.arange(128)[:, None]
    k1 = np.arange(128)[None, :]
    ang = 2 * np.pi * n1 * (k1 + 0.5) / 128
    F1 = np.concatenate([np.cos(ang), -np.sin(ang)], 1)
    n2 = np.arange(64)[:, None, None]
    k1_ = np.arange(128)[None, :, None]
    k2 = np.arange(32)[None, None, :]
    ang = 2 * np.pi * n2 * (k1_ + 128 * k2 + 0.5) / N
    gr, gi = np.cos(ang), -np.sin(ang)
    G = np.zeros((128, 128, 64))
    G[0:64, :, 0:32] = gr
    G[64:128, :, 0:32] = -gi
    G[0:64, :, 32:64] = gi
    G[64:128, :, 32:64] = gr
    Grr = np.concatenate([G[:, :, 0:32], G[:, :, 0:32]], 2)
    Gii = np.concatenate([G[:, :, 32:64], G[:, :, 32:64]], 2)
    k2 = np.arange(32)[:, None]
    n2 = np.arange(64)[None, :]
    ang = 2 * np.pi * n2 * k2 / 64
    mr, mi = np.cos(ang), np.sin(ang)
    M1 = np.zeros((64, 128))
    M1[0:32, 0:64] = mr
    M1[32:64, 0:64] = -mi
    M1[0:32, 64:128] = mi
    M1[32:64, 64:128] = mr
    M1p = np.concatenate([M1[32:64], -M1[0:32]], 0)
    k1 = np.arange(128)[:, None, None]
    n2 = np.arange(64)[None, :, None]
    n1 = np.arange(64)[None, None, :]
    ang = 2 * np.pi * (k1 + 0.5) * (n1 / 128 + n2 / N)
    T2r = (2.0 / N) * np.cos(ang)
    T2in = -(2.0 / N) * np.sin(ang)
    Np = 2 * LP
    n = np.arange(Np)[:, None]
    k = np.arange(LP)[None, :]
    ang = 2 * np.pi * n * (k + 0.5) / Np
    FPm = np.concatenate([np.cos(ang), -np.sin(ang)], 1)
    FP = FPm.reshape(4, 128, 512).transpose(1, 0, 2)
    k = np.arange(LP)[:, None]
    n = np.arange(LP)[None, :]
    ang = 2 * np.pi * n * (k + 0.5) / Np
    IPm = np.concatenate([(2.0 / Np) * np.cos(ang), -(2.0 / Np) * np.sin(ang)], 0)
    IP = IPm.reshape(4, 128, 256).transpose(1, 0, 2)
    return dict(F1=F1, G=G, Grr=Grr, Gii=Gii, M1=M1, M1p=M1p, T2r=T2r, T2in=T2in, FP=FP, IP=IP)


def make_consts():
    c = {}
    c["ident"] = _bf16(np.eye(128))
    pos = np.arange(LS)
    row = (pos // 64).astype(np.float64)
    col = (pos % 64).astype(np.float64)
    nf = 16
    inv = 10000.0 ** (-np.arange(nf, dtype=np.float64) / nf)
    cos = np.zeros((64, LS))
    sin = np.zeros((64, LS))
    for d in range(64):
        halfi = d // 32
        w = d % 32
        f = w % 16
        p = row if halfi == 0 else col
        ang = p * inv[f]
        cos[d] = np.cos(ang)
        sin[d] = -np.sin(ang) if w < 16 else np.sin(ang)
    c["rope_cos"] = np.tile(cos, (2, 1)).astype(np.float32)
    c["rope_sin"] = np.tile(sin, (2, 1)).astype(np.float32)
    Pm = np.zeros((128, 128))
    for m in range(128):
        hh = m // 64
        d = m % 64
        w = d % 32
        partner = d + 16 if w < 16 else d - 16
        Pm[hh * 64 + partner, m] = 1.0
    c["ropeP"] = _bf16(Pm)
    c["epsb"] = np.full((128, 1), EPS, np.float32)
    for nm, Lf in (("S", LS), ("P", LP)):
        t = (np.arange(Lf, dtype=np.float32) / np.float32(Lf)).astype(np.float64)
        freqs = np.linspace(1e-4, 15.0, 16).astype(np.float32).astype(np.float64)
        ang = 2.0 * np.pi * t[:, None] * freqs[None, :]
        z = np.concatenate([t[:, None], np.cos(ang), -np.sin(ang)], 1)
        c["zemb" + nm] = np.ascontiguousarray(z.T).astype(np.float32)
        c["tpos" + nm] = np.ascontiguousarray(np.tile(t[None, :], (128, 1))).astype(np.float32)
    c.update({k: _bf16(v) for k, v in _fft_consts().items()})
    si = np.arange(128)[:, None]
    qi = np.arange(128)[None, :]
    c["amask"] = _bf16(np.concatenate([(si >= qi), np.ones((128, 128)), (si <= qi)], 1).astype(np.float32))
    return c


CONST_SPECS = {
    "ident": ([128, 128], BF16), "rope_cos": ([128, LS], F32), "rope_sin": ([128, LS], F32),
    "ropeP": ([128, 128], BF16), "epsb": ([128, 1], F32), "amask": ([128, 384], BF16),
    "zembS": ([33, LS], F32), "zembP": ([33, LP], F32), "tposS": ([128, LS], F32), "tposP": ([128, LP], F32),
    "F1": ([128, 256], BF16), "G": ([128, 128, 64], BF16), "Grr": ([128, 128, 64], BF16),
    "Gii": ([128, 128, 64], BF16), "M1": ([64, 128], BF16), "M1p": ([64, 128], BF16),
    "T2r": ([128, 64, 64], BF16), "T2in": ([128, 64, 64], BF16),
    "FP": ([128, 4, 512], BF16), "IP": ([128, 4, 256], BF16),
}

IN_SPECS = {
    "x": [NTOK, D], "ck": [512, 128], "cv": [512, 128], "st": [2, D], "cvecT": [128, 32],
    "mod_w": [2, D, 3 * D], "mod_b": [2, 3 * D], "norm_g": [2, D], "final_norm_g": [1, D],
    "a_w_in": [D, 6400], "a_w_out": [D, D], "a_sink": [1, 16],
    "hsw": [128, 4, 24], "hy_w1": [33, 64], "hy_w2": [64, 64], "hy_bb": [64, 2], "hy_w3": [64, 4096],
    "hy_decT": [128, 32], "hy_biasT": [128, 16], "hy_bias": [2, 1024],
    "c_w_in": [D, 2 * D], "c_w_out": [D, D], "c_wa": [2, 8, 256, 256], "c_wx": [2, 8, 256, 256],
    "lcw": [128, 5, 16], "lba": [128, 2, 16], "lbx": [128, 2, 16], "llam": [128, 2, 16], "stT": [128, 2, 16],
}

SCRATCH = {
    "modv": ([2, 2, 3 * D], F32),
    "qT": ([1024, NTOK], BF16), "kT": ([2, 128, NTOK], BF16), "gaT": ([1024, NTOK], BF16),
    "hyT": ([3072, NTOK], BF16), "ghT": ([1024, NTOK], BF16), "vtok": ([NTOK, 128], BF16),
    "mixT": ([2048, NTOK], BF16),
    "ucT": ([3072, NTOK], BF16), "tapsS": ([2, 1024, 2 * LS], BF16), "tapsP": ([2, 1024, 2 * LP], BF16),
    "Hs": ([2, 16, 2, 64, 64 * 128], BF16),
    "x1": ([NTOK, D], F32), "xbT": ([D, NTOK], F32), "gateT": ([D, NTOK], BF16), "mix1T": ([D, NTOK], BF16),
}

OUT_SPECS = {"y": [NTOK, D], "nk": [512, 128], "nv": [512, 128], "ns": [128, 64]}


def build_program(debug_scratch=(), stop_after=None, skip=(), ext_in=()):
    nc = bass.Bass("TRN2", target_bir_lowering=False)
    T = {}
    for name, shp in IN_SPECS.items():
        T[name] = nc.dram_tensor(name, shp, F32, kind="ExternalInput").ap()
    for name, (shp, dt) in CONST_SPECS.items():
        T["d_" + name] = nc.dram_tensor("c_" + name, shp, dt, kind="ExternalInput").ap()
    for name, shp in OUT_SPECS.items():
        T[name] = nc.dram_tensor(name, shp, F32, kind="ExternalOutput").ap()
    for name, (shp, dt) in SCRATCH.items():
        kind = "ExternalOutput" if name in debug_scratch else ("ExternalInput" if name in ext_in else "Internal")
        T[name] = nc.dram_tensor("s_" + name, shp, dt, kind=kind).ap()
    T["rope_cos"] = T["d_rope_cos"]
    T["rope_sin"] = T["d_rope_sin"]
    import contextlib
    with contextlib.ExitStack() as es:
        sems = [es.enter_context(nc.semaphore("sem%d" % i)) for i in range(60)]
        T["ps"] = [es.enter_context(nc.psum_tensor("ps%d" % i, [128, 512], F32)) for i in range(8)]
        ident = es.enter_context(nc.sbuf_tensor("ident", [128, 128], BF16))
        ropeP = es.enter_context(nc.sbuf_tensor("ropeP", [128, 128], BF16))
        epsb = es.enter_context(nc.sbuf_tensor("epsb", [128, 1], F32))
        T["ident"], T["ropeP"], T["epsb"] = ident, ropeP, epsb
        P = Prog(nc, sems)
        b_c = P.bufs_n(3)
        P.dma("sp", ident[:], T["d_ident"], b_c[0], writes=[b_c[0]])
        P.dma("sp", ropeP[:], T["d_ropeP"], b_c[1], writes=[b_c[1]])
        P.dma("sp", epsb[:], T["d_epsb"], b_c[2], writes=[b_c[2]])
        P.barrier()
        if "M" not in skip:
            phase_M(P, nc, T)
        if stop_after != "M":
            if "A" not in skip:
                phase_A(P, nc, T)
        if stop_after not in ("M", "A") and "B" not in skip:
            phase_B(P, nc, T)
        if stop_after not in ("M", "A", "B"):
            if "C0" not in skip:
                phase_C0(P, nc, T)
            if "CF" not in skip:
                phase_CF(P, nc, T, LP, T["tapsP"], T["d_zembP"], T["d_tposP"])
                phase_CF(P, nc, T, LS, T["tapsS"], T["d_zembS"], T["d_tposS"])
        if stop_after not in ("M", "A", "B", "CF"):
            if "CT" not in skip:
                phase_CT(P, nc, T)
            if "CS" not in skip:
                phase_CS(P, nc, T)
            if "CP" not in skip:
                phase_CP(P, nc, T)
        if stop_after not in ("M", "A", "B", "CF", "C"):
            if "D" not in skip:
                phase_outproj(P, nc, T, 0, T["a_w_out"], T["x"], "mixT", T["x1"], False)
        if stop_after not in ("M", "A", "B", "CF", "C", "D"):
            if "E" not in skip:
                phase_E(P, nc, T)
        if stop_after not in ("M", "A", "B", "CF", "C", "D", "E"):
            if "F" not in skip:
                phase_F(P, nc, T)
        if stop_after not in ("M", "A", "B", "CF", "C", "D", "E", "F"):
            if "G" not in skip:
                phase_outproj(P, nc, T, 1, T["c_w_out"], T["x1"], "mix1T", T["y"], True)
        P.barrier()
        P.flush()
    return nc


def make_in_maps(inputs):
    consts = make_consts()
    f = lambda a: np.ascontiguousarray(np.asarray(a, dtype=np.float32))
    x_prompt, x_sample = f(inputs["x_prompt"]), f(inputs["x_sample"])
    ck, cv = f(inputs["cache_k"]), f(inputs["cache_v"])
    st = f(inputs["state_lru"])
    c, c_ctx = f(inputs["c"]), f(inputs["c_ctx"])
    shared = {
        "mod_w": f(inputs["mod_w"]), "mod_b": f(inputs["mod_b"]), "norm_g": f(inputs["norm_g"]),
        "final_norm_g": f(inputs["final_norm_g"]).reshape(1, D),
        "a_w_in": f(inputs["a_w_in"])[0], "a_w_out": f(inputs["a_w_out"])[0], "a_sink": f(inputs["a_sink"]),
        "hsw": np.ascontiguousarray(np.concatenate([f(inputs["hy_short_w"])[0], f(inputs["hy_short_b"])[0][None]], 0)
                                    .reshape(4, 24, 128).transpose(2, 0, 1)),
        "hy_w1": f(inputs["hy_w1"])[0], "hy_w2": f(inputs["hy_w2"])[0],
        "hy_bb": np.ascontiguousarray(np.stack([f(inputs["hy_b1"])[0], f(inputs["hy_b2"])[0]], 1)),
        "hy_w3": f(inputs["hy_w3"])[0],
        "hy_decT": np.ascontiguousarray(f(inputs["hy_decay"])[0].reshape(32, 128).T),
        "hy_biasT": np.ascontiguousarray(f(inputs["hy_bias"])[0].reshape(16, 128).T),
        "hy_bias": f(inputs["hy_bias"])[0],
        "c_w_in": f(inputs["c_w_in"])[0], "c_w_out": f(inputs["c_w_out"])[0],
        "c_wa": f(inputs["c_wa"])[0], "c_wx": f(inputs["c_wx"])[0],
        "lcw": np.ascontiguousarray(np.concatenate([f(inputs["c_conv_w"])[0], f(inputs["c_conv_b"])[0][None]], 0)
                                    .reshape(5, 16, 128).transpose(2, 0, 1)),
        "lba": np.ascontiguousarray(f(inputs["c_ba"])[0].reshape(2, 16, 128).transpose(2, 0, 1)),
        "lbx": np.ascontiguousarray(f(inputs["c_bx"])[0].reshape(2, 16, 128).transpose(2, 0, 1)),
        "llam": np.ascontiguousarray(f(inputs["c_lambda"])[0].reshape(2, 16, 128).transpose(2, 0, 1)),
    }
    for k, v in consts.items():
        shared["c_" + k] = v
    maps = []
    for i in range(NCORES):
        m = dict(shared)
        m["x"] = np.ascontiguousarray(np.concatenate([x_sample[i], x_prompt[2 * i], x_prompt[2 * i + 1]], 0))
        m["ck"] = np.ascontiguousarray(ck[i, 0].reshape(512, 128))
        m["cv"] = np.ascontiguousarray(cv[i, 0].reshape(512, 128))
        m["st"] = np.ascontiguousarray(st[i, 0])
        m["stT"] = np.ascontiguousarray(st[i, 0].reshape(2, 16, 128).transpose(2, 0, 1))
        cvec = np.stack([c[i], c_ctx], 0)
        m["cvecT"] = np.ascontiguousarray(cvec.reshape(2, 16, 128).transpose(2, 1, 0).reshape(128, 32))
        maps.append(m)
    return maps


def kernel(**inputs):
    nc = build_program()
    maps = make_in_maps(inputs)
    res = run_bass_kernel_spmd(nc, maps, core_ids=list(range(NCORES)))
    R = res.results
    y_s = np.stack([R[i]["y"][:LS] for i in range(NCORES)], 0)
    y_p = np.concatenate([R[i]["y"][LS:].reshape(2, LP, D) for i in range(NCORES)], 0)
    nk = np.concatenate([R[i]["nk"].reshape(2, 1, LP, 2, 64) for i in range(NCORES)], 0)
    nv = np.concatenate([R[i]["nv"].reshape(2, 1, LP, 2, 64) for i in range(NCORES)], 0)
    ns = np.concatenate([R[i]["ns"].reshape(128, 2, 2, 16).transpose(1, 2, 3, 0).reshape(2, 1, 2, D)
                         for i in range(NCORES)], 0)
    return (y_p.astype(np.float32), y_s.astype(np.float32), nk.astype(np.float32), nv.astype(np.float32),
            ns.astype(np.float32))
```

```python
import numpy as np
import ml_dtypes
import concourse.bass as bass
import concourse.mybir as mybir
from concourse.bass_utils import run_bass_kernel_spmd

F32, BF16 = mybir.dt.float32, mybir.dt.bfloat16
AF = mybir.ActivationFunctionType
ALU = mybir.AluOpType
AX = mybir.AxisListType

D = 2048
LS = 4096
LP = 256
NPS = 2
NTOK = LS + NPS * LP
EPS = 1e-6
NCORES = 8
DBG = {}


class Sem:
    def __init__(self, h):
        self.h = h
        self.n = 0


class Buf:
    def __init__(self, name=""):
        self.w = []
        self.r = []
        self.gen_r = []
        self.name = name
        self.sem = None


class Prog:
    ENG = ("pe", "act", "dve", "pool", "sp")
    COMPUTE = ("pe", "act", "dve", "pool")

    def __init__(self, nc, handles):
        self.nc = nc
        self.q = {e: [] for e in self.ENG}
        hs = list(handles)
        self.csem = {e: Sem(hs.pop()) for e in self.COMPUTE}
        self.bar = Sem(hs.pop())
        self.dpool = [Sem(h) for h in hs]
        nsw = len(self.dpool) // 3
        self.dfree = {"pool": self.dpool[:nsw], "sp": self.dpool[nsw:]}
        self.waited = {e: {} for e in self.ENG}
        self.pending = []
        self.bufs = []
        self.nops = {e: 0 for e in self.COMPUTE}
        self.entries = {e: {} for e in self.COMPUTE}
        self.sig_idx = {e: [] for e in self.COMPUTE}
        self.sig_cnt = {e: [] for e in self.COMPUTE}

    def buf(self, name=""):
        b = Buf(name)
        self.bufs.append(b)
        return b

    def bufs_n(self, n, name=""):
        return [self.buf(name + str(i)) for i in range(n)]

    def _resolve(self, tok):
        import bisect
        _, eng, idx = tok
        si = self.sig_idx[eng]
        p = bisect.bisect_left(si, idx)
        if p < len(si):
            return self.csem[eng], self.sig_cnt[eng][p]
        ent = self.entries[eng][idx]
        sem = self.csem[eng]
        sem.n += 1
        ent[1] = sem.h
        si.append(idx)
        self.sig_cnt[eng].append(sem.n)
        return sem, sem.n

    def _wait(self, eng, tok):
        if tok[0] == "op":
            if tok[1] == eng and eng == "pe":
                return
            sem, tgt = self._resolve(tok)
        else:
            sem, tgt, _ = tok
        if self.waited[eng].get(id(sem), 0) >= tgt:
            return
        self.waited[eng][id(sem)] = tgt
        self.q[eng].append([lambda e, h=sem.h, t=tgt: e.wait_ge(h, t), None])

    def _wait_many(self, eng, toks):
        best = {}
        for t in toks:
            if t[0] == "op":
                k = ("op", t[1])
                if k not in best or best[k][2] < t[2]:
                    best[k] = t
            else:
                k = id(t[0])
                if k not in best or best[k][1] < t[1]:
                    best[k] = t
        for t in best.values():
            self._wait(eng, t)

    def _hazards(self, eng, reads, writes, partial):
        toks = []
        for b in reads:
            toks += b.w
        for b in writes:
            if partial == "nowaw":
                if b.r:
                    b.gen_r = b.r
                    b.r = []
                    b.w = []
                toks += b.gen_r
            else:
                toks += b.w
                toks += b.r
                toks += b.gen_r
        self._wait_many(eng, toks)

    def _commit(self, tok, reads, writes, partial):
        for b in reads:
            b.r.append(tok)
        for b in writes:
            if partial:
                b.w.append(tok)
                if partial != "nowaw":
                    b.r = []
            else:
                b.w = [tok]
                b.r = []
                b.gen_r = []

    def op(self, eng, fn, reads=(), writes=(), partial=False):
        self._hazards(eng, reads, writes, partial)
        idx = self.nops[eng]
        self.nops[eng] += 1
        ent = [fn, None]
        self.entries[eng][idx] = ent
        self.q[eng].append(ent)
        tok = ("op", eng, idx)
        self._commit(tok, reads, writes, partial)
        return tok

    def dma(self, eng, out, in_, sbuf_buf, reads=(), writes=(), partial=False):
        self._hazards(eng, reads, writes, partial)
        if sbuf_buf.sem is None:
            sbuf_buf.sem = {}
        if eng not in sbuf_buf.sem:
            sbuf_buf.sem[eng] = self.dfree[eng].pop()
        sem = sbuf_buf.sem[eng]
        sem.n += 16
        tok = (sem, sem.n, "dma")
        self.q[eng].append([lambda e, o=out, i=in_: e.dma_start(out=o, in_=i), sem.h, 16])
        self._commit(tok, reads, writes, partial)
        self.pending.append(tok)
        return tok

    def barrier(self):
        for e in self.COMPUTE:
            if self.nops[e] > 0:
                self._wait("sp", ("op", e, self.nops[e] - 1))
        self._wait_many("sp", self.pending)
        self.pending = []
        self.bar.n += 1
        k = self.bar.n
        self.q["sp"].append([lambda e, h=self.bar.h: e.sem_inc(h, 1), None])
        for e in self.COMPUTE:
            self._wait(e, (self.bar, k, "bar"))
        for b in self.bufs:
            if b.sem is not None:
                for en, sm in b.sem.items():
                    self.dfree[en].append(sm)
                b.sem = None
            b.w = []
            b.r = []
            b.gen_r = []
        self.bufs = []

    def flush(self):
        nc = self.nc
        q = self.q

        def run(e, lst):
            for ent in lst:
                ins = ent[0](e)
                if ent[1] is not None:
                    ins.then_inc(ent[1], ent[2] if len(ent) > 2 else 1)

        with nc.Block() as blk:
            @blk.tensor
            def _(e):
                run(e, q["pe"])

            @blk.scalar
            def _(e):
                run(e, q["act"])

            @blk.vector
            def _(e):
                run(e, q["dve"])

            @blk.gpsimd
            def _(e):
                run(e, q["pool"])

            @blk.sync
            def _(e):
                run(e, q["sp"])
        self.q = {e: [] for e in self.ENG}
        for e in self.COMPUTE:
            self.entries[e] = {}


_UID = [0]


def sb(es, nc, name, shape, dt):
    _UID[0] += 1
    return es.enter_context(nc.sbuf_tensor("%s_%d" % (name, _UID[0]), shape, dt))


def rev_ap(ap):
    a = [list(p) for p in ap.ap]
    step, cnt = a[-1]
    off = ap.offset + step * (cnt - 1)
    a[-1] = [-step, cnt]
    return bass.AP(ap.tensor, off, a)


def phase_M(P, nc, T):
    with (
        nc.sbuf_tensor("m_cT", [128, 32], F32) as cT,
        nc.sbuf_tensor("m_sg", [128, 32], F32) as sg,
        nc.sbuf_tensor("m_sT", [128, 32], BF16) as sT,
        nc.sbuf_tensor("m_w0", [128, 3072], BF16) as w0,
        nc.sbuf_tensor("m_w1", [128, 3072], BF16) as w1,
        nc.sbuf_tensor("m_mrow", [2, 6144], F32) as mrow,
        nc.sbuf_tensor("m_brow", [2, 6144], F32) as brow,
        nc.sbuf_tensor("m_ng", [2, 2048], F32) as ng,
        nc.sbuf_tensor("m_orow", [2, 6144], F32) as orow,
    ):
        ps = T["ps"]
        b_cT, b_sT = P.buf(), P.buf()
        b_w = P.bufs_n(2)
        wt = [w0, w1]
        b_ps = P.bufs_n(6)
        b_mrow, b_brow, b_ng, b_orow = P.buf(), P.buf(), P.buf(), P.buf()
        P.dma("sp", cT[:], T["cvecT"], b_cT, writes=[b_cT])
        P.op("act", lambda e: e.activation(sg[:], cT[:], AF.Sigmoid), reads=[b_cT], writes=[b_sT])
        P.op("dve", lambda e: e.tensor_tensor(sT[:], sg[:], cT[:], ALU.mult), reads=[b_cT, b_sT], writes=[b_sT])
        cnt = 0
        for l in range(2):
            P.dma("sp", brow[:], T["mod_b"][l:l + 1, :].partition_broadcast(2), b_brow, writes=[b_brow])
            P.dma("sp", ng[:], T["norm_g"][l:l + 1, :].partition_broadcast(2), b_ng, writes=[b_ng])
            for hf in range(2):
                for k in range(16):
                    s = cnt % 2
                    cnt += 1
                    P.dma("pool", wt[s][:], T["mod_w"][l, k * 128:(k + 1) * 128, hf * 3072:(hf + 1) * 3072],
                          b_w[s], writes=[b_w[s]])
                    for j in range(6):
                        P.op("pe", lambda e, s=s, j=j, k=k: e.matmul(
                            ps[j][0:2, :], sT[:, 2 * k:2 * k + 2], wt[s][:, j * 512:(j + 1) * 512],
                            start=(k == 0), stop=(k == 15)),
                            reads=[b_w[s], b_sT], writes=[b_ps[j]], partial=(k > 0))
                for j in range(6):
                    c0 = hf * 3072 + j * 512
                    P.op("dve", lambda e, j=j, c0=c0: e.tensor_tensor(
                        mrow[:, c0:c0 + 512], ps[j][0:2, :], brow[:, c0:c0 + 512], ALU.add),
                        reads=[b_ps[j], b_brow], writes=[b_mrow], partial=True)
            P.op("dve", lambda e: e.scalar_tensor_tensor(
                orow[:, 0:2048], mrow[:, 2048:4096], 1.0, ng[:], ALU.add, ALU.mult),
                reads=[b_mrow, b_ng], writes=[b_orow])
            P.op("dve", lambda e: e.tensor_copy(orow[:, 2048:4096], mrow[:, 0:2048]),
                 reads=[b_mrow], writes=[b_orow], partial=True)
            P.op("dve", lambda e: e.tensor_copy(orow[:, 4096:6144], mrow[:, 4096:6144]),
                 reads=[b_mrow], writes=[b_orow], partial=True)
            P.dma("sp", T["modv"][l], orow[:], b_orow, reads=[b_orow])
        P.barrier()
        P.flush()


def load_bcast_rows(P, nc, T, l, tiles, bufs):
    modv = T["modv"]
    for key, (j, r) in (("A_s", (0, 0)), ("sh_s", (1, 0)), ("A_p", (0, 1)), ("sh_p", (1, 1))):
        P.dma("sp", tiles[key][:], modv[l][r:r + 1, j * 2048:(j + 1) * 2048].partition_broadcast(128),
              bufs[key], writes=[bufs[key]])


def norm_transpose_half(P, nc, T, xsrc, tok_tiles, hT, b_hT, bc, b_bc, work):
    ps = T["ps"]
    ident = T["ident"]
    xt, b_xt = work["xt"], work["b_xt"]
    st, b_st = work["st"], work["b_st"]
    hb, b_hb = work["hb"], work["b_hb"]
    b_pst = work["b_pst"]

    def front(i):
        toff, isp = tok_tiles[i]
        s = i % 2
        x4 = i % len(xt)
        P.dma("sp", xt[x4][:], xsrc[toff:toff + 128, :], b_xt[x4], writes=[b_xt[x4]])
        jk, bjk = work["junk2"][s], work["b_junk2"][s]
        P.op("act", lambda e, s=s, jk=jk, x4=x4: e.activation(jk[:], xt[x4][:], AF.Square, accum_out=st[s][:, 0:1]),
             reads=[b_xt[x4]], writes=[bjk, b_st[s]])
        P.op("act", lambda e, s=s: e.activation(st[s][:, 1:2], st[s][:, 0:1], AF.Sqrt, scale=1.0 / D,
                                                 bias=T["epsb"][:, 0:1]),
             reads=[b_st[s]], writes=[b_st[s]], partial=True)
        P.op("dve", lambda e, s=s: e.reciprocal(st[s][:, 2:3], st[s][:, 1:2]), reads=[b_st[s]], writes=[b_st[s]],
             partial=True)
        A = bc["A_p" if isp else "A_s"]
        SH = bc["sh_p" if isp else "sh_s"]
        bA = b_bc["A_p" if isp else "A_s"]
        bS = b_bc["sh_p" if isp else "sh_s"]
        tm, btm = work["tmp2"][s], work["b_tmp2"][s]
        P.op("dve", lambda e, s=s, A=A, tm=tm, x4=x4: e.scalar_tensor_tensor(tm[:], xt[x4][:], st[s][:, 2:3], A[:],
                                                                             ALU.mult, ALU.mult),
             reads=[b_xt[x4], b_st[s], bA], writes=[btm])
        P.op("pool", lambda e, s=s, SH=SH, tm=tm: e.tensor_tensor(hb[s][:], tm[:], SH[:], ALU.add),
             reads=[btm, bS], writes=[b_hb[s]])

    def back(i):
        s = i % 2
        for half in range(2):
            pb = ps[2 * s + half]
            bp = b_pst[2 * s + half]
            pbv = pb[:].bitcast(BF16)
            for kk in range(8):
                k = half * 8 + kk
                P.op("pe", lambda e, pbv=pbv, kk=kk, k=k, s=s: e.transpose(
                    pbv[:, kk * 128:(kk + 1) * 128], hb[s][:, k * 128:(k + 1) * 128], ident[:]),
                    reads=[b_hb[s]], writes=[bp], partial=(kk > 0))
            dst = hT[:, half * 8:(half + 1) * 8, i * 128:(i + 1) * 128]
            src = pbv.rearrange("p (k t) -> p k t", k=8)
            if half == 0:
                P.op("act", lambda e, dst=dst, src=src: e.copy(dst, src), reads=[bp], writes=[b_hT], partial="nowaw")
            else:
                P.op("dve", lambda e, dst=dst, src=src: e.tensor_copy(dst, src), reads=[bp], writes=[b_hT],
                     partial="nowaw")

    n = len(tok_tiles)
    front(0)
    for i in range(n):
        if i + 1 < n:
            front(i + 1)
        back(i)


def phase_A(P, nc, T, debug=False):
    ps = T["ps"]
    W = T["a_w_in"]
    groups = []
    for g in range(2):
        groups.append(("q", [("qT", (g * 4 + j) * 128, (g * 4 + j) * 128) for j in range(4)]))
    groups.append(("k", [("kT0", 0, 1024), ("kT1", 0, 1088)]))
    for g in range(2):
        groups.append(("ga", [("gaT", (g * 4 + j) * 128, 1280 + (g * 4 + j) * 128) for j in range(4)]))
    for g in range(6):
        groups.append(("hy", [("hyT", (g * 4 + j) * 128, 2304 + (g * 4 + j) * 128) for j in range(4)]))
    for g in range(2):
        groups.append(("gh", [("ghT", (g * 4 + j) * 128, 5376 + (g * 4 + j) * 128) for j in range(4)]))

    Wv = W.rearrange("(k p) c -> p k c", p=128)
    for half in range(DBG.get('halves', 2)):
        tok_tiles = [(half * 2048 + i * 128, False) for i in range(16)] + \
                    [(LS + half * 256 + i * 128, True) for i in range(2)]
        import contextlib
        with contextlib.ExitStack() as es0:
            hT = sb(es0, nc, "a_hT", [128, 16, 2304], BF16)
            b_hT = P.buf("hT")
            with contextlib.ExitStack() as es1:
                A_ = lambda n, shp, dt: sb(es1, nc, n, shp, dt)
                xt0 = A_("a_xt0", [128, 2048], F32); xt1 = A_("a_xt1", [128, 2048], F32)
                xt2 = A_("a_xt2", [128, 2048], F32); xt3 = A_("a_xt3", [128, 2048], F32)
                junk = A_("a_junk", [128, 2048], F32); tmp = A_("a_tmp", [128, 2048], F32)
                junkb = A_("a_junkb", [128, 2048], F32); tmpb = A_("a_tmpb", [128, 2048], F32)
                st0 = A_("a_st0", [128, 4], F32); st1 = A_("a_st1", [128, 4], F32)
                hb0 = A_("a_hb0", [128, 2048], BF16); hb1 = A_("a_hb1", [128, 2048], BF16)
                As = A_("a_As", [128, 2048], F32); shs = A_("a_shs", [128, 2048], F32)
                Ap = A_("a_Ap", [128, 2048], F32); shp = A_("a_shp", [128, 2048], F32)
                bc = {"A_s": As, "sh_s": shs, "A_p": Ap, "sh_p": shp}
                b_bc = {k: P.buf(k) for k in bc}
                load_bcast_rows(P, nc, T, 0, bc, b_bc)
                work = dict(xt=[xt0, xt1, xt2, xt3], b_xt=P.bufs_n(4), junk=junk, b_junk=P.buf(), st=[st0, st1],
                            b_st=P.bufs_n(2), tmp=tmp, b_tmp=P.buf(), hb=[hb0, hb1], b_hb=P.bufs_n(2),
                            b_pst=P.bufs_n(4), junk2=[junk, junkb], b_junk2=P.bufs_n(2), tmp2=[tmp, tmpb],
                            b_tmp2=P.bufs_n(2))
                norm_transpose_half(P, nc, T, T["x"], tok_tiles, hT, b_hT, bc, b_bc, work)
                P.barrier()
                P.flush()
            if DBG.get('norm_only'):
                continue
            with contextlib.ExitStack() as es2:
                A_ = lambda n, shp, dt: sb(es2, nc, n, shp, dt)
                wg0 = A_("a_w0", [128, 16, 512], BF16); wg1 = A_("a_w1", [128, 16, 512], BF16)
                wkv = A_("a_wkv", [128, 16, 256], BF16)
                cos_t = A_("a_cos", [128, 2048], F32); sin_t = A_("a_sin", [128, 2048], F32)
                stg0 = A_("a_stg0", [128, 512], BF16); stg1 = A_("a_stg1", [128, 512], BF16)
                stg2 = A_("a_stg2", [128, 512], BF16); stg3 = A_("a_stg3", [128, 512], BF16)
                qs0 = A_("a_qs0", [128, 512], BF16); qs1 = A_("a_qs1", [128, 512], BF16)
                t1 = A_("a_t1", [128, 512], F32); t2 = A_("a_t2", [128, 512], F32)
                kvf0 = A_("a_kvf0", [128, 256], F32); kvf1 = A_("a_kvf1", [128, 256], F32)
                vb0 = A_("a_vb0", [128, 128], BF16); vb1 = A_("a_vb1", [128, 128], BF16)
                b_hT = P.buf("hT2")
                wg = [wg0, wg1]
                b_wg = P.bufs_n(2)
                b_wkv = P.buf()
                b_cos, b_sin = P.buf(), P.buf()
                stg = [stg0, stg1, stg2, stg3]
                b_stg = P.bufs_n(4)
                qs = [qs0, qs1]
                b_qs = P.bufs_n(2)
                b_t1, b_t2 = P.buf(), P.buf()
                kvf = [kvf0, kvf1]
                b_kvf = P.bufs_n(2)
                vb = [vb0, vb1]
                b_vb = P.bufs_n(2)
                b_ps = P.bufs_n(8)
                P.dma("sp", cos_t[:], T["rope_cos"][:, half * 2048:(half + 1) * 2048], b_cos, writes=[b_cos])
                P.dma("sp", sin_t[:], T["rope_sin"][:, half * 2048:(half + 1) * 2048], b_sin, writes=[b_sin])
                for kq in range(16):
                    P.dma("pool", wkv[:, kq, :], Wv[:, kq, 1024:1280], b_wkv, writes=[b_wkv], partial=(kq > 0))
                for i, (toff, isp) in enumerate(tok_tiles[:DBG.get('nkv', 99)]):
                    pi = i % 2
                    pst = ps[6 + pi]
                    for k in range(16):
                        P.op("pe", lambda e, pst=pst, k=k, i=i: e.matmul(
                            pst[:, 0:256], hT[:, k, i * 128:(i + 1) * 128], wkv[:, k, :],
                            start=(k == 0), stop=(k == 15)),
                            reads=[b_hT, b_wkv], writes=[b_ps[6 + pi]], partial=(k > 0))
                    if isp and not DBG.get('no_isp'):
                        P.op("act", lambda e, pst=pst, pi=pi: e.copy(kvf[pi][:], pst[:, 0:256]),
                             reads=[b_ps[6 + pi]], writes=[b_kvf[pi]])
                        r0 = toff - LS
                        if not DBG.get('no_nk'):
                            P.dma("sp", T["nk"][r0:r0 + 128, :], kvf[pi][:, 0:128], b_kvf[pi], reads=[b_kvf[pi]])
                        if not DBG.get('no_nv'):
                            P.dma("sp", T["nv"][r0:r0 + 128, :], kvf[pi][:, 128:256], b_kvf[pi], reads=[b_kvf[pi]])
                    P.op("act", lambda e, pst=pst, pi=pi: e.copy(vb[pi][:], pst[:, 128:256]),
                         reads=[b_ps[6 + pi]], writes=[b_vb[pi]])
                    P.dma("sp", T["vtok"][toff:toff + 128, :], vb[pi][:], b_vb[pi], reads=[b_vb[pi]])
                chunks = [(c * 512, 512, half * 2048 + c * 512, False) for c in range(4)] + \
                         [(2048, 256, LS + half * 256, True)]
                evac_i = 0
                psi = 0
                for gi, (kind, blocks) in enumerate(groups[:DBG.get('ngroups', 99)] if not DBG.get('gsel') else [groups[i] for i in DBG['gsel']]):
                    s = gi % 2
                    if kind == "k":
                        for j, (name, r0, c0) in enumerate(blocks):
                            for dup in range(2):
                                for kq in range(16):
                                    P.dma("pool", wg[s][:, kq, j * 128 + dup * 64:j * 128 + dup * 64 + 64],
                                          Wv[:, kq, c0:c0 + 64], b_wg[s], writes=[b_wg[s]],
                                          partial=(j + dup + kq > 0))
                    else:
                        c0 = blocks[0][2]
                        for kq in range(16):
                            P.dma("pool", wg[s][:, kq, 0:512], Wv[:, kq, c0:c0 + 512],
                                  b_wg[s], writes=[b_wg[s]], partial=(kq > 0))
                    for (l0, n, g0, isp) in chunks:
                        for j, (name, r0, c0) in enumerate(blocks):
                            pj = psi % 4
                            psi += 1
                            pst = ps[pj]
                            for k in range(16):
                                P.op("pe", lambda e, pst=pst, k=k, s=s, j=j, l0=l0, n=n: e.matmul(
                                    pst[:, 0:n], wg[s][:, k, j * 128:(j + 1) * 128], hT[:, k, l0:l0 + n],
                                    start=(k == 0), stop=(k == 15)),
                                    reads=[b_hT, b_wg[s]], writes=[b_ps[pj]], partial=(k > 0))
                            if name.startswith("kT"):
                                dst = T["kT"][int(name[2]), :, g0:g0 + n]
                            else:
                                dst = T[name][r0:r0 + 128, g0:g0 + n]
                            si = evac_i % 4
                            evac_i += 1
                            if kind in ("q", "k") and not isp:
                                qi = evac_i % 2
                                P.op("act", lambda e, pst=pst, qi=qi, n=n: e.copy(qs[qi][:, 0:n], pst[:, 0:n]),
                                     reads=[b_ps[pj]], writes=[b_qs[qi]])
                                pr = ps[4 + qi]
                                P.op("pe", lambda e, pr=pr, qi=qi, n=n: e.matmul(
                                    pr[:, 0:n], T["ropeP"][:], qs[qi][:, 0:n], start=True, stop=True),
                                    reads=[b_qs[qi]], writes=[b_ps[4 + qi]])
                                P.op("dve", lambda e, qi=qi, l0=l0, n=n: e.tensor_tensor(
                                    t1[:, 0:n], qs[qi][:, 0:n], cos_t[:, l0:l0 + n], ALU.mult),
                                    reads=[b_qs[qi], b_cos], writes=[b_t1])
                                P.op("dve", lambda e, pr=pr, l0=l0, n=n: e.tensor_tensor(
                                    t2[:, 0:n], pr[:, 0:n], sin_t[:, l0:l0 + n], ALU.mult),
                                    reads=[b_ps[4 + qi], b_sin], writes=[b_t2])
                                P.op("dve", lambda e, si=si, n=n: e.tensor_tensor(
                                    stg[si][:, 0:n], t1[:, 0:n], t2[:, 0:n], ALU.add),
                                    reads=[b_t1, b_t2], writes=[b_stg[si]])
                            else:
                                if evac_i % 2 == 0:
                                    P.op("act", lambda e, pst=pst, si=si, n=n: e.copy(stg[si][:, 0:n], pst[:, 0:n]),
                                         reads=[b_ps[pj]], writes=[b_stg[si]])
                                else:
                                    P.op("dve", lambda e, pst=pst, si=si, n=n: e.tensor_copy(
                                        stg[si][:, 0:n], pst[:, 0:n]), reads=[b_ps[pj]], writes=[b_stg[si]])
                            P.dma("sp", dst, stg[si][:, 0:n], b_stg[si], reads=[b_stg[si]])
                P.barrier()
                P.flush()


def bcast_last(ap2d, n):
    a = [list(p) for p in ap2d.ap]
    return bass.AP(ap2d.tensor, ap2d.offset, a + [[0, n]])


def phase_B(P, nc, T):
    import contextlib
    ps = T["ps"]
    ident = T["ident"]
    seqs = [(0, LS, True), (LS, LP, False), (LS + LP, LP, False)]
    with contextlib.ExitStack() as es:
        A_ = lambda n, shp, dt: sb(es, nc, n, shp, dt)
        kT = [A_("b_kT0", [128, LS], BF16), A_("b_kT1", [128, LS], BF16)]
        kcT = [A_("b_kc0", [128, 512], BF16), A_("b_kc1", [128, 512], BF16)]
        ckd = A_("b_ckd", [128, 4, 2, 2, 64], BF16)
        vaug = A_("b_vaug", [128, LS // 128, 2, 65], BF16)
        cvaug = A_("b_cvaug", [128, 4, 2, 65], BF16)
        qT = [A_("b_q0", [128, LS], BF16), A_("b_q1", [128, LS], BF16)]
        ga = [A_("b_ga0", [128, LS], BF16), A_("b_ga1", [128, LS], BF16)]
        sga = A_("b_sga", [128, LS], BF16)
        ptA = [A_("b_ptA0", [128, 512], BF16), A_("b_ptA1", [128, 512], BF16)]
        ptB = [A_("b_ptB0", [128, 384], BF16), A_("b_ptB1", [128, 384], BF16)]
        att = [A_("b_att0", [128, 128], BF16), A_("b_att1", [128, 128], BF16)]
        stage = [A_("b_stg0", [128, 512], BF16), A_("b_stg1", [128, 512], BF16)]
        mask = A_("b_mask", [128, 384], BF16)
        sinkb = A_("b_sinkb", [128, 16], F32)
        esink = A_("b_esink", [128, 16], F32)
        den = [A_("b_den0", [128, 2], F32), A_("b_den1", [128, 2], F32)]
        rden = [A_("b_rden0", [128, 2], F32), A_("b_rden1", [128, 2], F32)]

        b_mask, b_es = P.buf(), P.buf()
        P.dma("sp", mask[:], T["d_amask"], b_mask, writes=[b_mask])
        P.dma("sp", sinkb[:], T["a_sink"][0:1, :].partition_broadcast(128), b_es, writes=[b_es])
        P.op("act", lambda e: e.activation(esink[:], sinkb[:], AF.Exp), reads=[b_es], writes=[b_es])
        P.op("dve", lambda e: e.memset(vaug[:], 1.0), writes=[b_mask], partial=True)
        P.op("dve", lambda e: e.memset(cvaug[:], 1.0), writes=[b_mask], partial=True)
        b_ckd, b_kc, b_cv = P.buf(), P.bufs_n(2), P.buf()
        ckv = T["ck"].rearrange("(t p) c -> p t c", p=128)
        cvv = T["cv"].rearrange("(t p) c -> p t c", p=128)
        for kv in range(2):
            for dup in range(2):
                P.dma("pool", ckd[:, :, kv, dup, :], ckv[:, :, kv * 64:(kv + 1) * 64], b_ckd, writes=[b_ckd],
                      partial=True)
            P.dma("pool", cvaug[:, :, kv, 0:64], cvv[:, :, kv * 64:(kv + 1) * 64], b_cv, reads=[b_mask],
                  writes=[b_cv], partial=True)
        b_pt = P.bufs_n(8)
        for kv in range(2):
            for st_ in range(4):
                pb = ps[st_ % 2][:].bitcast(BF16)
                P.op("pe", lambda e, pb=pb, st_=st_, kv=kv: e.transpose(
                    pb[:, 0:128], ckd[:, st_, kv].rearrange("p a b -> p (a b)"), ident[:]),
                    reads=[b_ckd], writes=[b_pt[st_ % 2]])
                P.op("dve", lambda e, pb=pb, st_=st_, kv=kv: e.tensor_copy(
                    kcT[kv][:, st_ * 128:(st_ + 1) * 128], pb[:, 0:128]),
                    reads=[b_pt[st_ % 2]], writes=[b_kc[kv]], partial=True)
        P.barrier()
        b_kT, b_v = P.bufs_n(2), P.buf()
        b_q, b_ga, b_sga = P.bufs_n(2), P.bufs_n(2), P.buf()
        b_ptA, b_ptB, b_att, b_stage = P.bufs_n(2), P.bufs_n(2), P.bufs_n(2), P.bufs_n(2)
        b_den = P.bufs_n(2)
        b_ps = P.bufs_n(8)
        cnt = 0
        for (tok0, L, has_ctx) in seqs:
            nqb = L // 128
            for kv in range(2):
                P.dma("sp", kT[kv][:, 0:L], T["kT"][kv, :, tok0:tok0 + L], b_kT[kv], writes=[b_kT[kv]])
                P.dma("sp", vaug[:, 0:nqb, kv, 0:64],
                      T["vtok"][tok0:tok0 + L, kv * 64:(kv + 1) * 64].rearrange("(t p) c -> p t c", p=128),
                      b_v, writes=[b_v], partial=(kv > 0))
            for hp in range(8):
                s = cnt % 2
                cnt += 1
                kv = hp // 4
                P.dma("sp", qT[s][:, 0:L], T["qT"][hp * 128:(hp + 1) * 128, tok0:tok0 + L], b_q[s], writes=[b_q[s]])
                P.dma("sp", ga[s][:, 0:L], T["gaT"][hp * 128:(hp + 1) * 128, tok0:tok0 + L], b_ga[s],
                      writes=[b_ga[s]])
                P.op("act", lambda e, s=s, L=L: e.activation(sga[:, 0:L], ga[s][:, 0:L], AF.Silu),
                     reads=[b_ga[s]], writes=[b_sga])
                def geom(qb, nqb=nqb, has_ctx=has_ctx):
                    if has_ctx:
                        loc = [(j, j - qb + 1) for j in (qb - 1, qb, qb + 1) if 0 <= j < nqb]
                    else:
                        loc = [(j, j) for j in range(nqb)]
                    return loc, loc[0][1], loc[-1][1] + 1

                def S_unit(qb, hh, s=s, kv=kv, has_ctx=has_ctx):
                    loc, lo, hi = geom(qb)
                    pr = slice(hh * 64, (hh + 1) * 64)
                    pa, pbk = ps[hh * 2], ps[hh * 2 + 1]
                    qsl = qT[s][pr, qb * 128:(qb + 1) * 128]
                    if has_ctx:
                        for c in range(4):
                            P.op("pe", lambda e, pa=pa, c=c, pr=pr, qsl=qsl, kv=kv: e.matmul(
                                pa[:, c * 128:(c + 1) * 128], kcT[kv][pr, c * 128:(c + 1) * 128], qsl,
                                start=True, stop=True),
                                reads=[b_kc[kv], b_q[s]], writes=[b_ps[hh * 2]], partial=(c > 0))
                        P.op("act", lambda e, pa=pa, hh=hh: e.activation(ptA[hh][:], pa[:], AF.Exp, scale=0.125),
                             reads=[b_ps[hh * 2]], writes=[b_ptA[hh]])
                    for n_, (j, sl) in enumerate(loc):
                        P.op("pe", lambda e, pbk=pbk, sl=sl, j=j, pr=pr, qsl=qsl, kv=kv: e.matmul(
                            pbk[:, sl * 128:(sl + 1) * 128], kT[kv][pr, j * 128:(j + 1) * 128], qsl,
                            start=True, stop=True),
                            reads=[b_kT[kv], b_q[s]], writes=[b_ps[hh * 2 + 1]], partial=(n_ > 0))
                    P.op("act", lambda e, pbk=pbk, hh=hh, lo=lo, hi=hi: e.activation(
                        ptB[hh][:, lo * 128:hi * 128], pbk[:, lo * 128:hi * 128], AF.Exp, scale=0.125),
                        reads=[b_ps[hh * 2 + 1]], writes=[b_ptB[hh]])
                    if has_ctx:
                        P.op("dve", lambda e, hh=hh, lo=lo, hi=hi: e.tensor_tensor(
                            ptB[hh][:, lo * 128:hi * 128], ptB[hh][:, lo * 128:hi * 128],
                            mask[:, lo * 128:hi * 128], ALU.mult),
                            reads=[b_ptB[hh], b_mask], writes=[b_ptB[hh]])

                def PV_unit(qb, hh, kv=kv, has_ctx=has_ctx):
                    loc, lo, hi = geom(qb)
                    o = qb % 2
                    psOv = ps[4 + o][:, 0:130].rearrange("p (h c) -> p h c", h=2)
                    mm = []
                    if has_ctx:
                        for c in range(4):
                            mm.append((ptA[hh][:, c * 128:(c + 1) * 128], cvaug[:, c, kv, :], b_ptA[hh], b_cv))
                    for (j, sl) in loc:
                        mm.append((ptB[hh][:, sl * 128:(sl + 1) * 128], vaug[:, j, kv, :], b_ptB[hh], b_v))
                    for n_, (lh, rh, bl, br) in enumerate(mm):
                        P.op("pe", lambda e, psOv=psOv, hh=hh, lh=lh, rh=rh, n_=n_, nm=len(mm): e.matmul(
                            psOv[:, hh, :], lh, rh, start=(n_ == 0), stop=(n_ == nm - 1)),
                            reads=[bl, br], writes=[b_ps[4 + o]], partial=(n_ > 0 or hh > 0))

                def FIN_unit(qb, s=s, hp=hp, nqb=nqb, tok0=tok0, cnt=cnt):
                    o = qb % 2
                    psOv = ps[4 + o][:, 0:130].rearrange("p (h c) -> p h c", h=2)
                    P.op("dve", lambda e, psOv=psOv, o=o, hp=hp: e.tensor_tensor(
                        den[o][:], psOv[:, :, 64], esink[:, hp * 2:hp * 2 + 2], ALU.add),
                        reads=[b_ps[4 + o], b_es], writes=[b_den[o]])
                    P.op("dve", lambda e, o=o: e.reciprocal(rden[o][:], den[o][:]), reads=[b_den[o]],
                         writes=[b_den[o]], partial=True)
                    P.op("dve", lambda e, psOv=psOv, o=o: e.tensor_tensor(
                        att[o][:].rearrange("p (h c) -> p h c", h=2), psOv[:, :, 0:64], bcast_last(rden[o][:], 64),
                        ALU.mult),
                        reads=[b_ps[4 + o], b_den[o]], writes=[b_att[o]])
                    pT = ps[6 + o][:].bitcast(BF16)
                    P.op("pe", lambda e, pT=pT, o=o: e.transpose(pT[:, 0:128], att[o][:], ident[:]),
                         reads=[b_att[o]], writes=[b_ps[6 + o]])
                    g4 = qb // 4
                    sg_ = (cnt * 1024 + g4) % 2
                    P.op("dve", lambda e, pT=pT, sg_=sg_, qb=qb: e.tensor_tensor(
                        stage[sg_][:, (qb % 4) * 128:(qb % 4 + 1) * 128], pT[:, 0:128],
                        sga[:, qb * 128:(qb + 1) * 128], ALU.mult),
                        reads=[b_ps[6 + o], b_sga], writes=[b_stage[sg_]], partial=(qb % 4 > 0))
                    if qb % 4 == 3 or qb == nqb - 1:
                        n = (qb % 4 + 1) * 128
                        t0 = tok0 + g4 * 512
                        P.dma("sp", T["mixT"][hp * 128:(hp + 1) * 128, t0:t0 + n], stage[sg_][:, 0:n],
                              b_stage[sg_], reads=[b_stage[sg_]])

                units = [(qb, hh) for qb in range(nqb) for hh in range(2)]
                S_unit(*units[0])
                pend_fin = None
                for ui, (qb, hh) in enumerate(units):
                    if ui + 1 < len(units):
                        S_unit(*units[ui + 1])
                    PV_unit(qb, hh)
                    if pend_fin is not None:
                        FIN_unit(pend_fin)
                        pend_fin = None
                    if hh == 1:
                        pend_fin = qb
                if pend_fin is not None:
                    FIN_unit(pend_fin)
        P.barrier()
        P.flush()


def phase_C0(P, nc, T):
    import contextlib
    seqs = [(0, LS), (LS, LP), (LS + LP, LP)]
    with contextlib.ExitStack() as es:
        A_ = lambda n, shp, dt: sb(es, nc, n, shp, dt)
        hsw = A_("c0_hsw", [128, 4, 24], F32)
        ut = [A_("c0_ut0", [128, LS + 2], BF16), A_("c0_ut1", [128, LS + 2], BF16), A_("c0_ut2", [128, LS + 2], BF16)]
        accs = [A_("c0_acc", [128, LS], F32), A_("c0_accb", [128, LS], F32)]
        acc2s = [A_("c0_acc2", [128, LS], F32), A_("c0_acc2b", [128, LS], F32)]
        ob = [A_("c0_ob0", [128, LS], BF16), A_("c0_ob1", [128, LS], BF16)]
        b_hsw, b_ut, b_accs, b_acc2s, b_ob = P.buf(), P.bufs_n(3), P.bufs_n(2), P.bufs_n(2), P.bufs_n(2)
        P.dma("sp", hsw[:], T["hsw"], b_hsw, writes=[b_hsw])
        items = [(tok0, L, ct) for (tok0, L) in seqs for ct in range(24)]

        def front(it):
            tok0, L, ct = items[it]
            u = it % 3
            P.op("pool", lambda e, u=u: e.memset(ut[u][:, 0:1], 0.0), writes=[b_ut[u]])
            P.op("pool", lambda e, u=u, L=L: e.memset(ut[u][:, L + 1:L + 2], 0.0), writes=[b_ut[u]], partial=True)
            P.dma("sp", ut[u][:, 1:L + 1], T["hyT"][ct * 128:(ct + 1) * 128, tok0:tok0 + L], b_ut[u],
                  writes=[b_ut[u]], partial=True)

        def back(it):
            tok0, L, ct = items[it]
            u = it % 3
            s = it % 2
            acc, acc2, b_acc, b_acc2 = accs[s], acc2s[s], b_accs[s], b_acc2s[s]
            P.op("act", lambda e, u=u, L=L, ct=ct, acc=acc: e.activation(
                acc[:, 0:L], ut[u][:, 0:L], AF.Identity, scale=hsw[:, 0, ct:ct + 1], bias=hsw[:, 3, ct:ct + 1]),
                reads=[b_ut[u], b_hsw], writes=[b_acc])
            P.op("dve", lambda e, u=u, L=L, ct=ct, acc=acc, acc2=acc2: e.scalar_tensor_tensor(
                acc2[:, 0:L], ut[u][:, 1:L + 1], hsw[:, 1, ct:ct + 1], acc[:, 0:L], ALU.mult, ALU.add),
                reads=[b_ut[u], b_acc, b_hsw], writes=[b_acc2])
            P.op("dve", lambda e, u=u, s=s, L=L, ct=ct, acc2=acc2: e.scalar_tensor_tensor(
                ob[s][:, 0:L], ut[u][:, 2:L + 2], hsw[:, 2, ct:ct + 1], acc2[:, 0:L], ALU.mult, ALU.add),
                reads=[b_ut[u], b_acc2, b_hsw], writes=[b_ob[s]])
            P.dma("sp", T["ucT"][ct * 128:(ct + 1) * 128, tok0:tok0 + L], ob[s][:, 0:L], b_ob[s],
                  reads=[b_ob[s]])

        n = len(items)
        front(0)
        if n > 1:
            front(1)
        for it in range(n):
            if it + 2 < n:
                front(it + 2)
            back(it)
        P.barrier()
        P.flush()


def phase_CF(P, nc, T, Lf, taps_dst, zemb, tposb):
    import contextlib
    import math
    ps = T["ps"]
    nch = max(1, Lf // 512)
    cw = min(512, Lf)
    with contextlib.ExitStack() as es:
        A_ = lambda n, shp, dt: sb(es, nc, n, shp, dt)
        ze = A_("cf_ze", [33, Lf], F32)
        tp_ = A_("cf_tpos", [128, Lf], F32)
        w1 = A_("cf_w1", [33, 64], F32)
        w2 = A_("cf_w2", [64, 64], F32)
        bb = A_("cf_bb", [64, 2], F32)
        w3 = A_("cf_w3", [64, 4096], BF16)
        dec = A_("cf_dec", [128, 32], F32)
        ndec = A_("cf_ndec", [128, 32], F32)
        h1 = A_("cf_h1", [64, Lf], F32)
        h2 = A_("cf_h2", [64, Lf], F32)
        h2b = A_("cf_h2b", [64, Lf], BF16)
        tmp = A_("cf_tmp", [64, 512], F32)
        tmq = A_("cf_tmq", [64, 512], F32)
        raw = [A_("cf_raw0", [128, Lf], F32), A_("cf_raw1", [128, Lf], F32)]
        win = [A_("cf_win0", [128, 512], F32), A_("cf_win1", [128, 512], F32)]
        junk = A_("cf_junk", [128, Lf], F32)
        ss = A_("cf_ss", [128, 8], F32)
        tpb = [A_("cf_tp0", [128, 2 * Lf], BF16), A_("cf_tp1", [128, 2 * Lf], BF16)]
        b_c = P.bufs_n(8)
        P.dma("sp", ze[:], zemb, b_c[0], writes=[b_c[0]])
        P.dma("sp", tp_[:], tposb, b_c[1], writes=[b_c[1]])
        P.dma("sp", w1[:], T["hy_w1"], b_c[2], writes=[b_c[2]])
        P.dma("sp", w2[:], T["hy_w2"], b_c[3], writes=[b_c[3]])
        P.dma("sp", bb[:], T["hy_bb"], b_c[4], writes=[b_c[4]])
        for q4 in range(4):
            P.dma("pool", w3[:, q4 * 1024:(q4 + 1) * 1024], T["hy_w3"][:, q4 * 1024:(q4 + 1) * 1024], b_c[5],
                  writes=[b_c[5]], partial=(q4 > 0))
        P.dma("sp", dec[:], T["hy_decT"], b_c[6], writes=[b_c[6]])
        P.op("act", lambda e: e.activation(dec[:], dec[:], AF.Abs), reads=[b_c[6]], writes=[b_c[6]])
        P.op("dve", lambda e: e.tensor_scalar(ndec[:], dec[:], -1.0, None, ALU.mult),
             reads=[b_c[6]], writes=[b_c[7]])
        b_h1, b_h2, b_h2b, b_tmp, b_tmq = P.buf(), P.buf(), P.buf(), P.buf(), P.buf()
        b_ps = P.bufs_n(8)
        TWO_PI = 2.0 * math.pi
        for layer in range(2):
            src = ze if layer == 0 else h1
            wm = w1 if layer == 0 else w2
            kk = 33 if layer == 0 else 64
            dst = h1 if layer == 0 else h2
            b_src = b_c[0] if layer == 0 else b_h1
            b_dst = b_h1 if layer == 0 else b_h2
            for c in range(nch):
                pj = c % 2
                P.op("pe", lambda e, pj=pj, c=c, src=src, wm=wm, kk=kk: e.matmul(
                    ps[pj][0:64, 0:cw], wm[0:kk, :], src[0:kk, c * cw:(c + 1) * cw], start=True, stop=True),
                    reads=[b_src, b_c[2], b_c[3]], writes=[b_ps[pj]])
                MAGIC = 12582912.0
                P.op("act", lambda e, pj=pj, layer=layer: e.activation(
                    tmp[:, 0:cw], ps[pj][0:64, 0:cw], AF.Identity, bias=bb[:, layer:layer + 1]),
                    reads=[b_ps[pj], b_c[4]], writes=[b_tmp])
                P.op("dve", lambda e: e.tensor_scalar(
                    tmq[:, 0:cw], tmp[:, 0:cw], 1.0 / TWO_PI, MAGIC, ALU.mult, ALU.add),
                    reads=[b_tmp], writes=[b_tmq])
                P.op("dve", lambda e: e.tensor_scalar(
                    tmq[:, 0:cw], tmq[:, 0:cw], -MAGIC, -TWO_PI, ALU.add, ALU.mult),
                    reads=[b_tmq], writes=[b_tmq])
                P.op("dve", lambda e: e.tensor_tensor(tmp[:, 0:cw], tmp[:, 0:cw], tmq[:, 0:cw], ALU.add),
                     reads=[b_tmp, b_tmq], writes=[b_tmp])
                P.op("act", lambda e, c=c, dst=dst: e.activation(dst[:, c * cw:(c + 1) * cw], tmp[:, 0:cw], AF.Sin),
                     reads=[b_tmp], writes=[b_dst], partial=(c > 0))
        P.op("act", lambda e: e.copy(h2b[:], h2[:]), reads=[b_h2], writes=[b_h2b])
        b_raw, b_win, b_junk, b_ss, b_tpb = P.bufs_n(2), P.bufs_n(2), P.buf(), P.buf(), P.bufs_n(2)
        it = 0
        for order in range(2):
            for blk in range(8):
                s = it % 2
                it += 1
                for d in range(2):
                    cbk = d * 16 + order * 8 + blk
                    for c in range(nch):
                        pj = 2 + (c % 2)
                        wj = c % 2
                        P.op("pe", lambda e, pj=pj, c=c, cbk=cbk: e.matmul(
                            ps[pj][:, 0:cw], w3[:, cbk * 128:(cbk + 1) * 128], h2b[:, c * cw:(c + 1) * cw],
                            start=True, stop=True),
                            reads=[b_h2b, b_c[5]], writes=[b_ps[pj]])
                        P.op("act", lambda e, wj=wj, c=c, cbk=cbk: e.activation(
                            win[wj][:, 0:cw], tp_[:, c * cw:(c + 1) * cw], AF.Exp, scale=ndec[:, cbk:cbk + 1]),
                            reads=[b_c[1], b_c[7]], writes=[b_win[wj]])
                        P.op("dve", lambda e, pj=pj, wj=wj, c=c, d=d: e.tensor_tensor(
                            raw[d][:, c * cw:(c + 1) * cw], ps[pj][:, 0:cw], win[wj][:, 0:cw], ALU.mult),
                            reads=[b_ps[pj], b_win[wj]], writes=[b_raw[d]], partial=(c > 0))
                P.op("act", lambda e: e.activation(junk[:], raw[0][:], AF.Square, accum_out=ss[:, 0:1]),
                     reads=[b_raw[0]], writes=[b_junk, b_ss])
                P.op("act", lambda e: e.activation(junk[:, 1:Lf], raw[1][:, 1:Lf], AF.Square, accum_out=ss[:, 1:2]),
                     reads=[b_raw[1]], writes=[b_junk, b_ss], partial=True)
                P.op("dve", lambda e: e.tensor_tensor(ss[:, 2:3], ss[:, 0:1], ss[:, 1:2], ALU.add),
                     reads=[b_ss], writes=[b_ss], partial=True)
                P.op("act", lambda e: e.activation(ss[:, 3:4], ss[:, 2:3], AF.Sqrt, bias=T["epsb"][:, 0:1]),
                     reads=[b_ss], writes=[b_ss], partial=True)
                P.op("dve", lambda e: e.reciprocal(ss[:, 4:5], ss[:, 3:4]), reads=[b_ss], writes=[b_ss], partial=True)
                P.op("dve", lambda e: e.tensor_scalar(ss[:, 5:6], ss[:, 4:5], -1.0, None, ALU.mult),
                     reads=[b_ss], writes=[b_ss], partial=True)
                P.op("act", lambda e, s=s: e.activation(tpb[s][:, 0:Lf], raw[0][:], AF.Copy, scale=ss[:, 4:5]),
                     reads=[b_raw[0], b_ss], writes=[b_tpb[s]])
                P.op("pool", lambda e, s=s: e.memset(tpb[s][:, Lf:Lf + 1], 0.0), writes=[b_tpb[s]], partial=True)
                P.op("dve", lambda e, s=s: e.tensor_scalar(
                    rev_ap(tpb[s][:, Lf + 1:2 * Lf]), raw[1][:, 1:Lf], ss[:, 5:6], None, ALU.mult),
                    reads=[b_raw[1], b_ss], writes=[b_tpb[s]], partial=True)
                P.dma("sp", taps_dst[order, blk * 128:(blk + 1) * 128, :], tpb[s][:], b_tpb[s], reads=[b_tpb[s]])
        P.barrier()
        P.flush()


def _split_dma(P, eng, dst, src, buf, nsplit, axis_len, mk_dst, mk_src, **kw):
    step = axis_len // nsplit
    for i in range(nsplit):
        P.dma(eng, mk_dst(i * step, (i + 1) * step), mk_src(i * step, (i + 1) * step), buf,
              partial=(i > 0 or kw.get("partial", False)), **{k: v for k, v in kw.items() if k != "partial"})


def _f1_stage(P, nc, T, xin, K, b_xin, F1, b_F1, Cb, b_Cb, b_ps, ev):
    ps = T["ps"]
    for g in range(16):
        pj = g % 2
        for cc in range(4):
            c = g * 4 + cc
            P.op("pe", lambda e, pj=pj, cc=cc, c=c: e.matmul(
                ps[pj][0:64, cc * 128:(cc + 1) * 128], xin[0:K, c, :], F1[0:K, 0:128], start=True, stop=True),
                reads=[b_xin, b_F1], writes=[b_ps[pj]], partial=(cc > 0))
            P.op("pe", lambda e, pj=pj, cc=cc, c=c: e.matmul(
                ps[pj][64:128, cc * 128:(cc + 1) * 128], xin[0:K, c, :], F1[0:K, 128:256], start=True, stop=True,
                tile_position=(0, 64)),
                reads=[b_xin, b_F1], writes=[b_ps[pj]], partial=True)
        src = ps[pj][:].rearrange("p (c k) -> p k c", c=4)
        dst = Cb[:, :, g * 4:(g + 1) * 4]
        ev(dst, src, [b_ps[pj]], [b_Cb], g)


def _evac_alt(P):
    def ev(dst, src, reads, writes, i):
        if i % 2 == 0:
            P.op("act", lambda e: e.copy(dst, src), reads=reads, writes=writes, partial="nowaw")
        else:
            P.op("dve", lambda e: e.tensor_copy(dst, src), reads=reads, writes=writes, partial="nowaw")
    return ev


def _evac_act(P):
    def ev(dst, src, reads, writes, i):
        P.op("act", lambda e: e.copy(dst, src), reads=reads, writes=writes, partial="nowaw")
    return ev


def phase_CT(P, nc, T):
    import contextlib
    ps = T["ps"]
    with contextlib.ExitStack() as es:
        A_ = lambda n, shp, dt: sb(es, nc, n, shp, dt)
        F1 = A_("ct_F1", [128, 256], BF16)
        Grr = A_("ct_Grr", [128, 128, 64], BF16)
        Gii = A_("ct_Gii", [128, 128, 64], BF16)
        xt = [A_("ct_xt0", [128, 64, 64], BF16), A_("ct_xt1", [128, 64, 64], BF16)]
        Cb = A_("ct_Cb", [128, 128, 64], BF16)
        Hr = [A_("ct_Hr0", [64, 64, 128], BF16), A_("ct_Hr1", [64, 64, 128], BF16)]
        Hi = [A_("ct_Hi0", [64, 64, 128], BF16), A_("ct_Hi1", [64, 64, 128], BF16)]
        b_F1, b_G = P.buf(), P.buf()
        P.dma("sp", F1[:], T["d_F1"], b_F1, writes=[b_F1])
        for q in range(4):
            P.dma("sp", Grr[:, q * 32:(q + 1) * 32, :], T["d_Grr"][:, q * 32:(q + 1) * 32, :], b_G, writes=[b_G],
                  partial=(q > 0))
            P.dma("sp", Gii[:, q * 32:(q + 1) * 32, :], T["d_Gii"][:, q * 32:(q + 1) * 32, :], b_G, writes=[b_G],
                  partial=True)
        b_xt, b_Cb, b_Hr, b_Hi = P.bufs_n(2), P.buf(), P.bufs_n(2), P.bufs_n(2)
        b_ps = P.bufs_n(8)
        ev = _evac_alt(P)
        it = 0
        def ct_load(order, cb, s):
            srcv = T["tapsS"][order, cb * 64:(cb + 1) * 64, :].rearrange("c (a b) -> a c b", b=64)
            for q in range(8):
                P.dma("sp", xt[s][:, q * 8:(q + 1) * 8, :], srcv[:, q * 8:(q + 1) * 8, :], b_xt[s],
                      writes=[b_xt[s]], partial=(q > 0))

        blocks = [(order, cb) for order in range(2) for cb in range(16)]
        ct_load(0, 0, 0)
        for (order, cb) in blocks:
                s = it % 2
                it += 1
                _f1_stage(P, nc, T, xt[s], 128, b_xt[s], F1, b_F1, Cb, b_Cb, b_ps, ev)
                if it < len(blocks):
                    ct_load(blocks[it][0], blocks[it][1], it % 2)
                for g in range(16):
                    pj = 2 + (g % 2) * 2
                    for kk in range(8):
                        k1 = g * 8 + kk
                        P.op("pe", lambda e, pj=pj, kk=kk, k1=k1: e.matmul(
                            ps[pj][0:64, kk * 64:(kk + 1) * 64], Grr[:, k1, :], Cb[:, k1, :], start=True, stop=True),
                            reads=[b_G, b_Cb], writes=[b_ps[pj]], partial=(kk > 0))
                        P.op("pe", lambda e, pj=pj, kk=kk, k1=k1: e.matmul(
                            ps[pj + 1][0:64, kk * 64:(kk + 1) * 64], Gii[:, k1, :], Cb[:, k1, :], start=True,
                            stop=True),
                            reads=[b_G, b_Cb], writes=[b_ps[pj + 1]], partial=(kk > 0))
                    P.op("act", lambda e, pj=pj, g=g, s=s: e.copy(
                        Hr[s][:, :, g * 8:(g + 1) * 8], ps[pj][0:64, :].rearrange("p (k c) -> p c k", k=8)),
                        reads=[b_ps[pj]], writes=[b_Hr[s]], partial="nowaw")
                    P.op("dve", lambda e, pj=pj, g=g, s=s: e.tensor_copy(
                        Hi[s][:, :, g * 8:(g + 1) * 8], ps[pj + 1][0:64, :].rearrange("p (k c) -> p c k", k=8)),
                        reads=[b_ps[pj + 1]], writes=[b_Hi[s]], partial="nowaw")
                P.dma("sp", T["Hs"][order, cb, 0], Hr[s][:].rearrange("p c k -> p (c k)"), b_Hr[s], reads=[b_Hr[s]])
                P.dma("sp", T["Hs"][order, cb, 1], Hi[s][:].rearrange("p c k -> p (c k)"), b_Hi[s], reads=[b_Hi[s]])
        P.barrier()
        P.flush()


def phase_CS(P, nc, T):
    import contextlib
    ps = T["ps"]
    with contextlib.ExitStack() as es:
        A_ = lambda n, shp, dt: sb(es, nc, n, shp, dt)
        F1 = A_("cs_F1", [128, 256], BF16)
        G = A_("cs_G", [128, 128, 64], BF16)
        M1 = A_("cs_M1", [64, 128], BF16)
        M1p = A_("cs_M1p", [64, 128], BF16)
        T2r = A_("cs_T2r", [128, 64, 64], BF16)
        T2i = A_("cs_T2i", [128, 64, 64], BF16)
        zc = A_("cs_zc", [64, 64, 64], BF16)
        x1 = A_("cs_x1", [64, 64, 64], BF16)
        x2 = A_("cs_x2", [64, 64, 64], BF16)
        gh = A_("cs_gh", [64, 64, 64], BF16)
        z2 = A_("cs_z2", [64, 64, 64], BF16)
        bz = A_("cs_bz", [64, 64, 64], BF16)
        cv = A_("cs_cv", [64, 64, 64], F32)
        bias = A_("cs_bias", [64, 2, 64], F32)
        Cb = A_("cs_Cb", [128, 128, 64], BF16)
        Db = A_("cs_Db", [128, 128, 64], BF16)
        P1 = A_("cs_P1", [64, 64, 128], BF16)
        P2 = A_("cs_P2", [64, 64, 128], BF16)
        Hr = A_("cs_Hr", [64, 64, 128], BF16)
        Hi = A_("cs_Hi", [64, 64, 128], BF16)
        Xs = [A_("cs_Xs0", [64, 512], BF16), A_("cs_Xs1", [64, 512], BF16)]
        b_k = P.bufs_n(6)
        P.dma("sp", F1[:], T["d_F1"], b_k[0], writes=[b_k[0]])
        for q in range(4):
            P.dma("sp", G[:, q * 32:(q + 1) * 32, :], T["d_G"][:, q * 32:(q + 1) * 32, :], b_k[1], writes=[b_k[1]],
                  partial=(q > 0))
        P.dma("sp", M1[:], T["d_M1"], b_k[2], writes=[b_k[2]])
        P.dma("sp", M1p[:], T["d_M1p"], b_k[3], writes=[b_k[3]])
        P.dma("sp", T2r[:], T["d_T2r"], b_k[4], writes=[b_k[4]])
        P.dma("sp", T2i[:], T["d_T2in"], b_k[5], writes=[b_k[5]])
        b_F1, b_G, b_M1, b_M1p, b_T2r, b_T2i = b_k
        b_zc, b_x1, b_x2, b_gh, b_sgh, b_z2, b_bz, b_cv, b_mx, b_bias = [P.buf() for _ in range(10)]
        b_Cb, b_Db, b_P1, b_P2, b_Hr, b_Hi = [P.buf() for _ in range(6)]
        b_Xs = P.bufs_n(2)
        b_ps = P.bufs_n(8)
        ev = _evac_act(P)

        def t64(rows0):
            return lambda a, b: T["ucT"][rows0 + a:rows0 + b, 0:LS].rearrange("c (n1 n2) -> n1 c n2", n2=64)

        def cs_load(cb, which):
            specs = {"zc": (zc, b_zc, 2048 + cb * 64, "ucT"), "x1": (x1, b_x1, cb * 64, "ucT"),
                     "x2": (x2, b_x2, 1024 + cb * 64, "ucT"), "gh": (gh, b_gh, cb * 64, "ghT")}
            for w in which:
                dst, bdst, r0, src_t = specs[w]
                for q in range(4):
                    srcv = T[src_t][r0 + q * 16:r0 + (q + 1) * 16, 0:LS].rearrange("c (n1 n2) -> n1 c n2", n2=64)
                    P.dma("sp", dst[:, q * 16:(q + 1) * 16, :], srcv, bdst, writes=[bdst], partial=(q > 0))

        cs_load(0, ("zc", "x1"))
        for cb in range(16):
            cs_load(cb, ("x2", "gh"))
            for o in range(2):
                P.dma("sp", bias[:, o, :], T["hy_bias"][o:o + 1, cb * 64:(cb + 1) * 64].partition_broadcast(64),
                      b_bias, writes=[b_bias], partial=(o > 0))
            P.op("act", lambda e: e.activation(gh[:], gh[:], AF.Silu), reads=[b_gh], writes=[b_gh])
            for o in range(2):
                zin, b_zin = (zc, b_zc) if o == 0 else (z2, b_z2)
                xo, b_xo = (x1, b_x1) if o == 0 else (x2, b_x2)
                P.dma("sp", Hr[:].rearrange("p c k -> p (c k)"), T["Hs"][o, cb, 0], b_Hr, writes=[b_Hr])
                P.dma("sp", Hi[:].rearrange("p c k -> p (c k)"), T["Hs"][o, cb, 1], b_Hi, writes=[b_Hi])
                if o == 1 and cb + 1 < 16:
                    cs_load(cb + 1, ("zc", "x1"))
                P.op("pool", lambda e, zin=zin, o=o: e.tensor_tensor(
                    bz[:], zin[:], bcast_last(bias[:, o, :], 64), ALU.mult),
                    reads=[b_zin, b_bias], writes=[b_bz])
                _f1_stage(P, nc, T, zin, 64, b_zin, F1, b_F1, Cb, b_Cb, b_ps, ev)
                for g in range(16):
                    pj = 2 + (g % 2)
                    xs = g % 2
                    for kk in range(8):
                        k1 = g * 8 + kk
                        P.op("pe", lambda e, pj=pj, kk=kk, k1=k1: e.matmul(
                            ps[pj][0:64, kk * 64:(kk + 1) * 64], G[:, k1, :], Cb[:, k1, :], start=True, stop=True),
                            reads=[b_G, b_Cb], writes=[b_ps[pj]], partial=(kk > 0))
                    P.op("act", lambda e, pj=pj, xs=xs: e.copy(Xs[xs][:], ps[pj][0:64, :]),
                         reads=[b_ps[pj]], writes=[b_Xs[xs]])
                    xv = Xs[xs][:].rearrange("p (k c) -> p c k", k=8)
                    P.op("dve", lambda e, xv=xv, g=g: e.tensor_tensor(
                        P1[:, :, g * 8:(g + 1) * 8], xv, Hr[:, :, g * 8:(g + 1) * 8], ALU.mult),
                        reads=[b_Xs[xs], b_Hr], writes=[b_P1], partial="nowaw")
                    P.op("dve", lambda e, xv=xv, g=g: e.tensor_tensor(
                        P2[:, :, g * 8:(g + 1) * 8], xv, Hi[:, :, g * 8:(g + 1) * 8], ALU.mult),
                        reads=[b_Xs[xs], b_Hi], writes=[b_P2], partial="nowaw")
                for g in range(16):
                    pj = 4 + (g % 2)
                    for cc in range(4):
                        c = g * 4 + cc
                        P.op("pe", lambda e, pj=pj, cc=cc, c=c: e.matmul(
                            ps[pj][:, cc * 128:(cc + 1) * 128], P1[:, c, :], M1[:], start=True, stop=False),
                            reads=[b_P1, b_M1], writes=[b_ps[pj]], partial=(cc > 0))
                        P.op("pe", lambda e, pj=pj, cc=cc, c=c: e.matmul(
                            ps[pj][:, cc * 128:(cc + 1) * 128], P2[:, c, :], M1p[:], start=False, stop=True),
                            reads=[b_P2, b_M1p], writes=[b_ps[pj]], partial=True)
                    src = ps[pj][:].rearrange("p (c q) -> p q c", c=4)
                    ev(Db[:, :, g * 4:(g + 1) * 4], src, [b_ps[pj]], [b_Db], g)
                for g in range(8):
                    pj = 6 + (g % 2)
                    for nn in range(8):
                        n2 = g * 8 + nn
                        P.op("pe", lambda e, pj=pj, nn=nn, n2=n2: e.matmul(
                            ps[pj][0:64, nn * 64:(nn + 1) * 64], T2r[:, n2, :], Db[:, n2, :], start=True, stop=False),
                            reads=[b_T2r, b_Db], writes=[b_ps[pj]], partial=(nn > 0))
                        P.op("pe", lambda e, pj=pj, nn=nn, n2=n2: e.matmul(
                            ps[pj][0:64, nn * 64:(nn + 1) * 64], T2i[:, n2, :], Db[:, 64 + n2, :], start=False,
                            stop=True),
                            reads=[b_T2i, b_Db], writes=[b_ps[pj]], partial=True)
                    P.op("dve", lambda e, pj=pj, g=g: e.tensor_tensor(
                        cv[:, :, g * 8:(g + 1) * 8], ps[pj][0:64, :].rearrange("p (n c) -> p c n", n=8),
                        bz[:, :, g * 8:(g + 1) * 8], ALU.add),
                        reads=[b_ps[pj], b_bz], writes=[b_cv], partial=(g > 0))
                if o == 0:
                    P.op("dve", lambda e: e.tensor_tensor(z2[:], cv[:], x1[:], ALU.mult),
                         reads=[b_cv, b_x1], writes=[b_z2])
                else:
                    P.op("dve", lambda e: e.tensor_tensor(cv[:], cv[:], x2[:], ALU.mult),
                         reads=[b_cv, b_x2], writes=[b_cv])
                    P.op("pool", lambda e: e.tensor_tensor(z2[:], cv[:], gh[:], ALU.mult),
                         reads=[b_cv, b_gh], writes=[b_z2])
                    for q in range(4):
                        r0 = 1024 + cb * 64 + q * 16
                        dstv = T["mixT"][r0:r0 + 16, 0:LS].rearrange("c (n1 n2) -> n1 c n2", n2=64)
                        P.dma("sp", dstv, z2[:, q * 16:(q + 1) * 16, :], b_z2, reads=[b_z2])
        P.barrier()
        P.flush()


def phase_CP(P, nc, T):
    import contextlib
    ps = T["ps"]
    ident = T["ident"]
    with contextlib.ExitStack() as es:
        A_ = lambda n, shp, dt: sb(es, nc, n, shp, dt)
        FP = A_("cp_FP", [128, 4, 512], BF16)
        IP = A_("cp_IP", [128, 4, 256], BF16)
        HA = A_("cp_HA", [128, 16, 512], F32)
        HB = A_("cp_HB", [128, 16, 512], F32)
        biasT = A_("cp_biasT", [128, 16], F32)
        tp = [A_("cp_tp0", [128, 512], BF16), A_("cp_tp1", [128, 512], BF16)]
        tt = A_("cp_tt", [128, 4, 128], BF16)
        zc = [A_("cp_zc0", [128, 256], BF16), A_("cp_zc1", [128, 256], BF16)]
        x1 = [A_("cp_x10", [128, 256], BF16), A_("cp_x11", [128, 256], BF16)]
        x2 = [A_("cp_x20", [128, 256], BF16), A_("cp_x21", [128, 256], BF16)]
        gh = [A_("cp_gh0", [128, 256], BF16), A_("cp_gh1", [128, 256], BF16)]
        sgh = A_("cp_sgh", [128, 256], F32)
        z2 = A_("cp_z2", [128, 256], BF16)
        zt = A_("cp_zt", [128, 2, 128], BF16)
        Aa = A_("cp_A", [128, 512], F32)
        Bb = A_("cp_B", [128, 512], F32)
        Y = A_("cp_Y", [128, 512], BF16)
        Yt = A_("cp_Yt", [128, 4, 128], BF16)
        t1 = A_("cp_t1", [128, 256], F32)
        t2 = A_("cp_t2", [128, 256], F32)
        mx = [A_("cp_mx0", [128, 256], BF16), A_("cp_mx1", [128, 256], BF16)]
        b_FP, b_IP, b_H, b_bias = P.buf(), P.buf(), P.buf(), P.buf()
        P.dma("sp", FP[:], T["d_FP"], b_FP, writes=[b_FP])
        P.dma("sp", IP[:], T["d_IP"], b_IP, writes=[b_IP])
        P.dma("sp", biasT[:], T["hy_biasT"], b_bias, writes=[b_bias])
        b_tp, b_tt = P.bufs_n(2), P.buf()
        b_ps = P.bufs_n(8)
        it = 0
        for o in range(2):
            for t in range(8):
                s = it % 2
                it += 1
                P.dma("sp", tp[s][:], T["tapsP"][o, t * 128:(t + 1) * 128, :], b_tp[s], writes=[b_tp[s]])
                pb = ps[s][:].bitcast(BF16)
                for j in range(4):
                    P.op("pe", lambda e, pb=pb, j=j, s=s: e.transpose(
                        pb[:, j * 128:(j + 1) * 128], tp[s][:, j * 128:(j + 1) * 128], ident[:]),
                        reads=[b_tp[s]], writes=[b_ps[s]], partial=(j > 0))
                P.op("dve", lambda e, pb=pb: e.tensor_copy(tt[:].rearrange("p a b -> p (a b)"), pb[:, 0:512]),
                     reads=[b_ps[s]], writes=[b_tt])
                pj = 2 + s
                for j in range(4):
                    P.op("pe", lambda e, pj=pj, j=j: e.matmul(
                        ps[pj][:, :], tt[:, j, :], FP[:, j, :], start=(j == 0), stop=(j == 3)),
                        reads=[b_tt, b_FP], writes=[b_ps[pj]], partial=(j > 0))
                i = o * 8 + t
                for h in range(2):
                    P.op("act", lambda e, pj=pj, i=i, h=h: e.copy(HA[:, i, h * 256:(h + 1) * 256], ps[pj][:, 0:256]),
                         reads=[b_ps[pj]], writes=[b_H], partial=True)
                    P.op("act", lambda e, pj=pj, i=i, h=h: e.copy(HB[:, i, h * 256:(h + 1) * 256],
                                                                  ps[pj][:, 256:512]),
                         reads=[b_ps[pj]], writes=[b_H], partial=True)
        b_zc, b_x1, b_x2, b_gh = P.bufs_n(2), P.bufs_n(2), P.bufs_n(2), P.bufs_n(2)
        b_sgh, b_z2, b_zt, b_A, b_B, b_Y, b_Yt, b_t1, b_t2 = [P.buf() for _ in range(9)]
        b_mx = P.bufs_n(2)
        it = 0
        for sq in range(2):
            tok0 = LS + sq * LP
            for t in range(8):
                s = it % 2
                it += 1
                r = t * 128
                P.dma("sp", zc[s][:], T["ucT"][2048 + r:2048 + r + 128, tok0:tok0 + LP], b_zc[s], writes=[b_zc[s]])
                P.dma("sp", x1[s][:], T["ucT"][r:r + 128, tok0:tok0 + LP], b_x1[s], writes=[b_x1[s]])
                P.dma("sp", x2[s][:], T["ucT"][1024 + r:1024 + r + 128, tok0:tok0 + LP], b_x2[s], writes=[b_x2[s]])
                P.dma("sp", gh[s][:], T["ghT"][r:r + 128, tok0:tok0 + LP], b_gh[s], writes=[b_gh[s]])
                P.op("act", lambda e, s=s: e.activation(sgh[:], gh[s][:], AF.Silu), reads=[b_gh[s]], writes=[b_sgh])
                for o in range(2):
                    zin, b_zin = (zc[s], b_zc[s]) if o == 0 else (z2, b_z2)
                    xo, b_xo = (x1[s], b_x1[s]) if o == 0 else (x2[s], b_x2[s])
                    i = o * 8 + t
                    pb = ps[0][:].bitcast(BF16)
                    for j in range(2):
                        P.op("pe", lambda e, pb=pb, j=j, zin=zin: e.transpose(
                            pb[:, j * 128:(j + 1) * 128], zin[:, j * 128:(j + 1) * 128], ident[:]),
                            reads=[b_zin], writes=[b_ps[0]], partial=(j > 0))
                    P.op("dve", lambda e, pb=pb: e.tensor_copy(zt[:].rearrange("p a b -> p (a b)"), pb[:, 0:256]),
                         reads=[b_ps[0]], writes=[b_zt])
                    for j in range(2):
                        P.op("pe", lambda e, j=j: e.matmul(ps[1][:, :], zt[:, j, :], FP[:, j, :], start=(j == 0),
                                                           stop=(j == 1)),
                             reads=[b_zt, b_FP], writes=[b_ps[1]], partial=(j > 0))
                    P.op("dve", lambda e, i=i: e.tensor_tensor(Aa[:], ps[1][:, :], HA[:, i, :], ALU.mult),
                         reads=[b_ps[1], b_H], writes=[b_A])
                    P.op("dve", lambda e, i=i: e.tensor_tensor(Bb[:], ps[1][:, :], HB[:, i, :], ALU.mult),
                         reads=[b_ps[1], b_H], writes=[b_B])
                    P.op("pool", lambda e: e.tensor_tensor(Y[:, 0:256], Aa[:, 0:256], Bb[:, 256:512], ALU.subtract),
                         reads=[b_A, b_B], writes=[b_Y])
                    P.op("pool", lambda e: e.tensor_tensor(Y[:, 256:512], Bb[:, 0:256], Aa[:, 256:512], ALU.add),
                         reads=[b_A, b_B], writes=[b_Y], partial=True)
                    pb2 = ps[2][:].bitcast(BF16)
                    for j in range(4):
                        P.op("pe", lambda e, pb2=pb2, j=j: e.transpose(
                            pb2[:, j * 128:(j + 1) * 128], Y[:, j * 128:(j + 1) * 128], ident[:]),
                            reads=[b_Y], writes=[b_ps[2]], partial=(j > 0))
                    P.op("act", lambda e, pb2=pb2: e.copy(Yt[:].rearrange("p a b -> p (a b)"), pb2[:, 0:512]),
                         reads=[b_ps[2]], writes=[b_Yt])
                    for j in range(4):
                        P.op("pe", lambda e, j=j: e.matmul(ps[3][:, 0:256], Yt[:, j, :], IP[:, j, :], start=(j == 0),
                                                           stop=(j == 3)),
                             reads=[b_Yt, b_IP], writes=[b_ps[3]], partial=(j > 0))
                    P.op("dve", lambda e, zin=zin, i=i: e.scalar_tensor_tensor(
                        t1[:], zin[:], biasT[:, i:i + 1], ps[3][:, 0:256], ALU.mult, ALU.add),
                        reads=[b_zin, b_bias, b_ps[3]], writes=[b_t1])
                    if o == 0:
                        P.op("pool", lambda e, xo=xo: e.tensor_tensor(z2[:], t1[:], xo[:], ALU.mult),
                             reads=[b_t1, b_xo], writes=[b_z2])
                    else:
                        P.op("pool", lambda e, xo=xo: e.tensor_tensor(t2[:], t1[:], xo[:], ALU.mult),
                             reads=[b_t1, b_xo], writes=[b_t2])
                        P.op("pool", lambda e, s=s: e.tensor_tensor(mx[s][:], t2[:], sgh[:], ALU.mult),
                             reads=[b_t2, b_sgh], writes=[b_mx[s]])
                        P.dma("sp", T["mixT"][1024 + r:1024 + r + 128, tok0:tok0 + LP], mx[s][:], b_mx[s],
                              reads=[b_mx[s]])
        P.barrier()
        P.flush()


def phase_outproj(P, nc, T, l, Wdram, xsrc, mixname, dst, final):
    import contextlib
    ps = T["ps"]
    with contextlib.ExitStack() as es:
        A_ = lambda n, shp, dt: sb(es, nc, n, shp, dt)
        Wo = A_("o_W", [128, 16, 2048], BF16)
        mix = [A_("o_mix0", [128, 16, 512], BF16), A_("o_mix1", [128, 16, 512], BF16)]
        xt = [A_("o_x0", [128, 2048], F32), A_("o_x1", [128, 2048], F32)]
        xo = [A_("o_xo0", [128, 2048], F32), A_("o_xo1", [128, 2048], F32)]
        gs = A_("o_gs", [128, 2048], F32)
        gp = A_("o_gp", [128, 2048], F32)
        tmp = [A_("o_tmp0", [128, 512], F32), A_("o_tmp1", [128, 512], F32)]
        b_W, b_mix, b_xt, b_xo, b_g, b_tmp = P.buf(), P.bufs_n(2), P.bufs_n(2), P.bufs_n(2), P.buf(), P.bufs_n(2)
        b_ps = P.bufs_n(8)
        if final:
            fg = A_("o_fg", [128, 2048], F32)
            junk = A_("o_junk", [128, 2048], F32)
            st = [A_("o_st0", [128, 4], F32), A_("o_st1", [128, 4], F32)]
            b_fg, b_junk, b_st = P.buf(), P.buf(), P.bufs_n(2)
            P.dma("sp", fg[:], T["final_norm_g"][0:1, :].partition_broadcast(128), b_fg, writes=[b_fg])
        Wv = Wdram.rearrange("(k p) c -> p k c", p=128)
        for k in range(16):
            P.dma("pool", Wo[:, k, :], Wv[:, k, :], b_W, writes=[b_W], partial=(k > 0))
        P.dma("sp", gs[:], T["modv"][l][0:1, 4096:6144].partition_broadcast(128), b_g, writes=[b_g])
        P.dma("sp", gp[:], T["modv"][l][1:2, 4096:6144].partition_broadcast(128), b_g, writes=[b_g], partial=True)
        mv = T[mixname].rearrange("(k p) t -> p k t", p=128)
        ti = 0
        ei = 0
        def mix_load(ch):
            s = ch % 2
            for q in range(4):
                P.dma("sp", mix[s][:, q * 4:(q + 1) * 4, :], mv[:, q * 4:(q + 1) * 4, ch * 512:(ch + 1) * 512],
                      b_mix[s], writes=[b_mix[s]], partial=(q > 0))

        mix_load(0)
        for ch in range(NTOK // 512):
            s = ch % 2
            for tt in range(4):
                if tt == 1 and ch + 1 < NTOK // 512:
                    mix_load(ch + 1)
                tok = ch * 512 + tt * 128
                xs = ti % 2
                ti += 1
                gg = gs if tok < LS else gp
                P.dma("sp", xt[xs][:], xsrc[tok:tok + 128, :], b_xt[xs], writes=[b_xt[xs]])
                for cbk in range(4):
                    pj = ei % 8
                    tj = ei % 2
                    ei += 1
                    for k in range(16):
                        P.op("pe", lambda e, pj=pj, k=k, s=s, tt=tt, cbk=cbk: e.matmul(
                            ps[pj][:, :], mix[s][:, k, tt * 128:(tt + 1) * 128], Wo[:, k, cbk * 512:(cbk + 1) * 512],
                            start=(k == 0), stop=(k == 15)),
                            reads=[b_mix[s], b_W], writes=[b_ps[pj]], partial=(k > 0))
                    P.op("dve", lambda e, pj=pj, tj=tj, cbk=cbk, gg=gg: e.tensor_tensor(
                        tmp[tj][:], ps[pj][:, :], gg[:, cbk * 512:(cbk + 1) * 512], ALU.mult),
                        reads=[b_ps[pj], b_g], writes=[b_tmp[tj]])
                    P.op("pool", lambda e, tj=tj, xs=xs, cbk=cbk: e.tensor_tensor(
                        xo[xs][:, cbk * 512:(cbk + 1) * 512], tmp[tj][:], xt[xs][:, cbk * 512:(cbk + 1) * 512],
                        ALU.add),
                        reads=[b_tmp[tj], b_xt[xs]], writes=[b_xo[xs]], partial=(cbk > 0))
                if final:
                    P.op("act", lambda e, xs=xs: e.activation(junk[:], xo[xs][:], AF.Square,
                                                               accum_out=st[xs][:, 0:1]),
                         reads=[b_xo[xs]], writes=[b_junk, b_st[xs]])
                    P.op("act", lambda e, xs=xs: e.activation(st[xs][:, 1:2], st[xs][:, 0:1], AF.Sqrt, scale=1.0 / D,
                                                               bias=T["epsb"][:, 0:1]),
                         reads=[b_st[xs]], writes=[b_st[xs]], partial=True)
                    P.op("dve", lambda e, xs=xs: e.reciprocal(st[xs][:, 2:3], st[xs][:, 1:2]),
                         reads=[b_st[xs]], writes=[b_st[xs]], partial=True)
                    P.op("dve", lambda e, xs=xs: e.scalar_tensor_tensor(
                        xo[xs][:], xo[xs][:], st[xs][:, 2:3], fg[:], ALU.mult, ALU.mult),
                        reads=[b_xo[xs], b_st[xs], b_fg], writes=[b_xo[xs]])
                P.dma("sp", dst[tok:tok + 128, :], xo[xs][:], b_xo[xs], reads=[b_xo[xs]])
        P.barrier()
        P.flush()


def phase_E(P, nc, T):
    import contextlib
    ps = T["ps"]
    Wv = T["c_w_in"].rearrange("(k p) c -> p k c", p=128)
    for half in range(2):
        tok_tiles = [(half * 2048 + i * 128, False) for i in range(16)] + \
                    [(LS + half * 256 + i * 128, True) for i in range(2)]
        with contextlib.ExitStack() as es0:
            hT = sb(es0, nc, "e_hT", [128, 16, 2304], BF16)
            b_hT = P.buf("hT")
            with contextlib.ExitStack() as es1:
                A_ = lambda n, shp, dt: sb(es1, nc, n, shp, dt)
                xt0 = A_("e_xt0", [128, 2048], F32); xt1 = A_("e_xt1", [128, 2048], F32)
                xt2 = A_("e_xt2", [128, 2048], F32); xt3 = A_("e_xt3", [128, 2048], F32)
                junk = A_("e_junk", [128, 2048], F32); tmp = A_("e_tmp", [128, 2048], F32)
                junkb = A_("e_junkb", [128, 2048], F32); tmpb = A_("e_tmpb", [128, 2048], F32)
                st0 = A_("e_st0", [128, 4], F32); st1 = A_("e_st1", [128, 4], F32)
                hb0 = A_("e_hb0", [128, 2048], BF16); hb1 = A_("e_hb1", [128, 2048], BF16)
                As = A_("e_As", [128, 2048], F32); shs = A_("e_shs", [128, 2048], F32)
                Ap = A_("e_Ap", [128, 2048], F32); shp = A_("e_shp", [128, 2048], F32)
                bc = {"A_s": As, "sh_s": shs, "A_p": Ap, "sh_p": shp}
                b_bc = {k: P.buf(k) for k in bc}
                load_bcast_rows(P, nc, T, 1, bc, b_bc)
                work = dict(xt=[xt0, xt1, xt2, xt3], b_xt=P.bufs_n(4), junk=junk, b_junk=P.buf(), st=[st0, st1],
                            b_st=P.bufs_n(2), tmp=tmp, b_tmp=P.buf(), hb=[hb0, hb1], b_hb=P.bufs_n(2),
                            b_pst=P.bufs_n(4), junk2=[junk, junkb], b_junk2=P.bufs_n(2), tmp2=[tmp, tmpb],
                            b_tmp2=P.bufs_n(2))
                norm_transpose_half(P, nc, T, T["x1"], tok_tiles, hT, b_hT, bc, b_bc, work)
                P.barrier()
                P.flush()
            with contextlib.ExitStack() as es2:
                A_ = lambda n, shp, dt: sb(es2, nc, n, shp, dt)
                wg = [A_("e_w0", [128, 16, 512], BF16), A_("e_w1", [128, 16, 512], BF16)]
                sf = [A_("e_sf0", [128, 512], F32), A_("e_sf1", [128, 512], F32)]
                sg = [A_("e_sg0", [128, 512], BF16), A_("e_sg1", [128, 512], BF16)]
                b_hT = P.buf("hT2")
                b_wg, b_sf, b_sg = P.bufs_n(2), P.bufs_n(2), P.bufs_n(2)
                b_ps = P.bufs_n(8)
                chunks = [(c * 512, 512, half * 2048 + c * 512) for c in range(4)] + [(2048, 256, LS + half * 256)]
                psi = 0
                ei = 0
                for g in range(8):
                    s = g % 2
                    for kq in range(16):
                        P.dma("pool", wg[s][:, kq, :], Wv[:, kq, g * 512:(g + 1) * 512], b_wg[s], writes=[b_wg[s]],
                              partial=(kq > 0))
                    for (l0, n, g0) in chunks:
                        for j in range(4):
                            pj = psi % 4
                            psi += 1
                            for k in range(16):
                                P.op("pe", lambda e, pj=pj, k=k, s=s, j=j, l0=l0, n=n: e.matmul(
                                    ps[pj][:, 0:n], wg[s][:, k, j * 128:(j + 1) * 128], hT[:, k, l0:l0 + n],
                                    start=(k == 0), stop=(k == 15)),
                                    reads=[b_hT, b_wg[s]], writes=[b_ps[pj]], partial=(k > 0))
                            row = (g * 4 + j) * 128
                            si = ei % 2
                            ei += 1
                            if row < 2048:
                                P.op("act", lambda e, pj=pj, si=si, n=n: e.copy(sf[si][:, 0:n], ps[pj][:, 0:n]),
                                     reads=[b_ps[pj]], writes=[b_sf[si]])
                                P.dma("sp", T["xbT"][row:row + 128, g0:g0 + n], sf[si][:, 0:n], b_sf[si],
                                      reads=[b_sf[si]])
                            else:
                                P.op("dve", lambda e, pj=pj, si=si, n=n: e.tensor_copy(sg[si][:, 0:n], ps[pj][:, 0:n]),
                                     reads=[b_ps[pj]], writes=[b_sg[si]])
                                P.dma("sp", T["gateT"][row - 2048:row - 2048 + 128, g0:g0 + n], sg[si][:, 0:n],
                                      b_sg[si], reads=[b_sg[si]])
                P.barrier()
                P.flush()


def phase_F(P, nc, T):
    import contextlib
    ps = T["ps"]
    seqs = [(0, LS, 0, 1), (LS, 2 * LP, 1, 2)]
    with contextlib.ExitStack() as es:
        A_ = lambda n, shp, dt: sb(es, nc, n, shp, dt)
        Rs = [A_("f_R0", [128, LS], F32), A_("f_R1", [128, LS], F32)]
        Is = [A_("f_I0", [128, LS], F32), A_("f_I1", [128, LS], F32)]
        Ss = [A_("f_S0", [128, LS], F32), A_("f_S1", [128, LS], F32)]
        H_ = [A_("f_H0", [128, LS + 3], F32), A_("f_H1", [128, LS + 3], F32)]
        xc = A_("f_xc", [128, 2, LS], F32)
        xp = H_[1]
        acc = H_[0]
        xcb = A_("f_xcb", [128, 2, LS], BF16)
        gate = A_("f_gate", [128, LS], BF16)
        stage = A_("f_stage", [128, LS], BF16)
        wq = A_("f_wq", [128, 2, 2, 2, 256], BF16)
        lcw = A_("f_lcw", [128, 5, 16], F32)
        lba = A_("f_lba", [128, 2, 16], F32)
        lbx = A_("f_lbx", [128, 2, 16], F32)
        llam = A_("f_llam", [128, 2, 16], F32)
        c8 = A_("f_c8", [128, 2, 16], F32)
        stT = A_("f_stT", [128, 2, 16], F32)
        nst = A_("f_nst", [128, 2, 2, 16], F32)
        b_k = P.bufs_n(6)
        P.dma("sp", lcw[:], T["lcw"], b_k[0], writes=[b_k[0]])
        P.dma("sp", lba[:], T["lba"], b_k[1], writes=[b_k[1]])
        P.dma("sp", lbx[:], T["lbx"], b_k[2], writes=[b_k[2]])
        P.dma("sp", llam[:], T["llam"], b_k[3], writes=[b_k[3]])
        P.dma("sp", stT[:], T["stT"], b_k[4], writes=[b_k[4]])
        P.op("act", lambda e: e.activation(c8[:], llam[:], AF.Exp, scale=-1.0), reads=[b_k[3]], writes=[b_k[5]])
        P.op("act", lambda e: e.activation(c8[:], c8[:], AF.Ln, bias=1.0), reads=[b_k[5]], writes=[b_k[5]])
        P.op("dve", lambda e: e.tensor_scalar(c8[:], c8[:], -8.0, None, ALU.mult), reads=[b_k[5]], writes=[b_k[5]])
        b_lcw, b_lba, b_lbx, _, b_stT, b_c8 = b_k
        b_xc, b_xcb, b_gate, b_stage, b_wq, b_nst = [P.buf() for _ in range(6)]
        b_H = P.bufs_n(2)
        b_Rs, b_Is, b_Ss = P.bufs_n(2), P.bufs_n(2), P.bufs_n(2)
        b_xp, b_acc = b_H[1], b_H[0]
        par = 0
        b_ps = P.bufs_n(8)
        psi = 0
        for h in range(8):
            for m in range(2):
                wsrc = T["c_wa"] if m == 0 else T["c_wx"]
                for d in range(2):
                    P.dma("pool", wq[:, m, d], wsrc[d, h].rearrange("(it p) j -> p it j", p=128), b_wq,
                          writes=[b_wq], partial=(m + d > 0))
            for (tok0, L, sidx, nseg) in seqs:
                nch = max(1, L // 512)
                cw = min(512, L)
                Ls = L // nseg
                Wp = Ls + 3

                def xpv(j, nseg=nseg, Ls=Ls, Wp=Wp):
                    if nseg == 1:
                        return xp[:, j:j + Ls]
                    return xp[:, 0:nseg * Wp].rearrange("p (g w) -> p g w", g=nseg)[:, :, j:j + Ls]

                def segv(ap2d, nseg=nseg):
                    if nseg == 1:
                        return ap2d
                    return ap2d.rearrange("p (g w) -> p g w", g=nseg)
                for ct in range(2):
                    cti = h * 2 + ct
                    row = cti * 128
                    P.op("pool", lambda e: e.memset(xp[:, 0:2], 0.0), writes=[b_xp])
                    for sg in range(nseg):
                        P.op("pool", lambda e, sg=sg, Wp=Wp, Ls=Ls: e.memset(
                            xp[:, sg * Wp + Ls + 2:min((sg + 1) * Wp + 2, xp.shape[1])], 0.0),
                            writes=[b_xp], partial=True)
                        P.dma("sp", xp[:, sg * Wp + 2:sg * Wp + 2 + Ls],
                              T["xbT"][row:row + 128, tok0 + sg * Ls:tok0 + (sg + 1) * Ls], b_xp, writes=[b_xp],
                              partial=True)
                    accv = segv(acc[:, 0:L])
                    P.op("act", lambda e, cti=cti, accv=accv, src=xpv(0): e.activation(
                        accv, src, AF.Identity, scale=lcw[:, 0, cti:cti + 1],
                        bias=lcw[:, 4, cti:cti + 1]), reads=[b_xp, b_lcw], writes=[b_acc])
                    for j in (1, 2):
                        P.op("dve", lambda e, cti=cti, j=j, accv=accv, src=xpv(j): e.scalar_tensor_tensor(
                            accv, src, lcw[:, j, cti:cti + 1], accv, ALU.mult, ALU.add),
                            reads=[b_xp, b_acc, b_lcw], writes=[b_acc])
                    P.op("dve", lambda e, cti=cti, ct=ct, accv=accv, src=xpv(3), dstv=segv(xc[:, ct, 0:L]):
                         e.scalar_tensor_tensor(dstv, src, lcw[:, 3, cti:cti + 1], accv, ALU.mult, ALU.add),
                         reads=[b_xp, b_acc, b_lcw], writes=[b_xc], partial=(ct > 0))
                    P.op("act", lambda e, L=L, ct=ct: e.copy(xcb[:, ct, 0:L], xc[:, ct, 0:L]),
                         reads=[b_xc], writes=[b_xcb], partial=(ct > 0))
                for jt in range(2):
                    cti = h * 2 + jt
                    row = cti * 128
                    P.dma("sp", gate[:, 0:L], T["gateT"][row:row + 128, tok0:tok0 + L], b_gate, writes=[b_gate])
                    for d in range(2):
                        par ^= 1
                        R_, I_, S_ = Rs[par], Is[par], Ss[par]
                        b_R, b_I, b_S = b_Rs[par], b_Is[par], b_Ss[par]
                        for (m, dstT, b_dst, bias_t) in ((0, R_, b_R, lba), (1, I_, b_I, lbx)):
                            for c in range(nch):
                                pj = psi % 4
                                psi += 1
                                for it_ in range(2):
                                    P.op("pe", lambda e, pj=pj, m=m, d=d, it_=it_, jt=jt, c=c, cw=cw: e.matmul(
                                        ps[pj][:, 0:cw], wq[:, m, d, it_, jt * 128:(jt + 1) * 128],
                                        xcb[:, it_, c * cw:(c + 1) * cw], start=(it_ == 0), stop=(it_ == 1)),
                                        reads=[b_wq, b_xcb], writes=[b_ps[pj]], partial=(it_ > 0))
                                P.op("act", lambda e, pj=pj, dstT=dstT, c=c, bias_t=bias_t, d=d, cti=cti, cw=cw: e.activation(
                                    dstT[:, c * cw:(c + 1) * cw], ps[pj][:, 0:cw], AF.Sigmoid,
                                    bias=bias_t[:, d, cti:cti + 1]),
                                    reads=[b_ps[pj], b_lba, b_lbx], writes=[b_dst], partial=(c > 0))
                        P.op("pool", lambda e, L=L, jt=jt, I_=I_: e.tensor_tensor(I_[:, 0:L], I_[:, 0:L], xc[:, jt, 0:L],
                                                                                 ALU.mult),
                             reads=[b_I, b_xc], writes=[b_I])
                        P.op("act", lambda e, L=L, d=d, cti=cti, R_=R_: e.activation(
                            R_[:, 0:L], R_[:, 0:L], AF.Exp, scale=c8[:, d, cti:cti + 1]),
                            reads=[b_R, b_c8], writes=[b_R])
                        P.op("dve", lambda e, L=L, R_=R_, S_=S_: e.tensor_tensor(S_[:, 0:L], R_[:, 0:L], R_[:, 0:L], ALU.mult),
                             reads=[b_R], writes=[b_S])
                        P.op("act", lambda e, L=L, S_=S_: e.activation(S_[:, 0:L], S_[:, 0:L], AF.Sqrt, scale=-1.0, bias=1.0),
                             reads=[b_S], writes=[b_S])
                        P.op("dve", lambda e, L=L, I_=I_, S_=S_: e.tensor_tensor(I_[:, 0:L], I_[:, 0:L], S_[:, 0:L], ALU.mult),
                             reads=[b_I, b_S], writes=[b_I])
                        init = stT[:, d, cti:cti + 1] if sidx == 0 else 0.0
                        for sg in range(nseg):
                            a0, a1 = sg * Ls, (sg + 1) * Ls
                            if d == 0:
                                P.op("dve", lambda e, a0=a0, a1=a1, init=init, R_=R_, I_=I_: e.tensor_tensor_scan(
                                    H_[0][:, a0:a1], R_[:, a0:a1], I_[:, a0:a1], init, ALU.mult, ALU.add),
                                    reads=[b_R, b_I, b_stT], writes=[b_H[0]], partial=(sg > 0))
                            else:
                                P.op("dve", lambda e, a0=a0, a1=a1, init=init, R_=R_, I_=I_: e.tensor_tensor_scan(
                                    rev_ap(H_[1][:, a0:a1]), rev_ap(R_[:, a0:a1]), rev_ap(I_[:, a0:a1]), init,
                                    ALU.mult, ALU.add),
                                    reads=[b_R, b_I, b_stT], writes=[b_H[1]], partial=(sg > 0))
                            if sidx > 0:
                                col = a1 - 1 if d == 0 else a0
                                P.op("act", lambda e, d=d, col=col, sg=sg, cti=cti: e.copy(
                                    nst[:, sg, d, cti:cti + 1], H_[d][:, col:col + 1]),
                                    reads=[b_H[d]], writes=[b_nst], partial=True)
                    P.op("pool", lambda e, L=L: e.tensor_tensor(H_[0][:, 0:L], H_[0][:, 0:L], H_[1][:, 0:L], ALU.add),
                         reads=[b_H[0], b_H[1]], writes=[b_H[0]])
                    P.op("act", lambda e, L=L, S_=S_: e.activation(S_[:, 0:L], gate[:, 0:L], AF.Silu),
                         reads=[b_gate], writes=[b_S])
                    P.op("dve", lambda e, L=L, S_=S_: e.tensor_tensor(stage[:, 0:L], H_[0][:, 0:L], S_[:, 0:L], ALU.mult),
                         reads=[b_H[0], b_S], writes=[b_stage])
                    P.dma("sp", T["mix1T"][row:row + 128, tok0:tok0 + L], stage[:, 0:L], b_stage, reads=[b_stage])
        P.dma("sp", T["ns"], nst[:].rearrange("p a b c -> p (a b c)"), b_nst, reads=[b_nst])
        P.barrier()
        P.flush()


def _bf16(a):
    return np.asarray(a, dtype=np.float32).astype(ml_dtypes.bfloat16)


def _fft_consts():
    N = 2 * LS
    n1 = np.arange(128)[:, None]
    k1 = np.arange(128)[None, :]
    ang = 2 * np.pi * n1 * (k1 + 0.5) / 128
    F1 = np.concatenate([np.cos(ang), -np.sin(ang)], 1)
    n2 = np.arange(64)[:, None, None]
    k1_ = np.arange(128)[None, :, None]
    k2 = np.arange(32)[None, None, :]
    ang = 2 * np.pi * n2 * (k1_ + 128 * k2 + 0.5) / N
    gr, gi = np.cos(ang), -np.sin(ang)
    G = np.zeros((128, 128, 64))
    G[0:64, :, 0:32] = gr
    G[64:128, :, 0:32] = -gi
    G[0:64, :, 32:64] = gi
    G[64:128, :, 32:64] = gr
    Grr = np.concatenate([G[:, :, 0:32], G[:, :, 0:32]], 2)
    Gii = np.concatenate([G[:, :, 32:64], G[:, :, 32:64]], 2)
    k2 = np.arange(32)[:, None]
    n2 = np.arange(64)[None, :]
    ang = 2 * np.pi * n2 * k2 / 64
    mr, mi = np.cos(ang), np.sin(ang)
    M1 = np.zeros((64, 128))
    M1[0:32, 0:64] = mr
    M1[32:64, 0:64] = -mi
    M1[0:32, 64:128] = mi
    M1[32:64, 64:128] = mr
    M1p = np.concatenate([M1[32:64], -M1[0:32]], 0)
    k1 = np.arange(128)[:, None, None]
    n2 = np.arange(64)[None, :, None]
    n1 = np.arange(64)[None, None, :]
    ang = 2 * np.pi * (k1 + 0.5) * (n1 / 128 + n2 / N)
    T2r = (2.0 / N) * np.cos(ang)
    T2in = -(2.0 / N) * np.sin(ang)
    Np = 2 * LP
    n = np.arange(Np)[:, None]
    k = np.arange(LP)[None, :]
    ang = 2 * np.pi * n * (k + 0.5) / Np
    FPm = np.concatenate([np.cos(ang), -np.sin(ang)], 1)
    FP = FPm.reshape(4, 128, 512).transpose(1, 0, 2)
    k = np.arange(LP)[:, None]
    n = np.arange(LP)[None, :]
    ang = 2 * np.pi * n * (k + 0.5) / Np
    IPm = np.concatenate([(2.0 / Np) * np.cos(ang), -(2.0 / Np) * np.sin(ang)], 0)
    IP = IPm.reshape(4, 128, 256).transpose(1, 0, 2)
    return dict(F1=F1, G=G, Grr=Grr, Gii=Gii, M1=M1, M1p=M1p, T2r=T2r, T2in=T2in, FP=FP, IP=IP)


def make_consts():
    c = {}
    c["ident"] = _bf16(np.eye(128))
    pos = np.arange(LS)
    row = (pos // 64).astype(np.float64)
    col = (pos % 64).astype(np.float64)
    nf = 16
    inv = 10000.0 ** (-np.arange(nf, dtype=np.float64) / nf)
    cos = np.zeros((64, LS))
    sin = np.zeros((64, LS))
    for d in range(64):
        halfi = d // 32
        w = d % 32
        f = w % 16
        p = row if halfi == 0 else col
        ang = p * inv[f]
        cos[d] = np.cos(ang)
        sin[d] = -np.sin(ang) if w < 16 else np.sin(ang)
    c["rope_cos"] = np.tile(cos, (2, 1)).astype(np.float32)
    c["rope_sin"] = np.tile(sin, (2, 1)).astype(np.float32)
    Pm = np.zeros((128, 128))
    for m in range(128):
        hh = m // 64
        d = m % 64
        w = d % 32
        partner = d + 16 if w < 16 else d - 16
        Pm[hh * 64 + partner, m] = 1.0
    c["ropeP"] = _bf16(Pm)
    c["epsb"] = np.full((128, 1), EPS, np.float32)
    for nm, Lf in (("S", LS), ("P", LP)):
        t = (np.arange(Lf, dtype=np.float32) / np.float32(Lf)).astype(np.float64)
        freqs = np.linspace(1e-4, 15.0, 16).astype(np.float32).astype(np.float64)
        ang = 2.0 * np.pi * t[:, None] * freqs[None, :]
        z = np.concatenate([t[:, None], np.cos(ang), -np.sin(ang)], 1)
        c["zemb" + nm] = np.ascontiguousarray(z.T).astype(np.float32)
        c["tpos" + nm] = np.ascontiguousarray(np.tile(t[None, :], (128, 1))).astype(np.float32)
    c.update({k: _bf16(v) for k, v in _fft_consts().items()})
    si = np.arange(128)[:, None]
    qi = np.arange(128)[None, :]
    c["amask"] = _bf16(np.concatenate([(si >= qi), np.ones((128, 128)), (si <= qi)], 1).astype(np.float32))
    return c


CONST_SPECS = {
    "ident": ([128, 128], BF16), "rope_cos": ([128, LS], F32), "rope_sin": ([128, LS], F32),
    "ropeP": ([128, 128], BF16), "epsb": ([128, 1], F32), "amask": ([128, 384], BF16),
    "zembS": ([33, LS], F32), "zembP": ([33, LP], F32), "tposS": ([128, LS], F32), "tposP": ([128, LP], F32),
    "F1": ([128, 256], BF16), "G": ([128, 128, 64], BF16), "Grr": ([128, 128, 64], BF16),
    "Gii": ([128, 128, 64], BF16), "M1": ([64, 128], BF16), "M1p": ([64, 128], BF16),
    "T2r": ([128, 64, 64], BF16), "T2in": ([128, 64, 64], BF16),
    "FP": ([128, 4, 512], BF16), "IP": ([128, 4, 256], BF16),
}

IN_SPECS = {
    "x": [NTOK, D], "ck": [512, 128], "cv": [512, 128], "st": [2, D], "cvecT": [128, 32],
    "mod_w": [2, D, 3 * D], "mod_b": [2, 3 * D], "norm_g": [2, D], "final_norm_g": [1, D],
    "a_w_in": [D, 6400], "a_w_out": [D, D], "a_sink": [1, 16],
    "hsw": [128, 4, 24], "hy_w1": [33, 64], "hy_w2": [64, 64], "hy_bb": [64, 2], "hy_w3": [64, 4096],
    "hy_decT": [128, 32], "hy_biasT": [128, 16], "hy_bias": [2, 1024],
    "c_w_in": [D, 2 * D], "c_w_out": [D, D], "c_wa": [2, 8, 256, 256], "c_wx": [2, 8, 256, 256],
    "lcw": [128, 5, 16], "lba": [128, 2, 16], "lbx": [128, 2, 16], "llam": [128, 2, 16], "stT": [128, 2, 16],
}

SCRATCH = {
    "modv": ([2, 2, 3 * D], F32),
    "qT": ([1024, NTOK], BF16), "kT": ([2, 128, NTOK], BF16), "gaT": ([1024, NTOK], BF16),
    "hyT": ([3072, NTOK], BF16), "ghT": ([1024, NTOK], BF16), "vtok": ([NTOK, 128], BF16),
    "mixT": ([2048, NTOK], BF16),
    "ucT": ([3072, NTOK], BF16), "tapsS": ([2, 1024, 2 * LS], BF16), "tapsP": ([2, 1024, 2 * LP], BF16),
    "Hs": ([2, 16, 2, 64, 64 * 128], BF16),
    "x1": ([NTOK, D], F32), "xbT": ([D, NTOK], F32), "gateT": ([D, NTOK], BF16), "mix1T": ([D, NTOK], BF16),
}

OUT_SPECS = {"y": [NTOK, D], "nk": [512, 128], "nv": [512, 128], "ns": [128, 64]}


def build_program(debug_scratch=(), stop_after=None, skip=(), ext_in=()):
    nc = bass.Bass("TRN2", target_bir_lowering=False)
    T = {}
    for name, shp in IN_SPECS.items():
        T[name] = nc.dram_tensor(name, shp, F32, kind="ExternalInput").ap()
    for name, (shp, dt) in CONST_SPECS.items():
        T["d_" + name] = nc.dram_tensor("c_" + name, shp, dt, kind="ExternalInput").ap()
    for name, shp in OUT_SPECS.items():
        T[name] = nc.dram_tensor(name, shp, F32, kind="ExternalOutput").ap()
    for name, (shp, dt) in SCRATCH.items():
        kind = "ExternalOutput" if name in debug_scratch else ("ExternalInput" if name in ext_in else "Internal")
        T[name] = nc.dram_tensor("s_" + name, shp, dt, kind=kind).ap()
    T["rope_cos"] = T["d_rope_cos"]
    T["rope_sin"] = T["d_rope_sin"]
    import contextlib
    with contextlib.ExitStack() as es:
        sems = [es.enter_context(nc.semaphore("sem%d" % i)) for i in range(60)]
        T["ps"] = [es.enter_context(nc.psum_tensor("ps%d" % i, [128, 512], F32)) for i in range(8)]
        ident = es.enter_context(nc.sbuf_tensor("ident", [128, 128], BF16))
        ropeP = es.enter_context(nc.sbuf_tensor("ropeP", [128, 128], BF16))
        epsb = es.enter_context(nc.sbuf_tensor("epsb", [128, 1], F32))
        T["ident"], T["ropeP"], T["epsb"] = ident, ropeP, epsb
        P = Prog(nc, sems)
        b_c = P.bufs_n(3)
        P.dma("sp", ident[:], T["d_ident"], b_c[0], writes=[b_c[0]])
        P.dma("sp", ropeP[:], T["d_ropeP"], b_c[1], writes=[b_c[1]])
        P.dma("sp", epsb[:], T["d_epsb"], b_c[2], writes=[b_c[2]])
        P.barrier()
        if "M" not in skip:
            phase_M(P, nc, T)
        if stop_after != "M":
            if "A" not in skip:
                phase_A(P, nc, T)
        if stop_after not in ("M", "A") and "B" not in skip:
            phase_B(P, nc, T)
        if stop_after not in ("M", "A", "B"):
            if "C0" not in skip:
                phase_C0(P, nc, T)
            if "CF" not in skip:
                phase_CF(P, nc, T, LP, T["tapsP"], T["d_zembP"], T["d_tposP"])
                phase_CF(P, nc, T, LS, T["tapsS"], T["d_zembS"], T["d_tposS"])
        if stop_after not in ("M", "A", "B", "CF"):
            if "CT" not in skip:
                phase_CT(P, nc, T)
            if "CS" not in skip:
                phase_CS(P, nc, T)
            if "CP" not in skip:
                phase_CP(P, nc, T)
        if stop_after not in ("M", "A", "B", "CF", "C"):
            if "D" not in skip:
                phase_outproj(P, nc, T, 0, T["a_w_out"], T["x"], "mixT", T["x1"], False)
        if stop_after not in ("M", "A", "B", "CF", "C", "D"):
            if "E" not in skip:
                phase_E(P, nc, T)
        if stop_after not in ("M", "A", "B", "CF", "C", "D", "E"):
            if "F" not in skip:
                phase_F(P, nc, T)
        if stop_after not in ("M", "A", "B", "CF", "C", "D", "E", "F"):
            if "G" not in skip:
                phase_outproj(P, nc, T, 1, T["c_w_out"], T["x1"], "mix1T", T["y"], True)
        P.barrier()
        P.flush()
    return nc


def make_in_maps(inputs):
    consts = make_consts()
    f = lambda a: np.ascontiguousarray(np.asarray(a, dtype=np.float32))
    x_prompt, x_sample = f(inputs["x_prompt"]), f(inputs["x_sample"])
    ck, cv = f(inputs["cache_k"]), f(inputs["cache_v"])
    st = f(inputs["state_lru"])
    c, c_ctx = f(inputs["c"]), f(inputs["c_ctx"])
    shared = {
        "mod_w": f(inputs["mod_w"]), "mod_b": f(inputs["mod_b"]), "norm_g": f(inputs["norm_g"]),
        "final_norm_g": f(inputs["final_norm_g"]).reshape(1, D),
        "a_w_in": f(inputs["a_w_in"])[0], "a_w_out": f(inputs["a_w_out"])[0], "a_sink": f(inputs["a_sink"]),
        "hsw": np.ascontiguousarray(np.concatenate([f(inputs["hy_short_w"])[0], f(inputs["hy_short_b"])[0][None]], 0)
                                    .reshape(4, 24, 128).transpose(2, 0, 1)),
        "hy_w1": f(inputs["hy_w1"])[0], "hy_w2": f(inputs["hy_w2"])[0],
        "hy_bb": np.ascontiguousarray(np.stack([f(inputs["hy_b1"])[0], f(inputs["hy_b2"])[0]], 1)),
        "hy_w3": f(inputs["hy_w3"])[0],
        "hy_decT": np.ascontiguousarray(f(inputs["hy_decay"])[0].reshape(32, 128).T),
        "hy_biasT": np.ascontiguousarray(f(inputs["hy_bias"])[0].reshape(16, 128).T),
        "hy_bias": f(inputs["hy_bias"])[0],
        "c_w_in": f(inputs["c_w_in"])[0], "c_w_out": f(inputs["c_w_out"])[0],
        "c_wa": f(inputs["c_wa"])[0], "c_wx": f(inputs["c_wx"])[0],
        "lcw": np.ascontiguousarray(np.concatenate([f(inputs["c_conv_w"])[0], f(inputs["c_conv_b"])[0][None]], 0)
                                    .reshape(5, 16, 128).transpose(2, 0, 1)),
        "lba": np.ascontiguousarray(f(inputs["c_ba"])[0].reshape(2, 16, 128).transpose(2, 0, 1)),
        "lbx": np.ascontiguousarray(f(inputs["c_bx"])[0].reshape(2, 16, 128).transpose(2, 0, 1)),
        "llam": np.ascontiguousarray(f(inputs["c_lambda"])[0].reshape(2, 16, 128).transpose(2, 0, 1)),
    }
    for k, v in consts.items():
        shared["c_" + k] = v
    maps = []
    for i in range(NCORES):
        m = dict(shared)
        m["x"] = np.ascontiguousarray(np.concatenate([x_sample[i], x_prompt[2 * i], x_prompt[2 * i + 1]], 0))
        m["ck"] = np.ascontiguousarray(ck[i, 0].reshape(512, 128))
        m["cv"] = np.ascontiguousarray(cv[i, 0].reshape(512, 128))
        m["st"] = np.ascontiguousarray(st[i, 0])
        m["stT"] = np.ascontiguousarray(st[i, 0].reshape(2, 16, 128).transpose(2, 0, 1))
        cvec = np.stack([c[i], c_ctx], 0)
        m["cvecT"] = np.ascontiguousarray(cvec.reshape(2, 16, 128).transpose(2, 1, 0).reshape(128, 32))
        maps.append(m)
    return maps


def kernel(**inputs):
    nc = build_program()
    maps = make_in_maps(inputs)
    res = run_bass_kernel_spmd(nc, maps, core_ids=list(range(NCORES)))
    R = res.results
    y_s = np.stack([R[i]["y"][:LS] for i in range(NCORES)], 0)
    y_p = np.concatenate([R[i]["y"][LS:].reshape(2, LP, D) for i in range(NCORES)], 0)
    nk = np.concatenate([R[i]["nk"].reshape(2, 1, LP, 2, 64) for i in range(NCORES)], 0)
    nv = np.concatenate([R[i]["nv"].reshape(2, 1, LP, 2, 64) for i in range(NCORES)], 0)
    ns = np.concatenate([R[i]["ns"].reshape(128, 2, 2, 16).transpose(1, 2, 3, 0).reshape(2, 1, 2, D)
                         for i in range(NCORES)], 0)
    return (y_p.astype(np.float32), y_s.astype(np.float32), nk.astype(np.float32), nv.astype(np.float32),
            ns.astype(np.float32))
```

```python
import numpy as np
import ml_dtypes
import concourse.bass as bass
import concourse.mybir as mybir
from concourse.bass_utils import run_bass_kernel_spmd

F32, BF16 = mybir.dt.float32, mybir.dt.bfloat16
AF = mybir.ActivationFunctionType
ALU = mybir.AluOpType
AX = mybir.AxisListType

D = 2048
LS = 4096
LP = 256
NPS = 2
NTOK = LS + NPS * LP
EPS = 1e-6
NCORES = 8
DBG = {}


class Sem:
    def __init__(self, h):
        self.h = h
        self.n = 0


class Buf:
    def __init__(self, name=""):
        self.w = []
        self.r = []
        self.gen_r = []
        self.name = name
        self.sem = None


class Prog:
    ENG = ("pe", "act", "dve", "pool", "sp")
    COMPUTE = ("pe", "act", "dve", "pool")

    def __init__(self, nc, handles):
        self.nc = nc
        self.q = {e: [] for e in self.ENG}
        hs = list(handles)
        self.csem = {e: Sem(hs.pop()) for e in self.COMPUTE}
        self.bar = Sem(hs.pop())
        self.dpool = [Sem(h) for h in hs]
        nsw = len(self.dpool) // 3
        self.dfree = {"pool": self.dpool[:nsw], "sp": self.dpool[nsw:]}
        self.waited = {e: {} for e in self.ENG}
        self.pending = []
        self.bufs = []
        self.nops = {e: 0 for e in self.COMPUTE}
        self.entries = {e: {} for e in self.COMPUTE}
        self.sig_idx = {e: [] for e in self.COMPUTE}
        self.sig_cnt = {e: [] for e in self.COMPUTE}

    def buf(self, name=""):
        b = Buf(name)
        self.bufs.append(b)
        return b

    def bufs_n(self, n, name=""):
        return [self.buf(name + str(i)) for i in range(n)]

    def _resolve(self, tok):
        import bisect
        _, eng, idx = tok
        si = self.sig_idx[eng]
        p = bisect.bisect_left(si, idx)
        if p < len(si):
            return self.csem[eng], self.sig_cnt[eng][p]
        ent = self.entries[eng][idx]
        sem = self.csem[eng]
        sem.n += 1
        ent[1] = sem.h
        si.append(idx)
        self.sig_cnt[eng].append(sem.n)
        return sem, sem.n

    def _wait(self, eng, tok):
        if tok[0] == "op":
            if tok[1] == eng and eng == "pe":
                return
            sem, tgt = self._resolve(tok)
        else:
            sem, tgt, _ = tok
        if self.waited[eng].get(id(sem), 0) >= tgt:
            return
        self.waited[eng][id(sem)] = tgt
        self.q[eng].append([lambda e, h=sem.h, t=tgt: e.wait_ge(h, t), None])

    def _wait_many(self, eng, toks):
        best = {}
        for t in toks:
            if t[0] == "op":
                k = ("op", t[1])
                if k not in best or best[k][2] < t[2]:
                    best[k] = t
            else:
                k = id(t[0])
                if k not in best or best[k][1] < t[1]:
                    best[k] = t
        for t in best.values():
            self._wait(eng, t)

    def _hazards(self, eng, reads, writes, partial):
        toks = []
        for b in reads:
            toks += b.w
        for b in writes:
            if partial == "nowaw":
                if b.r:
                    b.gen_r = b.r
                    b.r = []
                    b.w = []
                toks += b.gen_r
            else:
                toks += b.w
                toks += b.r
                toks += b.gen_r
        self._wait_many(eng, toks)

    def _commit(self, tok, reads, writes, partial):
        for b in reads:
            b.r.append(tok)
        for b in writes:
            if partial:
                b.w.append(tok)
                if partial != "nowaw":
                    b.r = []
            else:
                b.w = [tok]
                b.r = []
                b.gen_r = []

    def op(self, eng, fn, reads=(), writes=(), partial=False):
        self._hazards(eng, reads, writes, partial)
        idx = self.nops[eng]
        self.nops[eng] += 1
        ent = [fn, None]
        self.entries[eng][idx] = ent
        self.q[eng].append(ent)
        tok = ("op", eng, idx)
        self._commit(tok, reads, writes, partial)
        return tok

    def dma(self, eng, out, in_, sbuf_buf, reads=(), writes=(), partial=False):
        self._hazards(eng, reads, writes, partial)
        if sbuf_buf.sem is None:
            sbuf_buf.sem = {}
        if eng not in sbuf_buf.sem:
            sbuf_buf.sem[eng] = self.dfree[eng].pop()
        sem = sbuf_buf.sem[eng]
        sem.n += 16
        tok = (sem, sem.n, "dma")
        self.q[eng].append([lambda e, o=out, i=in_: e.dma_start(out=o, in_=i), sem.h, 16])
        self._commit(tok, reads, writes, partial)
        self.pending.append(tok)
        return tok

    def barrier(self):
        for e in self.COMPUTE:
            if self.nops[e] > 0:
                self._wait("sp", ("op", e, self.nops[e] - 1))
        self._wait_many("sp", self.pending)
        self.pending = []
        self.bar.n += 1
        k = self.bar.n
        self.q["sp"].append([lambda e, h=self.bar.h: e.sem_inc(h, 1), None])
        for e in self.COMPUTE:
            self._wait(e, (self.bar, k, "bar"))
        for b in self.bufs:
            if b.sem is not None:
                for en, sm in b.sem.items():
                    self.dfree[en].append(sm)
                b.sem = None
            b.w = []
            b.r = []
            b.gen_r = []
        self.bufs = []

    def flush(self):
        nc = self.nc
        q = self.q

        def run(e, lst):
            for ent in lst:
                ins = ent[0](e)
                if ent[1] is not None:
                    ins.then_inc(ent[1], ent[2] if len(ent) > 2 else 1)

        with nc.Block() as blk:
            @blk.tensor
            def _(e):
                run(e, q["pe"])

            @blk.scalar
            def _(e):
                run(e, q["act"])

            @blk.vector
            def _(e):
                run(e, q["dve"])

            @blk.gpsimd
            def _(e):
                run(e, q["pool"])

            @blk.sync
            def _(e):
                run(e, q["sp"])
        self.q = {e: [] for e in self.ENG}
        for e in self.COMPUTE:
            self.entries[e] = {}


_UID = [0]


def sb(es, nc, name, shape, dt):
    _UID[0] += 1
    return es.enter_context(nc.sbuf_tensor("%s_%d" % (name, _UID[0]), shape, dt))


def rev_ap(ap):
    a = [list(p) for p in ap.ap]
    step, cnt = a[-1]
    off = ap.offset + step * (cnt - 1)
    a[-1] = [-step, cnt]
    return bass.AP(ap.tensor, off, a)


def phase_M(P, nc, T):
    with (
        nc.sbuf_tensor("m_cT", [128, 32], F32) as cT,
        nc.sbuf_tensor("m_sg", [128, 32], F32) as sg,
        nc.sbuf_tensor("m_sT", [128, 32], BF16) as sT,
        nc.sbuf_tensor("m_w0", [128, 3072], BF16) as w0,
        nc.sbuf_tensor("m_w1", [128, 3072], BF16) as w1,
        nc.sbuf_tensor("m_mrow", [2, 6144], F32) as mrow,
        nc.sbuf_tensor("m_brow", [2, 6144], F32) as brow,
        nc.sbuf_tensor("m_ng", [2, 2048], F32) as ng,
        nc.sbuf_tensor("m_orow", [2, 6144], F32) as orow,
    ):
        ps = T["ps"]
        b_cT, b_sT = P.buf(), P.buf()
        b_w = P.bufs_n(2)
        wt = [w0, w1]
        b_ps = P.bufs_n(6)
        b_mrow, b_brow, b_ng, b_orow = P.buf(), P.buf(), P.buf(), P.buf()
        P.dma("sp", cT[:], T["cvecT"], b_cT, writes=[b_cT])
        P.op("act", lambda e: e.activation(sg[:], cT[:], AF.Sigmoid), reads=[b_cT], writes=[b_sT])
        P.op("dve", lambda e: e.tensor_tensor(sT[:], sg[:], cT[:], ALU.mult), reads=[b_cT, b_sT], writes=[b_sT])
        cnt = 0
        for l in range(2):
            P.dma("sp", brow[:], T["mod_b"][l:l + 1, :].partition_broadcast(2), b_brow, writes=[b_brow])
            P.dma("sp", ng[:], T["norm_g"][l:l + 1, :].partition_broadcast(2), b_ng, writes=[b_ng])
            for hf in range(2):
                for k in range(16):
                    s = cnt % 2
                    cnt += 1
                    P.dma("pool", wt[s][:], T["mod_w"][l, k * 128:(k + 1) * 128, hf * 3072:(hf + 1) * 3072],
                          b_w[s], writes=[b_w[s]])
                    for j in range(6):
                        P.op("pe", lambda e, s=s, j=j, k=k: e.matmul(
                            ps[j][0:2, :], sT[:, 2 * k:2 * k + 2], wt[s][:, j * 512:(j + 1) * 512],
                            start=(k == 0), stop=(k == 15)),
                            reads=[b_w[s], b_sT], writes=[b_ps[j]], partial=(k > 0))
                for j in range(6):
                    c0 = hf * 3072 + j * 512
                    P.op("dve", lambda e, j=j, c0=c0: e.tensor_tensor(
                        mrow[:, c0:c0 + 512], ps[j][0:2, :], brow[:, c0:c0 + 512], ALU.add),
                        reads=[b_ps[j], b_brow], writes=[b_mrow], partial=True)
            P.op("dve", lambda e: e.scalar_tensor_tensor(
                orow[:, 0:2048], mrow[:, 2048:4096], 1.0, ng[:], ALU.add, ALU.mult),
                reads=[b_mrow, b_ng], writes=[b_orow])
            P.op("dve", lambda e: e.tensor_copy(orow[:, 2048:4096], mrow[:, 0:2048]),
                 reads=[b_mrow], writes=[b_orow], partial=True)
            P.op("dve", lambda e: e.tensor_copy(orow[:, 4096:6144], mrow[:, 4096:6144]),
                 reads=[b_mrow], writes=[b_orow], partial=True)
            P.dma("sp", T["modv"][l], orow[:], b_orow, reads=[b_orow])
        P.barrier()
        P.flush()


def load_bcast_rows(P, nc, T, l, tiles, bufs):
    modv = T["modv"]
    for key, (j, r) in (("A_s", (0, 0)), ("sh_s", (1, 0)), ("A_p", (0, 1)), ("sh_p", (1, 1))):
        P.dma("sp", tiles[key][:], modv[l][r:r + 1, j * 2048:(j + 1) * 2048].partition_broadcast(128),
              bufs[key], writes=[bufs[key]])


def norm_transpose_half(P, nc, T, xsrc, tok_tiles, hT, b_hT, bc, b_bc, work):
    ps = T["ps"]
    ident = T["ident"]
    xt, b_xt = work["xt"], work["b_xt"]
    st, b_st = work["st"], work["b_st"]
    hb, b_hb = work["hb"], work["b_hb"]
    b_pst = work["b_pst"]

    def front(i):
        toff, isp = tok_tiles[i]
        s = i % 2
        x4 = i % len(xt)
        P.dma("sp", xt[x4][:], xsrc[toff:toff + 128, :], b_xt[x4], writes=[b_xt[x4]])
        jk, bjk = work["junk2"][s], work["b_junk2"][s]
        P.op("act", lambda e, s=s, jk=jk, x4=x4: e.activation(jk[:], xt[x4][:], AF.Square, accum_out=st[s][:, 0:1]),
             reads=[b_xt[x4]], writes=[bjk, b_st[s]])
        P.op("act", lambda e, s=s: e.activation(st[s][:, 1:2], st[s][:, 0:1], AF.Sqrt, scale=1.0 / D,
                                                 bias=T["epsb"][:, 0:1]),
             reads=[b_st[s]], writes=[b_st[s]], partial=True)
        P.op("dve", lambda e, s=s: e.reciprocal(st[s][:, 2:3], st[s][:, 1:2]), reads=[b_st[s]], writes=[b_st[s]],
             partial=True)
        A = bc["A_p" if isp else "A_s"]
        SH = bc["sh_p" if isp else "sh_s"]
        bA = b_bc["A_p" if isp else "A_s"]
        bS = b_bc["sh_p" if isp else "sh_s"]
        tm, btm = work["tmp2"][s], work["b_tmp2"][s]
        P.op("dve", lambda e, s=s, A=A, tm=tm, x4=x4: e.scalar_tensor_tensor(tm[:], xt[x4][:], st[s][:, 2:3], A[:],
                                                                             ALU.mult, ALU.mult),
             reads=[b_xt[x4], b_st[s], bA], writes=[btm])
        P.op("pool", lambda e, s=s, SH=SH, tm=tm: e.tensor_tensor(hb[s][:], tm[:], SH[:], ALU.add),
             reads=[btm, bS], writes=[b_hb[s]])

    def back(i):
        s = i % 2
        for half in range(2):
            pb = ps[2 * s + half]
            bp = b_pst[2 * s + half]
            pbv = pb[:].bitcast(BF16)
            for kk in range(8):
                k = half * 8 + kk
                P.op("pe", lambda e, pbv=pbv, kk=kk, k=k, s=s: e.transpose(
                    pbv[:, kk * 128:(kk + 1) * 128], hb[s][:, k * 128:(k + 1) * 128], ident[:]),
                    reads=[b_hb[s]], writes=[bp], partial=(kk > 0))
            dst = hT[:, half * 8:(half + 1) * 8, i * 128:(i + 1) * 128]
            src = pbv.rearrange("p (k t) -> p k t", k=8)
            if half == 0:
                P.op("act", lambda e, dst=dst, src=src: e.copy(dst, src), reads=[bp], writes=[b_hT], partial="nowaw")
            else:
                P.op("dve", lambda e, dst=dst, src=src: e.tensor_copy(dst, src), reads=[bp], writes=[b_hT],
                     partial="nowaw")

    n = len(tok_tiles)
    front(0)
    for i in range(n):
        if i + 1 < n:
            front(i + 1)
        back(i)


def phase_A(P, nc, T, debug=False):
    ps = T["ps"]
    W = T["a_w_in"]
    groups = []
    for g in range(2):
        groups.append(("q", [("qT", (g * 4 + j) * 128, (g * 4 + j) * 128) for j in range(4)]))
    groups.append(("k", [("kT0", 0, 1024), ("kT1", 0, 1088)]))
    for g in range(2):
        groups.append(("ga", [("gaT", (g * 4 + j) * 128, 1280 + (g * 4 + j) * 128) for j in range(4)]))
    for g in range(6):
        groups.append(("hy", [("hyT", (g * 4 + j) * 128, 2304 + (g * 4 + j) * 128) for j in range(4)]))
    for g in range(2):
        groups.append(("gh", [("ghT", (g * 4 + j) * 128, 5376 + (g * 4 + j) * 128) for j in range(4)]))

    Wv = W.rearrange("(k p) c -> p k c", p=128)
    for half in range(DBG.get('halves', 2)):
        tok_tiles = [(half * 2048 + i * 128, False) for i in range(16)] + \
                    [(LS + half * 256 + i * 128, True) for i in range(2)]
        import contextlib
        with contextlib.ExitStack() as es0:
            hT = sb(es0, nc, "a_hT", [128, 16, 2304], BF16)
            b_hT = P.buf("hT")
            with contextlib.ExitStack() as es1:
                A_ = lambda n, shp, dt: sb(es1, nc, n, shp, dt)
                xt0 = A_("a_xt0", [128, 2048], F32); xt1 = A_("a_xt1", [128, 2048], F32)
                xt2 = A_("a_xt2", [128, 2048], F32); xt3 = A_("a_xt3", [128, 2048], F32)
                junk = A_("a_junk", [128, 2048], F32); tmp = A_("a_tmp", [128, 2048], F32)
                junkb = A_("a_junkb", [128, 2048], F32); tmpb = A_("a_tmpb", [128, 2048], F32)
                st0 = A_("a_st0", [128, 4], F32); st1 = A_("a_st1", [128, 4], F32)
                hb0 = A_("a_hb0", [128, 2048], BF16); hb1 = A_("a_hb1", [128, 2048], BF16)
                As = A_("a_As", [128, 2048], F32); shs = A_("a_shs", [128, 2048], F32)
                Ap = A_("a_Ap", [128, 2048], F32); shp = A_("a_shp", [128, 2048], F32)
                bc = {"A_s": As, "sh_s": shs, "A_p": Ap, "sh_p": shp}
                b_bc = {k: P.buf(k) for k in bc}
                load_bcast_rows(P, nc, T, 0, bc, b_bc)
                work = dict(xt=[xt0, xt1, xt2, xt3], b_xt=P.bufs_n(4), junk=junk, b_junk=P.buf(), st=[st0, st1],
                            b_st=P.bufs_n(2), tmp=tmp, b_tmp=P.buf(), hb=[hb0, hb1], b_hb=P.bufs_n(2),
                            b_pst=P.bufs_n(4), junk2=[junk, junkb], b_junk2=P.bufs_n(2), tmp2=[tmp, tmpb],
                            b_tmp2=P.bufs_n(2))
                norm_transpose_half(P, nc, T, T["x"], tok_tiles, hT, b_hT, bc, b_bc, work)
                P.barrier()
                P.flush()
            if DBG.get('norm_only'):
                continue
            with contextlib.ExitStack() as es2:
                A_ = lambda n, shp, dt: sb(es2, nc, n, shp, dt)
                wg0 = A_("a_w0", [128, 16, 512], BF16); wg1 = A_("a_w1", [128, 16, 512], BF16)
                wkv = A_("a_wkv", [128, 16, 256], BF16)
                cos_t = A_("a_cos", [128, 2048], F32); sin_t = A_("a_sin", [128, 2048], F32)
                stg0 = A_("a_stg0", [128, 512], BF16); stg1 = A_("a_stg1", [128, 512], BF16)
                stg2 = A_("a_stg2", [128, 512], BF16); stg3 = A_("a_stg3", [128, 512], BF16)
                qs0 = A_("a_qs0", [128, 512], BF16); qs1 = A_("a_qs1", [128, 512], BF16)
                t1 = A_("a_t1", [128, 512], F32); t2 = A_("a_t2", [128, 512], F32)
                kvf0 = A_("a_kvf0", [128, 256], F32); kvf1 = A_("a_kvf1", [128, 256], F32)
                vb0 = A_("a_vb0", [128, 128], BF16); vb1 = A_("a_vb1", [128, 128], BF16)
                b_hT = P.buf("hT2")
                wg = [wg0, wg1]
                b_wg = P.bufs_n(2)
                b_wkv = P.buf()
                b_cos, b_sin = P.buf(), P.buf()
                stg = [stg0, stg1, stg2, stg3]
                b_stg = P.bufs_n(4)
                qs = [qs0, qs1]
                b_qs = P.bufs_n(2)
                b_t1, b_t2 = P.buf(), P.buf()
                kvf = [kvf0, kvf1]
                b_kvf = P.bufs_n(2)
                vb = [vb0, vb1]
                b_vb = P.bufs_n(2)
                b_ps = P.bufs_n(8)
                P.dma("sp", cos_t[:], T["rope_cos"][:, half * 2048:(half + 1) * 2048], b_cos, writes=[b_cos])
                P.dma("sp", sin_t[:], T["rope_sin"][:, half * 2048:(half + 1) * 2048], b_sin, writes=[b_sin])
                for kq in range(16):
                    P.dma("pool", wkv[:, kq, :], Wv[:, kq, 1024:1280], b_wkv, writes=[b_wkv], partial=(kq > 0))
                for i, (toff, isp) in enumerate(tok_tiles[:DBG.get('nkv', 99)]):
                    pi = i % 2
                    pst = ps[6 + pi]
                    for k in range(16):
                        P.op("pe", lambda e, pst=pst, k=k, i=i: e.matmul(
                            pst[:, 0:256], hT[:, k, i * 128:(i + 1) * 128], wkv[:, k, :],
                            start=(k == 0), stop=(k == 15)),
                            reads=[b_hT, b_wkv], writes=[b_ps[6 + pi]], partial=(k > 0))
                    if isp and not DBG.get('no_isp'):
                        P.op("act", lambda e, pst=pst, pi=pi: e.copy(kvf[pi][:], pst[:, 0:256]),
                             reads=[b_ps[6 + pi]], writes=[b_kvf[pi]])
                        r0 = toff - LS
                        if not DBG.get('no_nk'):
                            P.dma("sp", T["nk"][r0:r0 + 128, :], kvf[pi][:, 0:128], b_kvf[pi], reads=[b_kvf[pi]])
                        if not DBG.get('no_nv'):
                            P.dma("sp", T["nv"][r0:r0 + 128, :], kvf[pi][:, 128:256], b_kvf[pi], reads=[b_kvf[pi]])
                    P.op("act", lambda e, pst=pst, pi=pi: e.copy(vb[pi][:], pst[:, 128:256]),
                         reads=[b_ps[6 + pi]], writes=[b_vb[pi]])
                    P.dma("sp", T["vtok"][toff:toff + 128, :], vb[pi][:], b_vb[pi], reads=[b_vb[pi]])
                chunks = [(c * 512, 512, half * 2048 + c * 512, False) for c in range(4)] + \
                         [(2048, 256, LS + half * 256, True)]
                evac_i = 0
                psi = 0
                for gi, (kind, blocks) in enumerate(groups[:DBG.get('ngroups', 99)] if not DBG.get('gsel') else [groups[i] for i in DBG['gsel']]):
                    s = gi % 2
                    if kind == "k":
                        for j, (name, r0, c0) in enumerate(blocks):
                            for dup in range(2):
                                for kq in range(16):
                                    P.dma("pool", wg[s][:, kq, j * 128 + dup * 64:j * 128 + dup * 64 + 64],
                                          Wv[:, kq, c0:c0 + 64], b_wg[s], writes=[b_wg[s]],
                                          partial=(j + dup + kq > 0))
                    else:
                        c0 = blocks[0][2]
                        for kq in range(16):
                            P.dma("pool", wg[s][:, kq, 0:512], Wv[:, kq, c0:c0 + 512],
                                  b_wg[s], writes=[b_wg[s]], partial=(kq > 0))
                    for (l0, n, g0, isp) in chunks:
                        for j, (name, r0, c0) in enumerate(blocks):
                            pj = psi % 4
                            psi += 1
                            pst = ps[pj]
                            for k in range(16):
                                P.op("pe", lambda e, pst=pst, k=k, s=s, j=j, l0=l0, n=n: e.matmul(
                                    pst[:, 0:n], wg[s][:, k, j * 128:(j + 1) * 128], hT[:, k, l0:l0 + n],
                                    start=(k == 0), stop=(k == 15)),
                                    reads=[b_hT, b_wg[s]], writes=[b_ps[pj]], partial=(k > 0))
                            if name.startswith("kT"):
                                dst = T["kT"][int(name[2]), :, g0:g0 + n]
                            else:
                                dst = T[name][r0:r0 + 128, g0:g0 + n]
                            si = evac_i % 4
                            evac_i += 1
                            if kind in ("q", "k") and not isp:
                                qi = evac_i % 2
                                P.op("act", lambda e, pst=pst, qi=qi, n=n: e.copy(qs[qi][:, 0:n], pst[:, 0:n]),
                                     reads=[b_ps[pj]], writes=[b_qs[qi]])
                                pr = ps[4 + qi]
                                P.op("pe", lambda e, pr=pr, qi=qi, n=n: e.matmul(
                                    pr[:, 0:n], T["ropeP"][:], qs[qi][:, 0:n], start=True, stop=True),
                                    reads=[b_qs[qi]], writes=[b_ps[4 + qi]])
                                P.op("dve", lambda e, qi=qi, l0=l0, n=n: e.tensor_tensor(
                                    t1[:, 0:n], qs[qi][:, 0:n], cos_t[:, l0:l0 + n], ALU.mult),
                                    reads=[b_qs[qi], b_cos], writes=[b_t1])
                                P.op("dve", lambda e, pr=pr, l0=l0, n=n: e.tensor_tensor(
                                    t2[:, 0:n], pr[:, 0:n], sin_t[:, l0:l0 + n], ALU.mult),
                                    reads=[b_ps[4 + qi], b_sin], writes=[b_t2])
                                P.op("dve", lambda e, si=si, n=n: e.tensor_tensor(
                                    stg[si][:, 0:n], t1[:, 0:n], t2[:, 0:n], ALU.add),
                                    reads=[b_t1, b_t2], writes=[b_stg[si]])
                            else:
                                if evac_i % 2 == 0:
                                    P.op("act", lambda e, pst=pst, si=si, n=n: e.copy(stg[si][:, 0:n], pst[:, 0:n]),
                                         reads=[b_ps[pj]], writes=[b_stg[si]])
                                else:
                                    P.op("dve", lambda e, pst=pst, si=si, n=n: e.tensor_copy(
                                        stg[si][:, 0:n], pst[:, 0:n]), reads=[b_ps[pj]], writes=[b_stg[si]])
                            P.dma("sp", dst, stg[si][:, 0:n], b_stg[si], reads=[b_stg[si]])
                P.barrier()
                P.flush()


def bcast_last(ap2d, n):
    a = [list(p) for p in ap2d.ap]
    return bass.AP(ap2d.tensor, ap2d.offset, a + [[0, n]])


def phase_B(P, nc, T):
    import contextlib
    ps = T["ps"]
    ident = T["ident"]
    seqs = [(0, LS, True), (LS, LP, False), (LS + LP, LP, False)]
    with contextlib.ExitStack() as es:
        A_ = lambda n, shp, dt: sb(es, nc, n, shp, dt)
        kT = [A_("b_kT0", [128, LS], BF16), A_("b_kT1", [128, LS], BF16)]
        kcT = [A_("b_kc0", [128, 512], BF16), A_("b_kc1", [128, 512], BF16)]
        ckd = A_("b_ckd", [128, 4, 2, 2, 64], BF16)
        vaug = A_("b_vaug", [128, LS // 128, 2, 65], BF16)
        cvaug = A_("b_cvaug", [128, 4, 2, 65], BF16)
        qT = [A_("b_q0", [128, LS], BF16), A_("b_q1", [128, LS], BF16)]
        ga = [A_("b_ga0", [128, LS], BF16), A_("b_ga1", [128, LS], BF16)]
        sga = A_("b_sga", [128, LS], BF16)
        ptA = [A_("b_ptA0", [128, 512], BF16), A_("b_ptA1", [128, 512], BF16)]
        ptB = [A_("b_ptB0", [128, 384], BF16), A_("b_ptB1", [128, 384], BF16)]
        att = [A_("b_att0", [128, 128], BF16), A_("b_att1", [128, 128], BF16)]
        stage = [A_("b_stg0", [128, 512], BF16), A_("b_stg1", [128, 512], BF16)]
        mask = A_("b_mask", [128, 384], BF16)
        sinkb = A_("b_sinkb", [128, 16], F32)
        esink = A_("b_esink", [128, 16], F32)
        den = [A_("b_den0", [128, 2], F32), A_("b_den1", [128, 2], F32)]
        rden = [A_("b_rden0", [128, 2], F32), A_("b_rden1", [128, 2], F32)]

        b_mask, b_es = P.buf(), P.buf()
        P.dma("sp", mask[:], T["d_amask"], b_mask, writes=[b_mask])
        P.dma("sp", sinkb[:], T["a_sink"][0:1, :].partition_broadcast(128), b_es, writes=[b_es])
        P.op("act", lambda e: e.activation(esink[:], sinkb[:], AF.Exp), reads=[b_es], writes=[b_es])
        P.op("dve", lambda e: e.memset(vaug[:], 1.0), writes=[b_mask], partial=True)
        P.op("dve", lambda e: e.memset(cvaug[:], 1.0), writes=[b_mask], partial=True)
        b_ckd, b_kc, b_cv = P.buf(), P.bufs_n(2), P.buf()
        ckv = T["ck"].rearrange("(t p) c -> p t c", p=128)
        cvv = T["cv"].rearrange("(t p) c -> p t c", p=128)
        for kv in range(2):
            for dup in range(2):
                P.dma("pool", ckd[:, :, kv, dup, :], ckv[:, :, kv * 64:(kv + 1) * 64], b_ckd, writes=[b_ckd],
                      partial=True)
            P.dma("pool", cvaug[:, :, kv, 0:64], cvv[:, :, kv * 64:(kv + 1) * 64], b_cv, reads=[b_mask],
                  writes=[b_cv], partial=True)
        b_pt = P.bufs_n(8)
        for kv in range(2):
            for st_ in range(4):
                pb = ps[st_ % 2][:].bitcast(BF16)
                P.op("pe", lambda e, pb=pb, st_=st_, kv=kv: e.transpose(
                    pb[:, 0:128], ckd[:, st_, kv].rearrange("p a b -> p (a b)"), ident[:]),
                    reads=[b_ckd], writes=[b_pt[st_ % 2]])
                P.op("dve", lambda e, pb=pb, st_=st_, kv=kv: e.tensor_copy(
                    kcT[kv][:, st_ * 128:(st_ + 1) * 128], pb[:, 0:128]),
                    reads=[b_pt[st_ % 2]], writes=[b_kc[kv]], partial=True)
        P.barrier()
        b_kT, b_v = P.bufs_n(2), P.buf()
        b_q, b_ga, b_sga = P.bufs_n(2), P.bufs_n(2), P.buf()
        b_ptA, b_ptB, b_att, b_stage = P.bufs_n(2), P.bufs_n(2), P.bufs_n(2), P.bufs_n(2)
        b_den = P.bufs_n(2)
        b_ps = P.bufs_n(8)
        cnt = 0
        for (tok0, L, has_ctx) in seqs:
            nqb = L // 128
            for kv in range(2):
                P.dma("sp", kT[kv][:, 0:L], T["kT"][kv, :, tok0:tok0 + L], b_kT[kv], writes=[b_kT[kv]])
                P.dma("sp", vaug[:, 0:nqb, kv, 0:64],
                      T["vtok"][tok0:tok0 + L, kv * 64:(kv + 1) * 64].rearrange("(t p) c -> p t c", p=128),
                      b_v, writes=[b_v], partial=(kv > 0))
            for hp in range(8):
                s = cnt % 2
                cnt += 1
                kv = hp // 4
                P.dma("sp", qT[s][:, 0:L], T["qT"][hp * 128:(hp + 1) * 128, tok0:tok0 + L], b_q[s], writes=[b_q[s]])
                P.dma("sp", ga[s][:, 0:L], T["gaT"][hp * 128:(hp + 1) * 128, tok0:tok0 + L], b_ga[s],
                      writes=[b_ga[s]])
                P.op("act", lambda e, s=s, L=L: e.activation(sga[:, 0:L], ga[s][:, 0:L], AF.Silu),
                     reads=[b_ga[s]], writes=[b_sga])
                def geom(qb, nqb=nqb, has_ctx=has_ctx):
                    if has_ctx:
                        loc = [(j, j - qb + 1) for j in (qb - 1, qb, qb + 1) if 0 <= j < nqb]
                    else:
                        loc = [(j, j) for j in range(nqb)]
                    return loc, loc[0][1], loc[-1][1] + 1

                def S_unit(qb, hh, s=s, kv=kv, has_ctx=has_ctx):
                    loc, lo, hi = geom(qb)
                    pr = slice(hh * 64, (hh + 1) * 64)
                    pa, pbk = ps[hh * 2], ps[hh * 2 + 1]
                    qsl = qT[s][pr, qb * 128:(qb + 1) * 128]
                    if has_ctx:
                        for c in range(4):
                            P.op("pe", lambda e, pa=pa, c=c, pr=pr, qsl=qsl, kv=kv: e.matmul(
                                pa[:, c * 128:(c + 1) * 128], kcT[kv][pr, c * 128:(c + 1) * 128], qsl,
                                start=True, stop=True),
                                reads=[b_kc[kv], b_q[s]], writes=[b_ps[hh * 2]], partial=(c > 0))
                        P.op("act", lambda e, pa=pa, hh=hh: e.activation(ptA[hh][:], pa[:], AF.Exp, scale=0.125),
                             reads=[b_ps[hh * 2]], writes=[b_ptA[hh]])
                    for n_, (j, sl) in enumerate(loc):
                        P.op("pe", lambda e, pbk=pbk, sl=sl, j=j, pr=pr, qsl=qsl, kv=kv: e.matmul(
                            pbk[:, sl * 128:(sl + 1) * 128], kT[kv][pr, j * 128:(j + 1) * 128], qsl,
                            start=True, stop=True),
                            reads=[b_kT[kv], b_q[s]], writes=[b_ps[hh * 2 + 1]], partial=(n_ > 0))
                    P.op("act", lambda e, pbk=pbk, hh=hh, lo=lo, hi=hi: e.activation(
                        ptB[hh][:, lo * 128:hi * 128], pbk[:, lo * 128:hi * 128], AF.Exp, scale=0.125),
                        reads=[b_ps[hh * 2 + 1]], writes=[b_ptB[hh]])
                    if has_ctx:
                        P.op("dve", lambda e, hh=hh, lo=lo, hi=hi: e.tensor_tensor(
                            ptB[hh][:, lo * 128:hi * 128], ptB[hh][:, lo * 128:hi * 128],
                            mask[:, lo * 128:hi * 128], ALU.mult),
                            reads=[b_ptB[hh], b_mask], writes=[b_ptB[hh]])

                def PV_unit(qb, hh, kv=kv, has_ctx=has_ctx):
                    loc, lo, hi = geom(qb)
                    o = qb % 2
                    psOv = ps[4 + o][:, 0:130].rearrange("p (h c) -> p h c", h=2)
                    mm = []
                    if has_ctx:
                        for c in range(4):
                            mm.append((ptA[hh][:, c * 128:(c + 1) * 128], cvaug[:, c, kv, :], b_ptA[hh], b_cv))
                    for (j, sl) in loc:
                        mm.append((ptB[hh][:, sl * 128:(sl + 1) * 128], vaug[:, j, kv, :], b_ptB[hh], b_v))
                    for n_, (lh, rh, bl, br) in enumerate(mm):
                        P.op("pe", lambda e, psOv=psOv, hh=hh, lh=lh, rh=rh, n_=n_, nm=len(mm): e.matmul(
                            psOv[:, hh, :], lh, rh, start=(n_ == 0), stop=(n_ == nm - 1)),
                            reads=[bl, br], writes=[b_ps[4 + o]], partial=(n_ > 0 or hh > 0))

                def FIN_unit(qb, s=s, hp=hp, nqb=nqb, tok0=tok0, cnt=cnt):
                    o = qb % 2
                    psOv = ps[4 + o][:, 0:130].rearrange("p (h c) -> p h c", h=2)
                    P.op("dve", lambda e, psOv=psOv, o=o, hp=hp: e.tensor_tensor(
                        den[o][:], psOv[:, :, 64], esink[:, hp * 2:hp * 2 + 2], ALU.add),
                        reads=[b_ps[4 + o], b_es], writes=[b_den[o]])
                    P.op("dve", lambda e, o=o: e.reciprocal(rden[o][:], den[o][:]), reads=[b_den[o]],
                         writes=[b_den[o]], partial=True)
                    P.op("dve", lambda e, psOv=psOv, o=o: e.tensor_tensor(
                        att[o][:].rearrange("p (h c) -> p h c", h=2), psOv[:, :, 0:64], bcast_last(rden[o][:], 64),
                        ALU.mult),
                        reads=[b_ps[4 + o], b_den[o]], writes=[b_att[o]])
                    pT = ps[6 + o][:].bitcast(BF16)
                    P.op("pe", lambda e, pT=pT, o=o: e.transpose(pT[:, 0:128], att[o][:], ident[:]),
                         reads=[b_att[o]], writes=[b_ps[6 + o]])
                    g4 = qb // 4
                    sg_ = (cnt * 1024 + g4) % 2
                    P.op("dve", lambda e, pT=pT, sg_=sg_, qb=qb: e.tensor_tensor(
                        stage[sg_][:, (qb % 4) * 128:(qb % 4 + 1) * 128], pT[:, 0:128],
                        sga[:, qb * 128:(qb + 1) * 128], ALU.mult),
                        reads=[b_ps[6 + o], b_sga], writes=[b_stage[sg_]], partial=(qb % 4 > 0))
                    if qb % 4 == 3 or qb == nqb - 1:
                        n = (qb % 4 + 1) * 128
                        t0 = tok0 + g4 * 512
                        P.dma("sp", T["mixT"][hp * 128:(hp + 1) * 128, t0:t0 + n], stage[sg_][:, 0:n],
                              b_stage[sg_], reads=[b_stage[sg_]])

                units = [(qb, hh) for qb in range(nqb) for hh in range(2)]
                S_unit(*units[0])
                pend_fin = None
                for ui, (qb, hh) in enumerate(units):
                    if ui + 1 < len(units):
                        S_unit(*units[ui + 1])
                    PV_unit(qb, hh)
                    if pend_fin is not None:
                        FIN_unit(pend_fin)
                        pend_fin = None
                    if hh == 1:
                        pend_fin = qb
                if pend_fin is not None:
                    FIN_unit(pend_fin)
        P.barrier()
        P.flush()


def phase_C0(P, nc, T):
    import contextlib
    seqs = [(0, LS), (LS, LP), (LS + LP, LP)]
    with contextlib.ExitStack() as es:
        A_ = lambda n, shp, dt: sb(es, nc, n, shp, dt)
        hsw = A_("c0_hsw", [128, 4, 24], F32)
        ut = [A_("c0_ut0", [128, LS + 2], BF16), A_("c0_ut1", [128, LS + 2], BF16), A_("c0_ut2", [128, LS + 2], BF16)]
        accs = [A_("c0_acc", [128, LS], F32), A_("c0_accb", [128, LS], F32)]
        acc2s = [A_("c0_acc2", [128, LS], F32), A_("c0_acc2b", [128, LS], F32)]
        ob = [A_("c0_ob0", [128, LS], BF16), A_("c0_ob1", [128, LS], BF16)]
        b_hsw, b_ut, b_accs, b_acc2s, b_ob = P.buf(), P.bufs_n(3), P.bufs_n(2), P.bufs_n(2), P.bufs_n(2)
        P.dma("sp", hsw[:], T["hsw"], b_hsw, writes=[b_hsw])
        items = [(tok0, L, ct) for (tok0, L) in seqs for ct in range(24)]

        def front(it):
            tok0, L, ct = items[it]
            u = it % 3
            P.op("pool", lambda e, u=u: e.memset(ut[u][:, 0:1], 0.0), writes=[b_ut[u]])
            P.op("pool", lambda e, u=u, L=L: e.memset(ut[u][:, L + 1:L + 2], 0.0), writes=[b_ut[u]], partial=True)
            P.dma("sp", ut[u][:, 1:L + 1], T["hyT"][ct * 128:(ct + 1) * 128, tok0:tok0 + L], b_ut[u],
                  writes=[b_ut[u]], partial=True)

        def back(it):
            tok0, L, ct = items[it]
            u = it % 3
            s = it % 2
            acc, acc2, b_acc, b_acc2 = accs[s], acc2s[s], b_accs[s], b_acc2s[s]
            P.op("act", lambda e, u=u, L=L, ct=ct, acc=acc: e.activation(
                acc[:, 0:L], ut[u][:, 0:L], AF.Identity, scale=hsw[:, 0, ct:ct + 1], bias=hsw[:, 3, ct:ct + 1]),
                reads=[b_ut[u], b_hsw], writes=[b_acc])
            P.op("dve", lambda e, u=u, L=L, ct=ct, acc=acc, acc2=acc2: e.scalar_tensor_tensor(
                acc2[:, 0:L], ut[u][:, 1:L + 1], hsw[:, 1, ct:ct + 1], acc[:, 0:L], ALU.mult, ALU.add),
                reads=[b_ut[u], b_acc, b_hsw], writes=[b_acc2])
            P.op("dve", lambda e, u=u, s=s, L=L, ct=ct, acc2=acc2: e.scalar_tensor_tensor(
                ob[s][:, 0:L], ut[u][:, 2:L + 2], hsw[:, 2, ct:ct + 1], acc2[:, 0:L], ALU.mult, ALU.add),
                reads=[b_ut[u], b_acc2, b_hsw], writes=[b_ob[s]])
            P.dma("sp", T["ucT"][ct * 128:(ct + 1) * 128, tok0:tok0 + L], ob[s][:, 0:L], b_ob[s],
                  reads=[b_ob[s]])

        n = len(items)
        front(0)
        if n > 1:
            front(1)
        for it in range(n):
            if it + 2 < n:
                front(it + 2)
            back(it)
        P.barrier()
        P.flush()


def phase_CF(P, nc, T, Lf, taps_dst, zemb, tposb):
    import contextlib
    import math
    ps = T["ps"]
    nch = max(1, Lf // 512)
    cw = min(512, Lf)
    with contextlib.ExitStack() as es:
        A_ = lambda n, shp, dt: sb(es, nc, n, shp, dt)
        ze = A_("cf_ze", [33, Lf], F32)
        tp_ = A_("cf_tpos", [128, Lf], F32)
        w1 = A_("cf_w1", [33, 64], F32)
        w2 = A_("cf_w2", [64, 64], F32)
        bb = A_("cf_bb", [64, 2], F32)
        w3 = A_("cf_w3", [64, 4096], BF16)
        dec = A_("cf_dec", [128, 32], F32)
        ndec = A_("cf_ndec", [128, 32], F32)
        h1 = A_("cf_h1", [64, Lf], F32)
        h2 = A_("cf_h2", [64, Lf], F32)
        h2b = A_("cf_h2b", [64, Lf], BF16)
        tmp = A_("cf_tmp", [64, 512], F32)
        tmq = A_("cf_tmq", [64, 512], F32)
        raw = [A_("cf_raw0", [128, Lf], F32), A_("cf_raw1", [128, Lf], F32)]
        win = [A_("cf_win0", [128, 512], F32), A_("cf_win1", [128, 512], F32)]
        junk = A_("cf_junk", [128, Lf], F32)
        ss = A_("cf_ss", [128, 8], F32)
        tpb = [A_("cf_tp0", [128, 2 * Lf], BF16), A_("cf_tp1", [128, 2 * Lf], BF16)]
        b_c = P.bufs_n(8)
        P.dma("sp", ze[:], zemb, b_c[0], writes=[b_c[0]])
        P.dma("sp", tp_[:], tposb, b_c[1], writes=[b_c[1]])
        P.dma("sp", w1[:], T["hy_w1"], b_c[2], writes=[b_c[2]])
        P.dma("sp", w2[:], T["hy_w2"], b_c[3], writes=[b_c[3]])
        P.dma("sp", bb[:], T["hy_bb"], b_c[4], writes=[b_c[4]])
        for q4 in range(4):
            P.dma("pool", w3[:, q4 * 1024:(q4 + 1) * 1024], T["hy_w3"][:, q4 * 1024:(q4 + 1) * 1024], b_c[5],
                  writes=[b_c[5]], partial=(q4 > 0))
        P.dma("sp", dec[:], T["hy_decT"], b_c[6], writes=[b_c[6]])
        P.op("act", lambda e: e.activation(dec[:], dec[:], AF.Abs), reads=[b_c[6]], writes=[b_c[6]])
        P.op("dve", lambda e: e.tensor_scalar(ndec[:], dec[:], -1.0, None, ALU.mult),
             reads=[b_c[6]], writes=[b_c[7]])
        b_h1, b_h2, b_h2b, b_tmp, b_tmq = P.buf(), P.buf(), P.buf(), P.buf(), P.buf()
        b_ps = P.bufs_n(8)
        TWO_PI = 2.0 * math.pi
        for layer in range(2):
            src = ze if layer == 0 else h1
            wm = w1 if layer == 0 else w2
            kk = 33 if layer == 0 else 64
            dst = h1 if layer == 0 else h2
            b_src = b_c[0] if layer == 0 else b_h1
            b_dst = b_h1 if layer == 0 else b_h2
            for c in range(nch):
                pj = c % 2
                P.op("pe", lambda e, pj=pj, c=c, src=src, wm=wm, kk=kk: e.matmul(
                    ps[pj][0:64, 0:cw], wm[0:kk, :], src[0:kk, c * cw:(c + 1) * cw], start=True, stop=True),
                    reads=[b_src, b_c[2], b_c[3]], writes=[b_ps[pj]])
                MAGIC = 12582912.0
                P.op("act", lambda e, pj=pj, layer=layer: e.activation(
                    tmp[:, 0:cw], ps[pj][0:64, 0:cw], AF.Identity, bias=bb[:, layer:layer + 1]),
                    reads=[b_ps[pj], b_c[4]], writes=[b_tmp])
                P.op("dve", lambda e: e.tensor_scalar(
                    tmq[:, 0:cw], tmp[:, 0:cw], 1.0 / TWO_PI, MAGIC, ALU.mult, ALU.add),
                    reads=[b_tmp], writes=[b_tmq])
                P.op("dve", lambda e: e.tensor_scalar(
                    tmq[:, 0:cw], tmq[:, 0:cw], -MAGIC, -TWO_PI, ALU.add, ALU.mult),
                    reads=[b_tmq], writes=[b_tmq])
                P.op("dve", lambda e: e.tensor_tensor(tmp[:, 0:cw], tmp[:, 0:cw], tmq[:, 0:cw], ALU.add),
                     reads=[b_tmp, b_tmq], writes=[b_tmp])
                P.op("act", lambda e, c=c, dst=dst: e.activation(dst[:, c * cw:(c + 1) * cw], tmp[:, 0:cw], AF.Sin),
                     reads=[b_tmp], writes=[b_dst], partial=(c > 0))
        P.op("act", lambda e: e.copy(h2b[:], h2[:]), reads=[b_h2], writes=[b_h2b])
        b_raw, b_win, b_junk, b_ss, b_tpb = P.bufs_n(2), P.bufs_n(2), P.buf(), P.buf(), P.bufs_n(2)
        it = 0
        for order in range(2):
            for blk in range(8):
                s = it % 2
                it += 1
                for d in range(2):
                    cbk = d * 16 + order * 8 + blk
                    for c in range(nch):
                        pj = 2 + (c % 2)
                        wj = c % 2
                        P.op("pe", lambda e, pj=pj, c=c, cbk=cbk: e.matmul(
                            ps[pj][:, 0:cw], w3[:, cbk * 128:(cbk + 1) * 128], h2b[:, c * cw:(c + 1) * cw],
                            start=True, stop=True),
                            reads=[b_h2b, b_c[5]], writes=[b_ps[pj]])
                        P.op("act", lambda e, wj=wj, c=c, cbk=cbk: e.activation(
                            win[wj][:, 0:cw], tp_[:, c * cw:(c + 1) * cw], AF.Exp, scale=ndec[:, cbk:cbk + 1]),
                            reads=[b_c[1], b_c[7]], writes=[b_win[wj]])
                        P.op("dve", lambda e, pj=pj, wj=wj, c=c, d=d: e.tensor_tensor(
                            raw[d][:, c * cw:(c + 1) * cw], ps[pj][:, 0:cw], win[wj][:, 0:cw], ALU.mult),
                            reads=[b_ps[pj], b_win[wj]], writes=[b_raw[d]], partial=(c > 0))
                P.op("act", lambda e: e.activation(junk[:], raw[0][:], AF.Square, accum_out=ss[:, 0:1]),
                     reads=[b_raw[0]], writes=[b_junk, b_ss])
                P.op("act", lambda e: e.activation(junk[:, 1:Lf], raw[1][:, 1:Lf], AF.Square, accum_out=ss[:, 1:2]),
                     reads=[b_raw[1]], writes=[b_junk, b_ss], partial=True)
                P.op("dve", lambda e: e.tensor_tensor(ss[:, 2:3], ss[:, 0:1], ss[:, 1:2], ALU.add),
                     reads=[b_ss], writes=[b_ss], partial=True)
                P.op("act", lambda e: e.activation(ss[:, 3:4], ss[:, 2:3], AF.Sqrt, bias=T["epsb"][:, 0:1]),
                     reads=[b_ss], writes=[b_ss], partial=True)
                P.op("dve", lambda e: e.reciprocal(ss[:, 4:5], ss[:, 3:4]), reads=[b_ss], writes=[b_ss], partial=True)
                P.op("dve", lambda e: e.tensor_scalar(ss[:, 5:6], ss[:, 4:5], -1.0, None, ALU.mult),
                     reads=[b_ss], writes=[b_ss], partial=True)
                P.op("act", lambda e, s=s: e.activation(tpb[s][:, 0:Lf], raw[0][:], AF.Copy, scale=ss[:, 4:5]),
                     reads=[b_raw[0], b_ss], writes=[b_tpb[s]])
                P.op("pool", lambda e, s=s: e.memset(tpb[s][:, Lf:Lf + 1], 0.0), writes=[b_tpb[s]], partial=True)
                P.op("dve", lambda e, s=s: e.tensor_scalar(
                    rev_ap(tpb[s][:, Lf + 1:2 * Lf]), raw[1][:, 1:Lf], ss[:, 5:6], None, ALU.mult),
                    reads=[b_raw[1], b_ss], writes=[b_tpb[s]], partial=True)
                P.dma("sp", taps_dst[order, blk * 128:(blk + 1) * 128, :], tpb[s][:], b_tpb[s], reads=[b_tpb[s]])
        P.barrier()
        P.flush()


def _split_dma(P, eng, dst, src, buf, nsplit, axis_len, mk_dst, mk_src, **kw):
    step = axis_len // nsplit
    for i in range(nsplit):
        P.dma(eng, mk_dst(i * step, (i + 1) * step), mk_src(i * step, (i + 1) * step), buf,
              partial=(i > 0 or kw.get("partial", False)), **{k: v for k, v in kw.items() if k != "partial"})


def _f1_stage(P, nc, T, xin, K, b_xin, F1, b_F1, Cb, b_Cb, b_ps, ev):
    ps = T["ps"]
    for g in range(16):
        pj = g % 2
        for cc in range(4):
            c = g * 4 + cc
            P.op("pe", lambda e, pj=pj, cc=cc, c=c: e.matmul(
                ps[pj][0:64, cc * 128:(cc + 1) * 128], xin[0:K, c, :], F1[0:K, 0:128], start=True, stop=True),
                reads=[b_xin, b_F1], writes=[b_ps[pj]], partial=(cc > 0))
            P.op("pe", lambda e, pj=pj, cc=cc, c=c: e.matmul(
                ps[pj][64:128, cc * 128:(cc + 1) * 128], xin[0:K, c, :], F1[0:K, 128:256], start=True, stop=True,
                tile_position=(0, 64)),
                reads=[b_xin, b_F1], writes=[b_ps[pj]], partial=True)
        src = ps[pj][:].rearrange("p (c k) -> p k c", c=4)
        dst = Cb[:, :, g * 4:(g + 1) * 4]
        ev(dst, src, [b_ps[pj]], [b_Cb], g)


def _evac_alt(P):
    def ev(dst, src, reads, writes, i):
        if i % 2 == 0:
            P.op("act", lambda e: e.copy(dst, src), reads=reads, writes=writes, partial="nowaw")
        else:
            P.op("dve", lambda e: e.tensor_copy(dst, src), reads=reads, writes=writes, partial="nowaw")
    return ev


def _evac_act(P):
    def ev(dst, src, reads, writes, i):
        P.op("act", lambda e: e.copy(dst, src), reads=reads, writes=writes, partial="nowaw")
    return ev


def phase_CT(P, nc, T):
    import contextlib
    ps = T["ps"]
    with contextlib.ExitStack() as es:
        A_ = lambda n, shp, dt: sb(es, nc, n, shp, dt)
        F1 = A_("ct_F1", [128, 256], BF16)
        Grr = A_("ct_Grr", [128, 128, 64], BF16)
        Gii = A_("ct_Gii", [128, 128, 64], BF16)
        xt = [A_("ct_xt0", [128, 64, 64], BF16), A_("ct_xt1", [128, 64, 64], BF16)]
        Cb = A_("ct_Cb", [128, 128, 64], BF16)
        Hr = [A_("ct_Hr0", [64, 64, 128], BF16), A_("ct_Hr1", [64, 64, 128], BF16)]
        Hi = [A_("ct_Hi0", [64, 64, 128], BF16), A_("ct_Hi1", [64, 64, 128], BF16)]
        b_F1, b_G = P.buf(), P.buf()
        P.dma("sp", F1[:], T["d_F1"], b_F1, writes=[b_F1])
        for q in range(4):
            P.dma("sp", Grr[:, q * 32:(q + 1) * 32, :], T["d_Grr"][:, q * 32:(q + 1) * 32, :], b_G, writes=[b_G],
                  partial=(q > 0))
            P.dma("sp", Gii[:, q * 32:(q + 1) * 32, :], T["d_Gii"][:, q * 32:(q + 1) * 32, :], b_G, writes=[b_G],
                  partial=True)
        b_xt, b_Cb, b_Hr, b_Hi = P.bufs_n(2), P.buf(), P.bufs_n(2), P.bufs_n(2)
        b_ps = P.bufs_n(8)
        ev = _evac_alt(P)
        it = 0
        def ct_load(order, cb, s):
            srcv = T["tapsS"][order, cb * 64:(cb + 1) * 64, :].rearrange("c (a b) -> a c b", b=64)
            for q in range(8):
                P.dma("sp", xt[s][:, q * 8:(q + 1) * 8, :], srcv[:, q * 8:(q + 1) * 8, :], b_xt[s],
                      writes=[b_xt[s]], partial=(q > 0))

        blocks = [(order, cb) for order in range(2) for cb in range(16)]
        ct_load(0, 0, 0)
        for (order, cb) in blocks:
                s = it % 2
                it += 1
                _f1_stage(P, nc, T, xt[s], 128, b_xt[s], F1, b_F1, Cb, b_Cb, b_ps, ev)
                if it < len(blocks):
                    ct_load(blocks[it][0], blocks[it][1], it % 2)
                for g in range(16):
                    pj = 2 + (g % 2) * 2
                    for kk in range(8):
                        k1 = g * 8 + kk
                        P.op("pe", lambda e, pj=pj, kk=kk, k1=k1: e.matmul(
                            ps[pj][0:64, kk * 64:(kk + 1) * 64], Grr[:, k1, :], Cb[:, k1, :], start=True, stop=True),
                            reads=[b_G, b_Cb], writes=[b_ps[pj]], partial=(kk > 0))
                        P.op("pe", lambda e, pj=pj, kk=kk, k1=k1: e.matmul(
                            ps[pj + 1][0:64, kk * 64:(kk + 1) * 64], Gii[:, k1, :], Cb[:, k1, :], start=True,
                            stop=True),
                            reads=[b_G, b_Cb], writes=[b_ps[pj + 1]], partial=(kk > 0))
                    P.op("act", lambda e, pj=pj, g=g, s=s: e.copy(
                        Hr[s][:, :, g * 8:(g + 1) * 8], ps[pj][0:64, :].rearrange("p (k c) -> p c k", k=8)),
                        reads=[b_ps[pj]], writes=[b_Hr[s]], partial="nowaw")
                    P.op("dve", lambda e, pj=pj, g=g, s=s: e.tensor_copy(
                        Hi[s][:, :, g * 8:(g + 1) * 8], ps[pj + 1][0:64, :].rearrange("p (k c) -> p c k", k=8)),
                        reads=[b_ps[pj + 1]], writes=[b_Hi[s]], partial="nowaw")
                P.dma("sp", T["Hs"][order, cb, 0], Hr[s][:].rearrange("p c k -> p (c k)"), b_Hr[s], reads=[b_Hr[s]])
                P.dma("sp", T["Hs"][order, cb, 1], Hi[s][:].rearrange("p c k -> p (c k)"), b_Hi[s], reads=[b_Hi[s]])
        P.barrier()
        P.flush()


def phase_CS(P, nc, T):
    import contextlib
    ps = T["ps"]
    with contextlib.ExitStack() as es:
        A_ = lambda n, shp, dt: sb(es, nc, n, shp, dt)
        F1 = A_("cs_F1", [128, 256], BF16)
        G = A_("cs_G", [128, 128, 64], BF16)
        M1 = A_("cs_M1", [64, 128], BF16)
        M1p = A_("cs_M1p", [64, 128], BF16)
        T2r = A_("cs_T2r", [128, 64, 64], BF16)
        T2i = A_("cs_T2i", [128, 64, 64], BF16)
        zc = A_("cs_zc", [64, 64, 64], BF16)
        x1 = A_("cs_x1", [64, 64, 64], BF16)
        x2 = A_("cs_x2", [64, 64, 64], BF16)
        gh = A_("cs_gh", [64, 64, 64], BF16)
        z2 = A_("cs_z2", [64, 64, 64], BF16)
        bz = A_("cs_bz", [64, 64, 64], BF16)
        cv = A_("cs_cv", [64, 64, 64], F32)
        bias = A_("cs_bias", [64, 2, 64], F32)
        Cb = A_("cs_Cb", [128, 128, 64], BF16)
        Db = A_("cs_Db", [128, 128, 64], BF16)
        P1 = A_("cs_P1", [64, 64, 128], BF16)
        P2 = A_("cs_P2", [64, 64, 128], BF16)
        Hr = A_("cs_Hr", [64, 64, 128], BF16)
        Hi = A_("cs_Hi", [64, 64, 128], BF16)
        Xs = [A_("cs_Xs0", [64, 512], BF16), A_("cs_Xs1", [64, 512], BF16)]
        b_k = P.bufs_n(6)
        P.dma("sp", F1[:], T["d_F1"], b_k[0], writes=[b_k[0]])
        for q in range(4):
            P.dma("sp", G[:, q * 32:(q + 1) * 32, :], T["d_G"][:, q * 32:(q + 1) * 32, :], b_k[1], writes=[b_k[1]],
                  partial=(q > 0))
        P.dma("sp", M1[:], T["d_M1"], b_k[2], writes=[b_k[2]])
        P.dma("sp", M1p[:], T["d_M1p"], b_k[3], writes=[b_k[3]])
        P.dma("sp", T2r[:], T["d_T2r"], b_k[4], writes=[b_k[4]])
        P.dma("sp", T2i[:], T["d_T2in"], b_k[5], writes=[b_k[5]])
        b_F1, b_G, b_M1, b_M1p, b_T2r, b_T2i = b_k
        b_zc, b_x1, b_x2, b_gh, b_sgh, b_z2, b_bz, b_cv, b_mx, b_bias = [P.buf() for _ in range(10)]
        b_Cb, b_Db, b_P1, b_P2, b_Hr, b_Hi = [P.buf() for _ in range(6)]
        b_Xs = P.bufs_n(2)
        b_ps = P.bufs_n(8)
        ev = _evac_act(P)

        def t64(rows0):
            return lambda a, b: T["ucT"][rows0 + a:rows0 + b, 0:LS].rearrange("c (n1 n2) -> n1 c n2", n2=64)

        def cs_load(cb, which):
            specs = {"zc": (zc, b_zc, 2048 + cb * 64, "ucT"), "x1": (x1, b_x1, cb * 64, "ucT"),
                     "x2": (x2, b_x2, 1024 + cb * 64, "ucT"), "gh": (gh, b_gh, cb * 64, "ghT")}
            for w in which:
                dst, bdst, r0, src_t = specs[w]
                for q in range(4):
                    srcv = T[src_t][r0 + q * 16:r0 + (q + 1) * 16, 0:LS].rearrange("c (n1 n2) -> n1 c n2", n2=64)
                    P.dma("sp", dst[:, q * 16:(q + 1) * 16, :], srcv, bdst, writes=[bdst], partial=(q > 0))

        cs_load(0, ("zc", "x1"))
        for cb in range(16):
            cs_load(cb, ("x2", "gh"))
            for o in range(2):
                P.dma("sp", bias[:, o, :], T["hy_bias"][o:o + 1, cb * 64:(cb + 1) * 64].partition_broadcast(64),
                      b_bias, writes=[b_bias], partial=(o > 0))
            P.op("act", lambda e: e.activation(gh[:], gh[:], AF.Silu), reads=[b_gh], writes=[b_gh])
            for o in range(2):
                zin, b_zin = (zc, b_zc) if o == 0 else (z2, b_z2)
                xo, b_xo = (x1, b_x1) if o == 0 else (x2, b_x2)
                P.dma("sp", Hr[:].rearrange("p c k -> p (c k)"), T["Hs"][o, cb, 0], b_Hr, writes=[b_Hr])
                P.dma("sp", Hi[:].rearrange("p c k -> p (c k)"), T["Hs"][o, cb, 1], b_Hi, writes=[b_Hi])
                if o == 1 and cb + 1 < 16:
                    cs_load(cb + 1, ("zc", "x1"))
                P.op("pool", lambda e, zin=zin, o=o: e.tensor_tensor(
                    bz[:], zin[:], bcast_last(bias[:, o, :], 64), ALU.mult),
                    reads=[b_zin, b_bias], writes=[b_bz])
                _f1_stage(P, nc, T, zin, 64, b_zin, F1, b_F1, Cb, b_Cb, b_ps, ev)
                for g in range(16):
                    pj = 2 + (g % 2)
                    xs = g % 2
                    for kk in range(8):
                        k1 = g * 8 + kk
                        P.op("pe", lambda e, pj=pj, kk=kk, k1=k1: e.matmul(
                            ps[pj][0:64, kk * 64:(kk + 1) * 64], G[:, k1, :], Cb[:, k1, :], start=True, stop=True),
                            reads=[b_G, b_Cb], writes=[b_ps[pj]], partial=(kk > 0))
                    P.op("act", lambda e, pj=pj, xs=xs: e.copy(Xs[xs][:], ps[pj][0:64, :]),
                         reads=[b_ps[pj]], writes=[b_Xs[xs]])
                    xv = Xs[xs][:].rearrange("p (k c) -> p c k", k=8)
                    P.op("dve", lambda e, xv=xv, g=g: e.tensor_tensor(
                        P1[:, :, g * 8:(g + 1) * 8], xv, Hr[:, :, g * 8:(g + 1) * 8], ALU.mult),
                        reads=[b_Xs[xs], b_Hr], writes=[b_P1], partial="nowaw")
                    P.op("dve", lambda e, xv=xv, g=g: e.tensor_tensor(
                        P2[:, :, g * 8:(g + 1) * 8], xv, Hi[:, :, g * 8:(g + 1) * 8], ALU.mult),
                        reads=[b_Xs[xs], b_Hi], writes=[b_P2], partial="nowaw")
                for g in range(16):
                    pj = 4 + (g % 2)
                    for cc in range(4):
                        c = g * 4 + cc
                        P.op("pe", lambda e, pj=pj, cc=cc, c=c: e.matmul(
                            ps[pj][:, cc * 128:(cc + 1) * 128], P1[:, c, :], M1[:], start=True, stop=False),
                            reads=[b_P1, b_M1], writes=[b_ps[pj]], partial=(cc > 0))
                        P.op("pe", lambda e, pj=pj, cc=cc, c=c: e.matmul(
                            ps[pj][:, cc * 128:(cc + 1) * 128], P2[:, c, :], M1p[:], start=False, stop=True),
                            reads=[b_P2, b_M1p], writes=[b_ps[pj]], partial=True)
                    src = ps[pj][:].rearrange("p (c q) -> p q c", c=4)
                    ev(Db[:, :, g * 4:(g + 1) * 4], src, [b_ps[pj]], [b_Db], g)
                for g in range(8):
                    pj = 6 + (g % 2)
                    for nn in range(8):
                        n2 = g * 8 + nn
                        P.op("pe", lambda e, pj=pj, nn=nn, n2=n2: e.matmul(
                            ps[pj][0:64, nn * 64:(nn + 1) * 64], T2r[:, n2, :], Db[:, n2, :], start=True, stop=False),
                            reads=[b_T2r, b_Db], writes=[b_ps[pj]], partial=(nn > 0))
                        P.op("pe", lambda e, pj=pj, nn=nn, n2=n2: e.matmul(
                            ps[pj][0:64, nn * 64:(nn + 1) * 64], T2i[:, n2, :], Db[:, 64 + n2, :], start=False,
                            stop=True),
                            reads=[b_T2i, b_Db], writes=[b_ps[pj]], partial=True)
                    P.op("dve", lambda e, pj=pj, g=g: e.tensor_tensor(
                        cv[:, :, g * 8:(g + 1) * 8], ps[pj][0:64, :].rearrange("p (n c) -> p c n", n=8),
                        bz[:, :, g * 8:(g + 1) * 8], ALU.add),
                        reads=[b_ps[pj], b_bz], writes=[b_cv], partial=(g > 0))
                if o == 0:
                    P.op("dve", lambda e: e.tensor_tensor(z2[:], cv[:], x1[:], ALU.mult),
                         reads=[b_cv, b_x1], writes=[b_z2])
                else:
                    P.op("dve", lambda e: e.tensor_tensor(cv[:], cv[:], x2[:], ALU.mult),
                         reads=[b_cv, b_x2], writes=[b_cv])
                    P.op("pool", lambda e: e.tensor_tensor(z2[:], cv[:], gh[:], ALU.mult),
                         reads=[b_cv, b_gh], writes=[b_z2])
                    for q in range(4):
                        r0 = 1024 + cb * 64 + q * 16
                        dstv = T["mixT"][r0:r0 + 16, 0:LS].rearrange("c (n1 n2) -> n1 c n2", n2=64)
                        P.dma("sp", dstv, z2[:, q * 16:(q + 1) * 16, :], b_z2, reads=[b_z2])
        P.barrier()
        P.flush()


def phase_CP(P, nc, T):
    import contextlib
    ps = T["ps"]
    ident = T["ident"]
    with contextlib.ExitStack() as es:
        A_ = lambda n, shp, dt: sb(es, nc, n, shp, dt)
        FP = A_("cp_FP", [128, 4, 512], BF16)
        IP = A_("cp_IP", [128, 4, 256], BF16)
        HA = A_("cp_HA", [128, 16, 512], F32)
        HB = A_("cp_HB", [128, 16, 512], F32)
        biasT = A_("cp_biasT", [128, 16], F32)
        tp = [A_("cp_tp0", [128, 512], BF16), A_("cp_tp1", [128, 512], BF16)]
        tt = A_("cp_tt", [128, 4, 128], BF16)
        zc = [A_("cp_zc0", [128, 256], BF16), A_("cp_zc1", [128, 256], BF16)]
        x1 = [A_("cp_x10", [128, 256], BF16), A_("cp_x11", [128, 256], BF16)]
        x2 = [A_("cp_x20", [128, 256], BF16), A_("cp_x21", [128, 256], BF16)]
        gh = [A_("cp_gh0", [128, 256], BF16), A_("cp_gh1", [128, 256], BF16)]
        sgh = A_("cp_sgh", [128, 256], F32)
        z2 = A_("cp_z2", [128, 256], BF16)
        zt = A_("cp_zt", [128, 2, 128], BF16)
        Aa = A_("cp_A", [128, 512], F32)
        Bb = A_("cp_B", [128, 512], F32)
        Y = A_("cp_Y", [128, 512], BF16)
        Yt = A_("cp_Yt", [128, 4, 128], BF16)
        t1 = A_("cp_t1", [128, 256], F32)
        t2 = A_("cp_t2", [128, 256], F32)
        mx = [A_("cp_mx0", [128, 256], BF16), A_("cp_mx1", [128, 256], BF16)]
        b_FP, b_IP, b_H, b_bias = P.buf(), P.buf(), P.buf(), P.buf()
        P.dma("sp", FP[:], T["d_FP"], b_FP, writes=[b_FP])
        P.dma("sp", IP[:], T["d_IP"], b_IP, writes=[b_IP])
        P.dma("sp", biasT[:], T["hy_biasT"], b_bias, writes=[b_bias])
        b_tp, b_tt = P.bufs_n(2), P.buf()
        b_ps = P.bufs_n(8)
        it = 0
        for o in range(2):
            for t in range(8):
                s = it % 2
                it += 1
                P.dma("sp", tp[s][:], T["tapsP"][o, t * 128:(t + 1) * 128, :], b_tp[s], writes=[b_tp[s]])
                pb = ps[s][:].bitcast(BF16)
                for j in range(4):
                    P.op("pe", lambda e, pb=pb, j=j, s=s: e.transpose(
                        pb[:, j * 128:(j + 1) * 128], tp[s][:, j * 128:(j + 1) * 128], ident[:]),
                        reads=[b_tp[s]], writes=[b_ps[s]], partial=(j > 0))
                P.op("dve", lambda e, pb=pb: e.tensor_copy(tt[:].rearrange("p a b -> p (a b)"), pb[:, 0:512]),
                     reads=[b_ps[s]], writes=[b_tt])
                pj = 2 + s
                for j in range(4):
                    P.op("pe", lambda e, pj=pj, j=j: e.matmul(
                        ps[pj][:, :], tt[:, j, :], FP[:, j, :], start=(j == 0), stop=(j == 3)),
                        reads=[b_tt, b_FP], writes=[b_ps[pj]], partial=(j > 0))
                i = o * 8 + t
                for h in range(2):
                    P.op("act", lambda e, pj=pj, i=i, h=h: e.copy(HA[:, i, h * 256:(h + 1) * 256], ps[pj][:, 0:256]),
                         reads=[b_ps[pj]], writes=[b_H], partial=True)
                    P.op("act", lambda e, pj=pj, i=i, h=h: e.copy(HB[:, i, h * 256:(h + 1) * 256],
                                                                  ps[pj][:, 256:512]),
                         reads=[b_ps[pj]], writes=[b_H], partial=True)
        b_zc, b_x1, b_x2, b_gh = P.bufs_n(2), P.bufs_n(2), P.bufs_n(2), P.bufs_n(2)
        b_sgh, b_z2, b_zt, b_A, b_B, b_Y, b_Yt, b_t1, b_t2 = [P.buf() for _ in range(9)]
        b_mx = P.bufs_n(2)
        it = 0
        for sq in range(2):
            tok0 = LS + sq * LP
            for t in range(8):
                s = it % 2
                it += 1
                r = t * 128
                P.dma("sp", zc[s][:], T["ucT"][2048 + r:2048 + r + 128, tok0:tok0 + LP], b_zc[s], writes=[b_zc[s]])
                P.dma("sp", x1[s][:], T["ucT"][r:r + 128, tok0:tok0 + LP], b_x1[s], writes=[b_x1[s]])
                P.dma("sp", x2[s][:], T["ucT"][1024 + r:1024 + r + 128, tok0:tok0 + LP], b_x2[s], writes=[b_x2[s]])
                P.dma("sp", gh[s][:], T["ghT"][r:r + 128, tok0:tok0 + LP], b_gh[s], writes=[b_gh[s]])
                P.op("act", lambda e, s=s: e.activation(sgh[:], gh[s][:], AF.Silu), reads=[b_gh[s]], writes=[b_sgh])
                for o in range(2):
                    zin, b_zin = (zc[s], b_zc[s]) if o == 0 else (z2, b_z2)
                    xo, b_xo = (x1[s], b_x1[s]) if o == 0 else (x2[s], b_x2[s])
                    i = o * 8 + t
                    pb = ps[0][:].bitcast(BF16)
                    for j in range(2):
                        P.op("pe", lambda e, pb=pb, j=j, zin=zin: e.transpose(
                            pb[:, j * 128:(j + 1) * 128], zin[:, j * 128:(j + 1) * 128], ident[:]),
                            reads=[b_zin], writes=[b_ps[0]], partial=(j > 0))
                    P.op("dve", lambda e, pb=pb: e.tensor_copy(zt[:].rearrange("p a b -> p (a b)"), pb[:, 0:256]),
                         reads=[b_ps[0]], writes=[b_zt])
                    for j in range(2):
                        P.op("pe", lambda e, j=j: e.matmul(ps[1][:, :], zt[:, j, :], FP[:, j, :], start=(j == 0),
                                                           stop=(j == 1)),
                             reads=[b_zt, b_FP], writes=[b_ps[1]], partial=(j > 0))
                    P.op("dve", lambda e, i=i: e.tensor_tensor(Aa[:], ps[1][:, :], HA[:, i, :], ALU.mult),
                         reads=[b_ps[1], b_H], writes=[b_A])
                    P.op("dve", lambda e, i=i: e.tensor_tensor(Bb[:], ps[1][:, :], HB[:, i, :], ALU.mult),
                         reads=[b_ps[1], b_H], writes=[b_B])
                    P.op("pool", lambda e: e.tensor_tensor(Y[:, 0:256], Aa[:, 0:256], Bb[:, 256:512], ALU.subtract),
                         reads=[b_A, b_B], writes=[b_Y])
                    P.op("pool", lambda e: e.tensor_tensor(Y[:, 256:512], Bb[:, 0:256], Aa[:, 256:512], ALU.add),
                         reads=[b_A, b_B], writes=[b_Y], partial=True)
                    pb2 = ps[2][:].bitcast(BF16)
                    for j in range(4):
                        P.op("pe", lambda e, pb2=pb2, j=j: e.transpose(
                            pb2[:, j * 128:(j + 1) * 128], Y[:, j * 128:(j + 1) * 128], ident[:]),
                            reads=[b_Y], writes=[b_ps[2]], partial=(j > 0))
                    P.op("act", lambda e, pb2=pb2: e.copy(Yt[:].rearrange("p a b -> p (a b)"), pb2[:, 0:512]),
                         reads=[b_ps[2]], writes=[b_Yt])
                    for j in range(4):
                        P.op("pe", lambda e, j=j: e.matmul(ps[3][:, 0:256], Yt[:, j, :], IP[:, j, :], start=(j == 0),
                                                           stop=(j == 3)),
                             reads=[b_Yt, b_IP], writes=[b_ps[3]], partial=(j > 0))
                    P.op("dve", lambda e, zin=zin, i=i: e.scalar_tensor_tensor(
                        t1[:], zin[:], biasT[:, i:i + 1], ps[3][:, 0:256], ALU.mult, ALU.add),
                        reads=[b_zin, b_bias, b_ps[3]], writes=[b_t1])
                    if o == 0:
                        P.op("pool", lambda e, xo=xo: e.tensor_tensor(z2[:], t1[:], xo[:], ALU.mult),
                             reads=[b_t1, b_xo], writes=[b_z2])
                    else:
                        P.op("pool", lambda e, xo=xo: e.tensor_tensor(t2[:], t1[:], xo[:], ALU.mult),
                             reads=[b_t1, b_xo], writes=[b_t2])
                        P.op("pool", lambda e, s=s: e.tensor_tensor(mx[s][:], t2[:], sgh[:], ALU.mult),
                             reads=[b_t2, b_sgh], writes=[b_mx[s]])
                        P.dma("sp", T["mixT"][1024 + r:1024 + r + 128, tok0:tok0 + LP], mx[s][:], b_mx[s],
                              reads=[b_mx[s]])
        P.barrier()
        P.flush()


def phase_outproj(P, nc, T, l, Wdram, xsrc, mixname, dst, final):
    import contextlib
    ps = T["ps"]
    with contextlib.ExitStack() as es:
        A_ = lambda n, shp, dt: sb(es, nc, n, shp, dt)
        Wo = A_("o_W", [128, 16, 2048], BF16)
        mix = [A_("o_mix0", [128, 16, 512], BF16), A_("o_mix1", [128, 16, 512], BF16)]
        xt = [A_("o_x0", [128, 2048], F32), A_("o_x1", [128, 2048], F32)]
        xo = [A_("o_xo0", [128, 2048], F32), A_("o_xo1", [128, 2048], F32)]
        gs = A_("o_gs", [128, 2048], F32)
        gp = A_("o_gp", [128, 2048], F32)
        tmp = [A_("o_tmp0", [128, 512], F32), A_("o_tmp1", [128, 512], F32)]
        b_W, b_mix, b_xt, b_xo, b_g, b_tmp = P.buf(), P.bufs_n(2), P.bufs_n(2), P.bufs_n(2), P.buf(), P.bufs_n(2)
        b_ps = P.bufs_n(8)
        if final:
            fg = A_("o_fg", [128, 2048], F32)
            junk = A_("o_junk", [128, 2048], F32)
            st = [A_("o_st0", [128, 4], F32), A_("o_st1", [128, 4], F32)]
            b_fg, b_junk, b_st = P.buf(), P.buf(), P.bufs_n(2)
            P.dma("sp", fg[:], T["final_norm_g"][0:1, :].partition_broadcast(128), b_fg, writes=[b_fg])
        Wv = Wdram.rearrange("(k p) c -> p k c", p=128)
        for k in range(16):
            P.dma("pool", Wo[:, k, :], Wv[:, k, :], b_W, writes=[b_W], partial=(k > 0))
        P.dma("sp", gs[:], T["modv"][l][0:1, 4096:6144].partition_broadcast(128), b_g, writes=[b_g])
        P.dma("sp", gp[:], T["modv"][l][1:2, 4096:6144].partition_broadcast(128), b_g, writes=[b_g], partial=True)
        mv = T[mixname].rearrange("(k p) t -> p k t", p=128)
        ti = 0
        ei = 0
        def mix_load(ch):
            s = ch % 2
            for q in range(4):
                P.dma("sp", mix[s][:, q * 4:(q + 1) * 4, :], mv[:, q * 4:(q + 1) * 4, ch * 512:(ch + 1) * 512],
                      b_mix[s], writes=[b_mix[s]], partial=(q > 0))

        mix_load(0)
        for ch in range(NTOK // 512):
            s = ch % 2
            for tt in range(4):
                if tt == 1 and ch + 1 < NTOK // 512:
                    mix_load(ch + 1)
                tok = ch * 512 + tt * 128
                xs = ti % 2
                ti += 1
                gg = gs if tok < LS else gp
                P.dma("sp", xt[xs][:], xsrc[tok:tok + 128, :], b_xt[xs], writes=[b_xt[xs]])
                for cbk in range(4):
                    pj = ei % 8
                    tj = ei % 2
                    ei += 1
                    for k in range(16):
                        P.op("pe", lambda e, pj=pj, k=k, s=s, tt=tt, cbk=cbk: e.matmul(
                            ps[pj][:, :], mix[s][:, k, tt * 128:(tt + 1) * 128], Wo[:, k, cbk * 512:(cbk + 1) * 512],
                            start=(k == 0), stop=(k == 15)),
                            reads=[b_mix[s], b_W], writes=[b_ps[pj]], partial=(k > 0))
                    P.op("dve", lambda e, pj=pj, tj=tj, cbk=cbk, gg=gg: e.tensor_tensor(
                        tmp[tj][:], ps[pj][:, :], gg[:, cbk * 512:(cbk + 1) * 512], ALU.mult),
                        reads=[b_ps[pj], b_g], writes=[b_tmp[tj]])
                    P.op("pool", lambda e, tj=tj, xs=xs, cbk=cbk: e.tensor_tensor(
                        xo[xs][:, cbk * 512:(cbk + 1) * 512], tmp[tj][:], xt[xs][:, cbk * 512:(cbk + 1) * 512],
                        ALU.add),
                        reads=[b_tmp[tj], b_xt[xs]], writes=[b_xo[xs]], partial=(cbk > 0))
                if final:
                    P.op("act", lambda e, xs=xs: e.activation(junk[:], xo[xs][:], AF.Square,
                                                               accum_out=st[xs][:, 0:1]),
                         reads=[b_xo[xs]], writes=[b_junk, b_st[xs]])
                    P.op("act", lambda e, xs=xs: e.activation(st[xs][:, 1:2], st[xs][:, 0:1], AF.Sqrt, scale=1.0 / D,
                                                               bias=T["epsb"][:, 0:1]),
                         reads=[b_st[xs]], writes=[b_st[xs]], partial=True)
                    P.op("dve", lambda e, xs=xs: e.reciprocal(st[xs][:, 2:3], st[xs][:, 1:2]),
                         reads=[b_st[xs]], writes=[b_st[xs]], partial=True)
                    P.op("dve", lambda e, xs=xs: e.scalar_tensor_tensor(
                        xo[xs][:], xo[xs][:], st[xs][:, 2:3], fg[:], ALU.mult, ALU.mult),
                        reads=[b_xo[xs], b_st[xs], b_fg], writes=[b_xo[xs]])
                P.dma("sp", dst[tok:tok + 128, :], xo[xs][:], b_xo[xs], reads=[b_xo[xs]])
        P.barrier()
        P.flush()


def phase_E(P, nc, T):
    import contextlib
    ps = T["ps"]
    Wv = T["c_w_in"].rearrange("(k p) c -> p k c", p=128)
    for half in range(2):
        tok_tiles = [(half * 2048 + i * 128, False) for i in range(16)] + \
                    [(LS + half * 256 + i * 128, True) for i in range(2)]
        with contextlib.ExitStack() as es0:
            hT = sb(es0, nc, "e_hT", [128, 16, 2304], BF16)
            b_hT = P.buf("hT")
            with contextlib.ExitStack() as es1:
                A_ = lambda n, shp, dt: sb(es1, nc, n, shp, dt)
                xt0 = A_("e_xt0", [128, 2048], F32); xt1 = A_("e_xt1", [128, 2048], F32)
                xt2 = A_("e_xt2", [128, 2048], F32); xt3 = A_("e_xt3", [128, 2048], F32)
                junk = A_("e_junk", [128, 2048], F32); tmp = A_("e_tmp", [128, 2048], F32)
                junkb = A_("e_junkb", [128, 2048], F32); tmpb = A_("e_tmpb", [128, 2048], F32)
                st0 = A_("e_st0", [128, 4], F32); st1 = A_("e_st1", [128, 4], F32)
                hb0 = A_("e_hb0", [128, 2048], BF16); hb1 = A_("e_hb1", [128, 2048], BF16)
                As = A_("e_As", [128, 2048], F32); shs = A_("e_shs", [128, 2048], F32)
                Ap = A_("e_Ap", [128, 2048], F32); shp = A_("e_shp", [128, 2048], F32)
                bc = {"A_s": As, "sh_s": shs, "A_p": Ap, "sh_p": shp}
                b_bc = {k: P.buf(k) for k in bc}
                load_bcast_rows(P, nc, T, 1, bc, b_bc)
                work = dict(xt=[xt0, xt1, xt2, xt3], b_xt=P.bufs_n(4), junk=junk, b_junk=P.buf(), st=[st0, st1],
                            b_st=P.bufs_n(2), tmp=tmp, b_tmp=P.buf(), hb=[hb0, hb1], b_hb=P.bufs_n(2),
                            b_pst=P.bufs_n(4), junk2=[junk, junkb], b_junk2=P.bufs_n(2), tmp2=[tmp, tmpb],
                            b_tmp2=P.bufs_n(2))
                norm_transpose_half(P, nc, T, T["x1"], tok_tiles, hT, b_hT, bc, b_bc, work)
                P.barrier()
                P.flush()
            with contextlib.ExitStack() as es2:
                A_ = lambda n, shp, dt: sb(es2, nc, n, shp, dt)
                wg = [A_("e_w0", [128, 16, 512], BF16), A_("e_w1", [128, 16, 512], BF16)]
                sf = [A_("e_sf0", [128, 512], F32), A_("e_sf1", [128, 512], F32)]
                sg = [A_("e_sg0", [128, 512], BF16), A_("e_sg1", [128, 512], BF16)]
                b_hT = P.buf("hT2")
                b_wg, b_sf, b_sg = P.bufs_n(2), P.bufs_n(2), P.bufs_n(2)
                b_ps = P.bufs_n(8)
                chunks = [(c * 512, 512, half * 2048 + c * 512) for c in range(4)] + [(2048, 256, LS + half * 256)]
                psi = 0
                ei = 0
                for g in range(8):
                    s = g % 2
                    for kq in range(16):
                        P.dma("pool", wg[s][:, kq, :], Wv[:, kq, g * 512:(g + 1) * 512], b_wg[s], writes=[b_wg[s]],
                              partial=(kq > 0))
                    for (l0, n, g0) in chunks:
                        for j in range(4):
                            pj = psi % 4
                            psi += 1
                            for k in range(16):
                                P.op("pe", lambda e, pj=pj, k=k, s=s, j=j, l0=l0, n=n: e.matmul(
                                    ps[pj][:, 0:n], wg[s][:, k, j * 128:(j + 1) * 128], hT[:, k, l0:l0 + n],
                                    start=(k == 0), stop=(k == 15)),
                                    reads=[b_hT, b_wg[s]], writes=[b_ps[pj]], partial=(k > 0))
                            row = (g * 4 + j) * 128
                            si = ei % 2
                            ei += 1
                            if row < 2048:
                                P.op("act", lambda e, pj=pj, si=si, n=n: e.copy(sf[si][:, 0:n], ps[pj][:, 0:n]),
                                     reads=[b_ps[pj]], writes=[b_sf[si]])
                                P.dma("sp", T["xbT"][row:row + 128, g0:g0 + n], sf[si][:, 0:n], b_sf[si],
                                      reads=[b_sf[si]])
                            else:
                                P.op("dve", lambda e, pj=pj, si=si, n=n: e.tensor_copy(sg[si][:, 0:n], ps[pj][:, 0:n]),
                                     reads=[b_ps[pj]], writes=[b_sg[si]])
                                P.dma("sp", T["gateT"][row - 2048:row - 2048 + 128, g0:g0 + n], sg[si][:, 0:n],
                                      b_sg[si], reads=[b_sg[si]])
                P.barrier()
                P.flush()


def phase_F(P, nc, T):
    import contextlib
    ps = T["ps"]
    seqs = [(0, LS, 0, 1), (LS, 2 * LP, 1, 2)]
    with contextlib.ExitStack() as es:
        A_ = lambda n, shp, dt: sb(es, nc, n, shp, dt)
        Rs = [A_("f_R0", [128, LS], F32), A_("f_R1", [128, LS], F32)]
        Is = [A_("f_I0", [128, LS], F32), A_("f_I1", [128, LS], F32)]
        Ss = [A_("f_S0", [128, LS], F32), A_("f_S1", [128, LS], F32)]
        H_ = [A_("f_H0", [128, LS + 3], F32), A_("f_H1", [128, LS + 3], F32)]
        xc = A_("f_xc", [128, 2, LS], F32)
        xp = H_[1]
        acc = H_[0]
        xcb = A_("f_xcb", [128, 2, LS], BF16)
        gate = A_("f_gate", [128, LS], BF16)
        stage = A_("f_stage", [128, LS], BF16)
        wq = A_("f_wq", [128, 2, 2, 2, 256], BF16)
        lcw = A_("f_lcw", [128, 5, 16], F32)
        lba = A_("f_lba", [128, 2, 16], F32)
        lbx = A_("f_lbx", [128, 2, 16], F32)
        llam = A_("f_llam", [128, 2, 16], F32)
        c8 = A_("f_c8", [128, 2, 16], F32)
        c16 = A_("f_c16", [128, 2, 16], F32)
        stT = A_("f_stT", [128, 2, 16], F32)
        nst = A_("f_nst", [128, 2, 2, 16], F32)
        b_k = P.bufs_n(6)
        P.dma("sp", lcw[:], T["lcw"], b_k[0], writes=[b_k[0]])
        P.dma("sp", lba[:], T["lba"], b_k[1], writes=[b_k[1]])
        P.dma("sp", lbx[:], T["lbx"], b_k[2], writes=[b_k[2]])
        P.dma("sp", llam[:], T["llam"], b_k[3], writes=[b_k[3]])
        P.dma("sp", stT[:], T["stT"], b_k[4], writes=[b_k[4]])
        P.op("act", lambda e: e.activation(c8[:], llam[:], AF.Exp, scale=-1.0), reads=[b_k[3]], writes=[b_k[5]])
        P.op("act", lambda e: e.activation(c8[:], c8[:], AF.Ln, bias=1.0), reads=[b_k[5]], writes=[b_k[5]])
        P.op("dve", lambda e: e.tensor_scalar(c16[:], c8[:], -16.0, None, ALU.mult), reads=[b_k[5]], writes=[b_k[5]],
             partial=True)
        P.op("dve", lambda e: e.tensor_scalar(c8[:], c8[:], -8.0, None, ALU.mult), reads=[b_k[5]], writes=[b_k[5]],
             partial=True)
        b_lcw, b_lba, b_lbx, _, b_stT, b_c8 = b_k
        b_xc, b_xcb, b_gate, b_stage, b_wq, b_nst = [P.buf() for _ in range(6)]
        b_H = P.bufs_n(2)
        b_Rs, b_Is, b_Ss = P.bufs_n(2), P.bufs_n(2), P.bufs_n(2)
        b_xp, b_acc = b_H[1], b_H[0]
        par = 0
        b_ps = P.bufs_n(8)
        psi = 0
        for h in range(8):
            for m in range(2):
                wsrc = T["c_wa"] if m == 0 else T["c_wx"]
                for d in range(2):
                    P.dma("pool", wq[:, m, d], wsrc[d, h].rearrange("(it p) j -> p it j", p=128), b_wq,
                          writes=[b_wq], partial=(m + d > 0))
            for (tok0, L, sidx, nseg) in seqs:
                nch = max(1, L // 512)
                cw = min(512, L)
                Ls = L // nseg
                Wp = Ls + 3

                def xpv(j, nseg=nseg, Ls=Ls, Wp=Wp):
                    if nseg == 1:
                        return xp[:, j:j + Ls]
                    return xp[:, 0:nseg * Wp].rearrange("p (g w) -> p g w", g=nseg)[:, :, j:j + Ls]

                def segv(ap2d, nseg=nseg):
                    if nseg == 1:
                        return ap2d
                    return ap2d.rearrange("p (g w) -> p g w", g=nseg)
                for ct in range(2):
                    cti = h * 2 + ct
                    row = cti * 128
                    P.op("pool", lambda e: e.memset(xp[:, 0:2], 0.0), writes=[b_xp])
                    for sg in range(nseg):
                        P.op("pool", lambda e, sg=sg, Wp=Wp, Ls=Ls: e.memset(
                            xp[:, sg * Wp + Ls + 2:min((sg + 1) * Wp + 2, xp.shape[1])], 0.0),
                            writes=[b_xp], partial=True)
                        P.dma("sp", xp[:, sg * Wp + 2:sg * Wp + 2 + Ls],
                              T["xbT"][row:row + 128, tok0 + sg * Ls:tok0 + (sg + 1) * Ls], b_xp, writes=[b_xp],
                              partial=True)
                    accv = segv(acc[:, 0:L])
                    P.op("act", lambda e, cti=cti, accv=accv, src=xpv(0): e.activation(
                        accv, src, AF.Identity, scale=lcw[:, 0, cti:cti + 1],
                        bias=lcw[:, 4, cti:cti + 1]), reads=[b_xp, b_lcw], writes=[b_acc])
                    for j in (1, 2):
                        P.op("dve", lambda e, cti=cti, j=j, accv=accv, src=xpv(j): e.scalar_tensor_tensor(
                            accv, src, lcw[:, j, cti:cti + 1], accv, ALU.mult, ALU.add),
                            reads=[b_xp, b_acc, b_lcw], writes=[b_acc])
                    P.op("dve", lambda e, cti=cti, ct=ct, accv=accv, src=xpv(3), dstv=segv(xc[:, ct, 0:L]):
                         e.scalar_tensor_tensor(dstv, src, lcw[:, 3, cti:cti + 1], accv, ALU.mult, ALU.add),
                         reads=[b_xp, b_acc, b_lcw], writes=[b_xc], partial=(ct > 0))
                    P.op("act", lambda e, L=L, ct=ct: e.copy(xcb[:, ct, 0:L], xc[:, ct, 0:L]),
                         reads=[b_xc], writes=[b_xcb], partial=(ct > 0))
                for jt in range(2):
                    cti = h * 2 + jt
                    row = cti * 128
                    P.dma("sp", gate[:, 0:L], T["gateT"][row:row + 128, tok0:tok0 + L], b_gate, writes=[b_gate])
                    for d in range(2):
                        par ^= 1
                        R_, I_, S_ = Rs[par], Is[par], Ss[par]
                        b_R, b_I, b_S = b_Rs[par], b_Is[par], b_Ss[par]
                        for (m, dstT, b_dst, bias_t) in ((0, R_, b_R, lba), (1, I_, b_I, lbx)):
                            for c in range(nch):
                                pj = psi % 4
                                psi += 1
                                for it_ in range(2):
                                    P.op("pe", lambda e, pj=pj, m=m, d=d, it_=it_, jt=jt, c=c, cw=cw: e.matmul(
                                        ps[pj][:, 0:cw], wq[:, m, d, it_, jt * 128:(jt + 1) * 128],
                                        xcb[:, it_, c * cw:(c + 1) * cw], start=(it_ == 0), stop=(it_ == 1)),
                                        reads=[b_wq, b_xcb], writes=[b_ps[pj]], partial=(it_ > 0))
                                P.op("act", lambda e, pj=pj, dstT=dstT, c=c, bias_t=bias_t, d=d, cti=cti, cw=cw: e.activation(
                                    dstT[:, c * cw:(c + 1) * cw], ps[pj][:, 0:cw], AF.Sigmoid,
                                    bias=bias_t[:, d, cti:cti + 1]),
                                    reads=[b_ps[pj], b_lba, b_lbx], writes=[b_dst], partial=(c > 0))
                        P.op("pool", lambda e, L=L, jt=jt, I_=I_: e.tensor_tensor(I_[:, 0:L], I_[:, 0:L], xc[:, jt, 0:L],
                                                                                 ALU.mult),
                             reads=[b_I, b_xc], writes=[b_I])
                        P.op("act", lambda e, L=L, d=d, cti=cti, R_=R_, S_=S_: e.activation(
                            S_[:, 0:L], R_[:, 0:L], AF.Exp, scale=c16[:, d, cti:cti + 1]),
                            reads=[b_R, b_c8], writes=[b_S])
                        P.op("act", lambda e, L=L, d=d, cti=cti, R_=R_: e.activation(
                            R_[:, 0:L], R_[:, 0:L], AF.Exp, scale=c8[:, d, cti:cti + 1]),
                            reads=[b_R, b_c8], writes=[b_R])
                        P.op("act", lambda e, L=L, S_=S_: e.activation(S_[:, 0:L], S_[:, 0:L], AF.Sqrt, scale=-1.0, bias=1.0),
                             reads=[b_S], writes=[b_S])
                        P.op("dve", lambda e, L=L, I_=I_, S_=S_: e.tensor_tensor(I_[:, 0:L], I_[:, 0:L], S_[:, 0:L], ALU.mult),
                             reads=[b_I, b_S], writes=[b_I])
                        init = stT[:, d, cti:cti + 1] if sidx == 0 else 0.0
                        for sg in range(nseg):
                            a0, a1 = sg * Ls, (sg + 1) * Ls
                            if d == 0:
                                P.op("dve", lambda e, a0=a0, a1=a1, init=init, R_=R_, I_=I_: e.tensor_tensor_scan(
                                    H_[0][:, a0:a1], R_[:, a0:a1], I_[:, a0:a1], init, ALU.mult, ALU.add),
                                    reads=[b_R, b_I, b_stT], writes=[b_H[0]], partial=(sg > 0))
                            else:
                                P.op("dve", lambda e, a0=a0, a1=a1, init=init, R_=R_, I_=I_: e.tensor_tensor_scan(
                                    rev_ap(H_[1][:, a0:a1]), rev_ap(R_[:, a0:a1]), rev_ap(I_[:, a0:a1]), init,
                                    ALU.mult, ALU.add),
                                    reads=[b_R, b_I, b_stT], writes=[b_H[1]], partial=(sg > 0))
                            if sidx > 0:
                                col = a1 - 1 if d == 0 else a0
                                P.op("act", lambda e, d=d, col=col, sg=sg, cti=cti: e.copy(
                                    nst[:, sg, d, cti:cti + 1], H_[d][:, col:col + 1]),
                                    reads=[b_H[d]], writes=[b_nst], partial=True)
                    P.op("pool", lambda e, L=L: e.tensor_tensor(H_[0][:, 0:L], H_[0][:, 0:L], H_[1][:, 0:L], ALU.add),
                         reads=[b_H[0], b_H[1]], writes=[b_H[0]])
                    P.op("act", lambda e, L=L, S_=S_: e.activation(S_[:, 0:L], gate[:, 0:L], AF.Silu),
                         reads=[b_gate], writes=[b_S])
                    P.op("dve", lambda e, L=L, S_=S_: e.tensor_tensor(stage[:, 0:L], H_[0][:, 0:L], S_[:, 0:L], ALU.mult),
                         reads=[b_H[0], b_S], writes=[b_stage])
                    P.dma("sp", T["mix1T"][row:row + 128, tok0:tok0 + L], stage[:, 0:L], b_stage, reads=[b_stage])
        P.dma("sp", T["ns"], nst[:].rearrange("p a b c -> p (a b c)"), b_nst, reads=[b_nst])
        P.barrier()
        P.flush()


def _bf16(a):
    return np.asarray(a, dtype=np.float32).astype(ml_dtypes.bfloat16)


def _fft_consts():
    N = 2 * LS
    n1 = np.arange(128)[:, None]
    k1 = np.arange(128)[None, :]
    ang = 2 * np.pi * n1 * (k1 + 0.5) / 128
    F1 = np.concatenate([np.cos(ang), -np.sin(ang)], 1)
    n2 = np.arange(64)[:, None, None]
    k1_ = np.arange(128)[None, :, None]
    k2 = np.arange(32)[None, None, :]
    ang = 2 * np.pi * n2 * (k1_ + 128 * k2 + 0.5) / N
    gr, gi = np.cos(ang), -np.sin(ang)
    G = np.zeros((128, 128, 64))
    G[0:64, :, 0:32] = gr
    G[64:128, :, 0:32] = -gi
    G[0:64, :, 32:64] = gi
    G[64:128, :, 32:64] = gr
    Grr = np.concatenate([G[:, :, 0:32], G[:, :, 0:32]], 2)
    Gii = np.concatenate([G[:, :, 32:64], G[:, :, 32:64]], 2)
    k2 = np.arange(32)[:, None]
    n2 = np.arange(64)[None, :]
    ang = 2 * np.pi * n2 * k2 / 64
    mr, mi = np.cos(ang), np.sin(ang)
    M1 = np.zeros((64, 128))
    M1[0:32, 0:64] = mr
    M1[32:64, 0:64] = -mi
    M1[0:32, 64:128] = mi
    M1[32:64, 64:128] = mr
    M1p = np.concatenate([M1[32:64], -M1[0:32]], 0)
    k1 = np.arange(128)[:, None, None]
    n2 = np.arange(64)[None, :, None]
    n1 = np.arange(64)[None, None, :]
    ang = 2 * np.pi * (k1 + 0.5) * (n1 / 128 + n2 / N)
    T2r = (2.0 / N) * np.cos(ang)
    T2in = -(2.0 / N) * np.sin(ang)
    Np = 2 * LP
    n = np.arange(Np)[:, None]
    k = np.arange(LP)[None, :]
    ang = 2 * np.pi * n * (k + 0.5) / Np
    FPm = np.concatenate([np.cos(ang), -np.sin(ang)], 1)
    FP = FPm.reshape(4, 128, 512).transpose(1, 0, 2)
    k = np.arange(LP)[:, None]
    n = np.arange(LP)[None, :]
    ang = 2 * np.pi * n * (k + 0.5) / Np
    IPm = np.concatenate([(2.0 / Np) * np.cos(ang), -(2.0 / Np) * np.sin(ang)], 0)
    IP = IPm.reshape(4, 128, 256).transpose(1, 0, 2)
    return dict(F1=F1, G=G, Grr=Grr, Gii=Gii, M1=M1, M1p=M1p, T2r=T2r, T2in=T2in, FP=FP, IP=IP)


def make_consts():
    c = {}
    c["ident"] = _bf16(np.eye(128))
    pos = np.arange(LS)
    row = (pos // 64).astype(np.float64)
    col = (pos % 64).astype(np.float64)
    nf = 16
    inv = 10000.0 ** (-np.arange(nf, dtype=np.float64) / nf)
    cos = np.zeros((64, LS))
    sin = np.zeros((64, LS))
    for d in range(64):
        halfi = d // 32
        w = d % 32
        f = w % 16
        p = row if halfi == 0 else col
        ang = p * inv[f]
        cos[d] = np.cos(ang)
        sin[d] = -np.sin(ang) if w < 16 else np.sin(ang)
    c["rope_cos"] = np.tile(cos, (2, 1)).astype(np.float32)
    c["rope_sin"] = np.tile(sin, (2, 1)).astype(np.float32)
    Pm = np.zeros((128, 128))
    for m in range(128):
        hh = m // 64
        d = m % 64
        w = d % 32
        partner = d + 16 if w < 16 else d - 16
        Pm[hh * 64 + partner, m] = 1.0
    c["ropeP"] = _bf16(Pm)
    c["epsb"] = np.full((128, 1), EPS, np.float32)
    for nm, Lf in (("S", LS), ("P", LP)):
        t = (np.arange(Lf, dtype=np.float32) / np.float32(Lf)).astype(np.float64)
        freqs = np.linspace(1e-4, 15.0, 16).astype(np.float32).astype(np.float64)
        ang = 2.0 * np.pi * t[:, None] * freqs[None, :]
        z = np.concatenate([t[:, None], np.cos(ang), -np.sin(ang)], 1)
        c["zemb" + nm] = np.ascontiguousarray(z.T).astype(np.float32)
        c["tpos" + nm] = np.ascontiguousarray(np.tile(t[None, :], (128, 1))).astype(np.float32)
    c.update({k: _bf16(v) for k, v in _fft_consts().items()})
    si = np.arange(128)[:, None]
    qi = np.arange(128)[None, :]
    c["amask"] = _bf16(np.concatenate([(si >= qi), np.ones((128, 128)), (si <= qi)], 1).astype(np.float32))
    return c


CONST_SPECS = {
    "ident": ([128, 128], BF16), "rope_cos": ([128, LS], F32), "rope_sin": ([128, LS], F32),
    "ropeP": ([128, 128], BF16), "epsb": ([128, 1], F32), "amask": ([128, 384], BF16),
    "zembS": ([33, LS], F32), "zembP": ([33, LP], F32), "tposS": ([128, LS], F32), "tposP": ([128, LP], F32),
    "F1": ([128, 256], BF16), "G": ([128, 128, 64], BF16), "Grr": ([128, 128, 64], BF16),
    "Gii": ([128, 128, 64], BF16), "M1": ([64, 128], BF16), "M1p": ([64, 128], BF16),
    "T2r": ([128, 64, 64], BF16), "T2in": ([128, 64, 64], BF16),
    "FP": ([128, 4, 512], BF16), "IP": ([128, 4, 256], BF16),
}

IN_SPECS = {
    "x": [NTOK, D], "ck": [512, 128], "cv": [512, 128], "st": [2, D], "cvecT": [128, 32],
    "mod_w": [2, D, 3 * D], "mod_b": [2, 3 * D], "norm_g": [2, D], "final_norm_g": [1, D],
    "a_w_in": [D, 6400], "a_w_out": [D, D], "a_sink": [1, 16],
    "hsw": [128, 4, 24], "hy_w1": [33, 64], "hy_w2": [64, 64], "hy_bb": [64, 2], "hy_w3": [64, 4096],
    "hy_decT": [128, 32], "hy_biasT": [128, 16], "hy_bias": [2, 1024],
    "c_w_in": [D, 2 * D], "c_w_out": [D, D], "c_wa": [2, 8, 256, 256], "c_wx": [2, 8, 256, 256],
    "lcw": [128, 5, 16], "lba": [128, 2, 16], "lbx": [128, 2, 16], "llam": [128, 2, 16], "stT": [128, 2, 16],
}

SCRATCH = {
    "modv": ([2, 2, 3 * D], F32),
    "qT": ([1024, NTOK], BF16), "kT": ([2, 128, NTOK], BF16), "gaT": ([1024, NTOK], BF16),
    "hyT": ([3072, NTOK], BF16), "ghT": ([1024, NTOK], BF16), "vtok": ([NTOK, 128], BF16),
    "mixT": ([2048, NTOK], BF16),
    "ucT": ([3072, NTOK], BF16), "tapsS": ([2, 1024, 2 * LS], BF16), "tapsP": ([2, 1024, 2 * LP], BF16),
    "Hs": ([2, 16, 2, 64, 64 * 128], BF16),
    "x1": ([NTOK, D], F32), "xbT": ([D, NTOK], F32), "gateT": ([D, NTOK], BF16), "mix1T": ([D, NTOK], BF16),
}

OUT_SPECS = {"y": [NTOK, D], "nk": [512, 128], "nv": [512, 128], "ns": [128, 64]}


def build_program(debug_scratch=(), stop_after=None, skip=(), ext_in=()):
    nc = bass.Bass("TRN2", target_bir_lowering=False)
    T = {}
    for name, shp in IN_SPECS.items():
        T[name] = nc.dram_tensor(name, shp, F32, kind="ExternalInput").ap()
    for name, (shp, dt) in CONST_SPECS.items():
        T["d_" + name] = nc.dram_tensor("c_" + name, shp, dt, kind="ExternalInput").ap()
    for name, shp in OUT_SPECS.items():
        T[name] = nc.dram_tensor(name, shp, F32, kind="ExternalOutput").ap()
    for name, (shp, dt) in SCRATCH.items():
        kind = "ExternalOutput" if name in debug_scratch else ("ExternalInput" if name in ext_in else "Internal")
        T[name] = nc.dram_tensor("s_" + name, shp, dt, kind=kind).ap()
    T["rope_cos"] = T["d_rope_cos"]
    T["rope_sin"] = T["d_rope_sin"]
    import contextlib
    with contextlib.ExitStack() as es:
        sems = [es.enter_context(nc.semaphore("sem%d" % i)) for i in range(60)]
        T["ps"] = [es.enter_context(nc.psum_tensor("ps%d" % i, [128, 512], F32)) for i in range(8)]
        ident = es.enter_context(nc.sbuf_tensor("ident", [128, 128], BF16))
        ropeP = es.enter_context(nc.sbuf_tensor("ropeP", [128, 128], BF16))
        epsb = es.enter_context(nc.sbuf_tensor("epsb", [128, 1], F32))
        T["ident"], T["ropeP"], T["epsb"] = ident, ropeP, epsb
        P = Prog(nc, sems)
        b_c = P.bufs_n(3)
        P.dma("sp", ident[:], T["d_ident"], b_c[0], writes=[b_c[0]])
        P.dma("sp", ropeP[:], T["d_ropeP"], b_c[1], writes=[b_c[1]])
        P.dma("sp", epsb[:], T["d_epsb"], b_c[2], writes=[b_c[2]])
        P.barrier()
        if "M" not in skip:
            phase_M(P, nc, T)
        if stop_after != "M":
            if "A" not in skip:
                phase_A(P, nc, T)
        if stop_after not in ("M", "A") and "B" not in skip:
            phase_B(P, nc, T)
        if stop_after not in ("M", "A", "B"):
            if "C0" not in skip:
                phase_C0(P, nc, T)
            if "CF" not in skip:
                phase_CF(P, nc, T, LP, T["tapsP"], T["d_zembP"], T["d_tposP"])
                phase_CF(P, nc, T, LS, T["tapsS"], T["d_zembS"], T["d_tposS"])
        if stop_after not in ("M", "A", "B", "CF"):
            if "CT" not in skip:
                phase_CT(P, nc, T)
            if "CS" not in skip:
                phase_CS(P, nc, T)
            if "CP" not in skip:
                phase_CP(P, nc, T)
        if stop_after not in ("M", "A", "B", "CF", "C"):
            if "D" not in skip:
                phase_outproj(P, nc, T, 0, T["a_w_out"], T["x"], "mixT", T["x1"], False)
        if stop_after not in ("M", "A", "B", "CF", "C", "D"):
            if "E" not in skip:
                phase_E(P, nc, T)
        if stop_after not in ("M", "A", "B", "CF", "C", "D", "E"):
            if "F" not in skip:
                phase_F(P, nc, T)
        if stop_after not in ("M", "A", "B", "CF", "C", "D", "E", "F"):
            if "G" not in skip:
                phase_outproj(P, nc, T, 1, T["c_w_out"], T["x1"], "mix1T", T["y"], True)
        P.barrier()
        P.flush()
    return nc


def make_in_maps(inputs):
    consts = make_consts()
    f = lambda a: np.ascontiguousarray(np.asarray(a, dtype=np.float32))
    x_prompt, x_sample = f(inputs["x_prompt"]), f(inputs["x_sample"])
    ck, cv = f(inputs["cache_k"]), f(inputs["cache_v"])
    st = f(inputs["state_lru"])
    c, c_ctx = f(inputs["c"]), f(inputs["c_ctx"])
    shared = {
        "mod_w": f(inputs["mod_w"]), "mod_b": f(inputs["mod_b"]), "norm_g": f(inputs["norm_g"]),
        "final_norm_g": f(inputs["final_norm_g"]).reshape(1, D),
        "a_w_in": f(inputs["a_w_in"])[0], "a_w_out": f(inputs["a_w_out"])[0], "a_sink": f(inputs["a_sink"]),
        "hsw": np.ascontiguousarray(np.concatenate([f(inputs["hy_short_w"])[0], f(inputs["hy_short_b"])[0][None]], 0)
                                    .reshape(4, 24, 128).transpose(2, 0, 1)),
        "hy_w1": f(inputs["hy_w1"])[0], "hy_w2": f(inputs["hy_w2"])[0],
        "hy_bb": np.ascontiguousarray(np.stack([f(inputs["hy_b1"])[0], f(inputs["hy_b2"])[0]], 1)),
        "hy_w3": f(inputs["hy_w3"])[0],
        "hy_decT": np.ascontiguousarray(f(inputs["hy_decay"])[0].reshape(32, 128).T),
        "hy_biasT": np.ascontiguousarray(f(inputs["hy_bias"])[0].reshape(16, 128).T),
        "hy_bias": f(inputs["hy_bias"])[0],
        "c_w_in": f(inputs["c_w_in"])[0], "c_w_out": f(inputs["c_w_out"])[0],
        "c_wa": f(inputs["c_wa"])[0], "c_wx": f(inputs["c_wx"])[0],
        "lcw": np.ascontiguousarray(np.concatenate([f(inputs["c_conv_w"])[0], f(inputs["c_conv_b"])[0][None]], 0)
                                    .reshape(5, 16, 128).transpose(2, 0, 1)),
        "lba": np.ascontiguousarray(f(inputs["c_ba"])[0].reshape(2, 16, 128).transpose(2, 0, 1)),
        "lbx": np.ascontiguousarray(f(inputs["c_bx"])[0].reshape(2, 16, 128).transpose(2, 0, 1)),
        "llam": np.ascontiguousarray(f(inputs["c_lambda"])[0].reshape(2, 16, 128).transpose(2, 0, 1)),
    }
    for k, v in consts.items():
        shared["c_" + k] = v
    maps = []
    for i in range(NCORES):
        m = dict(shared)
        m["x"] = np.ascontiguousarray(np.concatenate([x_sample[i], x_prompt[2 * i], x_prompt[2 * i + 1]], 0))
        m["ck"] = np.ascontiguousarray(ck[i, 0].reshape(512, 128))
        m["cv"] = np.ascontiguousarray(cv[i, 0].reshape(512, 128))
        m["st"] = np.ascontiguousarray(st[i, 0])
        m["stT"] = np.ascontiguousarray(st[i, 0].reshape(2, 16, 128).transpose(2, 0, 1))
        cvec = np.stack([c[i], c_ctx], 0)
        m["cvecT"] = np.ascontiguousarray(cvec.reshape(2, 16, 128).transpose(2, 1, 0).reshape(128, 32))
        maps.append(m)
    return maps


def kernel(**inputs):
    nc = build_program()
    maps = make_in_maps(inputs)
    res = run_bass_kernel_spmd(nc, maps, core_ids=list(range(NCORES)))
    R = res.results
    y_s = np.stack([R[i]["y"][:LS] for i in range(NCORES)], 0)
    y_p = np.concatenate([R[i]["y"][LS:].reshape(2, LP, D) for i in range(NCORES)], 0)
    nk = np.concatenate([R[i]["nk"].reshape(2, 1, LP, 2, 64) for i in range(NCORES)], 0)
    nv = np.concatenate([R[i]["nv"].reshape(2, 1, LP, 2, 64) for i in range(NCORES)], 0)
    ns = np.concatenate([R[i]["ns"].reshape(128, 2, 2, 16).transpose(1, 2, 3, 0).reshape(2, 1, 2, D)
                         for i in range(NCORES)], 0)
    return (y_p.astype(np.float32), y_s.astype(np.float32), nk.astype(np.float32), nv.astype(np.float32),
            ns.astype(np.float32))
```

```python
import numpy as np
import ml_dtypes
import concourse.bass as bass
import concourse.mybir as mybir
from concourse.bass_utils import run_bass_kernel_spmd

F32, BF16 = mybir.dt.float32, mybir.dt.bfloat16
AF = mybir.ActivationFunctionType
ALU = mybir.AluOpType
AX = mybir.AxisListType

D = 2048
LS = 4096
LP = 256
NPS = 2
NTOK = LS + NPS * LP
EPS = 1e-6
NCORES = 8
DBG = {}


class Sem:
    def __init__(self, h):
        self.h = h
        self.n = 0


class Buf:
    def __init__(self, name=""):
        self.w = []
        self.r = []
        self.gen_r = []
        self.name = name
        self.sem = None


class Prog:
    ENG = ("pe", "act", "dve", "pool", "sp")
    COMPUTE = ("pe", "act", "dve", "pool")

    def __init__(self, nc, handles):
        self.nc = nc
        self.q = {e: [] for e in self.ENG}
        hs = list(handles)
        self.csem = {e: Sem(hs.pop()) for e in self.COMPUTE}
        self.bar = Sem(hs.pop())
        self.dpool = [Sem(h) for h in hs]
        nsw = len(self.dpool) // 3
        self.dfree = {"pool": self.dpool[:nsw], "sp": self.dpool[nsw:]}
        self.waited = {e: {} for e in self.ENG}
        self.pending = []
        self.bufs = []
        self.nops = {e: 0 for e in self.COMPUTE}
        self.entries = {e: {} for e in self.COMPUTE}
        self.sig_idx = {e: [] for e in self.COMPUTE}
        self.sig_cnt = {e: [] for e in self.COMPUTE}

    def buf(self, name=""):
        b = Buf(name)
        self.bufs.append(b)
        return b

    def bufs_n(self, n, name=""):
        return [self.buf(name + str(i)) for i in range(n)]

    def _resolve(self, tok):
        import bisect
        _, eng, idx = tok
        si = self.sig_idx[eng]
        p = bisect.bisect_left(si, idx)
        if p < len(si):
            return self.csem[eng], self.sig_cnt[eng][p]
        ent = self.entries[eng][idx]
        sem = self.csem[eng]
        sem.n += 1
        ent[1] = sem.h
        si.append(idx)
        self.sig_cnt[eng].append(sem.n)
        return sem, sem.n

    def _wait(self, eng, tok):
        if tok[0] == "op":
            if tok[1] == eng and eng == "pe":
                return
            sem, tgt = self._resolve(tok)
        else:
            sem, tgt, _ = tok
        if self.waited[eng].get(id(sem), 0) >= tgt:
            return
        self.waited[eng][id(sem)] = tgt
        self.q[eng].append([lambda e, h=sem.h, t=tgt: e.wait_ge(h, t), None])

    def _wait_many(self, eng, toks):
        best = {}
        for t in toks:
            if t[0] == "op":
                k = ("op", t[1])
                if k not in best or best[k][2] < t[2]:
                    best[k] = t
            else:
                k = id(t[0])
                if k not in best or best[k][1] < t[1]:
                    best[k] = t
        for t in best.values():
            self._wait(eng, t)

    def _hazards(self, eng, reads, writes, partial):
        toks = []
        for b in reads:
            toks += b.w
        for b in writes:
            if partial == "nowaw":
                if b.r:
                    b.gen_r = b.r
                    b.r = []
                    b.w = []
                toks += b.gen_r
            else:
                toks += b.w
                toks += b.r
                toks += b.gen_r
        self._wait_many(eng, toks)

    def _commit(self, tok, reads, writes, partial):
        for b in reads:
            b.r.append(tok)
        for b in writes:
            if partial:
                b.w.append(tok)
                if partial != "nowaw":
                    b.r = []
            else:
                b.w = [tok]
                b.r = []
                b.gen_r = []

    def op(self, eng, fn, reads=(), writes=(), partial=False):
        self._hazards(eng, reads, writes, partial)
        idx = self.nops[eng]
        self.nops[eng] += 1
        ent = [fn, None]
        self.entries[eng][idx] = ent
        self.q[eng].append(ent)
        tok = ("op", eng, idx)
        self._commit(tok, reads, writes, partial)
        return tok

    def dma(self, eng, out, in_, sbuf_buf, reads=(), writes=(), partial=False):
        self._hazards(eng, reads, writes, partial)
        if sbuf_buf.sem is None:
            sbuf_buf.sem = {}
        if eng not in sbuf_buf.sem:
            sbuf_buf.sem[eng] = self.dfree[eng].pop()
        sem = sbuf_buf.sem[eng]
        sem.n += 16
        tok = (sem, sem.n, "dma")
        self.q[eng].append([lambda e, o=out, i=in_: e.dma_start(out=o, in_=i), sem.h, 16])
        self._commit(tok, reads, writes, partial)
        self.pending.append(tok)
        return tok

    def barrier(self):
        for e in self.COMPUTE:
            if self.nops[e] > 0:
                self._wait("sp", ("op", e, self.nops[e] - 1))
        self._wait_many("sp", self.pending)
        self.pending = []
        self.bar.n += 1
        k = self.bar.n
        self.q["sp"].append([lambda e, h=self.bar.h: e.sem_inc(h, 1), None])
        for e in self.COMPUTE:
            self._wait(e, (self.bar, k, "bar"))
        for b in self.bufs:
            if b.sem is not None:
                for en, sm in b.sem.items():
                    self.dfree[en].append(sm)
                b.sem = None
            b.w = []
            b.r = []
            b.gen_r = []
        self.bufs = []

    def flush(self):
        nc = self.nc
        q = self.q

        def run(e, lst):
            for ent in lst:
                ins = ent[0](e)
                if ent[1] is not None:
                    ins.then_inc(ent[1], ent[2] if len(ent) > 2 else 1)

        with nc.Block() as blk:
            @blk.tensor
            def _(e):
                run(e, q["pe"])

            @blk.scalar
            def _(e):
                run(e, q["act"])

            @blk.vector
            def _(e):
                run(e, q["dve"])

            @blk.gpsimd
            def _(e):
                run(e, q["pool"])

            @blk.sync
            def _(e):
                run(e, q["sp"])
        self.q = {e: [] for e in self.ENG}
        for e in self.COMPUTE:
            self.entries[e] = {}


_UID = [0]


def sb(es, nc, name, shape, dt):
    _UID[0] += 1
    return es.enter_context(nc.sbuf_tensor("%s_%d" % (name, _UID[0]), shape, dt))


def rev_ap(ap):
    a = [list(p) for p in ap.ap]
    step, cnt = a[-1]
    off = ap.offset + step * (cnt - 1)
    a[-1] = [-step, cnt]
    return bass.AP(ap.tensor, off, a)


def phase_M(P, nc, T):
    with (
        nc.sbuf_tensor("m_cT", [128, 32], F32) as cT,
        nc.sbuf_tensor("m_sg", [128, 32], F32) as sg,
        nc.sbuf_tensor("m_sT", [128, 32], BF16) as sT,
        nc.sbuf_tensor("m_w0", [128, 3072], BF16) as w0,
        nc.sbuf_tensor("m_w1", [128, 3072], BF16) as w1,
        nc.sbuf_tensor("m_mrow", [2, 6144], F32) as mrow,
        nc.sbuf_tensor("m_brow", [2, 6144], F32) as brow,
        nc.sbuf_tensor("m_ng", [2, 2048], F32) as ng,
        nc.sbuf_tensor("m_orow", [2, 6144], F32) as orow,
    ):
        ps = T["ps"]
        b_cT, b_sT = P.buf(), P.buf()
        b_w = P.bufs_n(2)
        wt = [w0, w1]
        b_ps = P.bufs_n(6)
        b_mrow, b_brow, b_ng, b_orow = P.buf(), P.buf(), P.buf(), P.buf()
        P.dma("sp", cT[:], T["cvecT"], b_cT, writes=[b_cT])
        P.op("act", lambda e: e.activation(sg[:], cT[:], AF.Sigmoid), reads=[b_cT], writes=[b_sT])
        P.op("dve", lambda e: e.tensor_tensor(sT[:], sg[:], cT[:], ALU.mult), reads=[b_cT, b_sT], writes=[b_sT])
        cnt = 0
        for l in range(2):
            P.dma("sp", brow[:], T["mod_b"][l:l + 1, :].partition_broadcast(2), b_brow, writes=[b_brow])
            P.dma("sp", ng[:], T["norm_g"][l:l + 1, :].partition_broadcast(2), b_ng, writes=[b_ng])
            for hf in range(2):
                for k in range(16):
                    s = cnt % 2
                    cnt += 1
                    P.dma("pool", wt[s][:], T["mod_w"][l, k * 128:(k + 1) * 128, hf * 3072:(hf + 1) * 3072],
                          b_w[s], writes=[b_w[s]])
                    for j in range(6):
                        P.op("pe", lambda e, s=s, j=j, k=k: e.matmul(
                            ps[j][0:2, :], sT[:, 2 * k:2 * k + 2], wt[s][:, j * 512:(j + 1) * 512],
                            start=(k == 0), stop=(k == 15)),
                            reads=[b_w[s], b_sT], writes=[b_ps[j]], partial=(k > 0))
                for j in range(6):
                    c0 = hf * 3072 + j * 512
                    P.op("dve", lambda e, j=j, c0=c0: e.tensor_tensor(
                        mrow[:, c0:c0 + 512], ps[j][0:2, :], brow[:, c0:c0 + 512], ALU.add),
                        reads=[b_ps[j], b_brow], writes=[b_mrow], partial=True)
            P.op("dve", lambda e: e.scalar_tensor_tensor(
                orow[:, 0:2048], mrow[:, 2048:4096], 1.0, ng[:], ALU.add, ALU.mult),
                reads=[b_mrow, b_ng], writes=[b_orow])
            P.op("dve", lambda e: e.tensor_copy(orow[:, 2048:4096], mrow[:, 0:2048]),
                 reads=[b_mrow], writes=[b_orow], partial=True)
            P.op("dve", lambda e: e.tensor_copy(orow[:, 4096:6144], mrow[:, 4096:6144]),
                 reads=[b_mrow], writes=[b_orow], partial=True)
            P.dma("sp", T["modv"][l], orow[:], b_orow, reads=[b_orow])
        P.barrier()
        P.flush()


def load_bcast_rows(P, nc, T, l, tiles, bufs):
    modv = T["modv"]
    for key, (j, r) in (("A_s", (0, 0)), ("sh_s", (1, 0)), ("A_p", (0, 1)), ("sh_p", (1, 1))):
        P.dma("sp", tiles[key][:], modv[l][r:r + 1, j * 2048:(j + 1) * 2048].partition_broadcast(128),
              bufs[key], writes=[bufs[key]])


def norm_transpose_half(P, nc, T, xsrc, tok_tiles, hT, b_hT, bc, b_bc, work):
    ps = T["ps"]
    ident = T["ident"]
    xt, b_xt = work["xt"], work["b_xt"]
    st, b_st = work["st"], work["b_st"]
    hb, b_hb = work["hb"], work["b_hb"]
    b_pst = work["b_pst"]

    def front(i):
        toff, isp = tok_tiles[i]
        s = i % 2
        x4 = i % len(xt)
        P.dma("sp", xt[x4][:], xsrc[toff:toff + 128, :], b_xt[x4], writes=[b_xt[x4]])
        jk, bjk = work["junk2"][s], work["b_junk2"][s]
        P.op("act", lambda e, s=s, jk=jk, x4=x4: e.activation(jk[:], xt[x4][:], AF.Square, accum_out=st[s][:, 0:1]),
             reads=[b_xt[x4]], writes=[bjk, b_st[s]])
        P.op("act", lambda e, s=s: e.activation(st[s][:, 1:2], st[s][:, 0:1], AF.Sqrt, scale=1.0 / D,
                                                 bias=T["epsb"][:, 0:1]),
             reads=[b_st[s]], writes=[b_st[s]], partial=True)
        P.op("dve", lambda e, s=s: e.reciprocal(st[s][:, 2:3], st[s][:, 1:2]), reads=[b_st[s]], writes=[b_st[s]],
             partial=True)
        A = bc["A_p" if isp else "A_s"]
        SH = bc["sh_p" if isp else "sh_s"]
        bA = b_bc["A_p" if isp else "A_s"]
        bS = b_bc["sh_p" if isp else "sh_s"]
        tm, btm = work["tmp2"][s], work["b_tmp2"][s]
        P.op("dve", lambda e, s=s, A=A, tm=tm, x4=x4: e.scalar_tensor_tensor(tm[:], xt[x4][:], st[s][:, 2:3], A[:],
                                                                             ALU.mult, ALU.mult),
             reads=[b_xt[x4], b_st[s], bA], writes=[btm])
        P.op("pool", lambda e, s=s, SH=SH, tm=tm: e.tensor_tensor(hb[s][:], tm[:], SH[:], ALU.add),
             reads=[btm, bS], writes=[b_hb[s]])

    def back(i):
        s = i % 2
        for half in range(2):
            pb = ps[2 * s + half]
            bp = b_pst[2 * s + half]
            pbv = pb[:].bitcast(BF16)
            for kk in range(8):
                k = half * 8 + kk
                P.op("pe", lambda e, pbv=pbv, kk=kk, k=k, s=s: e.transpose(
                    pbv[:, kk * 128:(kk + 1) * 128], hb[s][:, k * 128:(k + 1) * 128], ident[:]),
                    reads=[b_hb[s]], writes=[bp], partial=(kk > 0))
            dst = hT[:, half * 8:(half + 1) * 8, i * 128:(i + 1) * 128]
            src = pbv.rearrange("p (k t) -> p k t", k=8)
            if half == 0:
                P.op("act", lambda e, dst=dst, src=src: e.copy(dst, src), reads=[bp], writes=[b_hT], partial="nowaw")
            else:
                P.op("dve", lambda e, dst=dst, src=src: e.tensor_copy(dst, src), reads=[bp], writes=[b_hT],
                     partial="nowaw")

    n = len(tok_tiles)
    front(0)
    for i in range(n):
        if i + 1 < n:
            front(i + 1)
        back(i)


def phase_A(P, nc, T, debug=False):
    ps = T["ps"]
    W = T["a_w_in"]
    groups = []
    for g in range(2):
        groups.append(("q", [("qT", (g * 4 + j) * 128, (g * 4 + j) * 128) for j in range(4)]))
    groups.append(("k", [("kT0", 0, 1024), ("kT1", 0, 1088)]))
    for g in range(2):
        groups.append(("ga", [("gaT", (g * 4 + j) * 128, 1280 + (g * 4 + j) * 128) for j in range(4)]))
    for g in range(6):
        groups.append(("hy", [("hyT", (g * 4 + j) * 128, 2304 + (g * 4 + j) * 128) for j in range(4)]))
    for g in range(2):
        groups.append(("gh", [("ghT", (g * 4 + j) * 128, 5376 + (g * 4 + j) * 128) for j in range(4)]))

    Wv = W.rearrange("(k p) c -> p k c", p=128)
    for half in range(DBG.get('halves', 2)):
        tok_tiles = [(half * 2048 + i * 128, False) for i in range(16)] + \
                    [(LS + half * 256 + i * 128, True) for i in range(2)]
        import contextlib
        with contextlib.ExitStack() as es0:
            hT = sb(es0, nc, "a_hT", [128, 16, 2304], BF16)
            b_hT = P.buf("hT")
            with contextlib.ExitStack() as es1:
                A_ = lambda n, shp, dt: sb(es1, nc, n, shp, dt)
                xt0 = A_("a_xt0", [128, 2048], F32); xt1 = A_("a_xt1", [128, 2048], F32)
                xt2 = A_("a_xt2", [128, 2048], F32); xt3 = A_("a_xt3", [128, 2048], F32)
                junk = A_("a_junk", [128, 2048], F32); tmp = A_("a_tmp", [128, 2048], F32)
                junkb = A_("a_junkb", [128, 2048], F32); tmpb = A_("a_tmpb", [128, 2048], F32)
                st0 = A_("a_st0", [128, 4], F32); st1 = A_("a_st1", [128, 4], F32)
                hb0 = A_("a_hb0", [128, 2048], BF16); hb1 = A_("a_hb1", [128, 2048], BF16)
                As = A_("a_As", [128, 2048], F32); shs = A_("a_shs", [128, 2048], F32)
                Ap = A_("a_Ap", [128, 2048], F32); shp = A_("a_shp", [128, 2048], F32)
                bc = {"A_s": As, "sh_s": shs, "A_p": Ap, "sh_p": shp}
                b_bc = {k: P.buf(k) for k in bc}
                load_bcast_rows(P, nc, T, 0, bc, b_bc)
                work = dict(xt=[xt0, xt1, xt2, xt3], b_xt=P.bufs_n(4), junk=junk, b_junk=P.buf(), st=[st0, st1],
                            b_st=P.bufs_n(2), tmp=tmp, b_tmp=P.buf(), hb=[hb0, hb1], b_hb=P.bufs_n(2),
                            b_pst=P.bufs_n(4), junk2=[junk, junkb], b_junk2=P.bufs_n(2), tmp2=[tmp, tmpb],
                            b_tmp2=P.bufs_n(2))
                norm_transpose_half(P, nc, T, T["x"], tok_tiles, hT, b_hT, bc, b_bc, work)
                P.barrier()
                P.flush()
            if DBG.get('norm_only'):
                continue
            with contextlib.ExitStack() as es2:
                A_ = lambda n, shp, dt: sb(es2, nc, n, shp, dt)
                wg0 = A_("a_w0", [128, 16, 512], BF16); wg1 = A_("a_w1", [128, 16, 512], BF16)
                wkv = A_("a_wkv", [128, 16, 256], BF16)
                cos_t = A_("a_cos", [128, 2048], F32); sin_t = A_("a_sin", [128, 2048], F32)
                stg0 = A_("a_stg0", [128, 512], BF16); stg1 = A_("a_stg1", [128, 512], BF16)
                stg2 = A_("a_stg2", [128, 512], BF16); stg3 = A_("a_stg3", [128, 512], BF16)
                qs0 = A_("a_qs0", [128, 512], BF16); qs1 = A_("a_qs1", [128, 512], BF16)
                t1 = A_("a_t1", [128, 512], F32); t2 = A_("a_t2", [128, 512], F32)
                kvf0 = A_("a_kvf0", [128, 256], F32); kvf1 = A_("a_kvf1", [128, 256], F32)
                vb0 = A_("a_vb0", [128, 128], BF16); vb1 = A_("a_vb1", [128, 128], BF16)
                b_hT = P.buf("hT2")
                wg = [wg0, wg1]
                b_wg = P.bufs_n(2)
                b_wkv = P.buf()
                b_cos, b_sin = P.buf(), P.buf()
                stg = [stg0, stg1, stg2, stg3]
                b_stg = P.bufs_n(4)
                qs = [qs0, qs1]
                b_qs = P.bufs_n(2)
                b_t1, b_t2 = P.buf(), P.buf()
                kvf = [kvf0, kvf1]
                b_kvf = P.bufs_n(2)
                vb = [vb0, vb1]
                b_vb = P.bufs_n(2)
                b_ps = P.bufs_n(8)
                P.dma("sp", cos_t[:], T["rope_cos"][:, half * 2048:(half + 1) * 2048], b_cos, writes=[b_cos])
                P.dma("sp", sin_t[:], T["rope_sin"][:, half * 2048:(half + 1) * 2048], b_sin, writes=[b_sin])
                for kq in range(16):
                    P.dma("pool", wkv[:, kq, :], Wv[:, kq, 1024:1280], b_wkv, writes=[b_wkv], partial=(kq > 0))
                for i, (toff, isp) in enumerate(tok_tiles[:DBG.get('nkv', 99)]):
                    pi = i % 2
                    pst = ps[6 + pi]
                    for k in range(16):
                        P.op("pe", lambda e, pst=pst, k=k, i=i: e.matmul(
                            pst[:, 0:256], hT[:, k, i * 128:(i + 1) * 128], wkv[:, k, :],
                            start=(k == 0), stop=(k == 15)),
                            reads=[b_hT, b_wkv], writes=[b_ps[6 + pi]], partial=(k > 0))
                    if isp and not DBG.get('no_isp'):
                        P.op("act", lambda e, pst=pst, pi=pi: e.copy(kvf[pi][:], pst[:, 0:256]),
                             reads=[b_ps[6 + pi]], writes=[b_kvf[pi]])
                        r0 = toff - LS
                        if not DBG.get('no_nk'):
                            P.dma("sp", T["nk"][r0:r0 + 128, :], kvf[pi][:, 0:128], b_kvf[pi], reads=[b_kvf[pi]])
                        if not DBG.get('no_nv'):
                            P.dma("sp", T["nv"][r0:r0 + 128, :], kvf[pi][:, 128:256], b_kvf[pi], reads=[b_kvf[pi]])
                    P.op("act", lambda e, pst=pst, pi=pi: e.copy(vb[pi][:], pst[:, 128:256]),
                         reads=[b_ps[6 + pi]], writes=[b_vb[pi]])
                    P.dma("sp", T["vtok"][toff:toff + 128, :], vb[pi][:], b_vb[pi], reads=[b_vb[pi]])
                chunks = [(c * 512, 512, half * 2048 + c * 512, False) for c in range(4)] + \
                         [(2048, 256, LS + half * 256, True)]
                evac_i = 0
                psi = 0
                for gi, (kind, blocks) in enumerate(groups[:DBG.get('ngroups', 99)] if not DBG.get('gsel') else [groups[i] for i in DBG['gsel']]):
                    s = gi % 2
                    if kind == "k":
                        for j, (name, r0, c0) in enumerate(blocks):
                            for dup in range(2):
                                for kq in range(16):
                                    P.dma("pool", wg[s][:, kq, j * 128 + dup * 64:j * 128 + dup * 64 + 64],
                                          Wv[:, kq, c0:c0 + 64], b_wg[s], writes=[b_wg[s]],
                                          partial=(j + dup + kq > 0))
                    else:
                        c0 = blocks[0][2]
                        for kq in range(16):
                            P.dma("pool", wg[s][:, kq, 0:512], Wv[:, kq, c0:c0 + 512],
                                  b_wg[s], writes=[b_wg[s]], partial=(kq > 0))
                    for (l0, n, g0, isp) in chunks:
                        for j, (name, r0, c0) in enumerate(blocks):
                            pj = psi % 4
                            psi += 1
                            pst = ps[pj]
                            for k in range(16):
                                P.op("pe", lambda e, pst=pst, k=k, s=s, j=j, l0=l0, n=n: e.matmul(
                                    pst[:, 0:n], wg[s][:, k, j * 128:(j + 1) * 128], hT[:, k, l0:l0 + n],
                                    start=(k == 0), stop=(k == 15)),
                                    reads=[b_hT, b_wg[s]], writes=[b_ps[pj]], partial=(k > 0))
                            if name.startswith("kT"):
                                dst = T["kT"][int(name[2]), :, g0:g0 + n]
                            else:
                                dst = T[name][r0:r0 + 128, g0:g0 + n]
                            si = evac_i % 4
                            evac_i += 1
                            if kind in ("q", "k") and not isp:
                                qi = evac_i % 2
                                P.op("act", lambda e, pst=pst, qi=qi, n=n: e.copy(qs[qi][:, 0:n], pst[:, 0:n]),
                                     reads=[b_ps[pj]], writes=[b_qs[qi]])
                                pr = ps[4 + qi]
                                P.op("pe", lambda e, pr=pr, qi=qi, n=n: e.matmul(
                                    pr[:, 0:n], T["ropeP"][:], qs[qi][:, 0:n], start=True, stop=True),
                                    reads=[b_qs[qi]], writes=[b_ps[4 + qi]])
                                P.op("dve", lambda e, qi=qi, l0=l0, n=n: e.tensor_tensor(
                                    t1[:, 0:n], qs[qi][:, 0:n], cos_t[:, l0:l0 + n], ALU.mult),
                                    reads=[b_qs[qi], b_cos], writes=[b_t1])
                                P.op("dve", lambda e, pr=pr, l0=l0, n=n: e.tensor_tensor(
                                    t2[:, 0:n], pr[:, 0:n], sin_t[:, l0:l0 + n], ALU.mult),
                                    reads=[b_ps[4 + qi], b_sin], writes=[b_t2])
                                P.op("dve", lambda e, si=si, n=n: e.tensor_tensor(
                                    stg[si][:, 0:n], t1[:, 0:n], t2[:, 0:n], ALU.add),
                                    reads=[b_t1, b_t2], writes=[b_stg[si]])
                            else:
                                if evac_i % 2 == 0:
                                    P.op("act", lambda e, pst=pst, si=si, n=n: e.copy(stg[si][:, 0:n], pst[:, 0:n]),
                                         reads=[b_ps[pj]], writes=[b_stg[si]])
                                else:
                                    P.op("dve", lambda e, pst=pst, si=si, n=n: e.tensor_copy(
                                        stg[si][:, 0:n], pst[:, 0:n]), reads=[b_ps[pj]], writes=[b_stg[si]])
                            P.dma("sp", dst, stg[si][:, 0:n], b_stg[si], reads=[b_stg[si]])
                P.barrier()
                P.flush()


def bcast_last(ap2d, n):
    a = [list(p) for p in ap2d.ap]
    return bass.AP(ap2d.tensor, ap2d.offset, a + [[0, n]])


def phase_B(P, nc, T):
    import contextlib
    ps = T["ps"]
    ident = T["ident"]
    seqs = [(0, LS, True), (LS, LP, False), (LS + LP, LP, False)]
    with contextlib.ExitStack() as es:
        A_ = lambda n, shp, dt: sb(es, nc, n, shp, dt)
        kT = [A_("b_kT0", [128, LS], BF16), A_("b_kT1", [128, LS], BF16)]
        kcT = [A_("b_kc0", [128, 512], BF16), A_("b_kc1", [128, 512], BF16)]
        ckd = A_("b_ckd", [128, 4, 2, 2, 64], BF16)
        vaug = A_("b_vaug", [128, LS // 128, 2, 65], BF16)
        cvaug = A_("b_cvaug", [128, 4, 2, 65], BF16)
        qT = [A_("b_q0", [128, LS], BF16), A_("b_q1", [128, LS], BF16)]
        ga = [A_("b_ga0", [128, LS], BF16), A_("b_ga1", [128, LS], BF16)]
        sga = A_("b_sga", [128, LS], BF16)
        ptA = [A_("b_ptA0", [128, 512], BF16), A_("b_ptA1", [128, 512], BF16)]
        ptB = [A_("b_ptB0", [128, 384], BF16), A_("b_ptB1", [128, 384], BF16)]
        att = [A_("b_att0", [128, 128], BF16), A_("b_att1", [128, 128], BF16)]
        stage = [A_("b_stg0", [128, 512], BF16), A_("b_stg1", [128, 512], BF16)]
        mask = A_("b_mask", [128, 384], BF16)
        sinkb = A_("b_sinkb", [128, 16], F32)
        esink = A_("b_esink", [128, 16], F32)
        den = [A_("b_den0", [128, 2], F32), A_("b_den1", [128, 2], F32)]
        rden = [A_("b_rden0", [128, 2], F32), A_("b_rden1", [128, 2], F32)]

        b_mask, b_es = P.buf(), P.buf()
        P.dma("sp", mask[:], T["d_amask"], b_mask, writes=[b_mask])
        P.dma("sp", sinkb[:], T["a_sink"][0:1, :].partition_broadcast(128), b_es, writes=[b_es])
        P.op("act", lambda e: e.activation(esink[:], sinkb[:], AF.Exp), reads=[b_es], writes=[b_es])
        P.op("dve", lambda e: e.memset(vaug[:], 1.0), writes=[b_mask], partial=True)
        P.op("dve", lambda e: e.memset(cvaug[:], 1.0), writes=[b_mask], partial=True)
        b_ckd, b_kc, b_cv = P.buf(), P.bufs_n(2), P.buf()
        ckv = T["ck"].rearrange("(t p) c -> p t c", p=128)
        cvv = T["cv"].rearrange("(t p) c -> p t c", p=128)
        for kv in range(2):
            for dup in range(2):
                P.dma("pool", ckd[:, :, kv, dup, :], ckv[:, :, kv * 64:(kv + 1) * 64], b_ckd, writes=[b_ckd],
                      partial=True)
            P.dma("pool", cvaug[:, :, kv, 0:64], cvv[:, :, kv * 64:(kv + 1) * 64], b_cv, reads=[b_mask],
                  writes=[b_cv], partial=True)
        b_pt = P.bufs_n(8)
        for kv in range(2):
            for st_ in range(4):
                pb = ps[st_ % 2][:].bitcast(BF16)
                P.op("pe", lambda e, pb=pb, st_=st_, kv=kv: e.transpose(
                    pb[:, 0:128], ckd[:, st_, kv].rearrange("p a b -> p (a b)"), ident[:]),
                    reads=[b_ckd], writes=[b_pt[st_ % 2]])
                P.op("dve", lambda e, pb=pb, st_=st_, kv=kv: e.tensor_copy(
                    kcT[kv][:, st_ * 128:(st_ + 1) * 128], pb[:, 0:128]),
                    reads=[b_pt[st_ % 2]], writes=[b_kc[kv]], partial=True)
        P.barrier()
        b_kT, b_v = P.bufs_n(2), P.buf()
        b_q, b_ga, b_sga = P.bufs_n(2), P.bufs_n(2), P.buf()
        b_ptA, b_ptB, b_att, b_stage = P.bufs_n(2), P.bufs_n(2), P.bufs_n(2), P.bufs_n(2)
        b_den = P.bufs_n(2)
        b_ps = P.bufs_n(8)
        cnt = 0
        for (tok0, L, has_ctx) in seqs:
            nqb = L // 128
            for kv in range(2):
                P.dma("sp", kT[kv][:, 0:L], T["kT"][kv, :, tok0:tok0 + L], b_kT[kv], writes=[b_kT[kv]])
                P.dma("sp", vaug[:, 0:nqb, kv, 0:64],
                      T["vtok"][tok0:tok0 + L, kv * 64:(kv + 1) * 64].rearrange("(t p) c -> p t c", p=128),
                      b_v, writes=[b_v], partial=(kv > 0))
            for hp in range(8):
                s = cnt % 2
                cnt += 1
                kv = hp // 4
                P.dma("sp", qT[s][:, 0:L], T["qT"][hp * 128:(hp + 1) * 128, tok0:tok0 + L], b_q[s], writes=[b_q[s]])
                P.dma("sp", ga[s][:, 0:L], T["gaT"][hp * 128:(hp + 1) * 128, tok0:tok0 + L], b_ga[s],
                      writes=[b_ga[s]])
                P.op("act", lambda e, s=s, L=L: e.activation(sga[:, 0:L], ga[s][:, 0:L], AF.Silu),
                     reads=[b_ga[s]], writes=[b_sga])
                def geom(qb, nqb=nqb, has_ctx=has_ctx):
                    if has_ctx:
                        loc = [(j, j - qb + 1) for j in (qb - 1, qb, qb + 1) if 0 <= j < nqb]
                    else:
                        loc = [(j, j) for j in range(nqb)]
                    return loc, loc[0][1], loc[-1][1] + 1

                def S_unit(qb, hh, s=s, kv=kv, has_ctx=has_ctx):
                    loc, lo, hi = geom(qb)
                    pr = slice(hh * 64, (hh + 1) * 64)
                    pa, pbk = ps[hh * 2], ps[hh * 2 + 1]
                    qsl = qT[s][pr, qb * 128:(qb + 1) * 128]
                    if has_ctx:
                        for c in range(4):
                            P.op("pe", lambda e, pa=pa, c=c, pr=pr, qsl=qsl, kv=kv: e.matmul(
                                pa[:, c * 128:(c + 1) * 128], kcT[kv][pr, c * 128:(c + 1) * 128], qsl,
                                start=True, stop=True),
                                reads=[b_kc[kv], b_q[s]], writes=[b_ps[hh * 2]], partial=(c > 0))
                        P.op("act", lambda e, pa=pa, hh=hh: e.activation(ptA[hh][:], pa[:], AF.Exp, scale=0.125),
                             reads=[b_ps[hh * 2]], writes=[b_ptA[hh]])
                    for n_, (j, sl) in enumerate(loc):
                        P.op("pe", lambda e, pbk=pbk, sl=sl, j=j, pr=pr, qsl=qsl, kv=kv: e.matmul(
                            pbk[:, sl * 128:(sl + 1) * 128], kT[kv][pr, j * 128:(j + 1) * 128], qsl,
                            start=True, stop=True),
                            reads=[b_kT[kv], b_q[s]], writes=[b_ps[hh * 2 + 1]], partial=(n_ > 0))
                    P.op("act", lambda e, pbk=pbk, hh=hh, lo=lo, hi=hi: e.activation(
                        ptB[hh][:, lo * 128:hi * 128], pbk[:, lo * 128:hi * 128], AF.Exp, scale=0.125),
                        reads=[b_ps[hh * 2 + 1]], writes=[b_ptB[hh]])
                    if has_ctx:
                        P.op("dve", lambda e, hh=hh, lo=lo, hi=hi: e.tensor_tensor(
                            ptB[hh][:, lo * 128:hi * 128], ptB[hh][:, lo * 128:hi * 128],
                            mask[:, lo * 128:hi * 128], ALU.mult),
                            reads=[b_ptB[hh], b_mask], writes=[b_ptB[hh]])

                def PV_unit(qb, hh, kv=kv, has_ctx=has_ctx):
                    loc, lo, hi = geom(qb)
                    o = qb % 2
                    psOv = ps[4 + o][:, 0:130].rearrange("p (h c) -> p h c", h=2)
                    mm = []
                    if has_ctx:
                        for c in range(4):
                            mm.append((ptA[hh][:, c * 128:(c + 1) * 128], cvaug[:, c, kv, :], b_ptA[hh], b_cv))
                    for (j, sl) in loc:
                        mm.append((ptB[hh][:, sl * 128:(sl + 1) * 128], vaug[:, j, kv, :], b_ptB[hh], b_v))
                    for n_, (lh, rh, bl, br) in enumerate(mm):
                        P.op("pe", lambda e, psOv=psOv, hh=hh, lh=lh, rh=rh, n_=n_, nm=len(mm): e.matmul(
                            psOv[:, hh, :], lh, rh, start=(n_ == 0), stop=(n_ == nm - 1)),
                            reads=[bl, br], writes=[b_ps[4 + o]], partial=(n_ > 0 or hh > 0))

                def FIN_unit(qb, s=s, hp=hp, nqb=nqb, tok0=tok0, cnt=cnt):
                    o = qb % 2
                    psOv = ps[4 + o][:, 0:130].rearrange("p (h c) -> p h c", h=2)
                    P.op("dve", lambda e, psOv=psOv, o=o, hp=hp: e.tensor_tensor(
                        den[o][:], psOv[:, :, 64], esink[:, hp * 2:hp * 2 + 2], ALU.add),
                        reads=[b_ps[4 + o], b_es], writes=[b_den[o]])
                    P.op("dve", lambda e, o=o: e.reciprocal(rden[o][:], den[o][:]), reads=[b_den[o]],
                         writes=[b_den[o]], partial=True)
                    P.op("dve", lambda e, psOv=psOv, o=o: e.tensor_tensor(
                        att[o][:].rearrange("p (h c) -> p h c", h=2), psOv[:, :, 0:64], bcast_last(rden[o][:], 64),
                        ALU.mult),
                        reads=[b_ps[4 + o], b_den[o]], writes=[b_att[o]])
                    pT = ps[6 + o][:].bitcast(BF16)
                    P.op("pe", lambda e, pT=pT, o=o: e.transpose(pT[:, 0:128], att[o][:], ident[:]),
                         reads=[b_att[o]], writes=[b_ps[6 + o]])
                    g4 = qb // 4
                    sg_ = (cnt * 1024 + g4) % 2
                    P.op("dve", lambda e, pT=pT, sg_=sg_, qb=qb: e.tensor_tensor(
                        stage[sg_][:, (qb % 4) * 128:(qb % 4 + 1) * 128], pT[:, 0:128],
                        sga[:, qb * 128:(qb + 1) * 128], ALU.mult),
                        reads=[b_ps[6 + o], b_sga], writes=[b_stage[sg_]], partial=(qb % 4 > 0))
                    if qb % 4 == 3 or qb == nqb - 1:
                        n = (qb % 4 + 1) * 128
                        t0 = tok0 + g4 * 512
                        P.dma("sp", T["mixT"][hp * 128:(hp + 1) * 128, t0:t0 + n], stage[sg_][:, 0:n],
                              b_stage[sg_], reads=[b_stage[sg_]])

                units = [(qb, hh) for qb in range(nqb) for hh in range(2)]
                S_unit(*units[0])
                pend_fin = None
                for ui, (qb, hh) in enumerate(units):
                    if ui + 1 < len(units):
                        S_unit(*units[ui + 1])
                    PV_unit(qb, hh)
                    if pend_fin is not None:
                        FIN_unit(pend_fin)
                        pend_fin = None
                    if hh == 1:
                        pend_fin = qb
                if pend_fin is not None:
                    FIN_unit(pend_fin)
        P.barrier()
        P.flush()


def phase_C0(P, nc, T):
    import contextlib
    seqs = [(0, LS), (LS, LP), (LS + LP, LP)]
    with contextlib.ExitStack() as es:
        A_ = lambda n, shp, dt: sb(es, nc, n, shp, dt)
        hsw = A_("c0_hsw", [128, 4, 24], F32)
        ut = [A_("c0_ut0", [128, LS + 2], BF16), A_("c0_ut1", [128, LS + 2], BF16), A_("c0_ut2", [128, LS + 2], BF16)]
        accs = [A_("c0_acc", [128, LS], F32), A_("c0_accb", [128, LS], F32)]
        acc2s = [A_("c0_acc2", [128, LS], F32), A_("c0_acc2b", [128, LS], F32)]
        ob = [A_("c0_ob0", [128, LS], BF16), A_("c0_ob1", [128, LS], BF16)]
        b_hsw, b_ut, b_accs, b_acc2s, b_ob = P.buf(), P.bufs_n(3), P.bufs_n(2), P.bufs_n(2), P.bufs_n(2)
        P.dma("sp", hsw[:], T["hsw"], b_hsw, writes=[b_hsw])
        items = [(tok0, L, ct) for (tok0, L) in seqs for ct in range(24)]

        def front(it):
            tok0, L, ct = items[it]
            u = it % 3
            P.op("pool", lambda e, u=u: e.memset(ut[u][:, 0:1], 0.0), writes=[b_ut[u]])
            P.op("pool", lambda e, u=u, L=L: e.memset(ut[u][:, L + 1:L + 2], 0.0), writes=[b_ut[u]], partial=True)
            P.dma("sp", ut[u][:, 1:L + 1], T["hyT"][ct * 128:(ct + 1) * 128, tok0:tok0 + L], b_ut[u],
                  writes=[b_ut[u]], partial=True)

        def back(it):
            tok0, L, ct = items[it]
            u = it % 3
            s = it % 2
            acc, acc2, b_acc, b_acc2 = accs[s], acc2s[s], b_accs[s], b_acc2s[s]
            P.op("act", lambda e, u=u, L=L, ct=ct, acc=acc: e.activation(
                acc[:, 0:L], ut[u][:, 0:L], AF.Identity, scale=hsw[:, 0, ct:ct + 1], bias=hsw[:, 3, ct:ct + 1]),
                reads=[b_ut[u], b_hsw], writes=[b_acc])
            P.op("dve", lambda e, u=u, L=L, ct=ct, acc=acc, acc2=acc2: e.scalar_tensor_tensor(
                acc2[:, 0:L], ut[u][:, 1:L + 1], hsw[:, 1, ct:ct + 1], acc[:, 0:L], ALU.mult, ALU.add),
                reads=[b_ut[u], b_acc, b_hsw], writes=[b_acc2])
            P.op("dve", lambda e, u=u, s=s, L=L, ct=ct, acc2=acc2: e.scalar_tensor_tensor(
                ob[s][:, 0:L], ut[u][:, 2:L + 2], hsw[:, 2, ct:ct + 1], acc2[:, 0:L], ALU.mult, ALU.add),
                reads=[b_ut[u], b_acc2, b_hsw], writes=[b_ob[s]])
            P.dma("sp", T["ucT"][ct * 128:(ct + 1) * 128, tok0:tok0 + L], ob[s][:, 0:L], b_ob[s],
                  reads=[b_ob[s]])

        n = len(items)
        front(0)
        if n > 1:
            front(1)
        for it in range(n):
            if it + 2 < n:
                front(it + 2)
            back(it)
        P.barrier()
        P.flush()


def phase_CF(P, nc, T, Lf, taps_dst, zemb, tposb):
    import contextlib
    import math
    ps = T["ps"]
    nch = max(1, Lf // 512)
    cw = min(512, Lf)
    with contextlib.ExitStack() as es:
        A_ = lambda n, shp, dt: sb(es, nc, n, shp, dt)
        ze = A_("cf_ze", [33, Lf], F32)
        tp_ = A_("cf_tpos", [128, Lf], F32)
        w1 = A_("cf_w1", [33, 64], F32)
        w2 = A_("cf_w2", [64, 64], F32)
        bb = A_("cf_bb", [64, 2], F32)
        w3 = A_("cf_w3", [64, 4096], BF16)
        dec = A_("cf_dec", [128, 32], F32)
        ndec = A_("cf_ndec", [128, 32], F32)
        h1 = A_("cf_h1", [64, Lf], F32)
        h2 = A_("cf_h2", [64, Lf], F32)
        h2b = A_("cf_h2b", [64, Lf], BF16)
        tmp = A_("cf_tmp", [64, 512], F32)
        tmq = A_("cf_tmq", [64, 512], F32)
        raw = [A_("cf_raw0", [128, Lf], F32), A_("cf_raw1", [128, Lf], F32)]
        win = [A_("cf_win0", [128, 512], F32), A_("cf_win1", [128, 512], F32)]
        junk = A_("cf_junk", [128, Lf], F32)
        ss = A_("cf_ss", [128, 8], F32)
        tpb = [A_("cf_tp0", [128, 2 * Lf], BF16), A_("cf_tp1", [128, 2 * Lf], BF16)]
        b_c = P.bufs_n(8)
        P.dma("sp", ze[:], zemb, b_c[0], writes=[b_c[0]])
        P.dma("sp", tp_[:], tposb, b_c[1], writes=[b_c[1]])
        P.dma("sp", w1[:], T["hy_w1"], b_c[2], writes=[b_c[2]])
        P.dma("sp", w2[:], T["hy_w2"], b_c[3], writes=[b_c[3]])
        P.dma("sp", bb[:], T["hy_bb"], b_c[4], writes=[b_c[4]])
        for q4 in range(4):
            P.dma("pool", w3[:, q4 * 1024:(q4 + 1) * 1024], T["hy_w3"][:, q4 * 1024:(q4 + 1) * 1024], b_c[5],
                  writes=[b_c[5]], partial=(q4 > 0))
        P.dma("sp", dec[:], T["hy_decT"], b_c[6], writes=[b_c[6]])
        P.op("act", lambda e: e.activation(dec[:], dec[:], AF.Abs), reads=[b_c[6]], writes=[b_c[6]])
        P.op("dve", lambda e: e.tensor_scalar(ndec[:], dec[:], -1.0, None, ALU.mult),
             reads=[b_c[6]], writes=[b_c[7]])
        b_h1, b_h2, b_h2b, b_tmp, b_tmq = P.buf(), P.buf(), P.buf(), P.buf(), P.buf()
        b_ps = P.bufs_n(8)
        TWO_PI = 2.0 * math.pi
        for layer in range(2):
            src = ze if layer == 0 else h1
            wm = w1 if layer == 0 else w2
            kk = 33 if layer == 0 else 64
            dst = h1 if layer == 0 else h2
            b_src = b_c[0] if layer == 0 else b_h1
            b_dst = b_h1 if layer == 0 else b_h2
            for c in range(nch):
                pj = c % 2
                P.op("pe", lambda e, pj=pj, c=c, src=src, wm=wm, kk=kk: e.matmul(
                    ps[pj][0:64, 0:cw], wm[0:kk, :], src[0:kk, c * cw:(c + 1) * cw], start=True, stop=True),
                    reads=[b_src, b_c[2], b_c[3]], writes=[b_ps[pj]])
                MAGIC = 12582912.0
                P.op("act", lambda e, pj=pj, layer=layer: e.activation(
                    tmp[:, 0:cw], ps[pj][0:64, 0:cw], AF.Identity, bias=bb[:, layer:layer + 1]),
                    reads=[b_ps[pj], b_c[4]], writes=[b_tmp])
                P.op("dve", lambda e: e.tensor_scalar(
                    tmq[:, 0:cw], tmp[:, 0:cw], 1.0 / TWO_PI, MAGIC, ALU.mult, ALU.add),
                    reads=[b_tmp], writes=[b_tmq])
                P.op("dve", lambda e: e.tensor_scalar(
                    tmq[:, 0:cw], tmq[:, 0:cw], -MAGIC, -TWO_PI, ALU.add, ALU.mult),
                    reads=[b_tmq], writes=[b_tmq])
                P.op("dve", lambda e: e.tensor_tensor(tmp[:, 0:cw], tmp[:, 0:cw], tmq[:, 0:cw], ALU.add),
                     reads=[b_tmp, b_tmq], writes=[b_tmp])
                P.op("act", lambda e, c=c, dst=dst: e.activation(dst[:, c * cw:(c + 1) * cw], tmp[:, 0:cw], AF.Sin),
                     reads=[b_tmp], writes=[b_dst], partial=(c > 0))
        P.op("act", lambda e: e.copy(h2b[:], h2[:]), reads=[b_h2], writes=[b_h2b])
        b_raw, b_win, b_junk, b_ss, b_tpb = P.bufs_n(2), P.bufs_n(2), P.buf(), P.buf(), P.bufs_n(2)
        it = 0
        for order in range(2):
            for blk in range(8):
                s = it % 2
                it += 1
                for d in range(2):
                    cbk = d * 16 + order * 8 + blk
                    for c in range(nch):
                        pj = 2 + (c % 2)
                        wj = c % 2
                        P.op("pe", lambda e, pj=pj, c=c, cbk=cbk: e.matmul(
                            ps[pj][:, 0:cw], w3[:, cbk * 128:(cbk + 1) * 128], h2b[:, c * cw:(c + 1) * cw],
                            start=True, stop=True),
                            reads=[b_h2b, b_c[5]], writes=[b_ps[pj]])
                        P.op("act", lambda e, wj=wj, c=c, cbk=cbk: e.activation(
                            win[wj][:, 0:cw], tp_[:, c * cw:(c + 1) * cw], AF.Exp, scale=ndec[:, cbk:cbk + 1]),
                            reads=[b_c[1], b_c[7]], writes=[b_win[wj]])
                        P.op("dve", lambda e, pj=pj, wj=wj, c=c, d=d: e.tensor_tensor(
                            raw[d][:, c * cw:(c + 1) * cw], ps[pj][:, 0:cw], win[wj][:, 0:cw], ALU.mult),
                            reads=[b_ps[pj], b_win[wj]], writes=[b_raw[d]], partial=(c > 0))
                P.op("act", lambda e: e.activation(junk[:], raw[0][:], AF.Square, accum_out=ss[:, 0:1]),
                     reads=[b_raw[0]], writes=[b_junk, b_ss])
                P.op("act", lambda e: e.activation(junk[:, 1:Lf], raw[1][:, 1:Lf], AF.Square, accum_out=ss[:, 1:2]),
                     reads=[b_raw[1]], writes=[b_junk, b_ss], partial=True)
                P.op("dve", lambda e: e.tensor_tensor(ss[:, 2:3], ss[:, 0:1], ss[:, 1:2], ALU.add),
                     reads=[b_ss], writes=[b_ss], partial=True)
                P.op("act", lambda e: e.activation(ss[:, 3:4], ss[:, 2:3], AF.Sqrt, bias=T["epsb"][:, 0:1]),
                     reads=[b_ss], writes=[b_ss], partial=True)
                P.op("dve", lambda e: e.reciprocal(ss[:, 4:5], ss[:, 3:4]), reads=[b_ss], writes=[b_ss], partial=True)
                P.op("dve", lambda e: e.tensor_scalar(ss[:, 5:6], ss[:, 4:5], -1.0, None, ALU.mult),
                     reads=[b_ss], writes=[b_ss], partial=True)
                P.op("act", lambda e, s=s: e.activation(tpb[s][:, 0:Lf], raw[0][:], AF.Copy, scale=ss[:, 4:5]),
                     reads=[b_raw[0], b_ss], writes=[b_tpb[s]])
                P.op("pool", lambda e, s=s: e.memset(tpb[s][:, Lf:Lf + 1], 0.0), writes=[b_tpb[s]], partial=True)
                P.op("dve", lambda e, s=s: e.tensor_scalar(
                    rev_ap(tpb[s][:, Lf + 1:2 * Lf]), raw[1][:, 1:Lf], ss[:, 5:6], None, ALU.mult),
                    reads=[b_raw[1], b_ss], writes=[b_tpb[s]], partial=True)
                P.dma("sp", taps_dst[order, blk * 128:(blk + 1) * 128, :], tpb[s][:], b_tpb[s], reads=[b_tpb[s]])
        P.barrier()
        P.flush()


def _split_dma(P, eng, dst, src, buf, nsplit, axis_len, mk_dst, mk_src, **kw):
    step = axis_len // nsplit
    for i in range(nsplit):
        P.dma(eng, mk_dst(i * step, (i + 1) * step), mk_src(i * step, (i + 1) * step), buf,
              partial=(i > 0 or kw.get("partial", False)), **{k: v for k, v in kw.items() if k != "partial"})


def _f1_stage(P, nc, T, xin, K, b_xin, F1, b_F1, Cb, b_Cb, b_ps, ev):
    ps = T["ps"]
    for g in range(16):
        pj = g % 2
        for cc in range(4):
            c = g * 4 + cc
            P.op("pe", lambda e, pj=pj, cc=cc, c=c: e.matmul(
                ps[pj][0:64, cc * 128:(cc + 1) * 128], xin[0:K, c, :], F1[0:K, 0:128], start=True, stop=True),
                reads=[b_xin, b_F1], writes=[b_ps[pj]], partial=(cc > 0))
            P.op("pe", lambda e, pj=pj, cc=cc, c=c: e.matmul(
                ps[pj][64:128, cc * 128:(cc + 1) * 128], xin[0:K, c, :], F1[0:K, 128:256], start=True, stop=True,
                tile_position=(0, 64)),
                reads=[b_xin, b_F1], writes=[b_ps[pj]], partial=True)
        src = ps[pj][:].rearrange("p (c k) -> p k c", c=4)
        dst = Cb[:, :, g * 4:(g + 1) * 4]
        ev(dst, src, [b_ps[pj]], [b_Cb], g)


def _evac_alt(P):
    def ev(dst, src, reads, writes, i):
        if i % 2 == 0:
            P.op("act", lambda e: e.copy(dst, src), reads=reads, writes=writes, partial="nowaw")
        else:
            P.op("dve", lambda e: e.tensor_copy(dst, src), reads=reads, writes=writes, partial="nowaw")
    return ev


def _evac_act(P):
    def ev(dst, src, reads, writes, i):
        P.op("act", lambda e: e.copy(dst, src), reads=reads, writes=writes, partial="nowaw")
    return ev


def phase_CT(P, nc, T):
    import contextlib
    ps = T["ps"]
    with contextlib.ExitStack() as es:
        A_ = lambda n, shp, dt: sb(es, nc, n, shp, dt)
        F1 = A_("ct_F1", [128, 256], BF16)
        Grr = A_("ct_Grr", [128, 128, 64], BF16)
        Gii = A_("ct_Gii", [128, 128, 64], BF16)
        xt = [A_("ct_xt0", [128, 64, 64], BF16), A_("ct_xt1", [128, 64, 64], BF16)]
        Cb = A_("ct_Cb", [128, 128, 64], BF16)
        Hr = [A_("ct_Hr0", [64, 64, 128], BF16), A_("ct_Hr1", [64, 64, 128], BF16)]
        Hi = [A_("ct_Hi0", [64, 64, 128], BF16), A_("ct_Hi1", [64, 64, 128], BF16)]
        b_F1, b_G = P.buf(), P.buf()
        P.dma("sp", F1[:], T["d_F1"], b_F1, writes=[b_F1])
        for q in range(4):
            P.dma("sp", Grr[:, q * 32:(q + 1) * 32, :], T["d_Grr"][:, q * 32:(q + 1) * 32, :], b_G, writes=[b_G],
                  partial=(q > 0))
            P.dma("sp", Gii[:, q * 32:(q + 1) * 32, :], T["d_Gii"][:, q * 32:(q + 1) * 32, :], b_G, writes=[b_G],
                  partial=True)
        b_xt, b_Cb, b_Hr, b_Hi = P.bufs_n(2), P.buf(), P.bufs_n(2), P.bufs_n(2)
        b_ps = P.bufs_n(8)
        ev = _evac_alt(P)
        it = 0
        def ct_load(order, cb, s):
            srcv = T["tapsS"][order, cb * 64:(cb + 1) * 64, :].rearrange("c (a b) -> a c b", b=64)
            for q in range(8):
                P.dma("sp", xt[s][:, q * 8:(q + 1) * 8, :], srcv[:, q * 8:(q + 1) * 8, :], b_xt[s],
                      writes=[b_xt[s]], partial=(q > 0))

        blocks = [(order, cb) for order in range(2) for cb in range(16)]
        ct_load(0, 0, 0)
        for (order, cb) in blocks:
                s = it % 2
                it += 1
                _f1_stage(P, nc, T, xt[s], 128, b_xt[s], F1, b_F1, Cb, b_Cb, b_ps, ev)
                if it < len(blocks):
                    ct_load(blocks[it][0], blocks[it][1], it % 2)
                for g in range(16):
                    pj = 2 + (g % 2) * 2
                    for kk in range(8):
                        k1 = g * 8 + kk
                        P.op("pe", lambda e, pj=pj, kk=kk, k1=k1: e.matmul(
                            ps[pj][0:64, kk * 64:(kk + 1) * 64], Grr[:, k1, :], Cb[:, k1, :], start=True, stop=True),
                            reads=[b_G, b_Cb], writes=[b_ps[pj]], partial=(kk > 0))
                        P.op("pe", lambda e, pj=pj, kk=kk, k1=k1: e.matmul(
                            ps[pj + 1][0:64, kk * 64:(kk + 1) * 64], Gii[:, k1, :], Cb[:, k1, :], start=True,
                            stop=True),
                            reads=[b_G, b_Cb], writes=[b_ps[pj + 1]], partial=(kk > 0))
                    P.op("act", lambda e, pj=pj, g=g, s=s: e.copy(
                        Hr[s][:, :, g * 8:(g + 1) * 8], ps[pj][0:64, :].rearrange("p (k c) -> p c k", k=8)),
                        reads=[b_ps[pj]], writes=[b_Hr[s]], partial="nowaw")
                    P.op("dve", lambda e, pj=pj, g=g, s=s: e.tensor_copy(
                        Hi[s][:, :, g * 8:(g + 1) * 8], ps[pj + 1][0:64, :].rearrange("p (k c) -> p c k", k=8)),
                        reads=[b_ps[pj + 1]], writes=[b_Hi[s]], partial="nowaw")
                P.dma("sp", T["Hs"][order, cb, 0], Hr[s][:].rearrange("p c k -> p (c k)"), b_Hr[s], reads=[b_Hr[s]])
                P.dma("sp", T["Hs"][order, cb, 1], Hi[s][:].rearrange("p c k -> p (c k)"), b_Hi[s], reads=[b_Hi[s]])
        P.barrier()
        P.flush()


def phase_CS(P, nc, T):
    import contextlib
    ps = T["ps"]
    with contextlib.ExitStack() as es:
        A_ = lambda n, shp, dt: sb(es, nc, n, shp, dt)
        F1 = A_("cs_F1", [128, 256], BF16)
        G = A_("cs_G", [128, 128, 64], BF16)
        M1 = A_("cs_M1", [64, 128], BF16)
        M1p = A_("cs_M1p", [64, 128], BF16)
        T2r = A_("cs_T2r", [128, 64, 64], BF16)
        T2i = A_("cs_T2i", [128, 64, 64], BF16)
        zc = A_("cs_zc", [64, 64, 64], BF16)
        x1 = A_("cs_x1", [64, 64, 64], BF16)
        x2 = A_("cs_x2", [64, 64, 64], BF16)
        gh = A_("cs_gh", [64, 64, 64], BF16)
        z2 = A_("cs_z2", [64, 64, 64], BF16)
        bz = A_("cs_bz", [64, 64, 64], BF16)
        cv = A_("cs_cv", [64, 64, 64], F32)
        bias = A_("cs_bias", [64, 2, 64], F32)
        Cb = A_("cs_Cb", [128, 128, 64], BF16)
        Db = A_("cs_Db", [128, 128, 64], BF16)
        P1 = A_("cs_P1", [64, 64, 128], BF16)
        P2 = A_("cs_P2", [64, 64, 128], BF16)
        Hr = A_("cs_Hr", [64, 64, 128], BF16)
        Hi = A_("cs_Hi", [64, 64, 128], BF16)
        Xs = [A_("cs_Xs0", [64, 512], BF16), A_("cs_Xs1", [64, 512], BF16)]
        b_k = P.bufs_n(6)
        P.dma("sp", F1[:], T["d_F1"], b_k[0], writes=[b_k[0]])
        for q in range(4):
            P.dma("sp", G[:, q * 32:(q + 1) * 32, :], T["d_G"][:, q * 32:(q + 1) * 32, :], b_k[1], writes=[b_k[1]],
                  partial=(q > 0))
        P.dma("sp", M1[:], T["d_M1"], b_k[2], writes=[b_k[2]])
        P.dma("sp", M1p[:], T["d_M1p"], b_k[3], writes=[b_k[3]])
        P.dma("sp", T2r[:], T["d_T2r"], b_k[4], writes=[b_k[4]])
        P.dma("sp", T2i[:], T["d_T2in"], b_k[5], writes=[b_k[5]])
        b_F1, b_G, b_M1, b_M1p, b_T2r, b_T2i = b_k
        b_zc, b_x1, b_x2, b_gh, b_sgh, b_z2, b_bz, b_cv, b_mx, b_bias = [P.buf() for _ in range(10)]
        b_Cb, b_Db, b_P1, b_P2, b_Hr, b_Hi = [P.buf() for _ in range(6)]
        b_Xs = P.bufs_n(2)
        b_ps = P.bufs_n(8)
        ev = _evac_act(P)

        def t64(rows0):
            return lambda a, b: T["ucT"][rows0 + a:rows0 + b, 0:LS].rearrange("c (n1 n2) -> n1 c n2", n2=64)

        def cs_load(cb, which):
            specs = {"zc": (zc, b_zc, 2048 + cb * 64, "ucT"), "x1": (x1, b_x1, cb * 64, "ucT"),
                     "x2": (x2, b_x2, 1024 + cb * 64, "ucT"), "gh": (gh, b_gh, cb * 64, "ghT")}
            for w in which:
                dst, bdst, r0, src_t = specs[w]
                for q in range(4):
                    srcv = T[src_t][r0 + q * 16:r0 + (q + 1) * 16, 0:LS].rearrange("c (n1 n2) -> n1 c n2", n2=64)
                    P.dma("sp", dst[:, q * 16:(q + 1) * 16, :], srcv, bdst, writes=[bdst], partial=(q > 0))

        cs_load(0, ("zc", "x1"))
        for cb in range(16):
            cs_load(cb, ("x2", "gh"))
            for o in range(2):
                P.dma("sp", bias[:, o, :], T["hy_bias"][o:o + 1, cb * 64:(cb + 1) * 64].partition_broadcast(64),
                      b_bias, writes=[b_bias], partial=(o > 0))
            P.op("act", lambda e: e.activation(gh[:], gh[:], AF.Silu), reads=[b_gh], writes=[b_gh])
            for o in range(2):
                zin, b_zin = (zc, b_zc) if o == 0 else (z2, b_z2)
                xo, b_xo = (x1, b_x1) if o == 0 else (x2, b_x2)
                P.dma("sp", Hr[:].rearrange("p c k -> p (c k)"), T["Hs"][o, cb, 0], b_Hr, writes=[b_Hr])
                P.dma("sp", Hi[:].rearrange("p c k -> p (c k)"), T["Hs"][o, cb, 1], b_Hi, writes=[b_Hi])
                if o == 1 and cb + 1 < 16:
                    cs_load(cb + 1, ("zc", "x1"))
                P.op("pool", lambda e, zin=zin, o=o: e.tensor_tensor(
                    bz[:], zin[:], bcast_last(bias[:, o, :], 64), ALU.mult),
                    reads=[b_zin, b_bias], writes=[b_bz])
                _f1_stage(P, nc, T, zin, 64, b_zin, F1, b_F1, Cb, b_Cb, b_ps, ev)
                for g in range(16):
                    pj = 2 + (g % 2)
                    xs = g % 2
                    for kk in range(8):
                        k1 = g * 8 + kk
                        P.op("pe", lambda e, pj=pj, kk=kk, k1=k1: e.matmul(
                            ps[pj][0:64, kk * 64:(kk + 1) * 64], G[:, k1, :], Cb[:, k1, :], start=True, stop=True),
                            reads=[b_G, b_Cb], writes=[b_ps[pj]], partial=(kk > 0))
                    P.op("act", lambda e, pj=pj, xs=xs: e.copy(Xs[xs][:], ps[pj][0:64, :]),
                         reads=[b_ps[pj]], writes=[b_Xs[xs]])
                    xv = Xs[xs][:].rearrange("p (k c) -> p c k", k=8)
                    P.op("dve", lambda e, xv=xv, g=g: e.tensor_tensor(
                        P1[:, :, g * 8:(g + 1) * 8], xv, Hr[:, :, g * 8:(g + 1) * 8], ALU.mult),
                        reads=[b_Xs[xs], b_Hr], writes=[b_P1], partial="nowaw")
                    P.op("dve", lambda e, xv=xv, g=g: e.tensor_tensor(
                        P2[:, :, g * 8:(g + 1) * 8], xv, Hi[:, :, g * 8:(g + 1) * 8], ALU.mult),
                        reads=[b_Xs[xs], b_Hi], writes=[b_P2], partial="nowaw")
                for g in range(16):
                    pj = 4 + (g % 2)
                    for cc in range(4):
                        c = g * 4 + cc
                        P.op("pe", lambda e, pj=pj, cc=cc, c=c: e.matmul(
                            ps[pj][:, cc * 128:(cc + 1) * 128], P1[:, c, :], M1[:], start=True, stop=False),
                            reads=[b_P1, b_M1], writes=[b_ps[pj]], partial=(cc > 0))
                        P.op("pe", lambda e, pj=pj, cc=cc, c=c: e.matmul(
                            ps[pj][:, cc * 128:(cc + 1) * 128], P2[:, c, :], M1p[:], start=False, stop=True),
                            reads=[b_P2, b_M1p], writes=[b_ps[pj]], partial=True)
                    src = ps[pj][:].rearrange("p (c q) -> p q c", c=4)
                    ev(Db[:, :, g * 4:(g + 1) * 4], src, [b_ps[pj]], [b_Db], g)
                for g in range(8):
                    pj = 6 + (g % 2)
                    for nn in range(8):
                        n2 = g * 8 + nn
                        P.op("pe", lambda e, pj=pj, nn=nn, n2=n2: e.matmul(
                            ps[pj][0:64, nn * 64:(nn + 1) * 64], T2r[:, n2, :], Db[:, n2, :], start=True, stop=False),
                            reads=[b_T2r, b_Db], writes=[b_ps[pj]], partial=(nn > 0))
                        P.op("pe", lambda e, pj=pj, nn=nn, n2=n2: e.matmul(
                            ps[pj][0:64, nn * 64:(nn + 1) * 64], T2i[:, n2, :], Db[:, 64 + n2, :], start=False,
                            stop=True),
                            reads=[b_T2i, b_Db], writes=[b_ps[pj]], partial=True)
                    P.op("dve", lambda e, pj=pj, g=g: e.tensor_tensor(
                        cv[:, :, g * 8:(g + 1) * 8], ps[pj][0:64, :].rearrange("p (n c) -> p c n", n=8),
                        bz[:, :, g * 8:(g + 1) * 8], ALU.add),
                        reads=[b_ps[pj], b_bz], writes=[b_cv], partial=(g > 0))
                if o == 0:
                    P.op("dve", lambda e: e.tensor_tensor(z2[:], cv[:], x1[:], ALU.mult),
                         reads=[b_cv, b_x1], writes=[b_z2])
                else:
                    P.op("dve", lambda e: e.tensor_tensor(cv[:], cv[:], x2[:], ALU.mult),
                         reads=[b_cv, b_x2], writes=[b_cv])
                    P.op("pool", lambda e: e.tensor_tensor(z2[:], cv[:], gh[:], ALU.mult),
                         reads=[b_cv, b_gh], writes=[b_z2])
                    for q in range(4):
                        r0 = 1024 + cb * 64 + q * 16
                        dstv = T["mixT"][r0:r0 + 16, 0:LS].rearrange("c (n1 n2) -> n1 c n2", n2=64)
                        P.dma("sp", dstv, z2[:, q * 16:(q + 1) * 16, :], b_z2, reads=[b_z2])
        P.barrier()
        P.flush()


def phase_CP(P, nc, T):
    import contextlib
    ps = T["ps"]
    ident = T["ident"]
    with contextlib.ExitStack() as es:
        A_ = lambda n, shp, dt: sb(es, nc, n, shp, dt)
        FP = A_("cp_FP", [128, 4, 512], BF16)
        IP = A_("cp_IP", [128, 4, 256], BF16)
        HA = A_("cp_HA", [128, 16, 512], F32)
        HB = A_("cp_HB", [128, 16, 512], F32)
        biasT = A_("cp_biasT", [128, 16], F32)
        tp = [A_("cp_tp0", [128, 512], BF16), A_("cp_tp1", [128, 512], BF16)]
        tt = A_("cp_tt", [128, 4, 128], BF16)
        zc = [A_("cp_zc0", [128, 256], BF16), A_("cp_zc1", [128, 256], BF16)]
        x1 = [A_("cp_x10", [128, 256], BF16), A_("cp_x11", [128, 256], BF16)]
        x2 = [A_("cp_x20", [128, 256], BF16), A_("cp_x21", [128, 256], BF16)]
        gh = [A_("cp_gh0", [128, 256], BF16), A_("cp_gh1", [128, 256], BF16)]
        sgh = A_("cp_sgh", [128, 256], F32)
        z2 = A_("cp_z2", [128, 256], BF16)
        zt = A_("cp_zt", [128, 2, 128], BF16)
        Aa = A_("cp_A", [128, 512], F32)
        Bb = A_("cp_B", [128, 512], F32)
        Y = A_("cp_Y", [128, 512], BF16)
        Yt = A_("cp_Yt", [128, 4, 128], BF16)
        t1 = A_("cp_t1", [128, 256], F32)
        t2 = A_("cp_t2", [128, 256], F32)
        mx = [A_("cp_mx0", [128, 256], BF16), A_("cp_mx1", [128, 256], BF16)]
        b_FP, b_IP, b_H, b_bias = P.buf(), P.buf(), P.buf(), P.buf()
        P.dma("sp", FP[:], T["d_FP"], b_FP, writes=[b_FP])
        P.dma("sp", IP[:], T["d_IP"], b_IP, writes=[b_IP])
        P.dma("sp", biasT[:], T["hy_biasT"], b_bias, writes=[b_bias])
        b_tp, b_tt = P.bufs_n(2), P.buf()
        b_ps = P.bufs_n(8)
        it = 0
        for o in range(2):
            for t in range(8):
                s = it % 2
                it += 1
                P.dma("sp", tp[s][:], T["tapsP"][o, t * 128:(t + 1) * 128, :], b_tp[s], writes=[b_tp[s]])
                pb = ps[s][:].bitcast(BF16)
                for j in range(4):
                    P.op("pe", lambda e, pb=pb, j=j, s=s: e.transpose(
                        pb[:, j * 128:(j + 1) * 128], tp[s][:, j * 128:(j + 1) * 128], ident[:]),
                        reads=[b_tp[s]], writes=[b_ps[s]], partial=(j > 0))
                P.op("dve", lambda e, pb=pb: e.tensor_copy(tt[:].rearrange("p a b -> p (a b)"), pb[:, 0:512]),
                     reads=[b_ps[s]], writes=[b_tt])
                pj = 2 + s
                for j in range(4):
                    P.op("pe", lambda e, pj=pj, j=j: e.matmul(
                        ps[pj][:, :], tt[:, j, :], FP[:, j, :], start=(j == 0), stop=(j == 3)),
                        reads=[b_tt, b_FP], writes=[b_ps[pj]], partial=(j > 0))
                i = o * 8 + t
                for h in range(2):
                    P.op("act", lambda e, pj=pj, i=i, h=h: e.copy(HA[:, i, h * 256:(h + 1) * 256], ps[pj][:, 0:256]),
                         reads=[b_ps[pj]], writes=[b_H], partial=True)
                    P.op("act", lambda e, pj=pj, i=i, h=h: e.copy(HB[:, i, h * 256:(h + 1) * 256],
                                                                  ps[pj][:, 256:512]),
                         reads=[b_ps[pj]], writes=[b_H], partial=True)
        b_zc, b_x1, b_x2, b_gh = P.bufs_n(2), P.bufs_n(2), P.bufs_n(2), P.bufs_n(2)
        b_sgh, b_z2, b_zt, b_A, b_B, b_Y, b_Yt, b_t1, b_t2 = [P.buf() for _ in range(9)]
        b_mx = P.bufs_n(2)
        it = 0
        for sq in range(2):
            tok0 = LS + sq * LP
            for t in range(8):
                s = it % 2
                it += 1
                r = t * 128
                P.dma("sp", zc[s][:], T["ucT"][2048 + r:2048 + r + 128, tok0:tok0 + LP], b_zc[s], writes=[b_zc[s]])
                P.dma("sp", x1[s][:], T["ucT"][r:r + 128, tok0:tok0 + LP], b_x1[s], writes=[b_x1[s]])
                P.dma("sp", x2[s][:], T["ucT"][1024 + r:1024 + r + 128, tok0:tok0 + LP], b_x2[s], writes=[b_x2[s]])
                P.dma("sp", gh[s][:], T["ghT"][r:r + 128, tok0:tok0 + LP], b_gh[s], writes=[b_gh[s]])
                P.op("act", lambda e, s=s: e.activation(sgh[:], gh[s][:], AF.Silu), reads=[b_gh[s]], writes=[b_sgh])
                for o in range(2):
                    zin, b_zin = (zc[s], b_zc[s]) if o == 0 else (z2, b_z2)
                    xo, b_xo = (x1[s], b_x1[s]) if o == 0 else (x2[s], b_x2[s])
                    i = o * 8 + t
                    pb = ps[0][:].bitcast(BF16)
                    for j in range(2):
                        P.op("pe", lambda e, pb=pb, j=j, zin=zin: e.transpose(
                            pb[:, j * 128:(j + 1) * 128], zin[:, j * 128:(j + 1) * 128], ident[:]),
                            reads=[b_zin], writes=[b_ps[0]], partial=(j > 0))
                    P.op("dve", lambda e, pb=pb: e.tensor_copy(zt[:].rearrange("p a b -> p (a b)"), pb[:, 0:256]),
                         reads=[b_ps[0]], writes=[b_zt])
                    for j in range(2):
                        P.op("pe", lambda e, j=j: e.matmul(ps[1][:, :], zt[:, j, :], FP[:, j, :], start=(j == 0),
                                                           stop=(j == 1)),
                             reads=[b_zt, b_FP], writes=[b_ps[1]], partial=(j > 0))
                    P.op("dve", lambda e, i=i: e.tensor_tensor(Aa[:], ps[1][:, :], HA[:, i, :], ALU.mult),
                         reads=[b_ps[1], b_H], writes=[b_A])
                    P.op("dve", lambda e, i=i: e.tensor_tensor(Bb[:], ps[1][:, :], HB[:, i, :], ALU.mult),
                         reads=[b_ps[1], b_H], writes=[b_B])
                    P.op("pool", lambda e: e.tensor_tensor(Y[:, 0:256], Aa[:, 0:256], Bb[:, 256:512], ALU.subtract),
                         reads=[b_A, b_B], writes=[b_Y])
                    P.op("pool", lambda e: e.tensor_tensor(Y[:, 256:512], Bb[:, 0:256], Aa[:, 256:512], ALU.add),
                         reads=[b_A, b_B], writes=[b_Y], partial=True)
                    pb2 = ps[2][:].bitcast(BF16)
                    for j in range(4):
                        P.op("pe", lambda e, pb2=pb2, j=j: e.transpose(
                            pb2[:, j * 128:(j + 1) * 128], Y[:, j * 128:(j + 1) * 128], ident[:]),
                            reads=[b_Y], writes=[b_ps[2]], partial=(j > 0))
                    P.op("act", lambda e, pb2=pb2: e.copy(Yt[:].rearrange("p a b -> p (a b)"), pb2[:, 0:512]),
                         reads=[b_ps[2]], writes=[b_Yt])
                    for j in range(4):
                        P.op("pe", lambda e, j=j: e.matmul(ps[3][:, 0:256], Yt[:, j, :], IP[:, j, :], start=(j == 0),
                                                           stop=(j == 3)),
                             reads=[b_Yt, b_IP], writes=[b_ps[3]], partial=(j > 0))
                    P.op("dve", lambda e, zin=zin, i=i: e.scalar_tensor_tensor(
                        t1[:], zin[:], biasT[:, i:i + 1], ps[3][:, 0:256], ALU.mult, ALU.add),
                        reads=[b_zin, b_bias, b_ps[3]], writes=[b_t1])
                    if o == 0:
                        P.op("pool", lambda e, xo=xo: e.tensor_tensor(z2[:], t1[:], xo[:], ALU.mult),
                             reads=[b_t1, b_xo], writes=[b_z2])
                    else:
                        P.op("pool", lambda e, xo=xo: e.tensor_tensor(t2[:], t1[:], xo[:], ALU.mult),
                             reads=[b_t1, b_xo], writes=[b_t2])
                        P.op("pool", lambda e, s=s: e.tensor_tensor(mx[s][:], t2[:], sgh[:], ALU.mult),
                             reads=[b_t2, b_sgh], writes=[b_mx[s]])
                        P.dma("sp", T["mixT"][1024 + r:1024 + r + 128, tok0:tok0 + LP], mx[s][:], b_mx[s],
                              reads=[b_mx[s]])
        P.barrier()
        P.flush()


def phase_outproj(P, nc, T, l, Wdram, xsrc, mixname, dst, final):
    import contextlib
    ps = T["ps"]
    with contextlib.ExitStack() as es:
        A_ = lambda n, shp, dt: sb(es, nc, n, shp, dt)
        Wo = A_("o_W", [128, 16, 2048], BF16)
        mix = [A_("o_mix0", [128, 16, 512], BF16), A_("o_mix1", [128, 16, 512], BF16)]
        xt = [A_("o_x0", [128, 2048], F32), A_("o_x1", [128, 2048], F32)]
        xo = [A_("o_xo0", [128, 2048], F32), A_("o_xo1", [128, 2048], F32)]
        gs = A_("o_gs", [128, 2048], F32)
        gp = A_("o_gp", [128, 2048], F32)
        tmp = [A_("o_tmp0", [128, 512], F32), A_("o_tmp1", [128, 512], F32)]
        b_W, b_mix, b_xt, b_xo, b_g, b_tmp = P.buf(), P.bufs_n(2), P.bufs_n(2), P.bufs_n(2), P.buf(), P.bufs_n(2)
        b_ps = P.bufs_n(8)
        if final:
            fg = A_("o_fg", [128, 2048], F32)
            junk = A_("o_junk", [128, 2048], F32)
            st = [A_("o_st0", [128, 4], F32), A_("o_st1", [128, 4], F32)]
            b_fg, b_junk, b_st = P.buf(), P.buf(), P.bufs_n(2)
            P.dma("sp", fg[:], T["final_norm_g"][0:1, :].partition_broadcast(128), b_fg, writes=[b_fg])
        Wv = Wdram.rearrange("(k p) c -> p k c", p=128)
        for k in range(16):
            P.dma("pool", Wo[:, k, :], Wv[:, k, :], b_W, writes=[b_W], partial=(k > 0))
        P.dma("sp", gs[:], T["modv"][l][0:1, 4096:6144].partition_broadcast(128), b_g, writes=[b_g])
        P.dma("sp", gp[:], T["modv"][l][1:2, 4096:6144].partition_broadcast(128), b_g, writes=[b_g], partial=True)
        mv = T[mixname].rearrange("(k p) t -> p k t", p=128)
        ti = 0
        ei = 0
        def mix_load(ch):
            s = ch % 2
            for q in range(4):
                P.dma("sp", mix[s][:, q * 4:(q + 1) * 4, :], mv[:, q * 4:(q + 1) * 4, ch * 512:(ch + 1) * 512],
                      b_mix[s], writes=[b_mix[s]], partial=(q > 0))

        mix_load(0)
        for ch in range(NTOK // 512):
            s = ch % 2
            for tt in range(4):
                if tt == 1 and ch + 1 < NTOK // 512:
                    mix_load(ch + 1)
                tok = ch * 512 + tt * 128
                xs = ti % 2
                ti += 1
                gg = gs if tok < LS else gp
                if ti == 1:
                    P.dma("sp", xt[xs][:], xsrc[tok:tok + 128, :], b_xt[xs], writes=[b_xt[xs]])
                for cbk in range(4):
                    pj = ei % 8
                    tj = ei % 2
                    ei += 1
                    for k in range(16):
                        P.op("pe", lambda e, pj=pj, k=k, s=s, tt=tt, cbk=cbk: e.matmul(
                            ps[pj][:, :], mix[s][:, k, tt * 128:(tt + 1) * 128], Wo[:, k, cbk * 512:(cbk + 1) * 512],
                            start=(k == 0), stop=(k == 15)),
                            reads=[b_mix[s], b_W], writes=[b_ps[pj]], partial=(k > 0))
                    P.op("dve", lambda e, pj=pj, tj=tj, cbk=cbk, gg=gg: e.tensor_tensor(
                        tmp[tj][:], ps[pj][:, :], gg[:, cbk * 512:(cbk + 1) * 512], ALU.mult),
                        reads=[b_ps[pj], b_g], writes=[b_tmp[tj]])
                    P.op("pool", lambda e, tj=tj, xs=xs, cbk=cbk: e.tensor_tensor(
                        xo[xs][:, cbk * 512:(cbk + 1) * 512], tmp[tj][:], xt[xs][:, cbk * 512:(cbk + 1) * 512],
                        ALU.add),
                        reads=[b_tmp[tj], b_xt[xs]], writes=[b_xo[xs]], partial=(cbk > 0))
                if final:
                    P.op("act", lambda e, xs=xs: e.activation(junk[:], xo[xs][:], AF.Square,
                                                               accum_out=st[xs][:, 0:1]),
                         reads=[b_xo[xs]], writes=[b_junk, b_st[xs]])
                    P.op("act", lambda e, xs=xs: e.activation(st[xs][:, 1:2], st[xs][:, 0:1], AF.Sqrt, scale=1.0 / D,
                                                               bias=T["epsb"][:, 0:1]),
                         reads=[b_st[xs]], writes=[b_st[xs]], partial=True)
                    P.op("dve", lambda e, xs=xs: e.reciprocal(st[xs][:, 2:3], st[xs][:, 1:2]),
                         reads=[b_st[xs]], writes=[b_st[xs]], partial=True)
                    P.op("dve", lambda e, xs=xs: e.scalar_tensor_tensor(
                        xo[xs][:], xo[xs][:], st[xs][:, 2:3], fg[:], ALU.mult, ALU.mult),
                        reads=[b_xo[xs], b_st[xs], b_fg], writes=[b_xo[xs]])
                if tok + 128 < NTOK:
                    P.dma("sp", xt[ti % 2][:], xsrc[tok + 128:tok + 256, :], b_xt[ti % 2], writes=[b_xt[ti % 2]])
                P.dma("sp", dst[tok:tok + 128, :], xo[xs][:], b_xo[xs], reads=[b_xo[xs]])
        P.barrier()
        P.flush()


def phase_E(P, nc, T):
    import contextlib
    ps = T["ps"]
    Wv = T["c_w_in"].rearrange("(k p) c -> p k c", p=128)
    for half in range(2):
        tok_tiles = [(half * 2048 + i * 128, False) for i in range(16)] + \
                    [(LS + half * 256 + i * 128, True) for i in range(2)]
        with contextlib.ExitStack() as es0:
            hT = sb(es0, nc, "e_hT", [128, 16, 2304], BF16)
            b_hT = P.buf("hT")
            with contextlib.ExitStack() as es1:
                A_ = lambda n, shp, dt: sb(es1, nc, n, shp, dt)
                xt0 = A_("e_xt0", [128, 2048], F32); xt1 = A_("e_xt1", [128, 2048], F32)
                xt2 = A_("e_xt2", [128, 2048], F32); xt3 = A_("e_xt3", [128, 2048], F32)
                junk = A_("e_junk", [128, 2048], F32); tmp = A_("e_tmp", [128, 2048], F32)
                junkb = A_("e_junkb", [128, 2048], F32); tmpb = A_("e_tmpb", [128, 2048], F32)
                st0 = A_("e_st0", [128, 4], F32); st1 = A_("e_st1", [128, 4], F32)
                hb0 = A_("e_hb0", [128, 2048], BF16); hb1 = A_("e_hb1", [128, 2048], BF16)
                As = A_("e_As", [128, 2048], F32); shs = A_("e_shs", [128, 2048], F32)
                Ap = A_("e_Ap", [128, 2048], F32); shp = A_("e_shp", [128, 2048], F32)
                bc = {"A_s": As, "sh_s": shs, "A_p": Ap, "sh_p": shp}
                b_bc = {k: P.buf(k) for k in bc}
                load_bcast_rows(P, nc, T, 1, bc, b_bc)
                work = dict(xt=[xt0, xt1, xt2, xt3], b_xt=P.bufs_n(4), junk=junk, b_junk=P.buf(), st=[st0, st1],
                            b_st=P.bufs_n(2), tmp=tmp, b_tmp=P.buf(), hb=[hb0, hb1], b_hb=P.bufs_n(2),
                            b_pst=P.bufs_n(4), junk2=[junk, junkb], b_junk2=P.bufs_n(2), tmp2=[tmp, tmpb],
                            b_tmp2=P.bufs_n(2))
                norm_transpose_half(P, nc, T, T["x1"], tok_tiles, hT, b_hT, bc, b_bc, work)
                P.barrier()
                P.flush()
            with contextlib.ExitStack() as es2:
                A_ = lambda n, shp, dt: sb(es2, nc, n, shp, dt)
                wg = [A_("e_w0", [128, 16, 512], BF16), A_("e_w1", [128, 16, 512], BF16)]
                sf = [A_("e_sf0", [128, 512], F32), A_("e_sf1", [128, 512], F32)]
                sg = [A_("e_sg0", [128, 512], BF16), A_("e_sg1", [128, 512], BF16)]
                b_hT = P.buf("hT2")
                b_wg, b_sf, b_sg = P.bufs_n(2), P.bufs_n(2), P.bufs_n(2)
                b_ps = P.bufs_n(8)
                chunks = [(c * 512, 512, half * 2048 + c * 512) for c in range(4)] + [(2048, 256, LS + half * 256)]
                psi = 0
                ei = 0
                for g in range(8):
                    s = g % 2
                    for kq in range(16):
                        P.dma("pool", wg[s][:, kq, :], Wv[:, kq, g * 512:(g + 1) * 512], b_wg[s], writes=[b_wg[s]],
                              partial=(kq > 0))
                    for (l0, n, g0) in chunks:
                        for j in range(4):
                            pj = psi % 4
                            psi += 1
                            for k in range(16):
                                P.op("pe", lambda e, pj=pj, k=k, s=s, j=j, l0=l0, n=n: e.matmul(
                                    ps[pj][:, 0:n], wg[s][:, k, j * 128:(j + 1) * 128], hT[:, k, l0:l0 + n],
                                    start=(k == 0), stop=(k == 15)),
                                    reads=[b_hT, b_wg[s]], writes=[b_ps[pj]], partial=(k > 0))
                            row = (g * 4 + j) * 128
                            si = ei % 2
                            ei += 1
                            if row < 2048:
                                P.op("act", lambda e, pj=pj, si=si, n=n: e.copy(sf[si][:, 0:n], ps[pj][:, 0:n]),
                                     reads=[b_ps[pj]], writes=[b_sf[si]])
                                P.dma("sp", T["xbT"][row:row + 128, g0:g0 + n], sf[si][:, 0:n], b_sf[si],
                                      reads=[b_sf[si]])
                            else:
                                P.op("dve", lambda e, pj=pj, si=si, n=n: e.tensor_copy(sg[si][:, 0:n], ps[pj][:, 0:n]),
                                     reads=[b_ps[pj]], writes=[b_sg[si]])
                                P.dma("sp", T["gateT"][row - 2048:row - 2048 + 128, g0:g0 + n], sg[si][:, 0:n],
                                      b_sg[si], reads=[b_sg[si]])
                P.barrier()
                P.flush()


def phase_F(P, nc, T):
    import contextlib
    ps = T["ps"]
    seqs = [(0, LS, 0, 1), (LS, 2 * LP, 1, 2)]
    with contextlib.ExitStack() as es:
        A_ = lambda n, shp, dt: sb(es, nc, n, shp, dt)
        Rs = [A_("f_R0", [128, LS], F32), A_("f_R1", [128, LS], F32)]
        Is = [A_("f_I0", [128, LS], F32), A_("f_I1", [128, LS], F32)]
        Ss = [A_("f_S0", [128, LS], F32), A_("f_S1", [128, LS], F32)]
        H_ = [A_("f_H0", [128, LS + 3], F32), A_("f_H1", [128, LS + 3], F32)]
        xc = A_("f_xc", [128, 2, LS], F32)
        xp = H_[1]
        acc = H_[0]
        xcb = A_("f_xcb", [128, 2, LS], BF16)
        gate = A_("f_gate", [128, LS], BF16)
        stage = A_("f_stage", [128, LS], BF16)
        wq = A_("f_wq", [128, 2, 2, 2, 256], BF16)
        lcw = A_("f_lcw", [128, 5, 16], F32)
        lba = A_("f_lba", [128, 2, 16], F32)
        lbx = A_("f_lbx", [128, 2, 16], F32)
        llam = A_("f_llam", [128, 2, 16], F32)
        c8 = A_("f_c8", [128, 2, 16], F32)
        c16 = A_("f_c16", [128, 2, 16], F32)
        stT = A_("f_stT", [128, 2, 16], F32)
        nst = A_("f_nst", [128, 2, 2, 16], F32)
        b_k = P.bufs_n(6)
        P.dma("sp", lcw[:], T["lcw"], b_k[0], writes=[b_k[0]])
        P.dma("sp", lba[:], T["lba"], b_k[1], writes=[b_k[1]])
        P.dma("sp", lbx[:], T["lbx"], b_k[2], writes=[b_k[2]])
        P.dma("sp", llam[:], T["llam"], b_k[3], writes=[b_k[3]])
        P.dma("sp", stT[:], T["stT"], b_k[4], writes=[b_k[4]])
        P.op("act", lambda e: e.activation(c8[:], llam[:], AF.Exp, scale=-1.0), reads=[b_k[3]], writes=[b_k[5]])
        P.op("act", lambda e: e.activation(c8[:], c8[:], AF.Ln, bias=1.0), reads=[b_k[5]], writes=[b_k[5]])
        P.op("dve", lambda e: e.tensor_scalar(c16[:], c8[:], -16.0, None, ALU.mult), reads=[b_k[5]], writes=[b_k[5]],
             partial=True)
        P.op("dve", lambda e: e.tensor_scalar(c8[:], c8[:], -8.0, None, ALU.mult), reads=[b_k[5]], writes=[b_k[5]],
             partial=True)
        b_lcw, b_lba, b_lbx, _, b_stT, b_c8 = b_k
        b_xc, b_xcb, b_gate, b_stage, b_wq, b_nst = [P.buf() for _ in range(6)]
        b_H = P.bufs_n(2)
        b_Rs, b_Is, b_Ss = P.bufs_n(2), P.bufs_n(2), P.bufs_n(2)
        b_xp, b_acc = b_H[1], b_H[0]
        par = 0
        b_ps = P.bufs_n(8)
        psi = 0
        for h in range(8):
            for m in range(2):
                wsrc = T["c_wa"] if m == 0 else T["c_wx"]
                for d in range(2):
                    P.dma("pool", wq[:, m, d], wsrc[d, h].rearrange("(it p) j -> p it j", p=128), b_wq,
                          writes=[b_wq], partial=(m + d > 0))
            for (tok0, L, sidx, nseg) in seqs:
                nch = max(1, L // 512)
                cw = min(512, L)
                Ls = L // nseg
                Wp = Ls + 3

                def xpv(j, nseg=nseg, Ls=Ls, Wp=Wp):
                    if nseg == 1:
                        return xp[:, j:j + Ls]
                    return xp[:, 0:nseg * Wp].rearrange("p (g w) -> p g w", g=nseg)[:, :, j:j + Ls]

                def segv(ap2d, nseg=nseg):
                    if nseg == 1:
                        return ap2d
                    return ap2d.rearrange("p (g w) -> p g w", g=nseg)
                for ct in range(2):
                    cti = h * 2 + ct
                    row = cti * 128
                    P.op("pool", lambda e: e.memset(xp[:, 0:2], 0.0), writes=[b_xp])
                    for sg in range(nseg):
                        P.op("pool", lambda e, sg=sg, Wp=Wp, Ls=Ls: e.memset(
                            xp[:, sg * Wp + Ls + 2:min((sg + 1) * Wp + 2, xp.shape[1])], 0.0),
                            writes=[b_xp], partial=True)
                        P.dma("sp", xp[:, sg * Wp + 2:sg * Wp + 2 + Ls],
                              T["xbT"][row:row + 128, tok0 + sg * Ls:tok0 + (sg + 1) * Ls], b_xp, writes=[b_xp],
                              partial=True)
                    accv = segv(acc[:, 0:L])
                    P.op("act", lambda e, cti=cti, accv=accv, src=xpv(0): e.activation(
                        accv, src, AF.Identity, scale=lcw[:, 0, cti:cti + 1],
                        bias=lcw[:, 4, cti:cti + 1]), reads=[b_xp, b_lcw], writes=[b_acc])
                    for j in (1, 2):
                        P.op("dve", lambda e, cti=cti, j=j, accv=accv, src=xpv(j): e.scalar_tensor_tensor(
                            accv, src, lcw[:, j, cti:cti + 1], accv, ALU.mult, ALU.add),
                            reads=[b_xp, b_acc, b_lcw], writes=[b_acc])
                    P.op("dve", lambda e, cti=cti, ct=ct, accv=accv, src=xpv(3), dstv=segv(xc[:, ct, 0:L]):
                         e.scalar_tensor_tensor(dstv, src, lcw[:, 3, cti:cti + 1], accv, ALU.mult, ALU.add),
                         reads=[b_xp, b_acc, b_lcw], writes=[b_xc], partial=(ct > 0))
                    P.op("act", lambda e, L=L, ct=ct: e.copy(xcb[:, ct, 0:L], xc[:, ct, 0:L]),
                         reads=[b_xc], writes=[b_xcb], partial=(ct > 0))
                for jt in range(2):
                    cti = h * 2 + jt
                    row = cti * 128
                    P.dma("sp", gate[:, 0:L], T["gateT"][row:row + 128, tok0:tok0 + L], b_gate, writes=[b_gate])
                    for d in range(2):
                        par ^= 1
                        R_, I_, S_ = Rs[par], Is[par], Ss[par]
                        b_R, b_I, b_S = b_Rs[par], b_Is[par], b_Ss[par]
                        for (m, dstT, b_dst, bias_t) in ((0, R_, b_R, lba), (1, I_, b_I, lbx)):
                            for c in range(nch):
                                pj = psi % 4
                                psi += 1
                                for it_ in range(2):
                                    P.op("pe", lambda e, pj=pj, m=m, d=d, it_=it_, jt=jt, c=c, cw=cw: e.matmul(
                                        ps[pj][:, 0:cw], wq[:, m, d, it_, jt * 128:(jt + 1) * 128],
                                        xcb[:, it_, c * cw:(c + 1) * cw], start=(it_ == 0), stop=(it_ == 1)),
                                        reads=[b_wq, b_xcb], writes=[b_ps[pj]], partial=(it_ > 0))
                                P.op("act", lambda e, pj=pj, dstT=dstT, c=c, bias_t=bias_t, d=d, cti=cti, cw=cw: e.activation(
                                    dstT[:, c * cw:(c + 1) * cw], ps[pj][:, 0:cw], AF.Sigmoid,
                                    bias=bias_t[:, d, cti:cti + 1]),
                                    reads=[b_ps[pj], b_lba, b_lbx], writes=[b_dst], partial=(c > 0))
                        P.op("pool", lambda e, L=L, jt=jt, I_=I_: e.tensor_tensor(I_[:, 0:L], I_[:, 0:L], xc[:, jt, 0:L],
                                                                                 ALU.mult),
                             reads=[b_I, b_xc], writes=[b_I])
                        P.op("act", lambda e, L=L, d=d, cti=cti, R_=R_, S_=S_: e.activation(
                            S_[:, 0:L], R_[:, 0:L], AF.Exp, scale=c16[:, d, cti:cti + 1]),
                            reads=[b_R, b_c8], writes=[b_S])
                        P.op("act", lambda e, L=L, d=d, cti=cti, R_=R_: e.activation(
                            R_[:, 0:L], R_[:, 0:L], AF.Exp, scale=c8[:, d, cti:cti + 1]),
                            reads=[b_R, b_c8], writes=[b_R])
                        P.op("act", lambda e, L=L, S_=S_: e.activation(S_[:, 0:L], S_[:, 0:L], AF.Sqrt, scale=-1.0, bias=1.0),
                             reads=[b_S], writes=[b_S])
                        P.op("dve", lambda e, L=L, I_=I_, S_=S_: e.tensor_tensor(I_[:, 0:L], I_[:, 0:L], S_[:, 0:L], ALU.mult),
                             reads=[b_I, b_S], writes=[b_I])
                        init = stT[:, d, cti:cti + 1] if sidx == 0 else 0.0
                        for sg in range(nseg):
                            a0, a1 = sg * Ls, (sg + 1) * Ls
                            if d == 0:
                                P.op("dve", lambda e, a0=a0, a1=a1, init=init, R_=R_, I_=I_: e.tensor_tensor_scan(
                                    H_[0][:, a0:a1], R_[:, a0:a1], I_[:, a0:a1], init, ALU.mult, ALU.add),
                                    reads=[b_R, b_I, b_stT], writes=[b_H[0]], partial=(sg > 0))
                            else:
                                P.op("dve", lambda e, a0=a0, a1=a1, init=init, R_=R_, I_=I_: e.tensor_tensor_scan(
                                    rev_ap(H_[1][:, a0:a1]), rev_ap(R_[:, a0:a1]), rev_ap(I_[:, a0:a1]), init,
                                    ALU.mult, ALU.add),
                                    reads=[b_R, b_I, b_stT], writes=[b_H[1]], partial=(sg > 0))
                            if sidx > 0:
                                col = a1 - 1 if d == 0 else a0
                                P.op("act", lambda e, d=d, col=col, sg=sg, cti=cti: e.copy(
                                    nst[:, sg, d, cti:cti + 1], H_[d][:, col:col + 1]),
                                    reads=[b_H[d]], writes=[b_nst], partial=True)
                    P.op("pool", lambda e, L=L: e.tensor_tensor(H_[0][:, 0:L], H_[0][:, 0:L], H_[1][:, 0:L], ALU.add),
                         reads=[b_H[0], b_H[1]], writes=[b_H[0]])
                    P.op("act", lambda e, L=L, S_=S_: e.activation(S_[:, 0:L], gate[:, 0:L], AF.Silu),
                         reads=[b_gate], writes=[b_S])
                    P.op("dve", lambda e, L=L, S_=S_: e.tensor_tensor(stage[:, 0:L], H_[0][:, 0:L], S_[:, 0:L], ALU.mult),
                         reads=[b_H[0], b_S], writes=[b_stage])
                    P.dma("sp", T["mix1T"][row:row + 128, tok0:tok0 + L], stage[:, 0:L], b_stage, reads=[b_stage])
        P.dma("sp", T["ns"], nst[:].rearrange("p a b c -> p (a b c)"), b_nst, reads=[b_nst])
        P.barrier()
        P.flush()


def _bf16(a):
    return np.asarray(a, dtype=np.float32).astype(ml_dtypes.bfloat16)


def _fft_consts():
    N = 2 * LS
    n1 = np.arange(128)[:, None]
    k1 = np.arange(128)[None, :]
    ang = 2 * np.pi * n1 * (k1 + 0.5) / 128
    F1 = np.concatenate([np.cos(ang), -np.sin(ang)], 1)
    n2 = np.arange(64)[:, None, None]
    k1_ = np.arange(128)[None, :, None]
    k2 = np.arange(32)[None, None, :]
    ang = 2 * np.pi * n2 * (k1_ + 128 * k2 + 0.5) / N
    gr, gi = np.cos(ang), -np.sin(ang)
    G = np.zeros((128, 128, 64))
    G[0:64, :, 0:32] = gr
    G[64:128, :, 0:32] = -gi
    G[0:64, :, 32:64] = gi
    G[64:128, :, 32:64] = gr
    Grr = np.concatenate([G[:, :, 0:32], G[:, :, 0:32]], 2)
    Gii = np.concatenate([G[:, :, 32:64], G[:, :, 32:64]], 2)
    k2 = np.arange(32)[:, None]
    n2 = np.arange(64)[None, :]
    ang = 2 * np.pi * n2 * k2 / 64
    mr, mi = np.cos(ang), np.sin(ang)
    M1 = np.zeros((64, 128))
    M1[0:32, 0:64] = mr
    M1[32:64, 0:64] = -mi
    M1[0:32, 64:128] = mi
    M1[32:64, 64:128] = mr
    M1p = np.concatenate([M1[32:64], -M1[0:32]], 0)
    k1 = np.arange(128)[:, None, None]
    n2 = np.arange(64)[None, :, None]
    n1 = np.arange(64)[None, None, :]
    ang = 2 * np.pi * (k1 + 0.5) * (n1 / 128 + n2 / N)
    T2r = (2.0 / N) * np.cos(ang)
    T2in = -(2.0 / N) * np.sin(ang)
    Np = 2 * LP
    n = np.arange(Np)[:, None]
    k = np.arange(LP)[None, :]
    ang = 2 * np.pi * n * (k + 0.5) / Np
    FPm = np.concatenate([np.cos(ang), -np.sin(ang)], 1)
    FP = FPm.reshape(4, 128, 512).transpose(1, 0, 2)
    k = np.arange(LP)[:, None]
    n = np.arange(LP)[None, :]
    ang = 2 * np.pi * n * (k + 0.5) / Np
    IPm = np.concatenate([(2.0 / Np) * np.cos(ang), -(2.0 / Np) * np.sin(ang)], 0)
    IP = IPm.reshape(4, 128, 256).transpose(1, 0, 2)
    return dict(F1=F1, G=G, Grr=Grr, Gii=Gii, M1=M1, M1p=M1p, T2r=T2r, T2in=T2in, FP=FP, IP=IP)


def make_consts():
    c = {}
    c["ident"] = _bf16(np.eye(128))
    pos = np.arange(LS)
    row = (pos // 64).astype(np.float64)
    col = (pos % 64).astype(np.float64)
    nf = 16
    inv = 10000.0 ** (-np.arange(nf, dtype=np.float64) / nf)
    cos = np.zeros((64, LS))
    sin = np.zeros((64, LS))
    for d in range(64):
        halfi = d // 32
        w = d % 32
        f = w % 16
        p = row if halfi == 0 else col
        ang = p * inv[f]
        cos[d] = np.cos(ang)
        sin[d] = -np.sin(ang) if w < 16 else np.sin(ang)
    c["rope_cos"] = np.tile(cos, (2, 1)).astype(np.float32)
    c["rope_sin"] = np.tile(sin, (2, 1)).astype(np.float32)
    Pm = np.zeros((128, 128))
    for m in range(128):
        hh = m // 64
        d = m % 64
        w = d % 32
        partner = d + 16 if w < 16 else d - 16
        Pm[hh * 64 + partner, m] = 1.0
    c["ropeP"] = _bf16(Pm)
    c["epsb"] = np.full((128, 1), EPS, np.float32)
    for nm, Lf in (("S", LS), ("P", LP)):
        t = (np.arange(Lf, dtype=np.float32) / np.float32(Lf)).astype(np.float64)
        freqs = np.linspace(1e-4, 15.0, 16).astype(np.float32).astype(np.float64)
        ang = 2.0 * np.pi * t[:, None] * freqs[None, :]
        z = np.concatenate([t[:, None], np.cos(ang), -np.sin(ang)], 1)
        c["zemb" + nm] = np.ascontiguousarray(z.T).astype(np.float32)
        c["tpos" + nm] = np.ascontiguousarray(np.tile(t[None, :], (128, 1))).astype(np.float32)
    c.update({k: _bf16(v) for k, v in _fft_consts().items()})
    si = np.arange(128)[:, None]
    qi = np.arange(128)[None, :]
    c["amask"] = _bf16(np.concatenate([(si >= qi), np.ones((128, 128)), (si <= qi)], 1).astype(np.float32))
    return c


CONST_SPECS = {
    "ident": ([128, 128], BF16), "rope_cos": ([128, LS], F32), "rope_sin": ([128, LS], F32),
    "ropeP": ([128, 128], BF16), "epsb": ([128, 1], F32), "amask": ([128, 384], BF16),
    "zembS": ([33, LS], F32), "zembP": ([33, LP], F32), "tposS": ([128, LS], F32), "tposP": ([128, LP], F32),
    "F1": ([128, 256], BF16), "G": ([128, 128, 64], BF16), "Grr": ([128, 128, 64], BF16),
    "Gii": ([128, 128, 64], BF16), "M1": ([64, 128], BF16), "M1p": ([64, 128], BF16),
    "T2r": ([128, 64, 64], BF16), "T2in": ([128, 64, 64], BF16),
    "FP": ([128, 4, 512], BF16), "IP": ([128, 4, 256], BF16),
}

IN_SPECS = {
    "x": [NTOK, D], "ck": [512, 128], "cv": [512, 128], "st": [2, D], "cvecT": [128, 32],
    "mod_w": [2, D, 3 * D], "mod_b": [2, 3 * D], "norm_g": [2, D], "final_norm_g": [1, D],
    "a_w_in": [D, 6400], "a_w_out": [D, D], "a_sink": [1, 16],
    "hsw": [128, 4, 24], "hy_w1": [33, 64], "hy_w2": [64, 64], "hy_bb": [64, 2], "hy_w3": [64, 4096],
    "hy_decT": [128, 32], "hy_biasT": [128, 16], "hy_bias": [2, 1024],
    "c_w_in": [D, 2 * D], "c_w_out": [D, D], "c_wa": [2, 8, 256, 256], "c_wx": [2, 8, 256, 256],
    "lcw": [128, 5, 16], "lba": [128, 2, 16], "lbx": [128, 2, 16], "llam": [128, 2, 16], "stT": [128, 2, 16],
}

SCRATCH = {
    "modv": ([2, 2, 3 * D], F32),
    "qT": ([1024, NTOK], BF16), "kT": ([2, 128, NTOK], BF16), "gaT": ([1024, NTOK], BF16),
    "hyT": ([3072, NTOK], BF16), "ghT": ([1024, NTOK], BF16), "vtok": ([NTOK, 128], BF16),
    "mixT": ([2048, NTOK], BF16),
    "ucT": ([3072, NTOK], BF16), "tapsS": ([2, 1024, 2 * LS], BF16), "tapsP": ([2, 1024, 2 * LP], BF16),
    "Hs": ([2, 16, 2, 64, 64 * 128], BF16),
    "x1": ([NTOK, D], F32), "xbT": ([D, NTOK], F32), "gateT": ([D, NTOK], BF16), "mix1T": ([D, NTOK], BF16),
}

OUT_SPECS = {"y": [NTOK, D], "nk": [512, 128], "nv": [512, 128], "ns": [128, 64]}


def build_program(debug_scratch=(), stop_after=None, skip=(), ext_in=()):
    nc = bass.Bass("TRN2", target_bir_lowering=False)
    T = {}
    for name, shp in IN_SPECS.items():
        T[name] = nc.dram_tensor(name, shp, F32, kind="ExternalInput").ap()
    for name, (shp, dt) in CONST_SPECS.items():
        T["d_" + name] = nc.dram_tensor("c_" + name, shp, dt, kind="ExternalInput").ap()
    for name, shp in OUT_SPECS.items():
        T[name] = nc.dram_tensor(name, shp, F32, kind="ExternalOutput").ap()
    for name, (shp, dt) in SCRATCH.items():
        kind = "ExternalOutput" if name in debug_scratch else ("ExternalInput" if name in ext_in else "Internal")
        T[name] = nc.dram_tensor("s_" + name, shp, dt, kind=kind).ap()
    T["rope_cos"] = T["d_rope_cos"]
    T["rope_sin"] = T["d_rope_sin"]
    import contextlib
    with contextlib.ExitStack() as es:
        sems = [es.enter_context(nc.semaphore("sem%d" % i)) for i in range(60)]
        T["ps"] = [es.enter_context(nc.psum_tensor("ps%d" % i, [128, 512], F32)) for i in range(8)]
        ident = es.enter_context(nc.sbuf_tensor("ident", [128, 128], BF16))
        ropeP = es.enter_context(nc.sbuf_tensor("ropeP", [128, 128], BF16))
        epsb = es.enter_context(nc.sbuf_tensor("epsb", [128, 1], F32))
        T["ident"], T["ropeP"], T["epsb"] = ident, ropeP, epsb
        P = Prog(nc, sems)
        b_c = P.bufs_n(3)
        P.dma("sp", ident[:], T["d_ident"], b_c[0], writes=[b_c[0]])
        P.dma("sp", ropeP[:], T["d_ropeP"], b_c[1], writes=[b_c[1]])
        P.dma("sp", epsb[:], T["d_epsb"], b_c[2], writes=[b_c[2]])
        P.barrier()
        if "M" not in skip:
            phase_M(P, nc, T)
        if stop_after != "M":
            if "A" not in skip:
                phase_A(P, nc, T)
        if stop_after not in ("M", "A") and "B" not in skip:
            phase_B(P, nc, T)
        if stop_after not in ("M", "A", "B"):
            if "C0" not in skip:
                phase_C0(P, nc, T)
            if "CF" not in skip:
                phase_CF(P, nc, T, LP, T["tapsP"], T["d_zembP"], T["d_tposP"])
                phase_CF(P, nc, T, LS, T["tapsS"], T["d_zembS"], T["d_tposS"])
        if stop_after not in ("M", "A", "B", "CF"):
            if "CT" not in skip:
                phase_CT(P, nc, T)
            if "CS" not in skip:
                phase_CS(P, nc, T)
            if "CP" not in skip:
                phase_CP(P, nc, T)
        if stop_after not in ("M", "A", "B", "CF", "C"):
            if "D" not in skip:
                phase_outproj(P, nc, T, 0, T["a_w_out"], T["x"], "mixT", T["x1"], False)
        if stop_after not in ("M", "A", "B", "CF", "C", "D"):
            if "E" not in skip:
                phase_E(P, nc, T)
        if stop_after not in ("M", "A", "B", "CF", "C", "D", "E"):
            if "F" not in skip:
                phase_F(P, nc, T)
        if stop_after not in ("M", "A", "B", "CF", "C", "D", "E", "F"):
            if "G" not in skip:
                phase_outproj(P, nc, T, 1, T["c_w_out"], T["x1"], "mix1T", T["y"], True)
        P.barrier()
        P.flush()
    return nc


def make_in_maps(inputs):
    consts = make_consts()
    f = lambda a: np.ascontiguousarray(np.asarray(a, dtype=np.float32))
    x_prompt, x_sample = f(inputs["x_prompt"]), f(inputs["x_sample"])
    ck, cv = f(inputs["cache_k"]), f(inputs["cache_v"])
    st = f(inputs["state_lru"])
    c, c_ctx = f(inputs["c"]), f(inputs["c_ctx"])
    shared = {
        "mod_w": f(inputs["mod_w"]), "mod_b": f(inputs["mod_b"]), "norm_g": f(inputs["norm_g"]),
        "final_norm_g": f(inputs["final_norm_g"]).reshape(1, D),
        "a_w_in": f(inputs["a_w_in"])[0], "a_w_out": f(inputs["a_w_out"])[0], "a_sink": f(inputs["a_sink"]),
        "hsw": np.ascontiguousarray(np.concatenate([f(inputs["hy_short_w"])[0], f(inputs["hy_short_b"])[0][None]], 0)
                                    .reshape(4, 24, 128).transpose(2, 0, 1)),
        "hy_w1": f(inputs["hy_w1"])[0], "hy_w2": f(inputs["hy_w2"])[0],
        "hy_bb": np.ascontiguousarray(np.stack([f(inputs["hy_b1"])[0], f(inputs["hy_b2"])[0]], 1)),
        "hy_w3": f(inputs["hy_w3"])[0],
        "hy_decT": np.ascontiguousarray(f(inputs["hy_decay"])[0].reshape(32, 128).T),
        "hy_biasT": np.ascontiguousarray(f(inputs["hy_bias"])[0].reshape(16, 128).T),
        "hy_bias": f(inputs["hy_bias"])[0],
        "c_w_in": f(inputs["c_w_in"])[0], "c_w_out": f(inputs["c_w_out"])[0],
        "c_wa": f(inputs["c_wa"])[0], "c_wx": f(inputs["c_wx"])[0],
        "lcw": np.ascontiguousarray(np.concatenate([f(inputs["c_conv_w"])[0], f(inputs["c_conv_b"])[0][None]], 0)
                                    .reshape(5, 16, 128).transpose(2, 0, 1)),
        "lba": np.ascontiguousarray(f(inputs["c_ba"])[0].reshape(2, 16, 128).transpose(2, 0, 1)),
        "lbx": np.ascontiguousarray(f(inputs["c_bx"])[0].reshape(2, 16, 128).transpose(2, 0, 1)),
        "llam": np.ascontiguousarray(f(inputs["c_lambda"])[0].reshape(2, 16, 128).transpose(2, 0, 1)),
    }
    for k, v in consts.items():
        shared["c_" + k] = v
    maps = []
    for i in range(NCORES):
        m = dict(shared)
        m["x"] = np.ascontiguousarray(np.concatenate([x_sample[i], x_prompt[2 * i], x_prompt[2 * i + 1]], 0))
        m["ck"] = np.ascontiguousarray(ck[i, 0].reshape(512, 128))
        m["cv"] = np.ascontiguousarray(cv[i, 0].reshape(512, 128))
        m["st"] = np.ascontiguousarray(st[i, 0])
        m["stT"] = np.ascontiguousarray(st[i, 0].reshape(2, 16, 128).transpose(2, 0, 1))
        cvec = np.stack([c[i], c_ctx], 0)
        m["cvecT"] = np.ascontiguousarray(cvec.reshape(2, 16, 128).transpose(2, 1, 0).reshape(128, 32))
        maps.append(m)
    return maps


def kernel(**inputs):
    nc = build_program()
    maps = make_in_maps(inputs)
    res = run_bass_kernel_spmd(nc, maps, core_ids=list(range(NCORES)))
    R = res.results
    y_s = np.stack([R[i]["y"][:LS] for i in range(NCORES)], 0)
    y_p = np.concatenate([R[i]["y"][LS:].reshape(2, LP, D) for i in range(NCORES)], 0)
    nk = np.concatenate([R[i]["nk"].reshape(2, 1, LP, 2, 64) for i in range(NCORES)], 0)
    nv = np.concatenate([R[i]["nv"].reshape(2, 1, LP, 2, 64) for i in range(NCORES)], 0)
    ns = np.concatenate([R[i]["ns"].reshape(128, 2, 2, 16).transpose(1, 2, 3, 0).reshape(2, 1, 2, D)
                         for i in range(NCORES)], 0)
    return (y_p.astype(np.float32), y_s.astype(np.float32), nk.astype(np.float32), nv.astype(np.float32),
            ns.astype(np.float32))
```

```python
import numpy as np
import ml_dtypes
import concourse.bass as bass
import concourse.mybir as mybir
from concourse.bass_utils import run_bass_kernel_spmd

F32, BF16 = mybir.dt.float32, mybir.dt.bfloat16
AF = mybir.ActivationFunctionType
ALU = mybir.AluOpType
AX = mybir.AxisListType

D = 2048
LS = 4096
LP = 256
NPS = 2
NTOK = LS + NPS * LP
EPS = 1e-6
NCORES = 8
DBG = {}


class Sem:
    def __init__(self, h):
        self.h = h
        self.n = 0


class Buf:
    def __init__(self, name=""):
        self.w = []
        self.r = []
        self.gen_r = []
        self.name = name
        self.sem = None


class Prog:
    ENG = ("pe", "act", "dve", "pool", "sp")
    COMPUTE = ("pe", "act", "dve", "pool")

    def __init__(self, nc, handles):
        self.nc = nc
        self.q = {e: [] for e in self.ENG}
        hs = list(handles)
        self.csem = {e: Sem(hs.pop()) for e in self.COMPUTE}
        self.bar = Sem(hs.pop())
        self.dpool = [Sem(h) for h in hs]
        nsw = len(self.dpool) // 3
        self.dfree = {"pool": self.dpool[:nsw], "sp": self.dpool[nsw:]}
        self.waited = {e: {} for e in self.ENG}
        self.pending = []
        self.bufs = []
        self.nops = {e: 0 for e in self.COMPUTE}
        self.entries = {e: {} for e in self.COMPUTE}
        self.sig_idx = {e: [] for e in self.COMPUTE}
        self.sig_cnt = {e: [] for e in self.COMPUTE}

    def buf(self, name=""):
        b = Buf(name)
        self.bufs.append(b)
        return b

    def bufs_n(self, n, name=""):
        return [self.buf(name + str(i)) for i in range(n)]

    def _resolve(self, tok):
        import bisect
        _, eng, idx = tok
        si = self.sig_idx[eng]
        p = bisect.bisect_left(si, idx)
        if p < len(si):
            return self.csem[eng], self.sig_cnt[eng][p]
        ent = self.entries[eng][idx]
        sem = self.csem[eng]
        sem.n += 1
        ent[1] = sem.h
        si.append(idx)
        self.sig_cnt[eng].append(sem.n)
        return sem, sem.n

    def _wait(self, eng, tok):
        if tok[0] == "op":
            if tok[1] == eng and eng == "pe":
                return
            sem, tgt = self._resolve(tok)
        else:
            sem, tgt, _ = tok
        if self.waited[eng].get(id(sem), 0) >= tgt:
            return
        self.waited[eng][id(sem)] = tgt
        self.q[eng].append([lambda e, h=sem.h, t=tgt: e.wait_ge(h, t), None])

    def _wait_many(self, eng, toks):
        best = {}
        for t in toks:
            if t[0] == "op":
                k = ("op", t[1])
                if k not in best or best[k][2] < t[2]:
                    best[k] = t
            else:
                k = id(t[0])
                if k not in best or best[k][1] < t[1]:
                    best[k] = t
        for t in best.values():
            self._wait(eng, t)

    def _hazards(self, eng, reads, writes, partial):
        toks = []
        for b in reads:
            toks += b.w
        for b in writes:
            if partial == "nowaw":
                if b.r:
                    b.gen_r = b.r
                    b.r = []
                    b.w = []
                toks += b.gen_r
            else:
                toks += b.w
                toks += b.r
                toks += b.gen_r
        self._wait_many(eng, toks)

    def _commit(self, tok, reads, writes, partial):
        for b in reads:
            b.r.append(tok)
        for b in writes:
            if partial:
                b.w.append(tok)
                if partial != "nowaw":
                    b.r = []
            else:
                b.w = [tok]
                b.r = []
                b.gen_r = []

    def op(self, eng, fn, reads=(), writes=(), partial=False):
        self._hazards(eng, reads, writes, partial)
        idx = self.nops[eng]
        self.nops[eng] += 1
        ent = [fn, None]
        self.entries[eng][idx] = ent
        self.q[eng].append(ent)
        tok = ("op", eng, idx)
        self._commit(tok, reads, writes, partial)
        return tok

    def dma(self, eng, out, in_, sbuf_buf, reads=(), writes=(), partial=False):
        self._hazards(eng, reads, writes, partial)
        if sbuf_buf.sem is None:
            sbuf_buf.sem = {}
        if eng not in sbuf_buf.sem:
            sbuf_buf.sem[eng] = self.dfree[eng].pop()
        sem = sbuf_buf.sem[eng]
        sem.n += 16
        tok = (sem, sem.n, "dma")
        self.q[eng].append([lambda e, o=out, i=in_: e.dma_start(out=o, in_=i), sem.h, 16])
        self._commit(tok, reads, writes, partial)
        self.pending.append(tok)
        return tok

    def barrier(self):
        for e in self.COMPUTE:
            if self.nops[e] > 0:
                self._wait("sp", ("op", e, self.nops[e] - 1))
        self._wait_many("sp", self.pending)
        self.pending = []
        self.bar.n += 1
        k = self.bar.n
        self.q["sp"].append([lambda e, h=self.bar.h: e.sem_inc(h, 1), None])
        for e in self.COMPUTE:
            self._wait(e, (self.bar, k, "bar"))
        for b in self.bufs:
            if b.sem is not None:
                for en, sm in b.sem.items():
                    self.dfree[en].append(sm)
                b.sem = None
            b.w = []
            b.r = []
            b.gen_r = []
        self.bufs = []

    def flush(self):
        nc = self.nc
        q = self.q

        def run(e, lst):
            for ent in lst:
                ins = ent[0](e)
                if ent[1] is not None:
                    ins.then_inc(ent[1], ent[2] if len(ent) > 2 else 1)

        with nc.Block() as blk:
            @blk.tensor
            def _(e):
                run(e, q["pe"])

            @blk.scalar
            def _(e):
                run(e, q["act"])

            @blk.vector
            def _(e):
                run(e, q["dve"])

            @blk.gpsimd
            def _(e):
                run(e, q["pool"])

            @blk.sync
            def _(e):
                run(e, q["sp"])
        self.q = {e: [] for e in self.ENG}
        for e in self.COMPUTE:
            self.entries[e] = {}


_UID = [0]


def sb(es, nc, name, shape, dt):
    _UID[0] += 1
    return es.enter_context(nc.sbuf_tensor("%s_%d" % (name, _UID[0]), shape, dt))


def rev_ap(ap):
    a = [list(p) for p in ap.ap]
    step, cnt = a[-1]
    off = ap.offset + step * (cnt - 1)
    a[-1] = [-step, cnt]
    return bass.AP(ap.tensor, off, a)


def phase_M(P, nc, T):
    with (
        nc.sbuf_tensor("m_cT", [128, 32], F32) as cT,
        nc.sbuf_tensor("m_sg", [128, 32], F32) as sg,
        nc.sbuf_tensor("m_sT", [128, 32], BF16) as sT,
        nc.sbuf_tensor("m_w0", [128, 3072], BF16) as w0,
        nc.sbuf_tensor("m_w1", [128, 3072], BF16) as w1,
        nc.sbuf_tensor("m_mrow", [2, 6144], F32) as mrow,
        nc.sbuf_tensor("m_brow", [2, 6144], F32) as brow,
        nc.sbuf_tensor("m_ng", [2, 2048], F32) as ng,
        nc.sbuf_tensor("m_orow", [2, 6144], F32) as orow,
    ):
        ps = T["ps"]
        b_cT, b_sT = P.buf(), P.buf()
        b_w = P.bufs_n(2)
        wt = [w0, w1]
        b_ps = P.bufs_n(6)
        b_mrow, b_brow, b_ng, b_orow = P.buf(), P.buf(), P.buf(), P.buf()
        P.dma("sp", cT[:], T["cvecT"], b_cT, writes=[b_cT])
        P.op("act", lambda e: e.activation(sg[:], cT[:], AF.Sigmoid), reads=[b_cT], writes=[b_sT])
        P.op("dve", lambda e: e.tensor_tensor(sT[:], sg[:], cT[:], ALU.mult), reads=[b_cT, b_sT], writes=[b_sT])
        cnt = 0
        for l in range(2):
            P.dma("sp", brow[:], T["mod_b"][l:l + 1, :].partition_broadcast(2), b_brow, writes=[b_brow])
            P.dma("sp", ng[:], T["norm_g"][l:l + 1, :].partition_broadcast(2), b_ng, writes=[b_ng])
            for hf in range(2):
                for k in range(16):
                    s = cnt % 2
                    cnt += 1
                    P.dma("pool", wt[s][:], T["mod_w"][l, k * 128:(k + 1) * 128, hf * 3072:(hf + 1) * 3072],
                          b_w[s], writes=[b_w[s]])
                    for j in range(6):
                        P.op("pe", lambda e, s=s, j=j, k=k: e.matmul(
                            ps[j][0:2, :], sT[:, 2 * k:2 * k + 2], wt[s][:, j * 512:(j + 1) * 512],
                            start=(k == 0), stop=(k == 15)),
                            reads=[b_w[s], b_sT], writes=[b_ps[j]], partial=(k > 0))
                for j in range(6):
                    c0 = hf * 3072 + j * 512
                    P.op("dve", lambda e, j=j, c0=c0: e.tensor_tensor(
                        mrow[:, c0:c0 + 512], ps[j][0:2, :], brow[:, c0:c0 + 512], ALU.add),
                        reads=[b_ps[j], b_brow], writes=[b_mrow], partial=True)
            P.op("dve", lambda e: e.scalar_tensor_tensor(
                orow[:, 0:2048], mrow[:, 2048:4096], 1.0, ng[:], ALU.add, ALU.mult),
                reads=[b_mrow, b_ng], writes=[b_orow])
            P.op("dve", lambda e: e.tensor_copy(orow[:, 2048:4096], mrow[:, 0:2048]),
                 reads=[b_mrow], writes=[b_orow], partial=True)
            P.op("dve", lambda e: e.tensor_copy(orow[:, 4096:6144], mrow[:, 4096:6144]),
                 reads=[b_mrow], writes=[b_orow], partial=True)
            P.dma("sp", T["modv"][l], orow[:], b_orow, reads=[b_orow])
        P.barrier()
        P.flush()


def load_bcast_rows(P, nc, T, l, tiles, bufs):
    modv = T["modv"]
    for key, (j, r) in (("A_s", (0, 0)), ("sh_s", (1, 0)), ("A_p", (0, 1)), ("sh_p", (1, 1))):
        P.dma("sp", tiles[key][:], modv[l][r:r + 1, j * 2048:(j + 1) * 2048].partition_broadcast(128),
              bufs[key], writes=[bufs[key]])


def norm_transpose_half(P, nc, T, xsrc, tok_tiles, hT, b_hT, bc, b_bc, work):
    ps = T["ps"]
    ident = T["ident"]
    xt, b_xt = work["xt"], work["b_xt"]
    st, b_st = work["st"], work["b_st"]
    hb, b_hb = work["hb"], work["b_hb"]
    b_pst = work["b_pst"]

    def front(i):
        toff, isp = tok_tiles[i]
        s = i % 2
        x4 = i % len(xt)
        P.dma("sp", xt[x4][:], xsrc[toff:toff + 128, :], b_xt[x4], writes=[b_xt[x4]])
        jk, bjk = work["junk2"][s], work["b_junk2"][s]
        P.op("act", lambda e, s=s, jk=jk, x4=x4: e.activation(jk[:], xt[x4][:], AF.Square, accum_out=st[s][:, 0:1]),
             reads=[b_xt[x4]], writes=[bjk, b_st[s]])
        P.op("act", lambda e, s=s: e.activation(st[s][:, 1:2], st[s][:, 0:1], AF.Sqrt, scale=1.0 / D,
                                                 bias=T["epsb"][:, 0:1]),
             reads=[b_st[s]], writes=[b_st[s]], partial=True)
        P.op("dve", lambda e, s=s: e.reciprocal(st[s][:, 2:3], st[s][:, 1:2]), reads=[b_st[s]], writes=[b_st[s]],
             partial=True)
        A = bc["A_p" if isp else "A_s"]
        SH = bc["sh_p" if isp else "sh_s"]
        bA = b_bc["A_p" if isp else "A_s"]
        bS = b_bc["sh_p" if isp else "sh_s"]
        tm, btm = work["tmp2"][s], work["b_tmp2"][s]
        P.op("dve", lambda e, s=s, A=A, tm=tm, x4=x4: e.scalar_tensor_tensor(tm[:], xt[x4][:], st[s][:, 2:3], A[:],
                                                                             ALU.mult, ALU.mult),
             reads=[b_xt[x4], b_st[s], bA], writes=[btm])
        P.op("pool", lambda e, s=s, SH=SH, tm=tm: e.tensor_tensor(hb[s][:], tm[:], SH[:], ALU.add),
             reads=[btm, bS], writes=[b_hb[s]])

    def back(i):
        s = i % 2
        for half in range(2):
            pb = ps[2 * s + half]
            bp = b_pst[2 * s + half]
            pbv = pb[:].bitcast(BF16)
            for kk in range(8):
                k = half * 8 + kk
                P.op("pe", lambda e, pbv=pbv, kk=kk, k=k, s=s: e.transpose(
                    pbv[:, kk * 128:(kk + 1) * 128], hb[s][:, k * 128:(k + 1) * 128], ident[:]),
                    reads=[b_hb[s]], writes=[bp], partial=(kk > 0))
            dst = hT[:, half * 8:(half + 1) * 8, i * 128:(i + 1) * 128]
            src = pbv.rearrange("p (k t) -> p k t", k=8)
            if half == 0:
                P.op("act", lambda e, dst=dst, src=src: e.copy(dst, src), reads=[bp], writes=[b_hT], partial="nowaw")
            else:
                P.op("dve", lambda e, dst=dst, src=src: e.tensor_copy(dst, src), reads=[bp], writes=[b_hT],
                     partial="nowaw")

    n = len(tok_tiles)
    front(0)
    for i in range(n):
        if i + 1 < n:
            front(i + 1)
        back(i)


def phase_A(P, nc, T, debug=False):
    ps = T["ps"]
    W = T["a_w_in"]
    groups = []
    for g in range(2):
        groups.append(("q", [("qT", (g * 4 + j) * 128, (g * 4 + j) * 128) for j in range(4)]))
    groups.append(("k", [("kT0", 0, 1024), ("kT1", 0, 1088)]))
    for g in range(2):
        groups.append(("ga", [("gaT", (g * 4 + j) * 128, 1280 + (g * 4 + j) * 128) for j in range(4)]))
    for g in range(6):
        groups.append(("hy", [("hyT", (g * 4 + j) * 128, 2304 + (g * 4 + j) * 128) for j in range(4)]))
    for g in range(2):
        groups.append(("gh", [("ghT", (g * 4 + j) * 128, 5376 + (g * 4 + j) * 128) for j in range(4)]))

    Wv = W.rearrange("(k p) c -> p k c", p=128)
    for half in range(DBG.get('halves', 2)):
        tok_tiles = [(half * 2048 + i * 128, False) for i in range(16)] + \
                    [(LS + half * 256 + i * 128, True) for i in range(2)]
        import contextlib
        with contextlib.ExitStack() as es0:
            hT = sb(es0, nc, "a_hT", [128, 16, 2304], BF16)
            b_hT = P.buf("hT")
            with contextlib.ExitStack() as es1:
                A_ = lambda n, shp, dt: sb(es1, nc, n, shp, dt)
                xt0 = A_("a_xt0", [128, 2048], F32); xt1 = A_("a_xt1", [128, 2048], F32)
                xt2 = A_("a_xt2", [128, 2048], F32); xt3 = A_("a_xt3", [128, 2048], F32)
                junk = A_("a_junk", [128, 2048], F32); tmp = A_("a_tmp", [128, 2048], F32)
                junkb = A_("a_junkb", [128, 2048], F32); tmpb = A_("a_tmpb", [128, 2048], F32)
                st0 = A_("a_st0", [128, 4], F32); st1 = A_("a_st1", [128, 4], F32)
                hb0 = A_("a_hb0", [128, 2048], BF16); hb1 = A_("a_hb1", [128, 2048], BF16)
                As = A_("a_As", [128, 2048], F32); shs = A_("a_shs", [128, 2048], F32)
                Ap = A_("a_Ap", [128, 2048], F32); shp = A_("a_shp", [128, 2048], F32)
                bc = {"A_s": As, "sh_s": shs, "A_p": Ap, "sh_p": shp}
                b_bc = {k: P.buf(k) for k in bc}
                load_bcast_rows(P, nc, T, 0, bc, b_bc)
                work = dict(xt=[xt0, xt1, xt2, xt3], b_xt=P.bufs_n(4), junk=junk, b_junk=P.buf(), st=[st0, st1],
                            b_st=P.bufs_n(2), tmp=tmp, b_tmp=P.buf(), hb=[hb0, hb1], b_hb=P.bufs_n(2),
                            b_pst=P.bufs_n(4), junk2=[junk, junkb], b_junk2=P.bufs_n(2), tmp2=[tmp, tmpb],
                            b_tmp2=P.bufs_n(2))
                norm_transpose_half(P, nc, T, T["x"], tok_tiles, hT, b_hT, bc, b_bc, work)
                P.barrier()
                P.flush()
            if DBG.get('norm_only'):
                continue
            with contextlib.ExitStack() as es2:
                A_ = lambda n, shp, dt: sb(es2, nc, n, shp, dt)
                wg0 = A_("a_w0", [128, 16, 512], BF16); wg1 = A_("a_w1", [128, 16, 512], BF16)
                wkv = A_("a_wkv", [128, 16, 256], BF16)
                cos_t = A_("a_cos", [128, 2048], F32); sin_t = A_("a_sin", [128, 2048], F32)
                stg0 = A_("a_stg0", [128, 512], BF16); stg1 = A_("a_stg1", [128, 512], BF16)
                stg2 = A_("a_stg2", [128, 512], BF16); stg3 = A_("a_stg3", [128, 512], BF16)
                qs0 = A_("a_qs0", [128, 512], BF16); qs1 = A_("a_qs1", [128, 512], BF16)
                t1 = A_("a_t1", [128, 512], F32); t2 = A_("a_t2", [128, 512], F32)
                kvf0 = A_("a_kvf0", [128, 256], F32); kvf1 = A_("a_kvf1", [128, 256], F32)
                vb0 = A_("a_vb0", [128, 128], BF16); vb1 = A_("a_vb1", [128, 128], BF16)
                b_hT = P.buf("hT2")
                wg = [wg0, wg1]
                b_wg = P.bufs_n(2)
                b_wkv = P.buf()
                b_cos, b_sin = P.buf(), P.buf()
                stg = [stg0, stg1, stg2, stg3]
                b_stg = P.bufs_n(4)
                qs = [qs0, qs1]
                b_qs = P.bufs_n(2)
                b_t1, b_t2 = P.buf(), P.buf()
                kvf = [kvf0, kvf1]
                b_kvf = P.bufs_n(2)
                vb = [vb0, vb1]
                b_vb = P.bufs_n(2)
                b_ps = P.bufs_n(8)
                P.dma("sp", cos_t[:], T["rope_cos"][:, half * 2048:(half + 1) * 2048], b_cos, writes=[b_cos])
                P.dma("sp", sin_t[:], T["rope_sin"][:, half * 2048:(half + 1) * 2048], b_sin, writes=[b_sin])
                for kq in range(16):
                    P.dma("pool", wkv[:, kq, :], Wv[:, kq, 1024:1280], b_wkv, writes=[b_wkv], partial=(kq > 0))
                for i, (toff, isp) in enumerate(tok_tiles[:DBG.get('nkv', 99)]):
                    pi = i % 2
                    pst = ps[6 + pi]
                    for k in range(16):
                        P.op("pe", lambda e, pst=pst, k=k, i=i: e.matmul(
                            pst[:, 0:256], hT[:, k, i * 128:(i + 1) * 128], wkv[:, k, :],
                            start=(k == 0), stop=(k == 15)),
                            reads=[b_hT, b_wkv], writes=[b_ps[6 + pi]], partial=(k > 0))
                    if isp and not DBG.get('no_isp'):
                        P.op("act", lambda e, pst=pst, pi=pi: e.copy(kvf[pi][:], pst[:, 0:256]),
                             reads=[b_ps[6 + pi]], writes=[b_kvf[pi]])
                        r0 = toff - LS
                        if not DBG.get('no_nk'):
                            P.dma("sp", T["nk"][r0:r0 + 128, :], kvf[pi][:, 0:128], b_kvf[pi], reads=[b_kvf[pi]])
                        if not DBG.get('no_nv'):
                            P.dma("sp", T["nv"][r0:r0 + 128, :], kvf[pi][:, 128:256], b_kvf[pi], reads=[b_kvf[pi]])
                    P.op("act", lambda e, pst=pst, pi=pi: e.copy(vb[pi][:], pst[:, 128:256]),
                         reads=[b_ps[6 + pi]], writes=[b_vb[pi]])
                    P.dma("sp", T["vtok"][toff:toff + 128, :], vb[pi][:], b_vb[pi], reads=[b_vb[pi]])
                chunks = [(c * 512, 512, half * 2048 + c * 512, False) for c in range(4)] + \
                         [(2048, 256, LS + half * 256, True)]
                evac_i = 0
                psi = 0
                for gi, (kind, blocks) in enumerate(groups[:DBG.get('ngroups', 99)] if not DBG.get('gsel') else [groups[i] for i in DBG['gsel']]):
                    s = gi % 2
                    if kind == "k":
                        for j, (name, r0, c0) in enumerate(blocks):
                            for dup in range(2):
                                for kq in range(16):
                                    P.dma("pool", wg[s][:, kq, j * 128 + dup * 64:j * 128 + dup * 64 + 64],
                                          Wv[:, kq, c0:c0 + 64], b_wg[s], writes=[b_wg[s]],
                                          partial=(j + dup + kq > 0))
                    else:
                        c0 = blocks[0][2]
                        for kq in range(16):
                            P.dma("pool", wg[s][:, kq, 0:512], Wv[:, kq, c0:c0 + 512],
                                  b_wg[s], writes=[b_wg[s]], partial=(kq > 0))
                    for (l0, n, g0, isp) in chunks:
                        for j, (name, r0, c0) in enumerate(blocks):
                            pj = psi % 4
                            psi += 1
                            pst = ps[pj]
                            for k in range(16):
                                P.op("pe", lambda e, pst=pst, k=k, s=s, j=j, l0=l0, n=n: e.matmul(
                                    pst[:, 0:n], wg[s][:, k, j * 128:(j + 1) * 128], hT[:, k, l0:l0 + n],
                                    start=(k == 0), stop=(k == 15)),
                                    reads=[b_hT, b_wg[s]], writes=[b_ps[pj]], partial=(k > 0))
                            if name.startswith("kT"):
                                dst = T["kT"][int(name[2]), :, g0:g0 + n]
                            else:
                                dst = T[name][r0:r0 + 128, g0:g0 + n]
                            si = evac_i % 4
                            evac_i += 1
                            if kind in ("q", "k") and not isp:
                                qi = evac_i % 2
                                P.op("act", lambda e, pst=pst, qi=qi, n=n: e.copy(qs[qi][:, 0:n], pst[:, 0:n]),
                                     reads=[b_ps[pj]], writes=[b_qs[qi]])
                                pr = ps[4 + qi]
                                P.op("pe", lambda e, pr=pr, qi=qi, n=n: e.matmul(
                                    pr[:, 0:n], T["ropeP"][:], qs[qi][:, 0:n], start=True, stop=True),
                                    reads=[b_qs[qi]], writes=[b_ps[4 + qi]])
                                P.op("dve", lambda e, qi=qi, l0=l0, n=n: e.tensor_tensor(
                                    t1[:, 0:n], qs[qi][:, 0:n], cos_t[:, l0:l0 + n], ALU.mult),
                                    reads=[b_qs[qi], b_cos], writes=[b_t1])
                                P.op("dve", lambda e, pr=pr, l0=l0, n=n: e.tensor_tensor(
                                    t2[:, 0:n], pr[:, 0:n], sin_t[:, l0:l0 + n], ALU.mult),
                                    reads=[b_ps[4 + qi], b_sin], writes=[b_t2])
                                P.op("dve", lambda e, si=si, n=n: e.tensor_tensor(
                                    stg[si][:, 0:n], t1[:, 0:n], t2[:, 0:n], ALU.add),
                                    reads=[b_t1, b_t2], writes=[b_stg[si]])
                            else:
                                if evac_i % 2 == 0:
                                    P.op("act", lambda e, pst=pst, si=si, n=n: e.copy(stg[si][:, 0:n], pst[:, 0:n]),
                                         reads=[b_ps[pj]], writes=[b_stg[si]])
                                else:
                                    P.op("dve", lambda e, pst=pst, si=si, n=n: e.tensor_copy(
                                        stg[si][:, 0:n], pst[:, 0:n]), reads=[b_ps[pj]], writes=[b_stg[si]])
                            P.dma("sp", dst, stg[si][:, 0:n], b_stg[si], reads=[b_stg[si]])
                P.barrier()
                P.flush()


def bcast_last(ap2d, n):
    a = [list(p) for p in ap2d.ap]
    return bass.AP(ap2d.tensor, ap2d.offset, a + [[0, n]])


def phase_B(P, nc, T):
    import contextlib
    ps = T["ps"]
    ident = T["ident"]
    seqs = [(0, LS, True), (LS, LP, False), (LS + LP, LP, False)]
    with contextlib.ExitStack() as es:
        A_ = lambda n, shp, dt: sb(es, nc, n, shp, dt)
        kT = [A_("b_kT0", [128, LS], BF16), A_("b_kT1", [128, LS], BF16)]
        kcT = [A_("b_kc0", [128, 512], BF16), A_("b_kc1", [128, 512], BF16)]
        ckd = A_("b_ckd", [128, 4, 2, 2, 64], BF16)
        vaug = A_("b_vaug", [128, LS // 128, 2, 65], BF16)
        cvaug = A_("b_cvaug", [128, 4, 2, 65], BF16)
        qT = [A_("b_q0", [128, LS], BF16), A_("b_q1", [128, LS], BF16)]
        ga = [A_("b_ga0", [128, LS], BF16), A_("b_ga1", [128, LS], BF16)]
        sga = A_("b_sga", [128, LS], BF16)
        ptA = [A_("b_ptA0", [128, 512], BF16), A_("b_ptA1", [128, 512], BF16)]
        ptB = [A_("b_ptB0", [128, 384], BF16), A_("b_ptB1", [128, 384], BF16)]
        att = [A_("b_att0", [128, 128], BF16), A_("b_att1", [128, 128], BF16)]
        stage = [A_("b_stg0", [128, 512], BF16), A_("b_stg1", [128, 512], BF16)]
        mask = A_("b_mask", [128, 384], BF16)
        sinkb = A_("b_sinkb", [128, 16], F32)
        esink = A_("b_esink", [128, 16], F32)
        den = [A_("b_den0", [128, 2], F32), A_("b_den1", [128, 2], F32)]
        rden = [A_("b_rden0", [128, 2], F32), A_("b_rden1", [128, 2], F32)]

        b_mask, b_es = P.buf(), P.buf()
        P.dma("sp", mask[:], T["d_amask"], b_mask, writes=[b_mask])
        P.dma("sp", sinkb[:], T["a_sink"][0:1, :].partition_broadcast(128), b_es, writes=[b_es])
        P.op("act", lambda e: e.activation(esink[:], sinkb[:], AF.Exp), reads=[b_es], writes=[b_es])
        P.op("dve", lambda e: e.memset(vaug[:], 1.0), writes=[b_mask], partial=True)
        P.op("dve", lambda e: e.memset(cvaug[:], 1.0), writes=[b_mask], partial=True)
        b_ckd, b_kc, b_cv = P.buf(), P.bufs_n(2), P.buf()
        ckv = T["ck"].rearrange("(t p) c -> p t c", p=128)
        cvv = T["cv"].rearrange("(t p) c -> p t c", p=128)
        for kv in range(2):
            for dup in range(2):
                P.dma("pool", ckd[:, :, kv, dup, :], ckv[:, :, kv * 64:(kv + 1) * 64], b_ckd, writes=[b_ckd],
                      partial=True)
            P.dma("pool", cvaug[:, :, kv, 0:64], cvv[:, :, kv * 64:(kv + 1) * 64], b_cv, reads=[b_mask],
                  writes=[b_cv], partial=True)
        b_pt = P.bufs_n(8)
        for kv in range(2):
            for st_ in range(4):
                pb = ps[st_ % 2][:].bitcast(BF16)
                P.op("pe", lambda e, pb=pb, st_=st_, kv=kv: e.transpose(
                    pb[:, 0:128], ckd[:, st_, kv].rearrange("p a b -> p (a b)"), ident[:]),
                    reads=[b_ckd], writes=[b_pt[st_ % 2]])
                P.op("dve", lambda e, pb=pb, st_=st_, kv=kv: e.tensor_copy(
                    kcT[kv][:, st_ * 128:(st_ + 1) * 128], pb[:, 0:128]),
                    reads=[b_pt[st_ % 2]], writes=[b_kc[kv]], partial=True)
        P.barrier()
        b_kT, b_v = P.bufs_n(2), P.buf()
        b_q, b_ga, b_sga = P.bufs_n(2), P.bufs_n(2), P.buf()
        b_ptA, b_ptB, b_att, b_stage = P.bufs_n(2), P.bufs_n(2), P.bufs_n(2), P.bufs_n(2)
        b_den = P.bufs_n(2)
        b_ps = P.bufs_n(8)
        cnt = 0
        for (tok0, L, has_ctx) in seqs:
            nqb = L // 128
            for kv in range(2):
                P.dma("sp", kT[kv][:, 0:L], T["kT"][kv, :, tok0:tok0 + L], b_kT[kv], writes=[b_kT[kv]])
                P.dma("sp", vaug[:, 0:nqb, kv, 0:64],
                      T["vtok"][tok0:tok0 + L, kv * 64:(kv + 1) * 64].rearrange("(t p) c -> p t c", p=128),
                      b_v, writes=[b_v], partial=(kv > 0))
            def load_hp(hp_, s_, L=L, tok0=tok0):
                P.dma("sp", qT[s_][:, 0:L], T["qT"][hp_ * 128:(hp_ + 1) * 128, tok0:tok0 + L], b_q[s_],
                      writes=[b_q[s_]])
                P.dma("sp", ga[s_][:, 0:L], T["gaT"][hp_ * 128:(hp_ + 1) * 128, tok0:tok0 + L], b_ga[s_],
                      writes=[b_ga[s_]])

            load_hp(0, cnt % 2)
            for hp in range(8):
                s = cnt % 2
                cnt += 1
                kv = hp // 4
                P.op("act", lambda e, s=s, L=L: e.activation(sga[:, 0:L], ga[s][:, 0:L], AF.Silu),
                     reads=[b_ga[s]], writes=[b_sga])
                def geom(qb, nqb=nqb, has_ctx=has_ctx):
                    if has_ctx:
                        loc = [(j, j - qb + 1) for j in (qb - 1, qb, qb + 1) if 0 <= j < nqb]
                    else:
                        loc = [(j, j) for j in range(nqb)]
                    return loc, loc[0][1], loc[-1][1] + 1

                def S_unit(qb, hh, s=s, kv=kv, has_ctx=has_ctx):
                    loc, lo, hi = geom(qb)
                    pr = slice(hh * 64, (hh + 1) * 64)
                    pa, pbk = ps[hh * 2], ps[hh * 2 + 1]
                    qsl = qT[s][pr, qb * 128:(qb + 1) * 128]
                    if has_ctx:
                        for c in range(4):
                            P.op("pe", lambda e, pa=pa, c=c, pr=pr, qsl=qsl, kv=kv: e.matmul(
                                pa[:, c * 128:(c + 1) * 128], kcT[kv][pr, c * 128:(c + 1) * 128], qsl,
                                start=True, stop=True),
                                reads=[b_kc[kv], b_q[s]], writes=[b_ps[hh * 2]], partial=(c > 0))
                        P.op("act", lambda e, pa=pa, hh=hh: e.activation(ptA[hh][:], pa[:], AF.Exp, scale=0.125),
                             reads=[b_ps[hh * 2]], writes=[b_ptA[hh]])
                    for n_, (j, sl) in enumerate(loc):
                        P.op("pe", lambda e, pbk=pbk, sl=sl, j=j, pr=pr, qsl=qsl, kv=kv: e.matmul(
                            pbk[:, sl * 128:(sl + 1) * 128], kT[kv][pr, j * 128:(j + 1) * 128], qsl,
                            start=True, stop=True),
                            reads=[b_kT[kv], b_q[s]], writes=[b_ps[hh * 2 + 1]], partial=(n_ > 0))
                    P.op("act", lambda e, pbk=pbk, hh=hh, lo=lo, hi=hi: e.activation(
                        ptB[hh][:, lo * 128:hi * 128], pbk[:, lo * 128:hi * 128], AF.Exp, scale=0.125),
                        reads=[b_ps[hh * 2 + 1]], writes=[b_ptB[hh]])
                    if has_ctx:
                        P.op("dve", lambda e, hh=hh, lo=lo, hi=hi: e.tensor_tensor(
                            ptB[hh][:, lo * 128:hi * 128], ptB[hh][:, lo * 128:hi * 128],
                            mask[:, lo * 128:hi * 128], ALU.mult),
                            reads=[b_ptB[hh], b_mask], writes=[b_ptB[hh]])

                def PV_unit(qb, hh, kv=kv, has_ctx=has_ctx):
                    loc, lo, hi = geom(qb)
                    o = qb % 2
                    psOv = ps[4 + o][:, 0:130].rearrange("p (h c) -> p h c", h=2)
                    mm = []
                    if has_ctx:
                        for c in range(4):
                            mm.append((ptA[hh][:, c * 128:(c + 1) * 128], cvaug[:, c, kv, :], b_ptA[hh], b_cv))
                    for (j, sl) in loc:
                        mm.append((ptB[hh][:, sl * 128:(sl + 1) * 128], vaug[:, j, kv, :], b_ptB[hh], b_v))
                    for n_, (lh, rh, bl, br) in enumerate(mm):
                        P.op("pe", lambda e, psOv=psOv, hh=hh, lh=lh, rh=rh, n_=n_, nm=len(mm): e.matmul(
                            psOv[:, hh, :], lh, rh, start=(n_ == 0), stop=(n_ == nm - 1)),
                            reads=[bl, br], writes=[b_ps[4 + o]], partial=(n_ > 0 or hh > 0))

                def FIN_unit(qb, s=s, hp=hp, nqb=nqb, tok0=tok0, cnt=cnt):
                    o = qb % 2
                    psOv = ps[4 + o][:, 0:130].rearrange("p (h c) -> p h c", h=2)
                    P.op("dve", lambda e, psOv=psOv, o=o, hp=hp: e.tensor_tensor(
                        den[o][:], psOv[:, :, 64], esink[:, hp * 2:hp * 2 + 2], ALU.add),
                        reads=[b_ps[4 + o], b_es], writes=[b_den[o]])
                    P.op("dve", lambda e, o=o: e.reciprocal(rden[o][:], den[o][:]), reads=[b_den[o]],
                         writes=[b_den[o]], partial=True)
                    P.op("dve", lambda e, psOv=psOv, o=o: e.tensor_tensor(
                        att[o][:].rearrange("p (h c) -> p h c", h=2), psOv[:, :, 0:64], bcast_last(rden[o][:], 64),
                        ALU.mult),
                        reads=[b_ps[4 + o], b_den[o]], writes=[b_att[o]])
                    pT = ps[6 + o][:].bitcast(BF16)
                    P.op("pe", lambda e, pT=pT, o=o: e.transpose(pT[:, 0:128], att[o][:], ident[:]),
                         reads=[b_att[o]], writes=[b_ps[6 + o]])
                    g4 = qb // 4
                    sg_ = (cnt * 1024 + g4) % 2
                    P.op("dve", lambda e, pT=pT, sg_=sg_, qb=qb: e.tensor_tensor(
                        stage[sg_][:, (qb % 4) * 128:(qb % 4 + 1) * 128], pT[:, 0:128],
                        sga[:, qb * 128:(qb + 1) * 128], ALU.mult),
                        reads=[b_ps[6 + o], b_sga], writes=[b_stage[sg_]], partial=(qb % 4 > 0))
                    if qb % 4 == 3 or qb == nqb - 1:
                        n = (qb % 4 + 1) * 128
                        t0 = tok0 + g4 * 512
                        P.dma("sp", T["mixT"][hp * 128:(hp + 1) * 128, t0:t0 + n], stage[sg_][:, 0:n],
                              b_stage[sg_], reads=[b_stage[sg_]])

                units = [(qb, hh) for qb in range(nqb) for hh in range(2)]
                S_unit(*units[0])
                pend_fin = None
                for ui, (qb, hh) in enumerate(units):
                    if ui == 1 and hp + 1 < 8:
                        load_hp(hp + 1, cnt % 2)
                    if ui + 1 < len(units):
                        S_unit(*units[ui + 1])
                    PV_unit(qb, hh)
                    if pend_fin is not None:
                        FIN_unit(pend_fin)
                        pend_fin = None
                    if hh == 1:
                        pend_fin = qb
                if pend_fin is not None:
                    FIN_unit(pend_fin)
        P.barrier()
        P.flush()


def phase_C0(P, nc, T):
    import contextlib
    seqs = [(0, LS), (LS, LP), (LS + LP, LP)]
    with contextlib.ExitStack() as es:
        A_ = lambda n, shp, dt: sb(es, nc, n, shp, dt)
        hsw = A_("c0_hsw", [128, 4, 24], F32)
        ut = [A_("c0_ut0", [128, LS + 2], BF16), A_("c0_ut1", [128, LS + 2], BF16), A_("c0_ut2", [128, LS + 2], BF16)]
        accs = [A_("c0_acc", [128, LS], F32), A_("c0_accb", [128, LS], F32)]
        acc2s = [A_("c0_acc2", [128, LS], F32), A_("c0_acc2b", [128, LS], F32)]
        ob = [A_("c0_ob0", [128, LS], BF16), A_("c0_ob1", [128, LS], BF16)]
        b_hsw, b_ut, b_accs, b_acc2s, b_ob = P.buf(), P.bufs_n(3), P.bufs_n(2), P.bufs_n(2), P.bufs_n(2)
        P.dma("sp", hsw[:], T["hsw"], b_hsw, writes=[b_hsw])
        items = [(tok0, L, ct) for (tok0, L) in seqs for ct in range(24)]

        def front(it):
            tok0, L, ct = items[it]
            u = it % 3
            P.op("pool", lambda e, u=u: e.memset(ut[u][:, 0:1], 0.0), writes=[b_ut[u]])
            P.op("pool", lambda e, u=u, L=L: e.memset(ut[u][:, L + 1:L + 2], 0.0), writes=[b_ut[u]], partial=True)
            P.dma("sp", ut[u][:, 1:L + 1], T["hyT"][ct * 128:(ct + 1) * 128, tok0:tok0 + L], b_ut[u],
                  writes=[b_ut[u]], partial=True)

        def back(it):
            tok0, L, ct = items[it]
            u = it % 3
            s = it % 2
            acc, acc2, b_acc, b_acc2 = accs[s], acc2s[s], b_accs[s], b_acc2s[s]
            P.op("act", lambda e, u=u, L=L, ct=ct, acc=acc: e.activation(
                acc[:, 0:L], ut[u][:, 0:L], AF.Identity, scale=hsw[:, 0, ct:ct + 1], bias=hsw[:, 3, ct:ct + 1]),
                reads=[b_ut[u], b_hsw], writes=[b_acc])
            P.op("dve", lambda e, u=u, L=L, ct=ct, acc=acc, acc2=acc2: e.scalar_tensor_tensor(
                acc2[:, 0:L], ut[u][:, 1:L + 1], hsw[:, 1, ct:ct + 1], acc[:, 0:L], ALU.mult, ALU.add),
                reads=[b_ut[u], b_acc, b_hsw], writes=[b_acc2])
            P.op("dve", lambda e, u=u, s=s, L=L, ct=ct, acc2=acc2: e.scalar_tensor_tensor(
                ob[s][:, 0:L], ut[u][:, 2:L + 2], hsw[:, 2, ct:ct + 1], acc2[:, 0:L], ALU.mult, ALU.add),
                reads=[b_ut[u], b_acc2, b_hsw], writes=[b_ob[s]])
            P.dma("sp", T["ucT"][ct * 128:(ct + 1) * 128, tok0:tok0 + L], ob[s][:, 0:L], b_ob[s],
                  reads=[b_ob[s]])

        n = len(items)
        front(0)
        if n > 1:
            front(1)
        for it in range(n):
            if it + 2 < n:
                front(it + 2)
            back(it)
        P.barrier()
        P.flush()


def phase_CF(P, nc, T, Lf, taps_dst, zemb, tposb):
    import contextlib
    import math
    ps = T["ps"]
    nch = max(1, Lf // 512)
    cw = min(512, Lf)
    with contextlib.ExitStack() as es:
        A_ = lambda n, shp, dt: sb(es, nc, n, shp, dt)
        ze = A_("cf_ze", [33, Lf], F32)
        tp_ = A_("cf_tpos", [128, Lf], F32)
        w1 = A_("cf_w1", [33, 64], F32)
        w2 = A_("cf_w2", [64, 64], F32)
        bb = A_("cf_bb", [64, 2], F32)
        w3 = A_("cf_w3", [64, 4096], BF16)
        dec = A_("cf_dec", [128, 32], F32)
        ndec = A_("cf_ndec", [128, 32], F32)
        h1 = A_("cf_h1", [64, Lf], F32)
        h2 = A_("cf_h2", [64, Lf], F32)
        h2b = A_("cf_h2b", [64, Lf], BF16)
        tmp = A_("cf_tmp", [64, 512], F32)
        tmq = A_("cf_tmq", [64, 512], F32)
        raw = [A_("cf_raw0", [128, Lf], F32), A_("cf_raw1", [128, Lf], F32)]
        win = [A_("cf_win0", [128, 512], F32), A_("cf_win1", [128, 512], F32)]
        junk = A_("cf_junk", [128, Lf], F32)
        ss = A_("cf_ss", [128, 8], F32)
        tpb = [A_("cf_tp0", [128, 2 * Lf], BF16), A_("cf_tp1", [128, 2 * Lf], BF16)]
        b_c = P.bufs_n(8)
        P.dma("sp", ze[:], zemb, b_c[0], writes=[b_c[0]])
        P.dma("sp", tp_[:], tposb, b_c[1], writes=[b_c[1]])
        P.dma("sp", w1[:], T["hy_w1"], b_c[2], writes=[b_c[2]])
        P.dma("sp", w2[:], T["hy_w2"], b_c[3], writes=[b_c[3]])
        P.dma("sp", bb[:], T["hy_bb"], b_c[4], writes=[b_c[4]])
        for q4 in range(4):
            P.dma("pool", w3[:, q4 * 1024:(q4 + 1) * 1024], T["hy_w3"][:, q4 * 1024:(q4 + 1) * 1024], b_c[5],
                  writes=[b_c[5]], partial=(q4 > 0))
        P.dma("sp", dec[:], T["hy_decT"], b_c[6], writes=[b_c[6]])
        P.op("act", lambda e: e.activation(dec[:], dec[:], AF.Abs), reads=[b_c[6]], writes=[b_c[6]])
        P.op("dve", lambda e: e.tensor_scalar(ndec[:], dec[:], -1.0, None, ALU.mult),
             reads=[b_c[6]], writes=[b_c[7]])
        b_h1, b_h2, b_h2b, b_tmp, b_tmq = P.buf(), P.buf(), P.buf(), P.buf(), P.buf()
        b_ps = P.bufs_n(8)
        TWO_PI = 2.0 * math.pi
        for layer in range(2):
            src = ze if layer == 0 else h1
            wm = w1 if layer == 0 else w2
            kk = 33 if layer == 0 else 64
            dst = h1 if layer == 0 else h2
            b_src = b_c[0] if layer == 0 else b_h1
            b_dst = b_h1 if layer == 0 else b_h2
            for c in range(nch):
                pj = c % 2
                P.op("pe", lambda e, pj=pj, c=c, src=src, wm=wm, kk=kk: e.matmul(
                    ps[pj][0:64, 0:cw], wm[0:kk, :], src[0:kk, c * cw:(c + 1) * cw], start=True, stop=True),
                    reads=[b_src, b_c[2], b_c[3]], writes=[b_ps[pj]])
                MAGIC = 12582912.0
                P.op("act", lambda e, pj=pj, layer=layer: e.activation(
                    tmp[:, 0:cw], ps[pj][0:64, 0:cw], AF.Identity, bias=bb[:, layer:layer + 1]),
                    reads=[b_ps[pj], b_c[4]], writes=[b_tmp])
                P.op("dve", lambda e: e.tensor_scalar(
                    tmq[:, 0:cw], tmp[:, 0:cw], 1.0 / TWO_PI, MAGIC, ALU.mult, ALU.add),
                    reads=[b_tmp], writes=[b_tmq])
                P.op("dve", lambda e: e.tensor_scalar(
                    tmq[:, 0:cw], tmq[:, 0:cw], -MAGIC, -TWO_PI, ALU.add, ALU.mult),
                    reads=[b_tmq], writes=[b_tmq])
                P.op("dve", lambda e: e.tensor_tensor(tmp[:, 0:cw], tmp[:, 0:cw], tmq[:, 0:cw], ALU.add),
                     reads=[b_tmp, b_tmq], writes=[b_tmp])
                P.op("act", lambda e, c=c, dst=dst: e.activation(dst[:, c * cw:(c + 1) * cw], tmp[:, 0:cw], AF.Sin),
                     reads=[b_tmp], writes=[b_dst], partial=(c > 0))
        P.op("act", lambda e: e.copy(h2b[:], h2[:]), reads=[b_h2], writes=[b_h2b])
        b_raw, b_win, b_junk, b_ss, b_tpb = P.bufs_n(2), P.bufs_n(2), P.buf(), P.buf(), P.bufs_n(2)
        it = 0
        for order in range(2):
            for blk in range(8):
                s = it % 2
                it += 1
                for d in range(2):
                    cbk = d * 16 + order * 8 + blk
                    for c in range(nch):
                        pj = 2 + (c % 2)
                        wj = c % 2
                        P.op("pe", lambda e, pj=pj, c=c, cbk=cbk: e.matmul(
                            ps[pj][:, 0:cw], w3[:, cbk * 128:(cbk + 1) * 128], h2b[:, c * cw:(c + 1) * cw],
                            start=True, stop=True),
                            reads=[b_h2b, b_c[5]], writes=[b_ps[pj]])
                        P.op("act", lambda e, wj=wj, c=c, cbk=cbk: e.activation(
                            win[wj][:, 0:cw], tp_[:, c * cw:(c + 1) * cw], AF.Exp, scale=ndec[:, cbk:cbk + 1]),
                            reads=[b_c[1], b_c[7]], writes=[b_win[wj]])
                        P.op("dve", lambda e, pj=pj, wj=wj, c=c, d=d: e.tensor_tensor(
                            raw[d][:, c * cw:(c + 1) * cw], ps[pj][:, 0:cw], win[wj][:, 0:cw], ALU.mult),
                            reads=[b_ps[pj], b_win[wj]], writes=[b_raw[d]], partial=(c > 0))
                P.op("act", lambda e: e.activation(junk[:], raw[0][:], AF.Square, accum_out=ss[:, 0:1]),
                     reads=[b_raw[0]], writes=[b_junk, b_ss])
                P.op("act", lambda e: e.activation(junk[:, 1:Lf], raw[1][:, 1:Lf], AF.Square, accum_out=ss[:, 1:2]),
                     reads=[b_raw[1]], writes=[b_junk, b_ss], partial=True)
                P.op("dve", lambda e: e.tensor_tensor(ss[:, 2:3], ss[:, 0:1], ss[:, 1:2], ALU.add),
                     reads=[b_ss], writes=[b_ss], partial=True)
                P.op("act", lambda e: e.activation(ss[:, 3:4], ss[:, 2:3], AF.Sqrt, bias=T["epsb"][:, 0:1]),
                     reads=[b_ss], writes=[b_ss], partial=True)
                P.op("dve", lambda e: e.reciprocal(ss[:, 4:5], ss[:, 3:4]), reads=[b_ss], writes=[b_ss], partial=True)
                P.op("dve", lambda e: e.tensor_scalar(ss[:, 5:6], ss[:, 4:5], -1.0, None, ALU.mult),
                     reads=[b_ss], writes=[b_ss], partial=True)
                P.op("act", lambda e, s=s: e.activation(tpb[s][:, 0:Lf], raw[0][:], AF.Copy, scale=ss[:, 4:5]),
                     reads=[b_raw[0], b_ss], writes=[b_tpb[s]])
                P.op("pool", lambda e, s=s: e.memset(tpb[s][:, Lf:Lf + 1], 0.0), writes=[b_tpb[s]], partial=True)
                P.op("dve", lambda e, s=s: e.tensor_scalar(
                    rev_ap(tpb[s][:, Lf + 1:2 * Lf]), raw[1][:, 1:Lf], ss[:, 5:6], None, ALU.mult),
                    reads=[b_raw[1], b_ss], writes=[b_tpb[s]], partial=True)
                P.dma("sp", taps_dst[order, blk * 128:(blk + 1) * 128, :], tpb[s][:], b_tpb[s], reads=[b_tpb[s]])
        P.barrier()
        P.flush()


def _split_dma(P, eng, dst, src, buf, nsplit, axis_len, mk_dst, mk_src, **kw):
    step = axis_len // nsplit
    for i in range(nsplit):
        P.dma(eng, mk_dst(i * step, (i + 1) * step), mk_src(i * step, (i + 1) * step), buf,
              partial=(i > 0 or kw.get("partial", False)), **{k: v for k, v in kw.items() if k != "partial"})


def _f1_stage(P, nc, T, xin, K, b_xin, F1, b_F1, Cb, b_Cb, b_ps, ev):
    ps = T["ps"]
    for g in range(16):
        pj = g % 2
        for cc in range(4):
            c = g * 4 + cc
            P.op("pe", lambda e, pj=pj, cc=cc, c=c: e.matmul(
                ps[pj][0:64, cc * 128:(cc + 1) * 128], xin[0:K, c, :], F1[0:K, 0:128], start=True, stop=True),
                reads=[b_xin, b_F1], writes=[b_ps[pj]], partial=(cc > 0))
            P.op("pe", lambda e, pj=pj, cc=cc, c=c: e.matmul(
                ps[pj][64:128, cc * 128:(cc + 1) * 128], xin[0:K, c, :], F1[0:K, 128:256], start=True, stop=True,
                tile_position=(0, 64)),
                reads=[b_xin, b_F1], writes=[b_ps[pj]], partial=True)
        src = ps[pj][:].rearrange("p (c k) -> p k c", c=4)
        dst = Cb[:, :, g * 4:(g + 1) * 4]
        ev(dst, src, [b_ps[pj]], [b_Cb], g)


def _evac_alt(P):
    def ev(dst, src, reads, writes, i):
        if i % 2 == 0:
            P.op("act", lambda e: e.copy(dst, src), reads=reads, writes=writes, partial="nowaw")
        else:
            P.op("dve", lambda e: e.tensor_copy(dst, src), reads=reads, writes=writes, partial="nowaw")
    return ev


def _evac_act(P):
    def ev(dst, src, reads, writes, i):
        P.op("act", lambda e: e.copy(dst, src), reads=reads, writes=writes, partial="nowaw")
    return ev


def phase_CT(P, nc, T):
    import contextlib
    ps = T["ps"]
    with contextlib.ExitStack() as es:
        A_ = lambda n, shp, dt: sb(es, nc, n, shp, dt)
        F1 = A_("ct_F1", [128, 256], BF16)
        Grr = A_("ct_Grr", [128, 128, 64], BF16)
        Gii = A_("ct_Gii", [128, 128, 64], BF16)
        xt = [A_("ct_xt0", [128, 64, 64], BF16), A_("ct_xt1", [128, 64, 64], BF16)]
        Cb = A_("ct_Cb", [128, 128, 64], BF16)
        Hr = [A_("ct_Hr0", [64, 64, 128], BF16), A_("ct_Hr1", [64, 64, 128], BF16)]
        Hi = [A_("ct_Hi0", [64, 64, 128], BF16), A_("ct_Hi1", [64, 64, 128], BF16)]
        b_F1, b_G = P.buf(), P.buf()
        P.dma("sp", F1[:], T["d_F1"], b_F1, writes=[b_F1])
        for q in range(4):
            P.dma("sp", Grr[:, q * 32:(q + 1) * 32, :], T["d_Grr"][:, q * 32:(q + 1) * 32, :], b_G, writes=[b_G],
                  partial=(q > 0))
            P.dma("sp", Gii[:, q * 32:(q + 1) * 32, :], T["d_Gii"][:, q * 32:(q + 1) * 32, :], b_G, writes=[b_G],
                  partial=True)
        b_xt, b_Cb, b_Hr, b_Hi = P.bufs_n(2), P.buf(), P.bufs_n(2), P.bufs_n(2)
        b_ps = P.bufs_n(8)
        ev = _evac_alt(P)
        it = 0
        def ct_load(order, cb, s):
            srcv = T["tapsS"][order, cb * 64:(cb + 1) * 64, :].rearrange("c (a b) -> a c b", b=64)
            for q in range(8):
                P.dma("sp", xt[s][:, q * 8:(q + 1) * 8, :], srcv[:, q * 8:(q + 1) * 8, :], b_xt[s],
                      writes=[b_xt[s]], partial=(q > 0))

        blocks = [(order, cb) for order in range(2) for cb in range(16)]
        ct_load(0, 0, 0)
        for (order, cb) in blocks:
                s = it % 2
                it += 1
                _f1_stage(P, nc, T, xt[s], 128, b_xt[s], F1, b_F1, Cb, b_Cb, b_ps, ev)
                if it < len(blocks):
                    ct_load(blocks[it][0], blocks[it][1], it % 2)
                for g in range(16):
                    pj = 2 + (g % 2) * 2
                    for kk in range(8):
                        k1 = g * 8 + kk
                        P.op("pe", lambda e, pj=pj, kk=kk, k1=k1: e.matmul(
                            ps[pj][0:64, kk * 64:(kk + 1) * 64], Grr[:, k1, :], Cb[:, k1, :], start=True, stop=True),
                            reads=[b_G, b_Cb], writes=[b_ps[pj]], partial=(kk > 0))
                        P.op("pe", lambda e, pj=pj, kk=kk, k1=k1: e.matmul(
                            ps[pj + 1][0:64, kk * 64:(kk + 1) * 64], Gii[:, k1, :], Cb[:, k1, :], start=True,
                            stop=True),
                            reads=[b_G, b_Cb], writes=[b_ps[pj + 1]], partial=(kk > 0))
                    P.op("act", lambda e, pj=pj, g=g, s=s: e.copy(
                        Hr[s][:, :, g * 8:(g + 1) * 8], ps[pj][0:64, :].rearrange("p (k c) -> p c k", k=8)),
                        reads=[b_ps[pj]], writes=[b_Hr[s]], partial="nowaw")
                    P.op("dve", lambda e, pj=pj, g=g, s=s: e.tensor_copy(
                        Hi[s][:, :, g * 8:(g + 1) * 8], ps[pj + 1][0:64, :].rearrange("p (k c) -> p c k", k=8)),
                        reads=[b_ps[pj + 1]], writes=[b_Hi[s]], partial="nowaw")
                P.dma("sp", T["Hs"][order, cb, 0], Hr[s][:].rearrange("p c k -> p (c k)"), b_Hr[s], reads=[b_Hr[s]])
                P.dma("sp", T["Hs"][order, cb, 1], Hi[s][:].rearrange("p c k -> p (c k)"), b_Hi[s], reads=[b_Hi[s]])
        P.barrier()
        P.flush()


def phase_CS(P, nc, T):
    import contextlib
    ps = T["ps"]
    with contextlib.ExitStack() as es:
        A_ = lambda n, shp, dt: sb(es, nc, n, shp, dt)
        F1 = A_("cs_F1", [128, 256], BF16)
        G = A_("cs_G", [128, 128, 64], BF16)
        M1 = A_("cs_M1", [64, 128], BF16)
        M1p = A_("cs_M1p", [64, 128], BF16)
        T2r = A_("cs_T2r", [128, 64, 64], BF16)
        T2i = A_("cs_T2i", [128, 64, 64], BF16)
        zc = A_("cs_zc", [64, 64, 64], BF16)
        x1 = A_("cs_x1", [64, 64, 64], BF16)
        x2 = A_("cs_x2", [64, 64, 64], BF16)
        gh = A_("cs_gh", [64, 64, 64], BF16)
        z2 = A_("cs_z2", [64, 64, 64], BF16)
        bz = A_("cs_bz", [64, 64, 64], BF16)
        cv = A_("cs_cv", [64, 64, 64], F32)
        bias = A_("cs_bias", [64, 2, 64], F32)
        Cb = A_("cs_Cb", [128, 128, 64], BF16)
        Db = A_("cs_Db", [128, 128, 64], BF16)
        P1 = A_("cs_P1", [64, 64, 128], BF16)
        P2 = A_("cs_P2", [64, 64, 128], BF16)
        Hr = A_("cs_Hr", [64, 64, 128], BF16)
        Hi = A_("cs_Hi", [64, 64, 128], BF16)
        Xs = [A_("cs_Xs0", [64, 512], BF16), A_("cs_Xs1", [64, 512], BF16)]
        b_k = P.bufs_n(6)
        P.dma("sp", F1[:], T["d_F1"], b_k[0], writes=[b_k[0]])
        for q in range(4):
            P.dma("sp", G[:, q * 32:(q + 1) * 32, :], T["d_G"][:, q * 32:(q + 1) * 32, :], b_k[1], writes=[b_k[1]],
                  partial=(q > 0))
        P.dma("sp", M1[:], T["d_M1"], b_k[2], writes=[b_k[2]])
        P.dma("sp", M1p[:], T["d_M1p"], b_k[3], writes=[b_k[3]])
        P.dma("sp", T2r[:], T["d_T2r"], b_k[4], writes=[b_k[4]])
        P.dma("sp", T2i[:], T["d_T2in"], b_k[5], writes=[b_k[5]])
        b_F1, b_G, b_M1, b_M1p, b_T2r, b_T2i = b_k
        b_zc, b_x1, b_x2, b_gh, b_sgh, b_z2, b_bz, b_cv, b_mx, b_bias = [P.buf() for _ in range(10)]
        b_Cb, b_Db, b_P1, b_P2, b_Hr, b_Hi = [P.buf() for _ in range(6)]
        b_Xs = P.bufs_n(2)
        b_ps = P.bufs_n(8)
        ev = _evac_act(P)

        def t64(rows0):
            return lambda a, b: T["ucT"][rows0 + a:rows0 + b, 0:LS].rearrange("c (n1 n2) -> n1 c n2", n2=64)

        def cs_load(cb, which):
            specs = {"zc": (zc, b_zc, 2048 + cb * 64, "ucT"), "x1": (x1, b_x1, cb * 64, "ucT"),
                     "x2": (x2, b_x2, 1024 + cb * 64, "ucT"), "gh": (gh, b_gh, cb * 64, "ghT")}
            for w in which:
                dst, bdst, r0, src_t = specs[w]
                for q in range(4):
                    srcv = T[src_t][r0 + q * 16:r0 + (q + 1) * 16, 0:LS].rearrange("c (n1 n2) -> n1 c n2", n2=64)
                    P.dma("sp", dst[:, q * 16:(q + 1) * 16, :], srcv, bdst, writes=[bdst], partial=(q > 0))

        cs_load(0, ("zc", "x1"))
        for cb in range(16):
            cs_load(cb, ("x2", "gh"))
            for o in range(2):
                P.dma("sp", bias[:, o, :], T["hy_bias"][o:o + 1, cb * 64:(cb + 1) * 64].partition_broadcast(64),
                      b_bias, writes=[b_bias], partial=(o > 0))
            P.op("act", lambda e: e.activation(gh[:], gh[:], AF.Silu), reads=[b_gh], writes=[b_gh])
            for o in range(2):
                zin, b_zin = (zc, b_zc) if o == 0 else (z2, b_z2)
                xo, b_xo = (x1, b_x1) if o == 0 else (x2, b_x2)
                P.dma("sp", Hr[:].rearrange("p c k -> p (c k)"), T["Hs"][o, cb, 0], b_Hr, writes=[b_Hr])
                P.dma("sp", Hi[:].rearrange("p c k -> p (c k)"), T["Hs"][o, cb, 1], b_Hi, writes=[b_Hi])
                if o == 1 and cb + 1 < 16:
                    cs_load(cb + 1, ("zc", "x1"))
                P.op("pool", lambda e, zin=zin, o=o: e.tensor_tensor(
                    bz[:], zin[:], bcast_last(bias[:, o, :], 64), ALU.mult),
                    reads=[b_zin, b_bias], writes=[b_bz])
                _f1_stage(P, nc, T, zin, 64, b_zin, F1, b_F1, Cb, b_Cb, b_ps, ev)
                for g in range(16):
                    pj = 2 + (g % 2)
                    xs = g % 2
                    for kk in range(8):
                        k1 = g * 8 + kk
                        P.op("pe", lambda e, pj=pj, kk=kk, k1=k1: e.matmul(
                            ps[pj][0:64, kk * 64:(kk + 1) * 64], G[:, k1, :], Cb[:, k1, :], start=True, stop=True),
                            reads=[b_G, b_Cb], writes=[b_ps[pj]], partial=(kk > 0))
                    P.op("act", lambda e, pj=pj, xs=xs: e.copy(Xs[xs][:], ps[pj][0:64, :]),
                         reads=[b_ps[pj]], writes=[b_Xs[xs]])
                    xv = Xs[xs][:].rearrange("p (k c) -> p c k", k=8)
                    P.op("dve", lambda e, xv=xv, g=g: e.tensor_tensor(
                        P1[:, :, g * 8:(g + 1) * 8], xv, Hr[:, :, g * 8:(g + 1) * 8], ALU.mult),
                        reads=[b_Xs[xs], b_Hr], writes=[b_P1], partial="nowaw")
                    P.op("dve", lambda e, xv=xv, g=g: e.tensor_tensor(
                        P2[:, :, g * 8:(g + 1) * 8], xv, Hi[:, :, g * 8:(g + 1) * 8], ALU.mult),
                        reads=[b_Xs[xs], b_Hi], writes=[b_P2], partial="nowaw")
                for g in range(16):
                    pj = 4 + (g % 2)
                    for cc in range(4):
                        c = g * 4 + cc
                        P.op("pe", lambda e, pj=pj, cc=cc, c=c: e.matmul(
                            ps[pj][:, cc * 128:(cc + 1) * 128], P1[:, c, :], M1[:], start=True, stop=False),
                            reads=[b_P1, b_M1], writes=[b_ps[pj]], partial=(cc > 0))
                        P.op("pe", lambda e, pj=pj, cc=cc, c=c: e.matmul(
                            ps[pj][:, cc * 128:(cc + 1) * 128], P2[:, c, :], M1p[:], start=False, stop=True),
                            reads=[b_P2, b_M1p], writes=[b_ps[pj]], partial=True)
                    src = ps[pj][:].rearrange("p (c q) -> p q c", c=4)
                    ev(Db[:, :, g * 4:(g + 1) * 4], src, [b_ps[pj]], [b_Db], g)
                for g in range(8):
                    pj = 6 + (g % 2)
                    for nn in range(8):
                        n2 = g * 8 + nn
                        P.op("pe", lambda e, pj=pj, nn=nn, n2=n2: e.matmul(
                            ps[pj][0:64, nn * 64:(nn + 1) * 64], T2r[:, n2, :], Db[:, n2, :], start=True, stop=False),
                            reads=[b_T2r, b_Db], writes=[b_ps[pj]], partial=(nn > 0))
                        P.op("pe", lambda e, pj=pj, nn=nn, n2=n2: e.matmul(
                            ps[pj][0:64, nn * 64:(nn + 1) * 64], T2i[:, n2, :], Db[:, 64 + n2, :], start=False,
                            stop=True),
                            reads=[b_T2i, b_Db], writes=[b_ps[pj]], partial=True)
                    P.op("dve", lambda e, pj=pj, g=g: e.tensor_tensor(
                        cv[:, :, g * 8:(g + 1) * 8], ps[pj][0:64, :].rearrange("p (n c) -> p c n", n=8),
                        bz[:, :, g * 8:(g + 1) * 8], ALU.add),
                        reads=[b_ps[pj], b_bz], writes=[b_cv], partial=(g > 0))
                if o == 0:
                    P.op("dve", lambda e: e.tensor_tensor(z2[:], cv[:], x1[:], ALU.mult),
                         reads=[b_cv, b_x1], writes=[b_z2])
                else:
                    P.op("dve", lambda e: e.tensor_tensor(cv[:], cv[:], x2[:], ALU.mult),
                         reads=[b_cv, b_x2], writes=[b_cv])
                    P.op("pool", lambda e: e.tensor_tensor(z2[:], cv[:], gh[:], ALU.mult),
                         reads=[b_cv, b_gh], writes=[b_z2])
                    for q in range(4):
                        r0 = 1024 + cb * 64 + q * 16
                        dstv = T["mixT"][r0:r0 + 16, 0:LS].rearrange("c (n1 n2) -> n1 c n2", n2=64)
                        P.dma("sp", dstv, z2[:, q * 16:(q + 1) * 16, :], b_z2, reads=[b_z2])
        P.barrier()
        P.flush()


def phase_CP(P, nc, T):
    import contextlib
    ps = T["ps"]
    ident = T["ident"]
    with contextlib.ExitStack() as es:
        A_ = lambda n, shp, dt: sb(es, nc, n, shp, dt)
        FP = A_("cp_FP", [128, 4, 512], BF16)
        IP = A_("cp_IP", [128, 4, 256], BF16)
        HA = A_("cp_HA", [128, 16, 512], F32)
        HB = A_("cp_HB", [128, 16, 512], F32)
        biasT = A_("cp_biasT", [128, 16], F32)
        tp = [A_("cp_tp0", [128, 512], BF16), A_("cp_tp1", [128, 512], BF16)]
        tt = A_("cp_tt", [128, 4, 128], BF16)
        zc = [A_("cp_zc0", [128, 256], BF16), A_("cp_zc1", [128, 256], BF16)]
        x1 = [A_("cp_x10", [128, 256], BF16), A_("cp_x11", [128, 256], BF16)]
        x2 = [A_("cp_x20", [128, 256], BF16), A_("cp_x21", [128, 256], BF16)]
        gh = [A_("cp_gh0", [128, 256], BF16), A_("cp_gh1", [128, 256], BF16)]
        sgh = A_("cp_sgh", [128, 256], F32)
        z2 = A_("cp_z2", [128, 256], BF16)
        zt = A_("cp_zt", [128, 2, 128], BF16)
        Aa = A_("cp_A", [128, 512], F32)
        Bb = A_("cp_B", [128, 512], F32)
        Y = A_("cp_Y", [128, 512], BF16)
        Yt = A_("cp_Yt", [128, 4, 128], BF16)
        t1 = A_("cp_t1", [128, 256], F32)
        t2 = A_("cp_t2", [128, 256], F32)
        mx = [A_("cp_mx0", [128, 256], BF16), A_("cp_mx1", [128, 256], BF16)]
        b_FP, b_IP, b_H, b_bias = P.buf(), P.buf(), P.buf(), P.buf()
        P.dma("sp", FP[:], T["d_FP"], b_FP, writes=[b_FP])
        P.dma("sp", IP[:], T["d_IP"], b_IP, writes=[b_IP])
        P.dma("sp", biasT[:], T["hy_biasT"], b_bias, writes=[b_bias])
        b_tp, b_tt = P.bufs_n(2), P.buf()
        b_ps = P.bufs_n(8)
        it = 0
        for o in range(2):
            for t in range(8):
                s = it % 2
                it += 1
                P.dma("sp", tp[s][:], T["tapsP"][o, t * 128:(t + 1) * 128, :], b_tp[s], writes=[b_tp[s]])
                pb = ps[s][:].bitcast(BF16)
                for j in range(4):
                    P.op("pe", lambda e, pb=pb, j=j, s=s: e.transpose(
                        pb[:, j * 128:(j + 1) * 128], tp[s][:, j * 128:(j + 1) * 128], ident[:]),
                        reads=[b_tp[s]], writes=[b_ps[s]], partial=(j > 0))
                P.op("dve", lambda e, pb=pb: e.tensor_copy(tt[:].rearrange("p a b -> p (a b)"), pb[:, 0:512]),
                     reads=[b_ps[s]], writes=[b_tt])
                pj = 2 + s
                for j in range(4):
                    P.op("pe", lambda e, pj=pj, j=j: e.matmul(
                        ps[pj][:, :], tt[:, j, :], FP[:, j, :], start=(j == 0), stop=(j == 3)),
                        reads=[b_tt, b_FP], writes=[b_ps[pj]], partial=(j > 0))
                i = o * 8 + t
                for h in range(2):
                    P.op("act", lambda e, pj=pj, i=i, h=h: e.copy(HA[:, i, h * 256:(h + 1) * 256], ps[pj][:, 0:256]),
                         reads=[b_ps[pj]], writes=[b_H], partial=True)
                    P.op("act", lambda e, pj=pj, i=i, h=h: e.copy(HB[:, i, h * 256:(h + 1) * 256],
                                                                  ps[pj][:, 256:512]),
                         reads=[b_ps[pj]], writes=[b_H], partial=True)
        b_zc, b_x1, b_x2, b_gh = P.bufs_n(2), P.bufs_n(2), P.bufs_n(2), P.bufs_n(2)
        b_sgh, b_z2, b_zt, b_A, b_B, b_Y, b_Yt, b_t1, b_t2 = [P.buf() for _ in range(9)]
        b_mx = P.bufs_n(2)
        it = 0
        for sq in range(2):
            tok0 = LS + sq * LP
            for t in range(8):
                s = it % 2
                it += 1
                r = t * 128
                P.dma("sp", zc[s][:], T["ucT"][2048 + r:2048 + r + 128, tok0:tok0 + LP], b_zc[s], writes=[b_zc[s]])
                P.dma("sp", x1[s][:], T["ucT"][r:r + 128, tok0:tok0 + LP], b_x1[s], writes=[b_x1[s]])
                P.dma("sp", x2[s][:], T["ucT"][1024 + r:1024 + r + 128, tok0:tok0 + LP], b_x2[s], writes=[b_x2[s]])
                P.dma("sp", gh[s][:], T["ghT"][r:r + 128, tok0:tok0 + LP], b_gh[s], writes=[b_gh[s]])
                P.op("act", lambda e, s=s: e.activation(sgh[:], gh[s][:], AF.Silu), reads=[b_gh[s]], writes=[b_sgh])
                for o in range(2):
                    zin, b_zin = (zc[s], b_zc[s]) if o == 0 else (z2, b_z2)
                    xo, b_xo = (x1[s], b_x1[s]) if o == 0 else (x2[s], b_x2[s])
                    i = o * 8 + t
                    pb = ps[0][:].bitcast(BF16)
                    for j in range(2):
                        P.op("pe", lambda e, pb=pb, j=j, zin=zin: e.transpose(
                            pb[:, j * 128:(j + 1) * 128], zin[:, j * 128:(j + 1) * 128], ident[:]),
                            reads=[b_zin], writes=[b_ps[0]], partial=(j > 0))
                    P.op("dve", lambda e, pb=pb: e.tensor_copy(zt[:].rearrange("p a b -> p (a b)"), pb[:, 0:256]),
                         reads=[b_ps[0]], writes=[b_zt])
                    for j in range(2):
                        P.op("pe", lambda e, j=j: e.matmul(ps[1][:, :], zt[:, j, :], FP[:, j, :], start=(j == 0),
                                                           stop=(j == 1)),
                             reads=[b_zt, b_FP], writes=[b_ps[1]], partial=(j > 0))
                    P.op("dve", lambda e, i=i: e.tensor_tensor(Aa[:], ps[1][:, :], HA[:, i, :], ALU.mult),
                         reads=[b_ps[1], b_H], writes=[b_A])
                    P.op("dve", lambda e, i=i: e.tensor_tensor(Bb[:], ps[1][:, :], HB[:, i, :], ALU.mult),
                         reads=[b_ps[1], b_H], writes=[b_B])
                    P.op("pool", lambda e: e.tensor_tensor(Y[:, 0:256], Aa[:, 0:256], Bb[:, 256:512], ALU.subtract),
                         reads=[b_A, b_B], writes=[b_Y])
                    P.op("pool", lambda e: e.tensor_tensor(Y[:, 256:512], Bb[:, 0:256], Aa[:, 256:512], ALU.add),
                         reads=[b_A, b_B], writes=[b_Y], partial=True)
                    pb2 = ps[2][:].bitcast(BF16)
                    for j in range(4):
                        P.op("pe", lambda e, pb2=pb2, j=j: e.transpose(
                            pb2[:, j * 128:(j + 1) * 128], Y[:, j * 128:(j + 1) * 128], ident[:]),
                            reads=[b_Y], writes=[b_ps[2]], partial=(j > 0))
                    P.op("act", lambda e, pb2=pb2: e.copy(Yt[:].rearrange("p a b -> p (a b)"), pb2[:, 0:512]),
                         reads=[b_ps[2]], writes=[b_Yt])
                    for j in range(4):
                        P.op("pe", lambda e, j=j: e.matmul(ps[3][:, 0:256], Yt[:, j, :], IP[:, j, :], start=(j == 0),
                                                           stop=(j == 3)),
                             reads=[b_Yt, b_IP], writes=[b_ps[3]], partial=(j > 0))
                    P.op("dve", lambda e, zin=zin, i=i: e.scalar_tensor_tensor(
                        t1[:], zin[:], biasT[:, i:i + 1], ps[3][:, 0:256], ALU.mult, ALU.add),
                        reads=[b_zin, b_bias, b_ps[3]], writes=[b_t1])
                    if o == 0:
                        P.op("pool", lambda e, xo=xo: e.tensor_tensor(z2[:], t1[:], xo[:], ALU.mult),
                             reads=[b_t1, b_xo], writes=[b_z2])
                    else:
                        P.op("pool", lambda e, xo=xo: e.tensor_tensor(t2[:], t1[:], xo[:], ALU.mult),
                             reads=[b_t1, b_xo], writes=[b_t2])
                        P.op("pool", lambda e, s=s: e.tensor_tensor(mx[s][:], t2[:], sgh[:], ALU.mult),
                             reads=[b_t2, b_sgh], writes=[b_mx[s]])
                        P.dma("sp", T["mixT"][1024 + r:1024 + r + 128, tok0:tok0 + LP], mx[s][:], b_mx[s],
                              reads=[b_mx[s]])
        P.barrier()
        P.flush()


def phase_outproj(P, nc, T, l, Wdram, xsrc, mixname, dst, final):
    import contextlib
    ps = T["ps"]
    with contextlib.ExitStack() as es:
        A_ = lambda n, shp, dt: sb(es, nc, n, shp, dt)
        Wo = A_("o_W", [128, 16, 2048], BF16)
        mix = [A_("o_mix0", [128, 16, 512], BF16), A_("o_mix1", [128, 16, 512], BF16)]
        xt = [A_("o_x0", [128, 2048], F32), A_("o_x1", [128, 2048], F32)]
        xo = [A_("o_xo0", [128, 2048], F32), A_("o_xo1", [128, 2048], F32)]
        gs = A_("o_gs", [128, 2048], F32)
        gp = A_("o_gp", [128, 2048], F32)
        tmp = [A_("o_tmp0", [128, 512], F32), A_("o_tmp1", [128, 512], F32)]
        b_W, b_mix, b_xt, b_xo, b_g, b_tmp = P.buf(), P.bufs_n(2), P.bufs_n(2), P.bufs_n(2), P.buf(), P.bufs_n(2)
        b_ps = P.bufs_n(8)
        if final:
            fg = A_("o_fg", [128, 2048], F32)
            junk = A_("o_junk", [128, 2048], F32)
            st = [A_("o_st0", [128, 4], F32), A_("o_st1", [128, 4], F32)]
            b_fg, b_junk, b_st = P.buf(), P.buf(), P.bufs_n(2)
            P.dma("sp", fg[:], T["final_norm_g"][0:1, :].partition_broadcast(128), b_fg, writes=[b_fg])
        Wv = Wdram.rearrange("(k p) c -> p k c", p=128)
        for k in range(16):
            P.dma("pool", Wo[:, k, :], Wv[:, k, :], b_W, writes=[b_W], partial=(k > 0))
        P.dma("sp", gs[:], T["modv"][l][0:1, 4096:6144].partition_broadcast(128), b_g, writes=[b_g])
        P.dma("sp", gp[:], T["modv"][l][1:2, 4096:6144].partition_broadcast(128), b_g, writes=[b_g], partial=True)
        mv = T[mixname].rearrange("(k p) t -> p k t", p=128)
        ti = 0
        ei = 0
        def mix_load(ch):
            s = ch % 2
            for q in range(4):
                P.dma("sp", mix[s][:, q * 4:(q + 1) * 4, :], mv[:, q * 4:(q + 1) * 4, ch * 512:(ch + 1) * 512],
                      b_mix[s], writes=[b_mix[s]], partial=(q > 0))

        mix_load(0)
        for ch in range(NTOK // 512):
            s = ch % 2
            for tt in range(4):
                if tt == 1 and ch + 1 < NTOK // 512:
                    mix_load(ch + 1)
                tok = ch * 512 + tt * 128
                xs = ti % 2
                ti += 1
                gg = gs if tok < LS else gp
                if ti == 1:
                    P.dma("sp", xt[xs][:], xsrc[tok:tok + 128, :], b_xt[xs], writes=[b_xt[xs]])
                for cbk in range(4):
                    pj = ei % 8
                    tj = ei % 2
                    ei += 1
                    for k in range(16):
                        P.op("pe", lambda e, pj=pj, k=k, s=s, tt=tt, cbk=cbk: e.matmul(
                            ps[pj][:, :], mix[s][:, k, tt * 128:(tt + 1) * 128], Wo[:, k, cbk * 512:(cbk + 1) * 512],
                            start=(k == 0), stop=(k == 15)),
                            reads=[b_mix[s], b_W], writes=[b_ps[pj]], partial=(k > 0))
                    P.op("dve", lambda e, pj=pj, tj=tj, cbk=cbk, gg=gg: e.tensor_tensor(
                        tmp[tj][:], ps[pj][:, :], gg[:, cbk * 512:(cbk + 1) * 512], ALU.mult),
                        reads=[b_ps[pj], b_g], writes=[b_tmp[tj]])
                    P.op("pool", lambda e, tj=tj, xs=xs, cbk=cbk: e.tensor_tensor(
                        xo[xs][:, cbk * 512:(cbk + 1) * 512], tmp[tj][:], xt[xs][:, cbk * 512:(cbk + 1) * 512],
                        ALU.add),
                        reads=[b_tmp[tj], b_xt[xs]], writes=[b_xo[xs]], partial=(cbk > 0))
                if final:
                    P.op("act", lambda e, xs=xs: e.activation(junk[:], xo[xs][:], AF.Square,
                                                               accum_out=st[xs][:, 0:1]),
                         reads=[b_xo[xs]], writes=[b_junk, b_st[xs]])
                    P.op("act", lambda e, xs=xs: e.activation(st[xs][:, 1:2], st[xs][:, 0:1], AF.Sqrt, scale=1.0 / D,
                                                               bias=T["epsb"][:, 0:1]),
                         reads=[b_st[xs]], writes=[b_st[xs]], partial=True)
                    P.op("dve", lambda e, xs=xs: e.reciprocal(st[xs][:, 2:3], st[xs][:, 1:2]),
                         reads=[b_st[xs]], writes=[b_st[xs]], partial=True)
                    P.op("dve", lambda e, xs=xs: e.scalar_tensor_tensor(
                        xo[xs][:], xo[xs][:], st[xs][:, 2:3], fg[:], ALU.mult, ALU.mult),
                        reads=[b_xo[xs], b_st[xs], b_fg], writes=[b_xo[xs]])
                if tok + 128 < NTOK:
                    P.dma("sp", xt[ti % 2][:], xsrc[tok + 128:tok + 256, :], b_xt[ti % 2], writes=[b_xt[ti % 2]])
                P.dma("sp", dst[tok:tok + 128, :], xo[xs][:], b_xo[xs], reads=[b_xo[xs]])
        P.barrier()
        P.flush()


def phase_E(P, nc, T):
    import contextlib
    ps = T["ps"]
    Wv = T["c_w_in"].rearrange("(k p) c -> p k c", p=128)
    for half in range(2):
        tok_tiles = [(half * 2048 + i * 128, False) for i in range(16)] + \
                    [(LS + half * 256 + i * 128, True) for i in range(2)]
        with contextlib.ExitStack() as es0:
            hT = sb(es0, nc, "e_hT", [128, 16, 2304], BF16)
            b_hT = P.buf("hT")
            with contextlib.ExitStack() as es1:
                A_ = lambda n, shp, dt: sb(es1, nc, n, shp, dt)
                xt0 = A_("e_xt0", [128, 2048], F32); xt1 = A_("e_xt1", [128, 2048], F32)
                xt2 = A_("e_xt2", [128, 2048], F32); xt3 = A_("e_xt3", [128, 2048], F32)
                junk = A_("e_junk", [128, 2048], F32); tmp = A_("e_tmp", [128, 2048], F32)
                junkb = A_("e_junkb", [128, 2048], F32); tmpb = A_("e_tmpb", [128, 2048], F32)
                st0 = A_("e_st0", [128, 4], F32); st1 = A_("e_st1", [128, 4], F32)
                hb0 = A_("e_hb0", [128, 2048], BF16); hb1 = A_("e_hb1", [128, 2048], BF16)
                As = A_("e_As", [128, 2048], F32); shs = A_("e_shs", [128, 2048], F32)
                Ap = A_("e_Ap", [128, 2048], F32); shp = A_("e_shp", [128, 2048], F32)
                bc = {"A_s": As, "sh_s": shs, "A_p": Ap, "sh_p": shp}
                b_bc = {k: P.buf(k) for k in bc}
                load_bcast_rows(P, nc, T, 1, bc, b_bc)
                work = dict(xt=[xt0, xt1, xt2, xt3], b_xt=P.bufs_n(4), junk=junk, b_junk=P.buf(), st=[st0, st1],
                            b_st=P.bufs_n(2), tmp=tmp, b_tmp=P.buf(), hb=[hb0, hb1], b_hb=P.bufs_n(2),
                            b_pst=P.bufs_n(4), junk2=[junk, junkb], b_junk2=P.bufs_n(2), tmp2=[tmp, tmpb],
                            b_tmp2=P.bufs_n(2))
                norm_transpose_half(P, nc, T, T["x1"], tok_tiles, hT, b_hT, bc, b_bc, work)
                P.barrier()
                P.flush()
            with contextlib.ExitStack() as es2:
                A_ = lambda n, shp, dt: sb(es2, nc, n, shp, dt)
                wg = [A_("e_w0", [128, 16, 512], BF16), A_("e_w1", [128, 16, 512], BF16)]
                sf = [A_("e_sf0", [128, 512], F32), A_("e_sf1", [128, 512], F32)]
                sg = [A_("e_sg0", [128, 512], BF16), A_("e_sg1", [128, 512], BF16)]
                b_hT = P.buf("hT2")
                b_wg, b_sf, b_sg = P.bufs_n(2), P.bufs_n(2), P.bufs_n(2)
                b_ps = P.bufs_n(8)
                chunks = [(c * 512, 512, half * 2048 + c * 512) for c in range(4)] + [(2048, 256, LS + half * 256)]
                psi = 0
                ei = 0
                for g in range(8):
                    s = g % 2
                    for kq in range(16):
                        P.dma("pool", wg[s][:, kq, :], Wv[:, kq, g * 512:(g + 1) * 512], b_wg[s], writes=[b_wg[s]],
                              partial=(kq > 0))
                    for (l0, n, g0) in chunks:
                        for j in range(4):
                            pj = psi % 4
                            psi += 1
                            for k in range(16):
                                P.op("pe", lambda e, pj=pj, k=k, s=s, j=j, l0=l0, n=n: e.matmul(
                                    ps[pj][:, 0:n], wg[s][:, k, j * 128:(j + 1) * 128], hT[:, k, l0:l0 + n],
                                    start=(k == 0), stop=(k == 15)),
                                    reads=[b_hT, b_wg[s]], writes=[b_ps[pj]], partial=(k > 0))
                            row = (g * 4 + j) * 128
                            si = ei % 2
                            ei += 1
                            if row < 2048:
                                P.op("act", lambda e, pj=pj, si=si, n=n: e.copy(sf[si][:, 0:n], ps[pj][:, 0:n]),
                                     reads=[b_ps[pj]], writes=[b_sf[si]])
                                P.dma("sp", T["xbT"][row:row + 128, g0:g0 + n], sf[si][:, 0:n], b_sf[si],
                                      reads=[b_sf[si]])
                            else:
                                P.op("dve", lambda e, pj=pj, si=si, n=n: e.tensor_copy(sg[si][:, 0:n], ps[pj][:, 0:n]),
                                     reads=[b_ps[pj]], writes=[b_sg[si]])
                                P.dma("sp", T["gateT"][row - 2048:row - 2048 + 128, g0:g0 + n], sg[si][:, 0:n],
                                      b_sg[si], reads=[b_sg[si]])
                P.barrier()
                P.flush()


def phase_F(P, nc, T):
    import contextlib
    ps = T["ps"]
    seqs = [(0, LS, 0, 1), (LS, 2 * LP, 1, 2)]
    with contextlib.ExitStack() as es:
        A_ = lambda n, shp, dt: sb(es, nc, n, shp, dt)
        Rs = [A_("f_R0", [128, LS], F32), A_("f_R1", [128, LS], F32)]
        Is = [A_("f_I0", [128, LS], F32), A_("f_I1", [128, LS], F32)]
        Ss = [A_("f_S0", [128, LS], F32), A_("f_S1", [128, LS], F32)]
        H_ = [A_("f_H0", [128, LS + 3], F32), A_("f_H1", [128, LS + 3], F32)]
        xc = A_("f_xc", [128, 2, LS], F32)
        xp = H_[1]
        acc = H_[0]
        xcb = A_("f_xcb", [128, 2, LS], BF16)
        gate = A_("f_gate", [128, LS], BF16)
        stage = A_("f_stage", [128, LS], BF16)
        wq = A_("f_wq", [128, 2, 2, 2, 256], BF16)
        lcw = A_("f_lcw", [128, 5, 16], F32)
        lba = A_("f_lba", [128, 2, 16], F32)
        lbx = A_("f_lbx", [128, 2, 16], F32)
        llam = A_("f_llam", [128, 2, 16], F32)
        c8 = A_("f_c8", [128, 2, 16], F32)
        c16 = A_("f_c16", [128, 2, 16], F32)
        stT = A_("f_stT", [128, 2, 16], F32)
        nst = A_("f_nst", [128, 2, 2, 16], F32)
        b_k = P.bufs_n(6)
        P.dma("sp", lcw[:], T["lcw"], b_k[0], writes=[b_k[0]])
        P.dma("sp", lba[:], T["lba"], b_k[1], writes=[b_k[1]])
        P.dma("sp", lbx[:], T["lbx"], b_k[2], writes=[b_k[2]])
        P.dma("sp", llam[:], T["llam"], b_k[3], writes=[b_k[3]])
        P.dma("sp", stT[:], T["stT"], b_k[4], writes=[b_k[4]])
        P.op("act", lambda e: e.activation(c8[:], llam[:], AF.Exp, scale=-1.0), reads=[b_k[3]], writes=[b_k[5]])
        P.op("act", lambda e: e.activation(c8[:], c8[:], AF.Ln, bias=1.0), reads=[b_k[5]], writes=[b_k[5]])
        P.op("dve", lambda e: e.tensor_scalar(c16[:], c8[:], -16.0, None, ALU.mult), reads=[b_k[5]], writes=[b_k[5]],
             partial=True)
        P.op("dve", lambda e: e.tensor_scalar(c8[:], c8[:], -8.0, None, ALU.mult), reads=[b_k[5]], writes=[b_k[5]],
             partial=True)
        b_lcw, b_lba, b_lbx, _, b_stT, b_c8 = b_k
        b_xc, b_xcb, b_gate, b_stage, b_wq, b_nst = [P.buf() for _ in range(6)]
        b_H = P.bufs_n(2)
        b_Rs, b_Is, b_Ss = P.bufs_n(2), P.bufs_n(2), P.bufs_n(2)
        b_xp, b_acc = b_H[1], b_H[0]
        par = 0
        b_ps = P.bufs_n(8)
        psi = 0
        for h in range(8):
            for m in range(2):
                wsrc = T["c_wa"] if m == 0 else T["c_wx"]
                for d in range(2):
                    P.dma("pool", wq[:, m, d], wsrc[d, h].rearrange("(it p) j -> p it j", p=128), b_wq,
                          writes=[b_wq], partial=(m + d > 0))
            for (tok0, L, sidx, nseg) in seqs:
                nch = max(1, L // 512)
                cw = min(512, L)
                Ls = L // nseg
                Wp = Ls + 3

                def xpv(j, nseg=nseg, Ls=Ls, Wp=Wp):
                    if nseg == 1:
                        return xp[:, j:j + Ls]
                    return xp[:, 0:nseg * Wp].rearrange("p (g w) -> p g w", g=nseg)[:, :, j:j + Ls]

                def segv(ap2d, nseg=nseg):
                    if nseg == 1:
                        return ap2d
                    return ap2d.rearrange("p (g w) -> p g w", g=nseg)
                for ct in range(2):
                    cti = h * 2 + ct
                    row = cti * 128
                    P.op("pool", lambda e: e.memset(xp[:, 0:2], 0.0), writes=[b_xp])
                    for sg in range(nseg):
                        P.op("pool", lambda e, sg=sg, Wp=Wp, Ls=Ls: e.memset(
                            xp[:, sg * Wp + Ls + 2:min((sg + 1) * Wp + 2, xp.shape[1])], 0.0),
                            writes=[b_xp], partial=True)
                        P.dma("sp", xp[:, sg * Wp + 2:sg * Wp + 2 + Ls],
                              T["xbT"][row:row + 128, tok0 + sg * Ls:tok0 + (sg + 1) * Ls], b_xp, writes=[b_xp],
                              partial=True)
                    accv = segv(acc[:, 0:L])
                    P.op("act", lambda e, cti=cti, accv=accv, src=xpv(0): e.activation(
                        accv, src, AF.Identity, scale=lcw[:, 0, cti:cti + 1],
                        bias=lcw[:, 4, cti:cti + 1]), reads=[b_xp, b_lcw], writes=[b_acc])
                    for j in (1, 2):
                        P.op("dve", lambda e, cti=cti, j=j, accv=accv, src=xpv(j): e.scalar_tensor_tensor(
                            accv, src, lcw[:, j, cti:cti + 1], accv, ALU.mult, ALU.add),
                            reads=[b_xp, b_acc, b_lcw], writes=[b_acc])
                    P.op("dve", lambda e, cti=cti, ct=ct, accv=accv, src=xpv(3), dstv=segv(xc[:, ct, 0:L]):
                         e.scalar_tensor_tensor(dstv, src, lcw[:, 3, cti:cti + 1], accv, ALU.mult, ALU.add),
                         reads=[b_xp, b_acc, b_lcw], writes=[b_xc], partial=(ct > 0))
                    P.op("act", lambda e, L=L, ct=ct: e.copy(xcb[:, ct, 0:L], xc[:, ct, 0:L]),
                         reads=[b_xc], writes=[b_xcb], partial=(ct > 0))
                for jt in range(2):
                    cti = h * 2 + jt
                    row = cti * 128
                    P.dma("sp", gate[:, 0:L], T["gateT"][row:row + 128, tok0:tok0 + L], b_gate, writes=[b_gate])
                    for d in range(2):
                        par ^= 1
                        R_, I_, S_ = Rs[par], Is[par], Ss[par]
                        b_R, b_I, b_S = b_Rs[par], b_Is[par], b_Ss[par]
                        for (m, dstT, b_dst, bias_t) in ((0, R_, b_R, lba), (1, I_, b_I, lbx)):
                            for c in range(nch):
                                pj = psi % 4
                                psi += 1
                                for it_ in range(2):
                                    P.op("pe", lambda e, pj=pj, m=m, d=d, it_=it_, jt=jt, c=c, cw=cw: e.matmul(
                                        ps[pj][:, 0:cw], wq[:, m, d, it_, jt * 128:(jt + 1) * 128],
                                        xcb[:, it_, c * cw:(c + 1) * cw], start=(it_ == 0), stop=(it_ == 1)),
                                        reads=[b_wq, b_xcb], writes=[b_ps[pj]], partial=(it_ > 0))
                                P.op("act", lambda e, pj=pj, dstT=dstT, c=c, bias_t=bias_t, d=d, cti=cti, cw=cw: e.activation(
                                    dstT[:, c * cw:(c + 1) * cw], ps[pj][:, 0:cw], AF.Sigmoid,
                                    bias=bias_t[:, d, cti:cti + 1]),
                                    reads=[b_ps[pj], b_lba, b_lbx], writes=[b_dst], partial=(c > 0))
                        P.op("pool", lambda e, L=L, jt=jt, I_=I_: e.tensor_tensor(I_[:, 0:L], I_[:, 0:L], xc[:, jt, 0:L],
                                                                                 ALU.mult),
                             reads=[b_I, b_xc], writes=[b_I])
                        P.op("act", lambda e, L=L, d=d, cti=cti, R_=R_, S_=S_: e.activation(
                            S_[:, 0:L], R_[:, 0:L], AF.Exp, scale=c16[:, d, cti:cti + 1]),
                            reads=[b_R, b_c8], writes=[b_S])
                        P.op("act", lambda e, L=L, d=d, cti=cti, R_=R_: e.activation(
                            R_[:, 0:L], R_[:, 0:L], AF.Exp, scale=c8[:, d, cti:cti + 1]),
                            reads=[b_R, b_c8], writes=[b_R])
                        P.op("act", lambda e, L=L, S_=S_: e.activation(S_[:, 0:L], S_[:, 0:L], AF.Sqrt, scale=-1.0, bias=1.0),
                             reads=[b_S], writes=[b_S])
                        P.op("dve", lambda e, L=L, I_=I_, S_=S_: e.tensor_tensor(I_[:, 0:L], I_[:, 0:L], S_[:, 0:L], ALU.mult),
                             reads=[b_I, b_S], writes=[b_I])
                        init = stT[:, d, cti:cti + 1] if sidx == 0 else 0.0
                        for sg in range(nseg):
                            a0, a1 = sg * Ls, (sg + 1) * Ls
                            if d == 0:
                                P.op("dve", lambda e, a0=a0, a1=a1, init=init, R_=R_, I_=I_: e.tensor_tensor_scan(
                                    H_[0][:, a0:a1], R_[:, a0:a1], I_[:, a0:a1], init, ALU.mult, ALU.add),
                                    reads=[b_R, b_I, b_stT], writes=[b_H[0]], partial=(sg > 0))
                            else:
                                P.op("dve", lambda e, a0=a0, a1=a1, init=init, R_=R_, I_=I_: e.tensor_tensor_scan(
                                    rev_ap(H_[1][:, a0:a1]), rev_ap(R_[:, a0:a1]), rev_ap(I_[:, a0:a1]), init,
                                    ALU.mult, ALU.add),
                                    reads=[b_R, b_I, b_stT], writes=[b_H[1]], partial=(sg > 0))
                            if sidx > 0:
                                col = a1 - 1 if d == 0 else a0
                                P.op("act", lambda e, d=d, col=col, sg=sg, cti=cti: e.copy(
                                    nst[:, sg, d, cti:cti + 1], H_[d][:, col:col + 1]),
                                    reads=[b_H[d]], writes=[b_nst], partial=True)
                    P.op("pool", lambda e, L=L: e.tensor_tensor(H_[0][:, 0:L], H_[0][:, 0:L], H_[1][:, 0:L], ALU.add),
                         reads=[b_H[0], b_H[1]], writes=[b_H[0]])
                    P.op("act", lambda e, L=L, S_=S_: e.activation(S_[:, 0:L], gate[:, 0:L], AF.Silu),
                         reads=[b_gate], writes=[b_S])
                    P.op("dve", lambda e, L=L, S_=S_: e.tensor_tensor(stage[:, 0:L], H_[0][:, 0:L], S_[:, 0:L], ALU.mult),
                         reads=[b_H[0], b_S], writes=[b_stage])
                    P.dma("sp", T["mix1T"][row:row + 128, tok0:tok0 + L], stage[:, 0:L], b_stage, reads=[b_stage])
        P.dma("sp", T["ns"], nst[:].rearrange("p a b c -> p (a b c)"), b_nst, reads=[b_nst])
        P.barrier()
        P.flush()


def _bf16(a):
    return np.asarray(a, dtype=np.float32).astype(ml_dtypes.bfloat16)


def _fft_consts():
    N = 2 * LS
    n1 = np.arange(128)[:, None]
    k1 = np.arange(128)[None, :]
    ang = 2 * np.pi * n1 * (k1 + 0.5) / 128
    F1 = np.concatenate([np.cos(ang), -np.sin(ang)], 1)
    n2 = np.arange(64)[:, None, None]
    k1_ = np.arange(128)[None, :, None]
    k2 = np.arange(32)[None, None, :]
    ang = 2 * np.pi * n2 * (k1_ + 128 * k2 + 0.5) / N
    gr, gi = np.cos(ang), -np.sin(ang)
    G = np.zeros((128, 128, 64))
    G[0:64, :, 0:32] = gr
    G[64:128, :, 0:32] = -gi
    G[0:64, :, 32:64] = gi
    G[64:128, :, 32:64] = gr
    Grr = np.concatenate([G[:, :, 0:32], G[:, :, 0:32]], 2)
    Gii = np.concatenate([G[:, :, 32:64], G[:, :, 32:64]], 2)
    k2 = np.arange(32)[:, None]
    n2 = np.arange(64)[None, :]
    ang = 2 * np.pi * n2 * k2 / 64
    mr, mi = np.cos(ang), np.sin(ang)
    M1 = np.zeros((64, 128))
    M1[0:32, 0:64] = mr
    M1[32:64, 0:64] = -mi
    M1[0:32, 64:128] = mi
    M1[32:64, 64:128] = mr
    M1p = np.concatenate([M1[32:64], -M1[0:32]], 0)
    k1 = np.arange(128)[:, None, None]
    n2 = np.arange(64)[None, :, None]
    n1 = np.arange(64)[None, None, :]
    ang = 2 * np.pi * (k1 + 0.5) * (n1 / 128 + n2 / N)
    T2r = (2.0 / N) * np.cos(ang)
    T2in = -(2.0 / N) * np.sin(ang)
    Np = 2 * LP
    n = np.arange(Np)[:, None]
    k = np.arange(LP)[None, :]
    ang = 2 * np.pi * n * (k + 0.5) / Np
    FPm = np.concatenate([np.cos(ang), -np.sin(ang)], 1)
    FP = FPm.reshape(4, 128, 512).transpose(1, 0, 2)
    k = np.arange(LP)[:, None]
    n = np.arange(LP)[None, :]
    ang = 2 * np.pi * n * (k + 0.5) / Np
    IPm = np.concatenate([(2.0 / Np) * np.cos(ang), -(2.0 / Np) * np.sin(ang)], 0)
    IP = IPm.reshape(4, 128, 256).transpose(1, 0, 2)
    return dict(F1=F1, G=G, Grr=Grr, Gii=Gii, M1=M1, M1p=M1p, T2r=T2r, T2in=T2in, FP=FP, IP=IP)


def make_consts():
    c = {}
    c["ident"] = _bf16(np.eye(128))
    pos = np.arange(LS)
    row = (pos // 64).astype(np.float64)
    col = (pos % 64).astype(np.float64)
    nf = 16
    inv = 10000.0 ** (-np.arange(nf, dtype=np.float64) / nf)
    cos = np.zeros((64, LS))
    sin = np.zeros((64, LS))
    for d in range(64):
        halfi = d // 32
        w = d % 32
        f = w % 16
        p = row if halfi == 0 else col
        ang = p * inv[f]
        cos[d] = np.cos(ang)
        sin[d] = -np.sin(ang) if w < 16 else np.sin(ang)
    c["rope_cos"] = np.tile(cos, (2, 1)).astype(np.float32)
    c["rope_sin"] = np.tile(sin, (2, 1)).astype(np.float32)
    Pm = np.zeros((128, 128))
    for m in range(128):
        hh = m // 64
        d = m % 64
        w = d % 32
        partner = d + 16 if w < 16 else d - 16
        Pm[hh * 64 + partner, m] = 1.0
    c["ropeP"] = _bf16(Pm)
    c["epsb"] = np.full((128, 1), EPS, np.float32)
    for nm, Lf in (("S", LS), ("P", LP)):
        t = (np.arange(Lf, dtype=np.float32) / np.float32(Lf)).astype(np.float64)
        freqs = np.linspace(1e-4, 15.0, 16).astype(np.float32).astype(np.float64)
        ang = 2.0 * np.pi * t[:, None] * freqs[None, :]
        z = np.concatenate([t[:, None], np.cos(ang), -np.sin(ang)], 1)
        c["zemb" + nm] = np.ascontiguousarray(z.T).astype(np.float32)
        c["tpos" + nm] = np.ascontiguousarray(np.tile(t[None, :], (128, 1))).astype(np.float32)
    c.update({k: _bf16(v) for k, v in _fft_consts().items()})
    si = np.arange(128)[:, None]
    qi = np.arange(128)[None, :]
    c["amask"] = _bf16(np.concatenate([(si >= qi), np.ones((128, 128)), (si <= qi)], 1).astype(np.float32))
    return c


CONST_SPECS = {
    "ident": ([128, 128], BF16), "rope_cos": ([128, LS], F32), "rope_sin": ([128, LS], F32),
    "ropeP": ([128, 128], BF16), "epsb": ([128, 1], F32), "amask": ([128, 384], BF16),
    "zembS": ([33, LS], F32), "zembP": ([33, LP], F32), "tposS": ([128, LS], F32), "tposP": ([128, LP], F32),
    "F1": ([128, 256], BF16), "G": ([128, 128, 64], BF16), "Grr": ([128, 128, 64], BF16),
    "Gii": ([128, 128, 64], BF16), "M1": ([64, 128], BF16), "M1p": ([64, 128], BF16),
    "T2r": ([128, 64, 64], BF16), "T2in": ([128, 64, 64], BF16),
    "FP": ([128, 4, 512], BF16), "IP": ([128, 4, 256], BF16),
}

IN_SPECS = {
    "x": [NTOK, D], "ck": [512, 128], "cv": [512, 128], "st": [2, D], "cvecT": [128, 32],
    "mod_w": [2, D, 3 * D], "mod_b": [2, 3 * D], "norm_g": [2, D], "final_norm_g": [1, D],
    "a_w_in": [D, 6400], "a_w_out": [D, D], "a_sink": [1, 16],
    "hsw": [128, 4, 24], "hy_w1": [33, 64], "hy_w2": [64, 64], "hy_bb": [64, 2], "hy_w3": [64, 4096],
    "hy_decT": [128, 32], "hy_biasT": [128, 16], "hy_bias": [2, 1024],
    "c_w_in": [D, 2 * D], "c_w_out": [D, D], "c_wa": [2, 8, 256, 256], "c_wx": [2, 8, 256, 256],
    "lcw": [128, 5, 16], "lba": [128, 2, 16], "lbx": [128, 2, 16], "llam": [128, 2, 16], "stT": [128, 2, 16],
}

SCRATCH = {
    "modv": ([2, 2, 3 * D], F32),
    "qT": ([1024, NTOK], BF16), "kT": ([2, 128, NTOK], BF16), "gaT": ([1024, NTOK], BF16),
    "hyT": ([3072, NTOK], BF16), "ghT": ([1024, NTOK], BF16), "vtok": ([NTOK, 128], BF16),
    "mixT": ([2048, NTOK], BF16),
    "ucT": ([3072, NTOK], BF16), "tapsS": ([2, 1024, 2 * LS], BF16), "tapsP": ([2, 1024, 2 * LP], BF16),
    "Hs": ([2, 16, 2, 64, 64 * 128], BF16),
    "x1": ([NTOK, D], F32), "xbT": ([D, NTOK], F32), "gateT": ([D, NTOK], BF16), "mix1T": ([D, NTOK], BF16),
}

OUT_SPECS = {"y": [NTOK, D], "nk": [512, 128], "nv": [512, 128], "ns": [128, 64]}


def build_program(debug_scratch=(), stop_after=None, skip=(), ext_in=()):
    nc = bass.Bass("TRN2", target_bir_lowering=False)
    T = {}
    for name, shp in IN_SPECS.items():
        T[name] = nc.dram_tensor(name, shp, F32, kind="ExternalInput").ap()
    for name, (shp, dt) in CONST_SPECS.items():
        T["d_" + name] = nc.dram_tensor("c_" + name, shp, dt, kind="ExternalInput").ap()
    for name, shp in OUT_SPECS.items():
        T[name] = nc.dram_tensor(name, shp, F32, kind="ExternalOutput").ap()
    for name, (shp, dt) in SCRATCH.items():
        kind = "ExternalOutput" if name in debug_scratch else ("ExternalInput" if name in ext_in else "Internal")
        T[name] = nc.dram_tensor("s_" + name, shp, dt, kind=kind).ap()
    T["rope_cos"] = T["d_rope_cos"]
    T["rope_sin"] = T["d_rope_sin"]
    import contextlib
    with contextlib.ExitStack() as es:
        sems = [es.enter_context(nc.semaphore("sem%d" % i)) for i in range(60)]
        T["ps"] = [es.enter_context(nc.psum_tensor("ps%d" % i, [128, 512], F32)) for i in range(8)]
        ident = es.enter_context(nc.sbuf_tensor("ident", [128, 128], BF16))
        ropeP = es.enter_context(nc.sbuf_tensor("ropeP", [128, 128], BF16))
        epsb = es.enter_context(nc.sbuf_tensor("epsb", [128, 1], F32))
        T["ident"], T["ropeP"], T["epsb"] = ident, ropeP, epsb
        P = Prog(nc, sems)
        b_c = P.bufs_n(3)
        P.dma("sp", ident[:], T["d_ident"], b_c[0], writes=[b_c[0]])
        P.dma("sp", ropeP[:], T["d_ropeP"], b_c[1], writes=[b_c[1]])
        P.dma("sp", epsb[:], T["d_epsb"], b_c[2], writes=[b_c[2]])
        P.barrier()
        if "M" not in skip:
            phase_M(P, nc, T)
        if stop_after != "M":
            if "A" not in skip:
                phase_A(P, nc, T)
        if stop_after not in ("M", "A") and "B" not in skip:
            phase_B(P, nc, T)
        if stop_after not in ("M", "A", "B"):
            if "C0" not in skip:
                phase_C0(P, nc, T)
            if "CF" not in skip:
                phase_CF(P, nc, T, LP, T["tapsP"], T["d_zembP"], T["d_tposP"])
                phase_CF(P, nc, T, LS, T["tapsS"], T["d_zembS"], T["d_tposS"])
        if stop_after not in ("M", "A", "B", "CF"):
            if "CT" not in skip:
                phase_CT(P, nc, T)
            if "CS" not in skip:
                phase_CS(P, nc, T)
            if "CP" not in skip:
                phase_CP(P, nc, T)
        if stop_after not in ("M", "A", "B", "CF", "C"):
            if "D" not in skip:
                phase_outproj(P, nc, T, 0, T["a_w_out"], T["x"], "mixT", T["x1"], False)
        if stop_after not in ("M", "A", "B", "CF", "C", "D"):
            if "E" not in skip:
                phase_E(P, nc, T)
        if stop_after not in ("M", "A", "B", "CF", "C", "D", "E"):
            if "F" not in skip:
                phase_F(P, nc, T)
        if stop_after not in ("M", "A", "B", "CF", "C", "D", "E", "F"):
            if "G" not in skip:
                phase_outproj(P, nc, T, 1, T["c_w_out"], T["x1"], "mix1T", T["y"], True)
        P.barrier()
        P.flush()
    return nc


def make_in_maps(inputs):
    consts = make_consts()
    f = lambda a: np.ascontiguousarray(np.asarray(a, dtype=np.float32))
    x_prompt, x_sample = f(inputs["x_prompt"]), f(inputs["x_sample"])
    ck, cv = f(inputs["cache_k"]), f(inputs["cache_v"])
    st = f(inputs["state_lru"])
    c, c_ctx = f(inputs["c"]), f(inputs["c_ctx"])
    shared = {
        "mod_w": f(inputs["mod_w"]), "mod_b": f(inputs["mod_b"]), "norm_g": f(inputs["norm_g"]),
        "final_norm_g": f(inputs["final_norm_g"]).reshape(1, D),
        "a_w_in": f(inputs["a_w_in"])[0], "a_w_out": f(inputs["a_w_out"])[0], "a_sink": f(inputs["a_sink"]),
        "hsw": np.ascontiguousarray(np.concatenate([f(inputs["hy_short_w"])[0], f(inputs["hy_short_b"])[0][None]], 0)
                                    .reshape(4, 24, 128).transpose(2, 0, 1)),
        "hy_w1": f(inputs["hy_w1"])[0], "hy_w2": f(inputs["hy_w2"])[0],
        "hy_bb": np.ascontiguousarray(np.stack([f(inputs["hy_b1"])[0], f(inputs["hy_b2"])[0]], 1)),
        "hy_w3": f(inputs["hy_w3"])[0],
        "hy_decT": np.ascontiguousarray(f(inputs["hy_decay"])[0].reshape(32, 128).T),
        "hy_biasT": np.ascontiguousarray(f(inputs["hy_bias"])[0].reshape(16, 128).T),
        "hy_bias": f(inputs["hy_bias"])[0],
        "c_w_in": f(inputs["c_w_in"])[0], "c_w_out": f(inputs["c_w_out"])[0],
        "c_wa": f(inputs["c_wa"])[0], "c_wx": f(inputs["c_wx"])[0],
        "lcw": np.ascontiguousarray(np.concatenate([f(inputs["c_conv_w"])[0], f(inputs["c_conv_b"])[0][None]], 0)
                                    .reshape(5, 16, 128).transpose(2, 0, 1)),
        "lba": np.ascontiguousarray(f(inputs["c_ba"])[0].reshape(2, 16, 128).transpose(2, 0, 1)),
        "lbx": np.ascontiguousarray(f(inputs["c_bx"])[0].reshape(2, 16, 128).transpose(2, 0, 1)),
        "llam": np.ascontiguousarray(f(inputs["c_lambda"])[0].reshape(2, 16, 128).transpose(2, 0, 1)),
    }
    for k, v in consts.items():
        shared["c_" + k] = v
    maps = []
    for i in range(NCORES):
        m = dict(shared)
        m["x"] = np.ascontiguousarray(np.concatenate([x_sample[i], x_prompt[2 * i], x_prompt[2 * i + 1]], 0))
        m["ck"] = np.ascontiguousarray(ck[i, 0].reshape(512, 128))
        m["cv"] = np.ascontiguousarray(cv[i, 0].reshape(512, 128))
        m["st"] = np.ascontiguousarray(st[i, 0])
        m["stT"] = np.ascontiguousarray(st[i, 0].reshape(2, 16, 128).transpose(2, 0, 1))
        cvec = np.stack([c[i], c_ctx], 0)
        m["cvecT"] = np.ascontiguousarray(cvec.reshape(2, 16, 128).transpose(2, 1, 0).reshape(128, 32))
        maps.append(m)
    return maps


def kernel(**inputs):
    nc = build_program()
    maps = make_in_maps(inputs)
    res = run_bass_kernel_spmd(nc, maps, core_ids=list(range(NCORES)))
    R = res.results
    y_s = np.stack([R[i]["y"][:LS] for i in range(NCORES)], 0)
    y_p = np.concatenate([R[i]["y"][LS:].reshape(2, LP, D) for i in range(NCORES)], 0)
    nk = np.concatenate([R[i]["nk"].reshape(2, 1, LP, 2, 64) for i in range(NCORES)], 0)
    nv = np.concatenate([R[i]["nv"].reshape(2, 1, LP, 2, 64) for i in range(NCORES)], 0)
    ns = np.concatenate([R[i]["ns"].reshape(128, 2, 2, 16).transpose(1, 2, 3, 0).reshape(2, 1, 2, D)
                         for i in range(NCORES)], 0)
    return (y_p.astype(np.float32), y_s.astype(np.float32), nk.astype(np.float32), nv.astype(np.float32),
            ns.astype(np.float32))
```
